# Optimizing a Trainium2 kernel written in Bass

```python
import jax, jax.numpy as jnp
from jax import lax
import numpy as np

D_MODEL = 1024
BATCH = 8
SEQ = 2048
DEPTH = 2

N_A = DEPTH // 2
N_B = DEPTH - N_A
CHUNK = 128
A_GROUPS = 8
A_GROUP_DIM = D_MODEL // A_GROUPS
N_HEADS = 16
HEAD_DIM = D_MODEL // N_HEADS
Q_BLOCK = 128
D_FF = 4 * D_MODEL
PLE_DIM = 256
EPS = 1e-6

kernel_name = "yoco_gmlp_stickbreaking_hybrid"


def rms_norm(x, g):
    xf = x.astype(jnp.float32)
    y = xf * lax.rsqrt(jnp.mean(xf * xf, axis=-1, keepdims=True) + EPS)
    return (y * g.astype(jnp.float32)).astype(x.dtype)


def sgu_mixer(h, w_in, g_v, w_s, b_s, w_out):
    bsz, seq, _ = h.shape
    z = jax.nn.gelu(h @ w_in)
    u, v = jnp.split(z, 2, axis=-1)
    v = rms_norm(v, g_v)
    v = v.reshape(bsz, seq // CHUNK, CHUNK, A_GROUPS, A_GROUP_DIM)
    causal = jnp.tril(jnp.ones((CHUNK, CHUNK), dtype=w_s.dtype))
    w = w_s * causal[None]
    mix = jnp.einsum('gts,bcsgd->bctgd', w, v) + jnp.transpose(b_s)[None, None, :, :, None]
    y = u * mix.reshape(bsz, seq, D_MODEL)
    return y @ w_out


def sqrelu_mlp(h, w_up, w_down):
    a = jax.nn.relu(h @ w_up)
    return (a * a) @ w_down


def shared_kv(x, ln_kv, w_kv, g_k):
    bsz, seq, _ = x.shape
    h = rms_norm(x, ln_kv)
    k, v = jnp.split(h @ w_kv, 2, axis=-1)
    k = rms_norm(k.reshape(bsz, seq, N_HEADS, HEAD_DIM), g_k)
    v = v.reshape(bsz, seq, N_HEADS, HEAD_DIM)
    return jnp.transpose(k, (0, 2, 1, 3)), jnp.transpose(v, (0, 2, 1, 3))


def stick_breaking(q, k, v):
    seq = q.shape[2]
    scale = HEAD_DIM ** -0.5
    outs = []
    for blk in range(seq // Q_BLOCK):
        t0 = blk * Q_BLOCK
        t1 = t0 + Q_BLOCK
        qb = q[:, :, t0:t1].astype(jnp.float32)
        kb = k[:, :, :t1].astype(jnp.float32)
        vb = v[:, :, :t1].astype(jnp.float32)
        z = jnp.einsum('bhqd,bhkd->bhqk', qb, kb) * scale
        q_idx = t0 + jnp.arange(Q_BLOCK)[:, None]
        k_idx = jnp.arange(t1)[None, :]
        causal = k_idx < q_idx
        log_1m_beta = jnp.where(causal, jax.nn.log_sigmoid(-z), 0.0)
        between = lax.cumsum(log_1m_beta, axis=3, reverse=True) - log_1m_beta
        a = jnp.where(causal, jnp.exp(jax.nn.log_sigmoid(z) + between), 0.0)
        o = jnp.einsum('bhqk,bhkd->bhqd', a, vb)
        outs.append(o.astype(v.dtype))
    return jnp.concatenate(outs, axis=2)


def stick_breaking_mixer(h, w_q, g_q, k, v, w_out):
    bsz, seq, _ = h.shape
    q = rms_norm((h @ w_q).reshape(bsz, seq, N_HEADS, HEAD_DIM), g_q)
    q = jnp.transpose(q, (0, 2, 1, 3))
    o = stick_breaking(q, k, v)
    o = jnp.transpose(o, (0, 2, 1, 3)).reshape(bsz, seq, D_MODEL)
    return o @ w_out


def setup_inputs(seed: int = 0) -> dict:
    key = jax.random.key(seed)
    ks = jax.random.split(key, 32)

    def nrm(k, shape, scale):
        return jax.random.normal(k, shape, dtype=jnp.float32) * scale

    def gain(k, shape):
        return 1.0 + nrm(k, shape, 0.02)

    return {
        "x": nrm(ks[0], (BATCH, SEQ, D_MODEL), 1.0),
        "p": nrm(ks[1], (DEPTH, BATCH, SEQ, PLE_DIM), 1.0),
        "ln_mix_a": gain(ks[2], (N_A, D_MODEL)),
        "w_in_a": nrm(ks[3], (N_A, D_MODEL, 2 * D_MODEL), D_MODEL ** -0.5),
        "g_v_a": gain(ks[4], (N_A, D_MODEL)),
        "w_spatial": nrm(ks[5], (N_A, A_GROUPS, CHUNK, CHUNK), CHUNK ** -0.5),
        "b_spatial": 1.0 + nrm(ks[6], (N_A, A_GROUPS, CHUNK), 0.02),
        "w_out_a": nrm(ks[7], (N_A, D_MODEL, D_MODEL), D_MODEL ** -0.5),
        "ln_kv": gain(ks[8], (D_MODEL,)),
        "w_kv": nrm(ks[9], (D_MODEL, 2 * D_MODEL), D_MODEL ** -0.5),
        "g_k": gain(ks[10], (HEAD_DIM,)),
        "ln_mix_b": gain(ks[11], (N_B, D_MODEL)),
        "w_q": nrm(ks[12], (N_B, D_MODEL, D_MODEL), D_MODEL ** -0.5),
        "g_q": gain(ks[13], (N_B, HEAD_DIM)),
        "w_out_b": nrm(ks[14], (N_B, D_MODEL, D_MODEL), D_MODEL ** -0.5),
        "ln_mlp": gain(ks[15], (DEPTH, D_MODEL)),
        "w_up": nrm(ks[16], (DEPTH, D_MODEL, D_FF), D_MODEL ** -0.5),
        "w_down": nrm(ks[17], (DEPTH, D_FF, D_MODEL), D_FF ** -0.5),
        "ln_ple": gain(ks[18], (DEPTH, D_MODEL)),
        "w_ple_gate": nrm(ks[19], (DEPTH, D_MODEL, D_MODEL), D_MODEL ** -0.5),
        "w_ple_proj": nrm(ks[20], (DEPTH, PLE_DIM, D_MODEL), PLE_DIM ** -0.5),
    }


def reference(x, p, ln_mix_a, w_in_a, g_v_a, w_spatial, b_spatial, w_out_a,
              ln_kv, w_kv, g_k, ln_mix_b, w_q, g_q, w_out_b,
              ln_mlp, w_up, w_down, ln_ple, w_ple_gate, w_ple_proj):
    k_shared = None
    v_shared = None
    for i in range(DEPTH):
        if i < N_A:
            h = rms_norm(x, ln_mix_a[i])
            x = x + sgu_mixer(h, w_in_a[i], g_v_a[i], w_spatial[i], b_spatial[i], w_out_a[i])
        else:
            j = i - N_A
            h = rms_norm(x, ln_mix_b[j])
            x = x + stick_breaking_mixer(h, w_q[j], g_q[j], k_shared, v_shared, w_out_b[j])
        x = x + sqrelu_mlp(rms_norm(x, ln_mlp[i]), w_up[i], w_down[i])
        gate = jax.nn.sigmoid(rms_norm(x, ln_ple[i]) @ w_ple_gate[i])
        x = x + (p[i] @ w_ple_proj[i]) * gate
        if i == N_A - 1:
            k_shared, v_shared = shared_kv(x, ln_kv, w_kv, g_k)
    return x
```

```python
import contextlib
import numpy as np
import concourse.bass as bass
import concourse.mybir as mybir
from concourse.bass_utils import run_bass_kernel_spmd

F32 = mybir.dt.float32
BF16 = mybir.dt.bfloat16
AF = mybir.ActivationFunctionType
ALU = mybir.AluOpType

SEQ = 2048
D = 1024
NT = 4
TT = 512
NC8 = 8
EPS = 1e-6
GR = 256
ENGS = ("pe", "act", "dve", "pool", "sp")

G_MIXA, G_MLP0, G_PLE0, G_KV, G_MIXB, G_MLP1, G_PLE1, G_K, G_Q, G_QS = 0, 8, 16, 24, 32, 40, 48, 56, 57, 58


class Op:
    __slots__ = ("eng", "fn", "deps", "signal", "idx", "dma", "dsem", "dval", "count", "waits")

    def __init__(self, eng, fn, dma):
        self.eng = eng
        self.fn = fn
        self.dma = dma
        self.signal = False
        self.count = 0
        self.dsem = 0
        self.dval = 0
        self.deps = None
        self.waits = None


class Ref:
    __slots__ = ("ap", "keys")

    def __init__(self, ap, keys):
        self.ap = ap
        self.keys = keys


class Sched:
    def __init__(self, nc):
        self.nc = nc
        self.ops = []
        self.lastw = {}
        self.rd = {}
        self.streams = {}

    def op(self, eng, fn, reads=(), writes=(), dma=None):
        o = Op(eng, fn, dma)
        o.idx = len(self.ops)
        deps = {}

        def add(d):
            k = ("d", d.dma) if d.dma is not None else d.eng
            p = deps.get(k)
            if p is None or p.idx < d.idx:
                deps[k] = d

        lastw = self.lastw
        rd = self.rd
        for r in reads:
            for k in r.keys:
                w = lastw.get(k)
                if w is not None:
                    add(w)
        for r in writes:
            for k in r.keys:
                w = lastw.get(k)
                if w is not None:
                    add(w)
                rr = rd.get(k)
                if rr:
                    for x in rr.values():
                        add(x)
        if dma is not None:
            p = self.streams.get(dma)
            if p is not None:
                add(p)
            self.streams[dma] = o
        mykey = ("d", dma) if dma is not None else eng
        if eng == "pe" and dma is None:
            deps.pop("pe", None)
        o.deps = deps
        for r in reads:
            for k in r.keys:
                d = rd.get(k)
                if d is None:
                    rd[k] = {mykey: o}
                else:
                    d[mykey] = o
        for r in writes:
            for k in r.keys:
                lastw[k] = o
                rd[k] = {}
        self.ops.append(o)
        return o

    def emit(self, final_waits=()):
        nc = self.nc
        by_eng = {e: [o for o in self.ops if o.eng == e] for e in ENGS}
        for e in ENGS:
            waited = {}
            for o in by_eng[e]:
                w = []
                for k, d in o.deps.items():
                    if waited.get(k, -1) >= d.idx:
                        continue
                    waited[k] = d.idx
                    d.signal = True
                    w.append(d)
                o.waits = w
        for o in final_waits:
            o.signal = True
        cnt = {e: 0 for e in ENGS}
        streams = {}
        for o in self.ops:
            if o.dma is not None:
                st = streams.get(o.dma)
                if st is None:
                    st = streams[o.dma] = [len(streams), 0]
                st[1] += 16
                o.dsem = st[0]
                o.dval = st[1]
            elif o.signal:
                cnt[o.eng] += 1
                o.count = cnt[o.eng]
        with contextlib.ExitStack() as es:
            esem = {e: es.enter_context(nc.semaphore("s_" + e)) for e in ENGS}
            dsem = [es.enter_context(nc.semaphore("d%d" % i)) for i in range(len(streams))]
            block = es.enter_context(nc.Block())

            def run(ename, eng):
                for o in by_eng[ename]:
                    for d in o.waits:
                        if d.dma is not None:
                            eng.wait_ge(dsem[d.dsem], d.dval)
                        else:
                            eng.wait_ge(esem[d.eng], d.count)
                    ins = o.fn(eng)
                    if o.dma is not None:
                        ins.then_inc(dsem[o.dsem], 16)
                    elif o.signal:
                        ins.then_inc(esem[ename], 1)
                if ename == "sp":
                    for o in final_waits:
                        if o.dma is not None:
                            eng.wait_ge(dsem[o.dsem], o.dval)
                        else:
                            eng.wait_ge(esem[o.eng], o.count)

            @block.tensor
            def _(e):
                run("pe", e)

            @block.scalar
            def _(e):
                run("act", e)

            @block.vector
            def _(e):
                run("dve", e)

            @block.gpsimd
            def _(e):
                run("pool", e)

            @block.sync
            def _(e):
                run("sp", e)


class Tile:
    def __init__(self, base_ap, off, shape, dt, parts=128, space="S", bank=0):
        self.off = off
        self.shape = tuple(shape)
        self.es = 4 if dt == F32 else 2
        self.parts = parts
        self.space = space
        self.bank = bank
        n = 1
        for s in shape:
            n *= s
        self.n = n
        if space == "S":
            v = base_ap[0:parts, off // 2: off // 2 + n * self.es // 2]
            if dt == F32:
                v = v.bitcast(F32)
        else:
            v = base_ap[0:parts, 0:n]
        if len(shape) == 2:
            v = v.rearrange("p (a b) -> p a b", a=shape[0])
        elif len(shape) == 3:
            v = v.rearrange("p (a b c) -> p a b c", a=shape[0], b=shape[1])
        self.ap = v

    def __getitem__(self, idx):
        if not isinstance(idx, tuple):
            idx = (idx,)
        ps = idx[0]
        fidx = list(idx[1:])
        while len(fidx) < len(self.shape):
            fidx.append(slice(None))
        ap = self.ap[(ps,) + tuple(fidx)]
        rng = []
        for i, s in zip(fidx, self.shape):
            if isinstance(i, int):
                rng.append((i, i + 1))
            else:
                a = 0 if i.start is None else i.start
                b = s if i.stop is None else i.stop
                rng.append((a, b))
        strides = []
        acc = 1
        for s in reversed(self.shape):
            strides.append(acc)
            acc *= s
        strides = strides[::-1]
        keys = set()
        outer = [()]
        for (a, b) in rng[:-1]:
            outer = [o + (i,) for o in outer for i in range(a, b)]
        la, lb = rng[-1]
        gr = GR if self.space == "S" else 512
        for o in outer:
            base = sum(i * st for i, st in zip(o, strides[:-1]))
            b0 = self.off + (base + la) * self.es
            b1 = self.off + (base + lb) * self.es
            for g in range(b0 // gr, (b1 - 1) // gr + 1):
                keys.add((self.space, self.bank, g))
        return Ref(ap, keys)


def build_program(nc, stop_after=99):
    dt_in = {}

    def din(name, shape):
        t = nc.dram_tensor(name, list(shape), F32, kind="ExternalInput").ap()
        dt_in[name] = t
        return t

    x = din("x", [SEQ, D])
    p = din("p", [2, SEQ, 256])
    ln_mix_a = din("ln_mix_a", [1, D])
    w_in_a = din("w_in_a", [1, D, 2 * D])
    g_v_a = din("g_v_a", [1, D])
    w_spatial = din("w_spatial", [1, 8, 128, 128])
    b_spatial = din("b_spatial", [1, 8, 128])
    w_out_a = din("w_out_a", [1, D, D])
    ln_kv = din("ln_kv", [D])
    w_kv = din("w_kv", [D, 2 * D])
    g_k = din("g_k", [64])
    ln_mix_b = din("ln_mix_b", [1, D])
    w_q = din("w_q", [1, D, D])
    g_q = din("g_q", [1, 64])
    w_out_b = din("w_out_b", [1, D, D])
    ln_mlp = din("ln_mlp", [2, D])
    w_up = din("w_up", [2, D, 4 * D])
    w_down = din("w_down", [2, 4 * D, D])
    ln_ple = din("ln_ple", [2, D])
    w_ple_gate = din("w_ple_gate", [2, D, D])
    w_ple_proj = din("w_ple_proj", [2, 256, D])
    c_ident = din("c_ident", [128, 128])
    c_maskle = din("c_maskle", [128, 128])
    c_pack = din("c_pack", [128, 784])
    c_ssel = din("c_ssel", [128, 2048])
    y = nc.dram_tensor("y", [SEQ, D], F32, kind="ExternalOutput").ap()

    es = contextlib.ExitStack()
    with es:
        ARENA_BYTES = 212736
        arena = es.enter_context(nc.sbuf_tensor("arena", [128, ARENA_BYTES // 2], BF16))
        banks = [es.enter_context(nc.psum_tensor("ps%d" % i, [128, 512], F32)) for i in range(8)]
        PB = [Tile(banks[i], 0, (512,), F32, space="P", bank=i) for i in range(8)]
        PB4 = [Tile(banks[i], 0, (4, 128), F32, space="P", bank=i) for i in range(8)]
        S = Sched(nc)

        def T(off, shape, dt, parts=128):
            return Tile(arena, off, shape, dt, parts=parts)

        XT = T(0, (8, SEQ), F32)
        HT = T(65536, (8, SEQ), BF16)
        WOFF = [98304 + i * 16384 for i in range(4)]
        SCR0 = 163840
        coff = ARENA_BYTES
        def calloc(nbytes):
            nonlocal coff
            coff -= (nbytes + 255) // 256 * 256
            return coff
        IDENT = T(calloc(512), (128,), F32)
        MASKLE = T(calloc(512), (128,), F32)
        CP = T(calloc(1568), (784,), BF16)
        SSEL = T(calloc(4096), (16, 128), BF16)
        GCOLS = T(calloc(256), (64,), F32)
        BSROW = T(calloc(2048), (1024,), BF16, parts=1)
        SMALL = T(calloc(256), (64,), F32)
        WP = T(calloc(4096), (2, 1024), BF16)
        SCR_END = coff
        ONES_S = lambda: CP[:, 0:128]
        ONES1 = lambda ps=slice(0, 128): CP[ps, 128:256]
        BDIAG = lambda: CP[:, 256:384]
        TRINEG = lambda: CP[:, 384:512]
        MASK01 = lambda: CP[:, 512:640]
        ESEL = lambda kb: CP[:, 640 + 15 - kb: 640 + 15 - kb + 128]

        scr = [SCR0]

        def salloc(shape, dt, parts=128):
            n = 1
            for s in shape:
                n *= s
            nb = n * (4 if dt == F32 else 2)
            nb = (nb + 255) // 256 * 256
            off = scr[0]
            scr[0] += nb
            assert scr[0] <= SCR_END, ("scratch overflow", scr[0], SCR_END)
            return T(off, shape, dt, parts=parts)

        def sreset():
            scr[0] = SCR0

        def act(out, in_, func, bias=None, scale=None, accum=None, extra=()):
            kw = {}
            if bias is not None:
                kw["bias"] = bias.ap if isinstance(bias, Ref) else bias
            if scale is not None:
                kw["scale"] = scale.ap if isinstance(scale, Ref) else scale
            if accum is not None:
                kw["accum_out"] = accum.ap
            rds = [in_] + [r for r in (bias, scale) if isinstance(r, Ref)] + list(extra)
            wr = [out] + ([accum] if accum is not None else [])
            return S.op("act", lambda e: e.activation(out.ap, in_.ap, func, **kw), reads=rds, writes=wr)

        def tt(eng, out, a, b, op):
            return S.op(eng, lambda e: e.tensor_tensor(out.ap, a.ap, b.ap, op), reads=[a, b], writes=[out])

        def stt(out, a, sc, b, op0, op1):
            scv = sc.ap if isinstance(sc, Ref) else sc
            rds = [a, b] + ([sc] if isinstance(sc, Ref) else [])
            return S.op("dve", lambda e: e.scalar_tensor_tensor(out.ap, a.ap, scv, b.ap, op0, op1), reads=rds, writes=[out])

        def ts(eng, out, a, s1, s2, op0, op1=None):
            s1v = s1.ap if isinstance(s1, Ref) else s1
            s2v = s2.ap if isinstance(s2, Ref) else s2
            rds = [a] + [r for r in (s1, s2) if isinstance(r, Ref)]
            if op1 is None:
                return S.op(eng, lambda e: e.tensor_scalar(out.ap, a.ap, s1v, None, op0), reads=rds, writes=[out])
            return S.op(eng, lambda e: e.tensor_scalar(out.ap, a.ap, s1v, s2v, op0, op1), reads=rds, writes=[out])

        def cp(eng, out, in_):
            if eng == "act":
                return act(out, in_, AF.Copy)
            return S.op(eng, lambda e: e.tensor_copy(out.ap, in_.ap), reads=[in_], writes=[out])

        def recip(out, in_):
            return S.op("dve", lambda e: e.reciprocal(out.ap, in_.ap), reads=[in_], writes=[out])

        def mm(out, pairs, start=True, stop=True):
            n = len(pairs)

            def fn(e):
                ins = None
                for i, (l, r) in enumerate(pairs):
                    ins = e.matmul(out.ap, l.ap, r.ap, start=(start and i == 0), stop=(stop and i == n - 1))
                return ins
            rds = []
            for l, r in pairs:
                rds.append(l)
                rds.append(r)
            return S.op("pe", fn, reads=rds, writes=[out])

        def tr(out, in_, ident):
            return S.op("pe", lambda e: e.transpose(out.ap, in_.ap, ident.ap), reads=[in_, ident], writes=[out])

        def dma(eng, out, in_ap, stream, reads=()):
            return S.op(eng, lambda e: e.dma_start(out=out.ap, in_=in_ap), reads=list(reads), writes=[out], dma=stream)

        class Rot:
            def __init__(self, ids):
                self.ids = list(ids)
                self.i = 0

            def next(self):
                b = self.ids[self.i % len(self.ids)]
                self.i += 1
                return b

        rot = Rot(range(8))
        tog = [0]

        def evac_eng():
            tog[0] ^= 1
            return "act" if tog[0] else "dve"

        def wslot(n, shape):
            return T(WOFF[n % 4], shape, BF16)

        def kp(ap2d):
            return ap2d.rearrange("(k p) f -> p k f", p=128)

        loaders = []

        def L_full(src2d, c0, c1):
            def f(n):
                w = wslot(n, (8, c1 - c0))
                dma("pool", w[:, :, :], kp(src2d)[:, :, c0:c1], "W%da" % (n % 4))
            return f

        def L_mlp(l, gi):
            def f(n):
                wu = Tile(arena, WOFF[n % 4], (8, 512), BF16)
                wd = Tile(arena, WOFF[n % 4] + 8192, (4, 1024), BF16)
                dma("pool", wu[:, :, :], kp(w_up[l])[:, :, gi * 512:(gi + 1) * 512], "W%da" % (n % 4))
                dma("pool", wd[:, :, :], kp(w_down[l][gi * 512:(gi + 1) * 512, :]), "W%db" % (n % 4))
            return f

        loaders.append(L_full(w_in_a[0], 0, 1024))
        loaders.append(L_full(w_in_a[0], 1024, 2048))
        loaders.append(L_full(w_out_a[0], 0, 1024))
        for gi in range(8):
            loaders.append(L_mlp(0, gi))
        loaders.append(L_full(w_ple_gate[0], 0, 1024))
        loaders += [None, None, None]
        for gi in range(8):
            loaders.append(L_mlp(1, gi))
        loaders.append(L_full(w_ple_gate[1], 0, 1024))
        issued = [0]

        def w_issue_upto(n):
            while issued[0] <= n and issued[0] < len(loaders):
                f = loaders[issued[0]]
                if f is not None:
                    f(issued[0])
                issued[0] += 1

        def w_done(n):
            w_issue_upto(n + 4)

        sreset()
        dma("sp", IDENT[:, :], c_ident, "c0")
        dma("sp", MASKLE[:, :], c_maskle, "c1")
        dma("pool", CP[:, :], c_pack, "c2")
        dma("pool", SSEL[:, :, :], c_ssel.rearrange("p (a b) -> p a b", a=16), "c3")
        dma("pool", BSROW[0:1, :], b_spatial[0].rearrange("(o g) t -> o (g t)", o=1), "c4")
        dma("pool", WP[:, :, :], kp(w_ple_proj[0]), "wp")
        w_issue_upto(2)
        gst = salloc((128,), F32, parts=64)
        gi_ = 0
        for src in (ln_mix_a[0], ln_mlp[0], ln_ple[0], ln_kv, ln_mix_b[0], ln_mlp[1], ln_ple[1]):
            dma("sp", gst[gi_ * 8:(gi_ + 1) * 8, :], src.rearrange("(k f) -> k f", f=128), "g%d" % gi_)
            gi_ += 1
        gk2 = g_k.rearrange("(o f) -> o f", o=1)
        dma("sp", gst[56:57, 0:64], gk2, "g7")
        dma("sp", gst[56:57, 64:128], gk2, "g8")
        dma("sp", gst[57:58, 0:64], g_q, "g9")
        dma("sp", gst[57:58, 64:128], g_q, "g10")
        b = rot.next()
        tr(PB[b][:, 0:58], gst[0:58, :], IDENT[0:58, 0:58])
        cp("dve", GCOLS[:, 0:58], PB[b][:, 0:58])
        ts("dve", GCOLS[:, 58:59], GCOLS[:, 57:58], 0.125, None, ALU.mult)

        xs = [T(WOFF[3] + i * 4096, (1024,), F32) for i in range(4)]

        def xload(t):
            for blk in range(t * 4, t * 4 + 4):
                st = xs[blk % 4]
                dma("sp", st[:, :], x[blk * 128:(blk + 1) * 128, :], "x%d" % (blk % 4))
                for half in range(2):
                    b = rot.next()
                    for cc in range(4):
                        c = half * 4 + cc
                        tr(PB4[b][:, cc, :], st[:, c * 128:(c + 1) * 128], IDENT[:, :])
                    cp(evac_eng(), XT[:, half * 4:(half + 1) * 4, blk * 128:(blk + 1) * 128], PB4[b][:, :, :])

        if stop_after < 1:
            for t in range(NT):
                xload(t)

        def norm_A(t, sq):
            tl = slice(t * TT, (t + 1) * TT)
            for c in range(8):
                act(sq[:, c, :], XT[:, c, tl], AF.Square)

        def norm_B(t, gcol0, sq, sd, rstd, dst=HT):
            tl = slice(t * TT, (t + 1) * TT)
            b = rot.next()
            mm(PB[b][:, :], [(ONES_S(), sq[:, c, :]) for c in range(8)])
            act(sd[:, :], PB[b][:, :], AF.Ln, bias=EPS)
            act(rstd[:, :], sd[:, :], AF.Exp, scale=-0.5)
            for c in range(8):
                if gcol0 is None:
                    tt("dve", dst[:, c, tl], XT[:, c, tl], rstd[:, :], ALU.mult)
                else:
                    stt(dst[:, c, tl], XT[:, c, tl], GCOLS[:, gcol0 + c:gcol0 + c + 1], rstd[:, :], ALU.mult, ALU.mult)

        def norm_tile(t, gcol0, sq, sd, rstd, dst=HT):
            norm_A(t, sq)
            norm_B(t, gcol0, sq, sd, rstd, dst)

        if stop_after >= 1:
            sreset()
            uT = salloc((8, TT), BF16)
            sq = uT
            yT = salloc((8, TT), BF16)
            v16 = [salloc((1024,), BF16) for _ in range(2)]
            vn = [salloc((1024,), BF16) for _ in range(2)]
            sd = salloc((TT,), F32)
            rstd = salloc((TT,), F32)
            wsT = salloc((8, 128), BF16)
            gvt = salloc((1024,), F32)
            wsl = T(yT.off, (8, 128), F32)
            dma("sp", gvt[:, :], g_v_a.partition_broadcast(128), "gv")
            dma("sp", wsl[:, :, :], w_spatial[0].rearrange("g t s -> t g s"), "ws")
            xload(0)
            for hb in range(2):
                b = rot.next()
                for gg in range(4):
                    tr(PB4[b][:, gg, :], wsl[:, hb * 4 + gg, :], IDENT[:, :])
                tt("dve", wsT[:, hb * 4:(hb + 1) * 4, :], PB4[b][:, :, :],
                   Ref(MASKLE.ap.unsqueeze(1).to_broadcast([128, 4, 128]), MASKLE[:, :].keys), ALU.mult)
            W0 = wslot(0, (8, 1024))
            W1 = wslot(1, (8, 1024))
            W2 = wslot(2, (8, 1024))
            norm_A(0, sq)
            norm_B(0, G_MIXA, sq, sd, rstd)
            for t in range(NT):
                tl = slice(t * TT, (t + 1) * TT)
                if t + 1 < NT:
                    xload(t + 1)
                    if t + 1 == NT - 1:
                        w_issue_upto(3)
                for f in range(8):
                    b = rot.next()
                    mm(PB[b][:, :], [(W0[:, k, f * 128:(f + 1) * 128], HT[:, k, tl]) for k in range(8)])
                    act(uT[:, f, :], PB[b][:, :], AF.Gelu_apprx_tanh)

                def VST(blk, t=t):
                    tb = t * 4 + blk
                    tbl = slice(tb * 128, (tb + 1) * 128)
                    for half in range(2):
                        b = rot.next()
                        mm(PB[b][:, :], [(HT[:, k, tbl], W1[:, k, half * 512:(half + 1) * 512]) for k in range(8)])
                        act(v16[blk % 2][:, half * 512:(half + 1) * 512], PB[b][:, :], AF.Gelu_apprx_tanh)

                def NST(blk):
                    vb = vn[blk % 2]
                    vv = v16[blk % 2]
                    c4 = (blk % 2) * 4
                    act(vb[:, :], vv[:, :], AF.Square, accum=SMALL[:, c4:c4 + 1])
                    ts("dve", SMALL[:, c4 + 1:c4 + 2], SMALL[:, c4:c4 + 1], 1.0 / 1024, EPS, ALU.mult, ALU.add)
                    act(SMALL[:, c4 + 2:c4 + 3], SMALL[:, c4 + 1:c4 + 2], AF.Sqrt)
                    recip(SMALL[:, c4 + 3:c4 + 4], SMALL[:, c4 + 2:c4 + 3])
                    stt(vb[:, :], vv[:, :], SMALL[:, c4 + 3:c4 + 4], gvt[:, :], ALU.mult, ALU.mult)

                def SPST(blk):
                    vb = vn[blk % 2]
                    for hb in range(2):
                        b = rot.next()

                        def fn(e, b=b, hb=hb, vb=vb):
                            ins = None
                            for gg in range(4):
                                g = hb * 4 + gg
                                e.matmul(PB4[b][:, gg, :].ap, vb[:, g * 128:(g + 1) * 128].ap, wsT[:, g, :].ap,
                                         start=True, stop=False)
                                ins = e.matmul(PB4[b][:, gg, :].ap, ONES1(slice(0, 1)).ap,
                                               BSROW[0:1, g * 128:(g + 1) * 128].ap, start=False, stop=True)
                            return ins
                        S.op("pe", fn, reads=[vb[:, hb * 512:(hb + 1) * 512], wsT[:, hb * 4:(hb + 1) * 4, :],
                                              ONES1(slice(0, 1)), BSROW[0:1, :]], writes=[PB4[b][:, :, :]])
                        tt("dve", yT[:, hb * 4:(hb + 1) * 4, blk * 128:(blk + 1) * 128], PB4[b][:, :, :],
                           uT[:, hb * 4:(hb + 1) * 4, blk * 128:(blk + 1) * 128], ALU.mult)

                VST(0)
                VST(1)
                NST(0)
                for blk in range(4):
                    if blk + 2 < 4:
                        VST(blk + 2)
                    if blk + 1 < 4:
                        NST(blk + 1)
                    SPST(blk)
                if t + 1 < NT:
                    norm_A(t + 1, sq)
                for dch in range(8):
                    b = rot.next()
                    mm(PB[b][:, :], [(W2[:, k, dch * 128:(dch + 1) * 128], yT[:, k, :]) for k in range(8)])
                    tt("dve", XT[:, dch, tl], PB[b][:, :], XT[:, dch, tl], ALU.add)
                if t + 1 < NT:
                    norm_B(t + 1, G_MIXA, sq, sd, rstd)
            w_done(0)
            w_done(1)
            w_done(2)

        def mlp_phase(l, blk0, gcol0):
            sreset()
            sq = salloc((8, TT), BF16)
            sd = salloc((TT,), F32)
            rstd = salloc((TT,), F32)
            aT = [salloc((8, TT), BF16) for _ in range(2)]
            rt = [salloc((TT,), BF16) for _ in range(2)]
            for t in range(NT):
                norm_tile(t, gcol0, sq, sd, rstd)
            it = 0
            for sg in range(4):
                nA, nB = blk0 + 2 * sg, blk0 + 2 * sg + 1
                wu = [Tile(arena, WOFF[n % 4], (8, 512), BF16) for n in (nA, nB)]
                wd = [Tile(arena, WOFF[n % 4] + 8192, (4, 1024), BF16) for n in (nA, nB)]
                for t in range(NT):
                    tl = slice(t * TT, (t + 1) * TT)
                    a = aT[it % 2]
                    it += 1
                    for f in range(8):
                        b = rot.next()
                        mm(PB[b][:, :], [(wu[f // 4][:, k, (f % 4) * 128:(f % 4 + 1) * 128], HT[:, k, tl]) for k in range(8)])
                        r = rt[f % 2]
                        act(r[:, :], PB[b][:, :], AF.Relu)
                        tt("dve", a[:, f, :], r[:, :], r[:, :], ALU.mult)
                    for dch in range(8):
                        b = rot.next()
                        mm(PB[b][:, :], [(wd[f // 4][:, f % 4, dch * 128:(dch + 1) * 128], a[:, f, :]) for f in range(8)])
                        tt("dve", XT[:, dch, tl], PB[b][:, :], XT[:, dch, tl], ALU.add)
                w_done(nA)
                w_done(nB)

        out_fin = []

        def emit_out(t, osb):
            for blk in range(t * 4, t * 4 + 4):
                ot = osb[blk % 2]
                for half in range(2):
                    b = rot.next()
                    for cc in range(4):
                        c = half * 4 + cc
                        tr(PB4[b][:, cc, :], XT[:, c, blk * 128:(blk + 1) * 128], IDENT[:, :])
                    cp(evac_eng(), ot[:, half * 512:(half + 1) * 512], PB[b][:, :])
                out_fin.append(S.op("sp", lambda e, ot=ot, blk=blk: e.dma_start(out=y[blk * 128:(blk + 1) * 128, :], in_=ot[:, :].ap),
                                    reads=[ot[:, :]], dma="o%d" % (blk % 2)))

        def ple_phase(l, blkn, gcol0, with_out=False):
            sreset()
            sq = salloc((8, TT), BF16)
            sd = salloc((TT,), F32)
            rstd = salloc((TT,), F32)
            pst = salloc((4, 256), F32)
            pT = salloc((2, TT), BF16)
            gate = [salloc((TT,), F32) for _ in range(2)]
            tmp2 = [salloc((TT,), F32) for _ in range(2)]
            osb = [salloc((1024,), F32) for _ in range(2)] if with_out else None
            Wg = wslot(blkn, (8, 1024))
            for t in range(NT):
                tl = slice(t * TT, (t + 1) * TT)
                dma("sp", pst[:, :, :], p[l][t * TT:(t + 1) * TT, :].rearrange("(b q) f -> q b f", q=128), "pl")
                norm_tile(t, gcol0, sq, sd, rstd)
                for kk in range(2):
                    b = rot.next()
                    for blk in range(4):
                        tr(PB4[b][:, blk, :], pst[:, blk, kk * 128:(kk + 1) * 128], IDENT[:, :])
                    cp(evac_eng(), pT[:, kk, :], PB[b][:, :])
                for dch in range(8):
                    b1 = rot.next()
                    mm(PB[b1][:, :], [(Wg[:, k, dch * 128:(dch + 1) * 128], HT[:, k, tl]) for k in range(8)])
                    g_ = gate[dch % 2]
                    act(g_[:, :], PB[b1][:, :], AF.Sigmoid)
                    b2 = rot.next()
                    mm(PB[b2][:, :], [(WP[:, kk, dch * 128:(dch + 1) * 128], pT[:, kk, :]) for kk in range(2)])
                    t2 = tmp2[dch % 2]
                    tt("dve", t2[:, :], PB[b2][:, :], g_[:, :], ALU.mult)
                    tt("dve", XT[:, dch, tl], XT[:, dch, tl], t2[:, :], ALU.add)
                if with_out:
                    emit_out(t, osb)
            w_done(blkn)

        if stop_after >= 2:
            mlp_phase(0, 3, G_MLP0)
        if stop_after >= 3:
            ple_phase(0, 11, G_PLE0)
            dma("pool", WP[:, :, :], kp(w_ple_proj[1]), "wp")

        if stop_after >= 4:
            sreset()
            sq = salloc((8, TT), BF16)
            sd = salloc((TT,), F32)
            rstd = salloc((TT,), F32)
            for t in range(NT):
                norm_tile(t, None, sq, sd, rstd)
            sreset()
            KT = salloc((SEQ,), BF16)
            QTh = [salloc((SEQ,), BF16) for _ in range(2)]
            VJ = salloc((16, 128), BF16)
            OT = salloc((SEQ,), BF16)
            e1 = [salloc((TT,), F32) for _ in range(2)]
            AT = [salloc((TT,), BF16) for _ in range(3)]
            csb = [salloc((TT,), BF16) for _ in range(2)]
            PW = [Tile(arena, WOFF[0] + i * 8192, (4096,), BF16) for i in range(2)]
            LS = [Tile(arena, WOFF[1], (16, TT), BF16), Tile(arena, WOFF[2], (16, TT), BF16)]
            sq4 = Tile(arena, WOFF[2], (4, TT), BF16)
            sd4 = Tile(arena, WOFF[2] + 4096, (4, TT), F32)
            rs4 = Tile(arena, WOFF[2] + 4096 + 8192, (4, TT), BF16)
            srot = Rot(range(5))
            zrot = Rot([0, 1])
            grot = Rot([2, 3])
            frot = Rot([6, 7])
            JUNK = 7

            def dummy(k):
                def fn(e):
                    ins = None
                    for _ in range(k):
                        ins = e.matmul(PB[JUNK][:, :].ap, ONES_S().ap, CP[:, 0:512].ap, start=True, stop=True)
                    return ins
                S.op("pe", fn, reads=[ONES_S(), CP[:, 0:512]], writes=[PB[JUNK][:, :]])
            CSB = 4
            OB = [5, 5]
            for bi in range(2):
                S.op("dve", lambda e, bi=bi: e.memset(csb[bi][:, :].ap, 0.0), writes=[csb[bi][:, :]])
                S.op("dve", lambda e, bi=bi: e.memset(QTh[bi][:, :].ap, 0.0), writes=[QTh[bi][:, :]])

            def pw_views(i):
                t_ = PW[i]
                off = t_.off
                wk = Tile(arena, off, (8, 128), BF16)
                wv = Tile(arena, off + 2048, (8, 128), BF16)
                wq = Tile(arena, off + 4096, (8, 128), BF16)
                wo = Tile(arena, off + 6144, (1024,), BF16)
                return wk, wv, wq, wo

            def load_pair(j):
                wk, wv, wq, wo = pw_views(j % 2)
                sl = slice(j * 128, (j + 1) * 128)
                dma("pool", wk[:, :, :], kp(w_kv)[:, :, j * 128:(j + 1) * 128], "pw%da" % (j % 2))
                dma("pool", wv[:, :, :], kp(w_kv)[:, :, 1024 + j * 128:1024 + (j + 1) * 128], "pw%db" % (j % 2))
                dma("pool", wq[:, :, :], kp(w_q[0])[:, :, j * 128:(j + 1) * 128], "pw%dc" % (j % 2))
                dma("pool", wo[:, :], w_out_b[0][sl, :], "pw%dd" % (j % 2))
                for w_, g0 in ((wk, G_KV), (wv, G_KV), (wq, G_MIXB)):
                    gb = Ref(GCOLS.ap[:, g0:g0 + 8].unsqueeze(2).to_broadcast([128, 8, 128]), GCOLS[:, g0:g0 + 8].keys)
                    tt("dve", w_[:, :, :], w_[:, :, :], gb, ALU.mult)

            load_pair(0)
            for j in range(8):
                wk, wv, wq, wo = pw_views(j % 2)
                if j + 1 < 8:
                    load_pair(j + 1)
                for (w_, dst, gc) in ((wk, KT, G_K), (wq, None, G_QS)):
                    mb = [0, 1, 2, 3]
                    sb_ = [4, 5, 6, 7]
                    for t in range(NT):
                        tl = slice(t * TT, (t + 1) * TT)
                        mm(PB[mb[t]][:, :], [(w_[:, k, :], HT[:, k, tl]) for k in range(8)])
                    for t in range(NT):
                        act(sq4[:, t, :], PB[mb[t]][:, :], AF.Square)
                    for t in range(NT):
                        mm(PB[sb_[t]][:, :], [(BDIAG(), sq4[:, t, :])])
                    for t in range(NT):
                        act(sd4[:, t, :], PB[sb_[t]][:, :], AF.Ln, bias=EPS)
                    for t in range(NT):
                        act(rs4[:, t, :], sd4[:, t, :], AF.Exp, scale=-0.5)
                    for t in range(NT):
                        tl = slice(t * TT, (t + 1) * TT)
                        if dst is not None:
                            stt(dst[:, tl], PB[mb[t]][:, :], GCOLS[:, gc:gc + 1], rs4[:, t, :], ALU.mult, ALU.mult)
                        else:
                            for h_ in range(2):
                                hp = slice(h_ * 64, (h_ + 1) * 64)
                                stt(QTh[h_][hp, tl], PB[mb[t]][hp, :], GCOLS[hp, gc:gc + 1], rs4[hp, t, :], ALU.mult, ALU.mult)
                for q4 in range(4):
                    b = srot.next()
                    for bb in range(4):
                        blk = q4 * 4 + bb
                        mm(PB4[b][:, bb, :], [(HT[:, k, blk * 128:(blk + 1) * 128], wv[:, k, :]) for k in range(8)])
                    cp("dve", VJ[:, q4 * 4:(q4 + 1) * 4, :], PB4[b][:, :, :])
                heads = (slice(0, 64), slice(64, 128))

                def cols(qt, kb):
                    c0 = max(0, kb - 4 * qt) * 128
                    return c0, slice(c0, TT), slice(qt * TT + c0, (qt + 1) * TT)

                def p1_steps(h, qt):
                    nkb = 4 * qt + 4
                    L = LS[h]
                    zb = {}

                    def Z(kb):
                        c0, cs, qs = cols(qt, kb)
                        zb[kb] = zrot.next()
                        mm(PB[zb[kb]][:, cs], [(KT[:, kb * 128:(kb + 1) * 128], QTh[h][:, qs])])

                    def EXP(kb):
                        c0, cs, qs = cols(qt, kb)
                        act(e1[kb % 2][:, cs], PB[zb[kb]][:, cs], AF.Exp)

                    def LN(kb):
                        c0, cs, qs = cols(qt, kb)
                        act(L[:, kb, cs], e1[kb % 2][:, cs], AF.Ln, bias=1.0)
                        if kb >= 4 * qt:
                            tt("pool", L[:, kb, c0:c0 + 128], L[:, kb, c0:c0 + 128], MASK01(), ALU.mult)

                    def CS(kb):
                        c0, cs, qs = cols(qt, kb)
                        mm(PB[CSB][:, cs], [(ESEL(kb), L[:, kb, cs])], start=(kb == 0), stop=(kb == nkb - 1))

                    steps = []
                    for i in range(nkb + 3):
                        def st(i=i):
                            if i < nkb:
                                Z(i)
                            if 0 <= i - 1 < nkb:
                                EXP(i - 1)
                            if 0 <= i - 2 < nkb:
                                LN(i - 2)
                            if 0 <= i - 3 < nkb:
                                CS(i - 3)
                        steps.append(st)

                    def fin():
                        cp("dve", csb[h][0:16, :], PB[CSB][0:16, :])
                    steps.append(fin)
                    return steps

                def p2_steps(h, qt):
                    nkb = 4 * qt + 4
                    L = LS[h]
                    pr = heads[h]
                    gb = {}

                    def G(kb):
                        c0, cs, qs = cols(qt, kb)
                        b = gb[kb] = grot.next()

                        def fn(e):
                            e.matmul(PB[b][:, cs].ap, SSEL[:, kb, :].ap, csb[h][:, cs].ap, start=True, stop=False)
                            e.matmul(PB[b][:, cs].ap, TRINEG().ap, L[:, kb, cs].ap, start=False, stop=False)
                            return e.matmul(PB[b][:, cs].ap, KT[:, kb * 128:(kb + 1) * 128].ap, QTh[h][:, qs].ap,
                                            start=False, stop=True)
                        S.op("pe", fn, reads=[TRINEG(), L[:, kb, cs], SSEL[:, kb, :], csb[h][:, cs],
                                              KT[:, kb * 128:(kb + 1) * 128], QTh[h][:, qs]], writes=[PB[b][:, cs]])

                    def EXPA(kb):
                        c0, cs, qs = cols(qt, kb)
                        a_ = AT[kb % 3]
                        act(a_[:, cs], PB[gb[kb]][:, cs], AF.Exp)
                        if kb >= 4 * qt:
                            tt("pool", a_[:, c0:c0 + 128], a_[:, c0:c0 + 128], MASK01(), ALU.mult)

                    def AV(kb):
                        c0, cs, qs = cols(qt, kb)
                        mm(PB[OB[h]][:, cs], [(VJ[:, kb, :], AT[kb % 3][:, cs])], start=(kb == 0), stop=(kb == nkb - 1))

                    steps = []
                    for i in range(nkb + 1):
                        def st(i=i):
                            if i < nkb:
                                G(i)
                            if 0 <= i - 1 < nkb:
                                EXPA(i - 1)
                                AV(i - 1)
                        steps.append(st)

                    def fin():
                        cp("dve", OT[pr, qt * TT:(qt + 1) * TT], PB[OB[h]][pr, :])
                    steps.append(fin)
                    return steps

                pending = []

                def filler(k):
                    for _ in range(k):
                        if not pending:
                            return
                        qt_, dch = pending.pop(0)
                        tl = slice(qt_ * TT, (qt_ + 1) * TT)
                        b = frot.next()
                        mm(PB[b][:, :], [(wo[:, dch * 128:(dch + 1) * 128], OT[:, tl])])
                        tt("dve", XT[:, dch, tl], PB[b][:, :], XT[:, dch, tl], ALU.add)

                units = [(h, qt) for qt in range(NT) for h in range(2)]
                prev = None
                for u in units + [None]:
                    s1 = p1_steps(*u) if u is not None else []
                    s2 = p2_steps(*prev) if prev is not None else []
                    n = max(len(s1), len(s2))
                    for i in range(n):
                        if i < len(s2):
                            s2[i]()
                        if i < len(s1):
                            s1[i]()
                        if i == n // 2 or i == n - 1:
                            filler(2)
                    if prev is not None and prev[0] == 1:
                        pending.extend((prev[1], dch) for dch in range(8))
                    prev = u
                filler(len(pending))
            w_done(12)
            w_done(13)
            w_done(14)

        if stop_after >= 5:
            w_issue_upto(18)
            mlp_phase(1, 15, G_MLP1)
        if stop_after >= 6:
            ple_phase(1, 23, G_PLE1, with_out=True)

        if not out_fin:
            sreset()
            osb = [salloc((1024,), F32) for _ in range(2)]
            for t in range(NT):
                emit_out(t, osb)
        S.emit(final_waits=out_fin[-2:])
    return nc


def make_consts():
    ident = np.eye(128, dtype=np.float32)
    s = np.arange(128)[:, None]
    t = np.arange(128)[None, :]
    maskle = (s <= t).astype(np.float32)
    pack = np.zeros((128, 784), np.float32)
    pack[:, 0:128] = 1.0 / 1024
    pack[:, 128:256] = 1.0
    bd = np.zeros((128, 128), np.float32)
    bd[0:64, 0:64] = 1.0 / 64
    bd[64:128, 64:128] = 1.0 / 64
    pack[:, 256:384] = bd
    pack[:, 384:512] = -(s >= t).astype(np.float32)
    pack[:, 512:640] = (s < t).astype(np.float32)
    pack[:, 640 + 15] = -1.0
    pack[:, 640 + 47] = -1.0
    ssel = np.zeros((128, 16, 128), np.float32)
    for kb in range(16):
        for r in range(16):
            if r > kb:
                ssel[r, kb, :] = 1.0
                ssel[32 + r, kb, :] = 1.0
    return {"c_ident": ident, "c_maskle": maskle, "c_pack": pack, "c_ssel": ssel.reshape(128, 2048)}


_CACHE = {}


def kernel(**inputs):
    n = 8
    if "nc" not in _CACHE:
        nc = bass.Bass("TRN2", target_bir_lowering=False)
        build_program(nc)
        _CACHE["nc"] = nc
    nc = _CACHE["nc"]
    consts = make_consts()
    shared = {k: np.ascontiguousarray(np.asarray(v, dtype=np.float32)) for k, v in inputs.items() if k not in ("x", "p")}
    shared.update(consts)
    x = np.asarray(inputs["x"], dtype=np.float32)
    p = np.asarray(inputs["p"], dtype=np.float32)
    in_maps = []
    for i in range(n):
        m = dict(shared)
        m["x"] = np.ascontiguousarray(x[i])
        m["p"] = np.ascontiguousarray(p[:, i])
        in_maps.append(m)
    res = run_bass_kernel_spmd(nc, in_maps, core_ids=list(range(n)))
    return np.stack([np.asarray(r["y"], dtype=np.float32) for r in res.results], axis=0)
```

```python
import contextlib
import numpy as np
import concourse.bass as bass
import concourse.mybir as mybir
from concourse.bass_utils import run_bass_kernel_spmd

F32 = mybir.dt.float32
BF16 = mybir.dt.bfloat16
AF = mybir.ActivationFunctionType
ALU = mybir.AluOpType

SEQ = 2048
D = 1024
NT = 4
TT = 512
NC8 = 8
EPS = 1e-6
GR = 256
ENGS = ("pe", "act", "dve", "pool", "sp")

G_MIXA, G_MLP0, G_PLE0, G_KV, G_MIXB, G_MLP1, G_PLE1, G_K, G_Q, G_QS = 0, 8, 16, 24, 32, 40, 48, 56, 57, 58


class Op:
    __slots__ = ("eng", "fn", "deps", "signal", "idx", "dma", "dsem", "dval", "count", "waits")

    def __init__(self, eng, fn, dma):
        self.eng = eng
        self.fn = fn
        self.dma = dma
        self.signal = False
        self.count = 0
        self.dsem = 0
        self.dval = 0
        self.deps = None
        self.waits = None


class Ref:
    __slots__ = ("ap", "keys")

    def __init__(self, ap, keys):
        self.ap = ap
        self.keys = keys


class Sched:
    def __init__(self, nc):
        self.nc = nc
        self.ops = []
        self.lastw = {}
        self.rd = {}
        self.streams = {}

    def op(self, eng, fn, reads=(), writes=(), dma=None):
        o = Op(eng, fn, dma)
        o.idx = len(self.ops)
        deps = {}

        def add(d):
            k = ("d", d.dma) if d.dma is not None else d.eng
            p = deps.get(k)
            if p is None or p.idx < d.idx:
                deps[k] = d

        lastw = self.lastw
        rd = self.rd
        for r in reads:
            for k in r.keys:
                w = lastw.get(k)
                if w is not None:
                    add(w)
        for r in writes:
            for k in r.keys:
                w = lastw.get(k)
                if w is not None:
                    add(w)
                rr = rd.get(k)
                if rr:
                    for x in rr.values():
                        add(x)
        if dma is not None:
            p = self.streams.get(dma)
            if p is not None:
                add(p)
            self.streams[dma] = o
        mykey = ("d", dma) if dma is not None else eng
        if eng == "pe" and dma is None:
            deps.pop("pe", None)
        o.deps = deps
        for r in reads:
            for k in r.keys:
                d = rd.get(k)
                if d is None:
                    rd[k] = {mykey: o}
                else:
                    d[mykey] = o
        for r in writes:
            for k in r.keys:
                lastw[k] = o
                rd[k] = {}
        self.ops.append(o)
        return o

    def emit(self, final_waits=()):
        nc = self.nc
        by_eng = {e: [o for o in self.ops if o.eng == e] for e in ENGS}
        for e in ENGS:
            waited = {}
            for o in by_eng[e]:
                w = []
                for k, d in o.deps.items():
                    if waited.get(k, -1) >= d.idx:
                        continue
                    waited[k] = d.idx
                    d.signal = True
                    w.append(d)
                o.waits = w
        for o in final_waits:
            o.signal = True
        cnt = {e: 0 for e in ENGS}
        streams = {}
        for o in self.ops:
            if o.dma is not None:
                st = streams.get(o.dma)
                if st is None:
                    st = streams[o.dma] = [len(streams), 0]
                st[1] += 16
                o.dsem = st[0]
                o.dval = st[1]
            elif o.signal:
                cnt[o.eng] += 1
                o.count = cnt[o.eng]
        with contextlib.ExitStack() as es:
            esem = {e: es.enter_context(nc.semaphore("s_" + e)) for e in ENGS}
            dsem = [es.enter_context(nc.semaphore("d%d" % i)) for i in range(len(streams))]
            block = es.enter_context(nc.Block())

            def run(ename, eng):
                for o in by_eng[ename]:
                    for d in o.waits:
                        if d.dma is not None:
                            eng.wait_ge(dsem[d.dsem], d.dval)
                        else:
                            eng.wait_ge(esem[d.eng], d.count)
                    ins = o.fn(eng)
                    if o.dma is not None:
                        ins.then_inc(dsem[o.dsem], 16)
                    elif o.signal:
                        ins.then_inc(esem[ename], 1)
                if ename == "sp":
                    for o in final_waits:
                        if o.dma is not None:
                            eng.wait_ge(dsem[o.dsem], o.dval)
                        else:
                            eng.wait_ge(esem[o.eng], o.count)

            @block.tensor
            def _(e):
                run("pe", e)

            @block.scalar
            def _(e):
                run("act", e)

            @block.vector
            def _(e):
                run("dve", e)

            @block.gpsimd
            def _(e):
                run("pool", e)

            @block.sync
            def _(e):
                run("sp", e)


class Tile:
    def __init__(self, base_ap, off, shape, dt, parts=128, space="S", bank=0):
        self.off = off
        self.shape = tuple(shape)
        self.es = 4 if dt == F32 else 2
        self.parts = parts
        self.space = space
        self.bank = bank
        n = 1
        for s in shape:
            n *= s
        self.n = n
        if space == "S":
            v = base_ap[0:parts, off // 2: off // 2 + n * self.es // 2]
            if dt == F32:
                v = v.bitcast(F32)
        else:
            v = base_ap[0:parts, 0:n]
        if len(shape) == 2:
            v = v.rearrange("p (a b) -> p a b", a=shape[0])
        elif len(shape) == 3:
            v = v.rearrange("p (a b c) -> p a b c", a=shape[0], b=shape[1])
        self.ap = v

    def __getitem__(self, idx):
        if not isinstance(idx, tuple):
            idx = (idx,)
        ps = idx[0]
        fidx = list(idx[1:])
        while len(fidx) < len(self.shape):
            fidx.append(slice(None))
        ap = self.ap[(ps,) + tuple(fidx)]
        rng = []
        for i, s in zip(fidx, self.shape):
            if isinstance(i, int):
                rng.append((i, i + 1))
            else:
                a = 0 if i.start is None else i.start
                b = s if i.stop is None else i.stop
                rng.append((a, b))
        strides = []
        acc = 1
        for s in reversed(self.shape):
            strides.append(acc)
            acc *= s
        strides = strides[::-1]
        keys = set()
        outer = [()]
        for (a, b) in rng[:-1]:
            outer = [o + (i,) for o in outer for i in range(a, b)]
        la, lb = rng[-1]
        gr = GR if self.space == "S" else 512
        for o in outer:
            base = sum(i * st for i, st in zip(o, strides[:-1]))
            b0 = self.off + (base + la) * self.es
            b1 = self.off + (base + lb) * self.es
            for g in range(b0 // gr, (b1 - 1) // gr + 1):
                keys.add((self.space, self.bank, g))
        return Ref(ap, keys)


def build_program(nc, stop_after=99):
    dt_in = {}

    def din(name, shape):
        t = nc.dram_tensor(name, list(shape), F32, kind="ExternalInput").ap()
        dt_in[name] = t
        return t

    x = din("x", [SEQ, D])
    p = din("p", [2, SEQ, 256])
    ln_mix_a = din("ln_mix_a", [1, D])
    w_in_a = din("w_in_a", [1, D, 2 * D])
    g_v_a = din("g_v_a", [1, D])
    w_spatial = din("w_spatial", [1, 8, 128, 128])
    b_spatial = din("b_spatial", [1, 8, 128])
    w_out_a = din("w_out_a", [1, D, D])
    ln_kv = din("ln_kv", [D])
    w_kv = din("w_kv", [D, 2 * D])
    g_k = din("g_k", [64])
    ln_mix_b = din("ln_mix_b", [1, D])
    w_q = din("w_q", [1, D, D])
    g_q = din("g_q", [1, 64])
    w_out_b = din("w_out_b", [1, D, D])
    ln_mlp = din("ln_mlp", [2, D])
    w_up = din("w_up", [2, D, 4 * D])
    w_down = din("w_down", [2, 4 * D, D])
    ln_ple = din("ln_ple", [2, D])
    w_ple_gate = din("w_ple_gate", [2, D, D])
    w_ple_proj = din("w_ple_proj", [2, 256, D])
    c_ident = din("c_ident", [128, 128])
    c_maskle = din("c_maskle", [128, 128])
    c_pack = din("c_pack", [128, 784])
    c_ssel = din("c_ssel", [128, 2048])
    y = nc.dram_tensor("y", [SEQ, D], F32, kind="ExternalOutput").ap()

    es = contextlib.ExitStack()
    with es:
        ARENA_BYTES = 212736
        arena = es.enter_context(nc.sbuf_tensor("arena", [128, ARENA_BYTES // 2], BF16))
        banks = [es.enter_context(nc.psum_tensor("ps%d" % i, [128, 512], F32)) for i in range(8)]
        PB = [Tile(banks[i], 0, (512,), F32, space="P", bank=i) for i in range(8)]
        PB4 = [Tile(banks[i], 0, (4, 128), F32, space="P", bank=i) for i in range(8)]
        S = Sched(nc)

        def T(off, shape, dt, parts=128):
            return Tile(arena, off, shape, dt, parts=parts)

        XT = T(0, (8, SEQ), F32)
        HT = T(65536, (8, SEQ), BF16)
        WOFF = [98304 + i * 16384 for i in range(4)]
        SCR0 = 163840
        coff = ARENA_BYTES
        def calloc(nbytes):
            nonlocal coff
            coff -= (nbytes + 255) // 256 * 256
            return coff
        IDENT = T(calloc(512), (128,), F32)
        MASKLE = T(calloc(512), (128,), F32)
        CP = T(calloc(1568), (784,), BF16)
        SSEL = T(calloc(4096), (16, 128), BF16)
        GCOLS = T(calloc(256), (64,), F32)
        BSROW = T(calloc(2048), (1024,), BF16, parts=1)
        SMALL = T(calloc(256), (64,), F32)
        WP = T(calloc(4096), (2, 1024), BF16)
        SCR_END = coff
        ONES_S = lambda: CP[:, 0:128]
        ONES1 = lambda ps=slice(0, 128): CP[ps, 128:256]
        BDIAG = lambda: CP[:, 256:384]
        TRINEG = lambda: CP[:, 384:512]
        MASK01 = lambda: CP[:, 512:640]
        ESEL = lambda kb: CP[:, 640 + 15 - kb: 640 + 15 - kb + 128]

        scr = [SCR0]

        def salloc(shape, dt, parts=128):
            n = 1
            for s in shape:
                n *= s
            nb = n * (4 if dt == F32 else 2)
            nb = (nb + 255) // 256 * 256
            off = scr[0]
            scr[0] += nb
            assert scr[0] <= SCR_END, ("scratch overflow", scr[0], SCR_END)
            return T(off, shape, dt, parts=parts)

        def sreset():
            scr[0] = SCR0

        def act(out, in_, func, bias=None, scale=None, accum=None, extra=()):
            kw = {}
            if bias is not None:
                kw["bias"] = bias.ap if isinstance(bias, Ref) else bias
            if scale is not None:
                kw["scale"] = scale.ap if isinstance(scale, Ref) else scale
            if accum is not None:
                kw["accum_out"] = accum.ap
            rds = [in_] + [r for r in (bias, scale) if isinstance(r, Ref)] + list(extra)
            wr = [out] + ([accum] if accum is not None else [])
            return S.op("act", lambda e: e.activation(out.ap, in_.ap, func, **kw), reads=rds, writes=wr)

        def tt(eng, out, a, b, op):
            return S.op(eng, lambda e: e.tensor_tensor(out.ap, a.ap, b.ap, op), reads=[a, b], writes=[out])

        def stt(out, a, sc, b, op0, op1):
            scv = sc.ap if isinstance(sc, Ref) else sc
            rds = [a, b] + ([sc] if isinstance(sc, Ref) else [])
            return S.op("dve", lambda e: e.scalar_tensor_tensor(out.ap, a.ap, scv, b.ap, op0, op1), reads=rds, writes=[out])

        def ts(eng, out, a, s1, s2, op0, op1=None):
            s1v = s1.ap if isinstance(s1, Ref) else s1
            s2v = s2.ap if isinstance(s2, Ref) else s2
            rds = [a] + [r for r in (s1, s2) if isinstance(r, Ref)]
            if op1 is None:
                return S.op(eng, lambda e: e.tensor_scalar(out.ap, a.ap, s1v, None, op0), reads=rds, writes=[out])
            return S.op(eng, lambda e: e.tensor_scalar(out.ap, a.ap, s1v, s2v, op0, op1), reads=rds, writes=[out])

        def cp(eng, out, in_):
            if eng == "act":
                return act(out, in_, AF.Copy)
            return S.op(eng, lambda e: e.tensor_copy(out.ap, in_.ap), reads=[in_], writes=[out])

        def recip(out, in_):
            return S.op("dve", lambda e: e.reciprocal(out.ap, in_.ap), reads=[in_], writes=[out])

        def mm(out, pairs, start=True, stop=True):
            n = len(pairs)

            def fn(e):
                ins = None
                for i, (l, r) in enumerate(pairs):
                    ins = e.matmul(out.ap, l.ap, r.ap, start=(start and i == 0), stop=(stop and i == n - 1))
                return ins
            rds = []
            for l, r in pairs:
                rds.append(l)
                rds.append(r)
            return S.op("pe", fn, reads=rds, writes=[out])

        def tr(out, in_, ident):
            return S.op("pe", lambda e: e.transpose(out.ap, in_.ap, ident.ap), reads=[in_, ident], writes=[out])

        def dma(eng, out, in_ap, stream, reads=()):
            return S.op(eng, lambda e: e.dma_start(out=out.ap, in_=in_ap), reads=list(reads), writes=[out], dma=stream)

        class Rot:
            def __init__(self, ids):
                self.ids = list(ids)
                self.i = 0

            def next(self):
                b = self.ids[self.i % len(self.ids)]
                self.i += 1
                return b

        rot = Rot(range(8))
        tog = [0]

        def evac_eng():
            tog[0] ^= 1
            return "act" if tog[0] else "dve"

        def wslot(n, shape):
            return T(WOFF[n % 4], shape, BF16)

        def kp(ap2d):
            return ap2d.rearrange("(k p) f -> p k f", p=128)

        loaders = []

        def L_full(src2d, c0, c1):
            def f(n):
                w = wslot(n, (8, c1 - c0))
                dma("pool", w[:, :, :], kp(src2d)[:, :, c0:c1], "W%da" % (n % 4))
            return f

        def L_mlp(l, gi):
            def f(n):
                wu = Tile(arena, WOFF[n % 4], (8, 512), BF16)
                wd = Tile(arena, WOFF[n % 4] + 8192, (4, 1024), BF16)
                dma("pool", wu[:, :, :], kp(w_up[l])[:, :, gi * 512:(gi + 1) * 512], "W%da" % (n % 4))
                dma("pool", wd[:, :, :], kp(w_down[l][gi * 512:(gi + 1) * 512, :]), "W%db" % (n % 4))
            return f

        loaders.append(L_full(w_in_a[0], 0, 1024))
        loaders.append(L_full(w_in_a[0], 1024, 2048))
        loaders.append(L_full(w_out_a[0], 0, 1024))
        for gi in range(8):
            loaders.append(L_mlp(0, gi))
        loaders.append(L_full(w_ple_gate[0], 0, 1024))
        loaders += [None, None, None]
        for gi in range(8):
            loaders.append(L_mlp(1, gi))
        loaders.append(L_full(w_ple_gate[1], 0, 1024))
        issued = [0]

        def w_issue_upto(n):
            while issued[0] <= n and issued[0] < len(loaders):
                f = loaders[issued[0]]
                if f is not None:
                    f(issued[0])
                issued[0] += 1

        def w_done(n):
            w_issue_upto(n + 4)

        sreset()
        dma("sp", IDENT[:, :], c_ident, "c0")
        dma("sp", MASKLE[:, :], c_maskle, "c1")
        dma("pool", CP[:, :], c_pack, "c2")
        dma("pool", SSEL[:, :, :], c_ssel.rearrange("p (a b) -> p a b", a=16), "c3")
        dma("pool", BSROW[0:1, :], b_spatial[0].rearrange("(o g) t -> o (g t)", o=1), "c4")
        dma("pool", WP[:, :, :], kp(w_ple_proj[0]), "wp")
        w_issue_upto(2)
        gst = salloc((128,), F32, parts=64)
        gi_ = 0
        for src in (ln_mix_a[0], ln_mlp[0], ln_ple[0], ln_kv, ln_mix_b[0], ln_mlp[1], ln_ple[1]):
            dma("sp", gst[gi_ * 8:(gi_ + 1) * 8, :], src.rearrange("(k f) -> k f", f=128), "g%d" % gi_)
            gi_ += 1
        gk2 = g_k.rearrange("(o f) -> o f", o=1)
        dma("sp", gst[56:57, 0:64], gk2, "g7")
        dma("sp", gst[56:57, 64:128], gk2, "g8")
        dma("sp", gst[57:58, 0:64], g_q, "g9")
        dma("sp", gst[57:58, 64:128], g_q, "g10")
        b = rot.next()
        tr(PB[b][:, 0:58], gst[0:58, :], IDENT[0:58, 0:58])
        cp("dve", GCOLS[:, 0:58], PB[b][:, 0:58])
        ts("dve", GCOLS[:, 58:59], GCOLS[:, 57:58], 0.125, None, ALU.mult)

        xs = [T(WOFF[3] + i * 4096, (1024,), F32) for i in range(4)]

        def xload(t):
            for blk in range(t * 4, t * 4 + 4):
                st = xs[blk % 4]
                dma("sp", st[:, :], x[blk * 128:(blk + 1) * 128, :], "x%d" % (blk % 4))
                for half in range(2):
                    b = rot.next()
                    for cc in range(4):
                        c = half * 4 + cc
                        tr(PB4[b][:, cc, :], st[:, c * 128:(c + 1) * 128], IDENT[:, :])
                    cp(evac_eng(), XT[:, half * 4:(half + 1) * 4, blk * 128:(blk + 1) * 128], PB4[b][:, :, :])

        if stop_after < 1:
            for t in range(NT):
                xload(t)

        def norm_A(t, sq):
            tl = slice(t * TT, (t + 1) * TT)
            for c in range(8):
                act(sq[:, c, :], XT[:, c, tl], AF.Square)

        def norm_B(t, gcol0, sq, sd, rstd, dst=HT):
            tl = slice(t * TT, (t + 1) * TT)
            b = rot.next()
            mm(PB[b][:, :], [(ONES_S(), sq[:, c, :]) for c in range(8)])
            act(sd[:, :], PB[b][:, :], AF.Ln, bias=EPS)
            act(rstd[:, :], sd[:, :], AF.Exp, scale=-0.5)
            for c in range(8):
                if gcol0 is None:
                    tt("dve", dst[:, c, tl], XT[:, c, tl], rstd[:, :], ALU.mult)
                else:
                    stt(dst[:, c, tl], XT[:, c, tl], GCOLS[:, gcol0 + c:gcol0 + c + 1], rstd[:, :], ALU.mult, ALU.mult)

        def norm_tile(t, gcol0, sq, sd, rstd, dst=HT):
            norm_A(t, sq)
            norm_B(t, gcol0, sq, sd, rstd, dst)

        if stop_after >= 1:
            sreset()
            uT = salloc((8, TT), BF16)
            sq = uT
            yT = salloc((8, TT), BF16)
            v16 = [salloc((1024,), BF16) for _ in range(2)]
            vn = [salloc((1024,), BF16) for _ in range(2)]
            sd = salloc((TT,), F32)
            rstd = salloc((TT,), F32)
            wsT = salloc((8, 128), BF16)
            gvt = salloc((1024,), F32)
            wsl = T(yT.off, (8, 128), F32)
            dma("sp", gvt[:, :], g_v_a.partition_broadcast(128), "gv")
            dma("sp", wsl[:, :, :], w_spatial[0].rearrange("g t s -> t g s"), "ws")
            xload(0)
            for hb in range(2):
                b = rot.next()
                for gg in range(4):
                    tr(PB4[b][:, gg, :], wsl[:, hb * 4 + gg, :], IDENT[:, :])
                tt("dve", wsT[:, hb * 4:(hb + 1) * 4, :], PB4[b][:, :, :],
                   Ref(MASKLE.ap.unsqueeze(1).to_broadcast([128, 4, 128]), MASKLE[:, :].keys), ALU.mult)
            W0 = wslot(0, (8, 1024))
            W1 = wslot(1, (8, 1024))
            W2 = wslot(2, (8, 1024))
            norm_A(0, sq)
            norm_B(0, G_MIXA, sq, sd, rstd)
            for t in range(NT):
                tl = slice(t * TT, (t + 1) * TT)
                if t + 1 < NT:
                    xload(t + 1)
                    if t + 1 == NT - 1:
                        w_issue_upto(3)
                for f in range(8):
                    b = rot.next()
                    mm(PB[b][:, :], [(W0[:, k, f * 128:(f + 1) * 128], HT[:, k, tl]) for k in range(8)])
                    act(uT[:, f, :], PB[b][:, :], AF.Gelu_apprx_tanh)

                def VST(blk, t=t):
                    tb = t * 4 + blk
                    tbl = slice(tb * 128, (tb + 1) * 128)
                    for half in range(2):
                        b = rot.next()
                        mm(PB[b][:, :], [(HT[:, k, tbl], W1[:, k, half * 512:(half + 1) * 512]) for k in range(8)])
                        act(v16[blk % 2][:, half * 512:(half + 1) * 512], PB[b][:, :], AF.Gelu_apprx_tanh)

                def NST(blk):
                    vb = vn[blk % 2]
                    vv = v16[blk % 2]
                    c4 = (blk % 2) * 4
                    act(vb[:, :], vv[:, :], AF.Square, accum=SMALL[:, c4:c4 + 1])
                    ts("dve", SMALL[:, c4 + 1:c4 + 2], SMALL[:, c4:c4 + 1], 1.0 / 1024, EPS, ALU.mult, ALU.add)
                    act(SMALL[:, c4 + 2:c4 + 3], SMALL[:, c4 + 1:c4 + 2], AF.Sqrt)
                    recip(SMALL[:, c4 + 3:c4 + 4], SMALL[:, c4 + 2:c4 + 3])
                    stt(vb[:, :], vv[:, :], SMALL[:, c4 + 3:c4 + 4], gvt[:, :], ALU.mult, ALU.mult)

                def SPST(blk):
                    vb = vn[blk % 2]
                    for hb in range(2):
                        b = rot.next()

                        def fn(e, b=b, hb=hb, vb=vb):
                            ins = None
                            for gg in range(4):
                                g = hb * 4 + gg
                                e.matmul(PB4[b][:, gg, :].ap, vb[:, g * 128:(g + 1) * 128].ap, wsT[:, g, :].ap,
                                         start=True, stop=False)
                                ins = e.matmul(PB4[b][:, gg, :].ap, ONES1(slice(0, 1)).ap,
                                               BSROW[0:1, g * 128:(g + 1) * 128].ap, start=False, stop=True)
                            return ins
                        S.op("pe", fn, reads=[vb[:, hb * 512:(hb + 1) * 512], wsT[:, hb * 4:(hb + 1) * 4, :],
                                              ONES1(slice(0, 1)), BSROW[0:1, :]], writes=[PB4[b][:, :, :]])
                        tt("dve", yT[:, hb * 4:(hb + 1) * 4, blk * 128:(blk + 1) * 128], PB4[b][:, :, :],
                           uT[:, hb * 4:(hb + 1) * 4, blk * 128:(blk + 1) * 128], ALU.mult)

                VST(0)
                VST(1)
                NST(0)
                for blk in range(4):
                    if blk + 2 < 4:
                        VST(blk + 2)
                    if blk + 1 < 4:
                        NST(blk + 1)
                    SPST(blk)
                if t + 1 < NT:
                    norm_A(t + 1, sq)
                for dch in range(8):
                    b = rot.next()
                    mm(PB[b][:, :], [(W2[:, k, dch * 128:(dch + 1) * 128], yT[:, k, :]) for k in range(8)])
                    tt("dve", XT[:, dch, tl], PB[b][:, :], XT[:, dch, tl], ALU.add)
                if t + 1 < NT:
                    norm_B(t + 1, G_MIXA, sq, sd, rstd)
            w_done(0)
            w_done(1)
            w_done(2)

        def mlp_phase(l, blk0, gcol0):
            sreset()
            sq = salloc((8, TT), BF16)
            sd = salloc((TT,), F32)
            rstd = salloc((TT,), F32)
            aT = [salloc((8, TT), BF16) for _ in range(2)]
            rt = [salloc((TT,), BF16) for _ in range(2)]
            for t in range(NT):
                norm_tile(t, gcol0, sq, sd, rstd)
            it = 0
            for sg in range(4):
                nA, nB = blk0 + 2 * sg, blk0 + 2 * sg + 1
                wu = [Tile(arena, WOFF[n % 4], (8, 512), BF16) for n in (nA, nB)]
                wd = [Tile(arena, WOFF[n % 4] + 8192, (4, 1024), BF16) for n in (nA, nB)]
                for t in range(NT):
                    tl = slice(t * TT, (t + 1) * TT)
                    a = aT[it % 2]
                    it += 1
                    for f in range(8):
                        b = rot.next()
                        mm(PB[b][:, :], [(wu[f // 4][:, k, (f % 4) * 128:(f % 4 + 1) * 128], HT[:, k, tl]) for k in range(8)])
                        r = rt[f % 2]
                        act(r[:, :], PB[b][:, :], AF.Relu)
                        tt("dve", a[:, f, :], r[:, :], r[:, :], ALU.mult)
                    for dch in range(8):
                        b = rot.next()
                        mm(PB[b][:, :], [(wd[f // 4][:, f % 4, dch * 128:(dch + 1) * 128], a[:, f, :]) for f in range(8)])
                        tt("dve", XT[:, dch, tl], PB[b][:, :], XT[:, dch, tl], ALU.add)
                w_done(nA)
                w_done(nB)

        out_fin = []

        def emit_out(t, osb):
            for blk in range(t * 4, t * 4 + 4):
                ot = osb[blk % 2]
                for half in range(2):
                    b = rot.next()
                    for cc in range(4):
                        c = half * 4 + cc
                        tr(PB4[b][:, cc, :], XT[:, c, blk * 128:(blk + 1) * 128], IDENT[:, :])
                    cp(evac_eng(), ot[:, half * 512:(half + 1) * 512], PB[b][:, :])
                out_fin.append(S.op("sp", lambda e, ot=ot, blk=blk: e.dma_start(out=y[blk * 128:(blk + 1) * 128, :], in_=ot[:, :].ap),
                                    reads=[ot[:, :]], dma="o%d" % (blk % 2)))

        def ple_phase(l, blkn, gcol0, with_out=False):
            sreset()
            sq = salloc((8, TT), BF16)
            sd = salloc((TT,), F32)
            rstd = salloc((TT,), F32)
            pst = salloc((4, 256), F32)
            pT = salloc((2, TT), BF16)
            gate = [salloc((TT,), F32) for _ in range(2)]
            tmp2 = [salloc((TT,), F32) for _ in range(2)]
            osb = [salloc((1024,), F32) for _ in range(2)] if with_out else None
            Wg = wslot(blkn, (8, 1024))
            for t in range(NT):
                tl = slice(t * TT, (t + 1) * TT)
                dma("sp", pst[:, :, :], p[l][t * TT:(t + 1) * TT, :].rearrange("(b q) f -> q b f", q=128), "pl")
                norm_tile(t, gcol0, sq, sd, rstd)
                for kk in range(2):
                    b = rot.next()
                    for blk in range(4):
                        tr(PB4[b][:, blk, :], pst[:, blk, kk * 128:(kk + 1) * 128], IDENT[:, :])
                    cp(evac_eng(), pT[:, kk, :], PB[b][:, :])
                for dch in range(8):
                    b1 = rot.next()
                    mm(PB[b1][:, :], [(Wg[:, k, dch * 128:(dch + 1) * 128], HT[:, k, tl]) for k in range(8)])
                    g_ = gate[dch % 2]
                    act(g_[:, :], PB[b1][:, :], AF.Sigmoid)
                    b2 = rot.next()
                    mm(PB[b2][:, :], [(WP[:, kk, dch * 128:(dch + 1) * 128], pT[:, kk, :]) for kk in range(2)])
                    t2 = tmp2[dch % 2]
                    tt("dve", t2[:, :], PB[b2][:, :], g_[:, :], ALU.mult)
                    tt("dve", XT[:, dch, tl], XT[:, dch, tl], t2[:, :], ALU.add)
                if with_out:
                    emit_out(t, osb)
            w_done(blkn)

        if stop_after >= 2:
            mlp_phase(0, 3, G_MLP0)
        if stop_after >= 3:
            ple_phase(0, 11, G_PLE0)
            dma("pool", WP[:, :, :], kp(w_ple_proj[1]), "wp")

        if stop_after >= 4:
            sreset()
            sq = salloc((8, TT), BF16)
            sd = salloc((TT,), F32)
            rstd = salloc((TT,), F32)
            for t in range(NT):
                norm_tile(t, None, sq, sd, rstd)
            sreset()
            KT = salloc((SEQ,), BF16)
            QTh = [salloc((SEQ,), BF16) for _ in range(2)]
            VJ = salloc((16, 128), BF16)
            OT = salloc((SEQ,), BF16)
            e1 = [salloc((TT,), F32) for _ in range(2)]
            AT = [salloc((TT,), BF16) for _ in range(3)]
            csb = [salloc((TT,), BF16) for _ in range(2)]
            PW = [Tile(arena, WOFF[0] + i * 8192, (4096,), BF16) for i in range(2)]
            LS = [Tile(arena, WOFF[1], (16, TT), BF16), Tile(arena, WOFF[2], (16, TT), BF16)]
            sqK = Tile(arena, WOFF[2], (4, TT), BF16)
            sqQ = Tile(arena, WOFF[2] + 4096, (4, TT), BF16)
            sd4 = Tile(arena, WOFF[2] + 8192, (4, TT), F32)
            rawK = Tile(arena, WOFF[1], (4, TT), F32)
            rawQ = Tile(arena, WOFF[1] + 8192, (4, TT), F32)
            srot = Rot(range(5))
            zrot = Rot([0, 1])
            grot = Rot([2, 3])
            frot = Rot([6, 7])
            JUNK = 7

            def dummy(k):
                def fn(e):
                    ins = None
                    for _ in range(k):
                        ins = e.matmul(PB[JUNK][:, :].ap, ONES_S().ap, CP[:, 0:512].ap, start=True, stop=True)
                    return ins
                S.op("pe", fn, reads=[ONES_S(), CP[:, 0:512]], writes=[PB[JUNK][:, :]])
            CSB = 4
            OB = [5, 5]
            for bi in range(2):
                S.op("dve", lambda e, bi=bi: e.memset(csb[bi][:, :].ap, 0.0), writes=[csb[bi][:, :]])
                S.op("dve", lambda e, bi=bi: e.memset(QTh[bi][:, :].ap, 0.0), writes=[QTh[bi][:, :]])

            def pw_views(i):
                t_ = PW[i]
                off = t_.off
                wk = Tile(arena, off, (8, 128), BF16)
                wv = Tile(arena, off + 2048, (8, 128), BF16)
                wq = Tile(arena, off + 4096, (8, 128), BF16)
                wo = Tile(arena, off + 6144, (1024,), BF16)
                return wk, wv, wq, wo

            def load_pair(j):
                wk, wv, wq, wo = pw_views(j % 2)
                sl = slice(j * 128, (j + 1) * 128)
                dma("pool", wk[:, :, :], kp(w_kv)[:, :, j * 128:(j + 1) * 128], "pw%da" % (j % 2))
                dma("pool", wv[:, :, :], kp(w_kv)[:, :, 1024 + j * 128:1024 + (j + 1) * 128], "pw%db" % (j % 2))
                dma("pool", wq[:, :, :], kp(w_q[0])[:, :, j * 128:(j + 1) * 128], "pw%dc" % (j % 2))
                dma("pool", wo[:, :], w_out_b[0][sl, :], "pw%dd" % (j % 2))
                for w_, g0 in ((wk, G_KV), (wv, G_KV), (wq, G_MIXB)):
                    gb = Ref(GCOLS.ap[:, g0:g0 + 8].unsqueeze(2).to_broadcast([128, 8, 128]), GCOLS[:, g0:g0 + 8].keys)
                    tt("dve", w_[:, :, :], w_[:, :, :], gb, ALU.mult)

            load_pair(0)
            for j in range(8):
                wk, wv, wq, wo = pw_views(j % 2)
                if j + 1 < 8:
                    load_pair(j + 1)
                KB = [0, 1, 2, 3]
                QB = [4, 5, 6, 7]
                for (w_, raw, sqx, mb) in ((wk, rawK, sqK, KB), (wq, rawQ, sqQ, QB)):
                    for t in range(NT):
                        tl = slice(t * TT, (t + 1) * TT)
                        mm(PB[mb[t]][:, :], [(w_[:, k, :], HT[:, k, tl]) for k in range(8)])
                        act(raw[:, t, :], PB[mb[t]][:, :], AF.Copy)
                        act(sqx[:, t, :], PB[mb[t]][:, :], AF.Square)
                for (sqx, mb) in ((sqK, KB), (sqQ, QB)):
                    for t in range(NT):
                        mm(PB[mb[t]][:, :], [(BDIAG(), sqx[:, t, :])])
                for t in range(NT):
                    act(sd4[:, t, :], PB[KB[t]][:, :], AF.Ln, bias=EPS)
                for t in range(NT):
                    act(sd4[:, t, :], sd4[:, t, :], AF.Exp, scale=-0.5)
                for t in range(NT):
                    tl = slice(t * TT, (t + 1) * TT)
                    stt(KT[:, tl], rawK[:, t, :], GCOLS[:, G_K:G_K + 1], sd4[:, t, :], ALU.mult, ALU.mult)
                for q4 in range(4):
                    b_ = KB[q4]
                    for bb in range(4):
                        blk = q4 * 4 + bb
                        mm(PB4[b_][:, bb, :], [(HT[:, k, blk * 128:(blk + 1) * 128], wv[:, k, :]) for k in range(8)])
                    cp("dve", VJ[:, q4 * 4:(q4 + 1) * 4, :], PB4[b_][:, :, :])
                for t in range(NT):
                    act(sd4[:, t, :], PB[QB[t]][:, :], AF.Ln, bias=EPS)
                for t in range(NT):
                    act(sd4[:, t, :], sd4[:, t, :], AF.Exp, scale=-0.5)
                for t in range(NT):
                    tl = slice(t * TT, (t + 1) * TT)
                    for h_ in range(2):
                        hp = slice(h_ * 64, (h_ + 1) * 64)
                        stt(QTh[h_][hp, tl], rawQ[hp, t, :], GCOLS[hp, G_QS:G_QS + 1], sd4[hp, t, :], ALU.mult, ALU.mult)
                heads = (slice(0, 64), slice(64, 128))

                def cols(qt, kb):
                    c0 = max(0, kb - 4 * qt) * 128
                    return c0, slice(c0, TT), slice(qt * TT + c0, (qt + 1) * TT)

                def p1_steps(h, qt):
                    nkb = 4 * qt + 4
                    L = LS[h]
                    zb = {}

                    def Z(kb):
                        c0, cs, qs = cols(qt, kb)
                        zb[kb] = zrot.next()
                        mm(PB[zb[kb]][:, cs], [(KT[:, kb * 128:(kb + 1) * 128], QTh[h][:, qs])])

                    def EXP(kb):
                        c0, cs, qs = cols(qt, kb)
                        act(e1[kb % 2][:, cs], PB[zb[kb]][:, cs], AF.Exp)

                    def LN(kb):
                        c0, cs, qs = cols(qt, kb)
                        act(L[:, kb, cs], e1[kb % 2][:, cs], AF.Ln, bias=1.0)
                        if kb >= 4 * qt:
                            tt("pool", L[:, kb, c0:c0 + 128], L[:, kb, c0:c0 + 128], MASK01(), ALU.mult)

                    def CS(kb):
                        c0, cs, qs = cols(qt, kb)
                        mm(PB[CSB][:, cs], [(ESEL(kb), L[:, kb, cs])], start=(kb == 0), stop=(kb == nkb - 1))

                    steps = []
                    for i in range(nkb + 3):
                        def st(i=i):
                            if i < nkb:
                                Z(i)
                            if 0 <= i - 1 < nkb:
                                EXP(i - 1)
                            if 0 <= i - 2 < nkb:
                                LN(i - 2)
                            if 0 <= i - 3 < nkb:
                                CS(i - 3)
                        steps.append(st)

                    def fin():
                        cp("dve", csb[h][0:16, :], PB[CSB][0:16, :])
                    steps.append(fin)
                    return steps

                def p2_steps(h, qt):
                    nkb = 4 * qt + 4
                    L = LS[h]
                    pr = heads[h]
                    gb = {}

                    def G(kb):
                        c0, cs, qs = cols(qt, kb)
                        b = gb[kb] = grot.next()

                        def fn(e):
                            e.matmul(PB[b][:, cs].ap, SSEL[:, kb, :].ap, csb[h][:, cs].ap, start=True, stop=False)
                            e.matmul(PB[b][:, cs].ap, TRINEG().ap, L[:, kb, cs].ap, start=False, stop=False)
                            return e.matmul(PB[b][:, cs].ap, KT[:, kb * 128:(kb + 1) * 128].ap, QTh[h][:, qs].ap,
                                            start=False, stop=True)
                        S.op("pe", fn, reads=[TRINEG(), L[:, kb, cs], SSEL[:, kb, :], csb[h][:, cs],
                                              KT[:, kb * 128:(kb + 1) * 128], QTh[h][:, qs]], writes=[PB[b][:, cs]])

                    def EXPA(kb):
                        c0, cs, qs = cols(qt, kb)
                        a_ = AT[kb % 3]
                        act(a_[:, cs], PB[gb[kb]][:, cs], AF.Exp)
                        if kb >= 4 * qt:
                            tt("pool", a_[:, c0:c0 + 128], a_[:, c0:c0 + 128], MASK01(), ALU.mult)

                    def AV(kb):
                        c0, cs, qs = cols(qt, kb)
                        mm(PB[OB[h]][:, cs], [(VJ[:, kb, :], AT[kb % 3][:, cs])], start=(kb == 0), stop=(kb == nkb - 1))

                    steps = []
                    for i in range(nkb + 1):
                        def st(i=i):
                            if i < nkb:
                                G(i)
                            if 0 <= i - 1 < nkb:
                                EXPA(i - 1)
                                AV(i - 1)
                        steps.append(st)

                    def fin():
                        cp("dve", OT[pr, qt * TT:(qt + 1) * TT], PB[OB[h]][pr, :])
                    steps.append(fin)
                    return steps

                pending = []

                def filler(k):
                    for _ in range(k):
                        if not pending:
                            return
                        qt_, dch = pending.pop(0)
                        tl = slice(qt_ * TT, (qt_ + 1) * TT)
                        b = frot.next()
                        mm(PB[b][:, :], [(wo[:, dch * 128:(dch + 1) * 128], OT[:, tl])])
                        tt("dve", XT[:, dch, tl], PB[b][:, :], XT[:, dch, tl], ALU.add)

                units = [(h, qt) for qt in range(NT) for h in range(2)]
                prev = None
                for u in units + [None]:
                    s1 = p1_steps(*u) if u is not None else []
                    s2 = p2_steps(*prev) if prev is not None else []
                    n = max(len(s1), len(s2))
                    for i in range(n):
                        if i < len(s2):
                            s2[i]()
                        if i < len(s1):
                            s1[i]()
                        if i == n // 2 or i == n - 1:
                            filler(2)
                    if prev is not None and prev[0] == 1:
                        pending.extend((prev[1], dch) for dch in range(8))
                    prev = u
                filler(len(pending))
            w_done(12)
            w_done(13)
            w_done(14)

        if stop_after >= 5:
            w_issue_upto(18)
            mlp_phase(1, 15, G_MLP1)
        if stop_after >= 6:
            ple_phase(1, 23, G_PLE1, with_out=True)

        if not out_fin:
            sreset()
            osb = [salloc((1024,), F32) for _ in range(2)]
            for t in range(NT):
                emit_out(t, osb)
        S.emit(final_waits=out_fin[-2:])
    return nc


def make_consts():
    ident = np.eye(128, dtype=np.float32)
    s = np.arange(128)[:, None]
    t = np.arange(128)[None, :]
    maskle = (s <= t).astype(np.float32)
    pack = np.zeros((128, 784), np.float32)
    pack[:, 0:128] = 1.0 / 1024
    pack[:, 128:256] = 1.0
    bd = np.zeros((128, 128), np.float32)
    bd[0:64, 0:64] = 1.0 / 64
    bd[64:128, 64:128] = 1.0 / 64
    pack[:, 256:384] = bd
    pack[:, 384:512] = -(s >= t).astype(np.float32)
    pack[:, 512:640] = (s < t).astype(np.float32)
    pack[:, 640 + 15] = -1.0
    pack[:, 640 + 47] = -1.0
    ssel = np.zeros((128, 16, 128), np.float32)
    for kb in range(16):
        for r in range(16):
            if r > kb:
                ssel[r, kb, :] = 1.0
                ssel[32 + r, kb, :] = 1.0
    return {"c_ident": ident, "c_maskle": maskle, "c_pack": pack, "c_ssel": ssel.reshape(128, 2048)}


_CACHE = {}


def kernel(**inputs):
    n = 8
    if "nc" not in _CACHE:
        nc = bass.Bass("TRN2", target_bir_lowering=False)
        build_program(nc)
        _CACHE["nc"] = nc
    nc = _CACHE["nc"]
    consts = make_consts()
    shared = {k: np.ascontiguousarray(np.asarray(v, dtype=np.float32)) for k, v in inputs.items() if k not in ("x", "p")}
    shared.update(consts)
    x = np.asarray(inputs["x"], dtype=np.float32)
    p = np.asarray(inputs["p"], dtype=np.float32)
    in_maps = []
    for i in range(n):
        m = dict(shared)
        m["x"] = np.ascontiguousarray(x[i])
        m["p"] = np.ascontiguousarray(p[:, i])
        in_maps.append(m)
    res = run_bass_kernel_spmd(nc, in_maps, core_ids=list(range(n)))
    return np.stack([np.asarray(r["y"], dtype=np.float32) for r in res.results], axis=0)
```

```python
import contextlib
import numpy as np
import concourse.bass as bass
import concourse.mybir as mybir
from concourse.bass_utils import run_bass_kernel_spmd

F32 = mybir.dt.float32
BF16 = mybir.dt.bfloat16
AF = mybir.ActivationFunctionType
ALU = mybir.AluOpType

SEQ = 2048
D = 1024
NT = 4
TT = 512
NC8 = 8
EPS = 1e-6
GR = 256
ENGS = ("pe", "act", "dve", "pool", "sp")

G_MIXA, G_MLP0, G_PLE0, G_KV, G_MIXB, G_MLP1, G_PLE1, G_K, G_Q, G_QS = 0, 8, 16, 24, 32, 40, 48, 56, 57, 58


class Op:
    __slots__ = ("eng", "fn", "deps", "signal", "idx", "dma", "dsem", "dval", "count", "waits")

    def __init__(self, eng, fn, dma):
        self.eng = eng
        self.fn = fn
        self.dma = dma
        self.signal = False
        self.count = 0
        self.dsem = 0
        self.dval = 0
        self.deps = None
        self.waits = None


class Ref:
    __slots__ = ("ap", "keys")

    def __init__(self, ap, keys):
        self.ap = ap
        self.keys = keys


class Sched:
    def __init__(self, nc):
        self.nc = nc
        self.ops = []
        self.lastw = {}
        self.rd = {}
        self.streams = {}

    def op(self, eng, fn, reads=(), writes=(), dma=None):
        o = Op(eng, fn, dma)
        o.idx = len(self.ops)
        deps = {}

        def add(d):
            k = ("d", d.dma) if d.dma is not None else d.eng
            p = deps.get(k)
            if p is None or p.idx < d.idx:
                deps[k] = d

        lastw = self.lastw
        rd = self.rd
        for r in reads:
            for k in r.keys:
                w = lastw.get(k)
                if w is not None:
                    add(w)
        for r in writes:
            for k in r.keys:
                w = lastw.get(k)
                if w is not None:
                    add(w)
                rr = rd.get(k)
                if rr:
                    for x in rr.values():
                        add(x)
        if dma is not None:
            p = self.streams.get(dma)
            if p is not None:
                add(p)
            self.streams[dma] = o
        mykey = ("d", dma) if dma is not None else eng
        if eng == "pe" and dma is None:
            deps.pop("pe", None)
        o.deps = deps
        for r in reads:
            for k in r.keys:
                d = rd.get(k)
                if d is None:
                    rd[k] = {mykey: o}
                else:
                    d[mykey] = o
        for r in writes:
            for k in r.keys:
                lastw[k] = o
                rd[k] = {}
        self.ops.append(o)
        return o

    def emit(self, final_waits=()):
        nc = self.nc
        by_eng = {e: [o for o in self.ops if o.eng == e] for e in ENGS}
        for e in ENGS:
            waited = {}
            for o in by_eng[e]:
                w = []
                for k, d in o.deps.items():
                    if waited.get(k, -1) >= d.idx:
                        continue
                    waited[k] = d.idx
                    d.signal = True
                    w.append(d)
                o.waits = w
        for o in final_waits:
            o.signal = True
        cnt = {e: 0 for e in ENGS}
        streams = {}
        for o in self.ops:
            if o.dma is not None:
                st = streams.get(o.dma)
                if st is None:
                    st = streams[o.dma] = [len(streams), 0]
                st[1] += 16
                o.dsem = st[0]
                o.dval = st[1]
            elif o.signal:
                cnt[o.eng] += 1
                o.count = cnt[o.eng]
        with contextlib.ExitStack() as es:
            esem = {e: es.enter_context(nc.semaphore("s_" + e)) for e in ENGS}
            dsem = [es.enter_context(nc.semaphore("d%d" % i)) for i in range(len(streams))]
            block = es.enter_context(nc.Block())

            def run(ename, eng):
                for o in by_eng[ename]:
                    for d in o.waits:
                        if d.dma is not None:
                            eng.wait_ge(dsem[d.dsem], d.dval)
                        else:
                            eng.wait_ge(esem[d.eng], d.count)
                    ins = o.fn(eng)
                    if o.dma is not None:
                        ins.then_inc(dsem[o.dsem], 16)
                    elif o.signal:
                        ins.then_inc(esem[ename], 1)
                if ename == "sp":
                    for o in final_waits:
                        if o.dma is not None:
                            eng.wait_ge(dsem[o.dsem], o.dval)
                        else:
                            eng.wait_ge(esem[o.eng], o.count)

            @block.tensor
            def _(e):
                run("pe", e)

            @block.scalar
            def _(e):
                run("act", e)

            @block.vector
            def _(e):
                run("dve", e)

            @block.gpsimd
            def _(e):
                run("pool", e)

            @block.sync
            def _(e):
                run("sp", e)


class Tile:
    def __init__(self, base_ap, off, shape, dt, parts=128, space="S", bank=0):
        self.off = off
        self.shape = tuple(shape)
        self.es = 4 if dt == F32 else 2
        self.parts = parts
        self.space = space
        self.bank = bank
        n = 1
        for s in shape:
            n *= s
        self.n = n
        if space == "S":
            v = base_ap[0:parts, off // 2: off // 2 + n * self.es // 2]
            if dt == F32:
                v = v.bitcast(F32)
        else:
            v = base_ap[0:parts, 0:n]
        if len(shape) == 2:
            v = v.rearrange("p (a b) -> p a b", a=shape[0])
        elif len(shape) == 3:
            v = v.rearrange("p (a b c) -> p a b c", a=shape[0], b=shape[1])
        self.ap = v

    def __getitem__(self, idx):
        if not isinstance(idx, tuple):
            idx = (idx,)
        ps = idx[0]
        fidx = list(idx[1:])
        while len(fidx) < len(self.shape):
            fidx.append(slice(None))
        ap = self.ap[(ps,) + tuple(fidx)]
        rng = []
        for i, s in zip(fidx, self.shape):
            if isinstance(i, int):
                rng.append((i, i + 1))
            else:
                a = 0 if i.start is None else i.start
                b = s if i.stop is None else i.stop
                rng.append((a, b))
        strides = []
        acc = 1
        for s in reversed(self.shape):
            strides.append(acc)
            acc *= s
        strides = strides[::-1]
        keys = set()
        outer = [()]
        for (a, b) in rng[:-1]:
            outer = [o + (i,) for o in outer for i in range(a, b)]
        la, lb = rng[-1]
        gr = GR if self.space == "S" else 512
        for o in outer:
            base = sum(i * st for i, st in zip(o, strides[:-1]))
            b0 = self.off + (base + la) * self.es
            b1 = self.off + (base + lb) * self.es
            for g in range(b0 // gr, (b1 - 1) // gr + 1):
                keys.add((self.space, self.bank, g))
        return Ref(ap, keys)


def build_program(nc, stop_after=99):
    dt_in = {}

    def din(name, shape):
        t = nc.dram_tensor(name, list(shape), F32, kind="ExternalInput").ap()
        dt_in[name] = t
        return t

    x = din("x", [SEQ, D])
    p = din("p", [2, SEQ, 256])
    ln_mix_a = din("ln_mix_a", [1, D])
    w_in_a = din("w_in_a", [1, D, 2 * D])
    g_v_a = din("g_v_a", [1, D])
    w_spatial = din("w_spatial", [1, 8, 128, 128])
    b_spatial = din("b_spatial", [1, 8, 128])
    w_out_a = din("w_out_a", [1, D, D])
    ln_kv = din("ln_kv", [D])
    w_kv = din("w_kv", [D, 2 * D])
    g_k = din("g_k", [64])
    ln_mix_b = din("ln_mix_b", [1, D])
    w_q = din("w_q", [1, D, D])
    g_q = din("g_q", [1, 64])
    w_out_b = din("w_out_b", [1, D, D])
    ln_mlp = din("ln_mlp", [2, D])
    w_up = din("w_up", [2, D, 4 * D])
    w_down = din("w_down", [2, 4 * D, D])
    ln_ple = din("ln_ple", [2, D])
    w_ple_gate = din("w_ple_gate", [2, D, D])
    w_ple_proj = din("w_ple_proj", [2, 256, D])
    c_ident = din("c_ident", [128, 128])
    c_maskle = din("c_maskle", [128, 128])
    c_pack = din("c_pack", [128, 784])
    c_ssel = din("c_ssel", [128, 2048])
    y = nc.dram_tensor("y", [SEQ, D], F32, kind="ExternalOutput").ap()

    es = contextlib.ExitStack()
    with es:
        ARENA_BYTES = 212736
        arena = es.enter_context(nc.sbuf_tensor("arena", [128, ARENA_BYTES // 2], BF16))
        banks = [es.enter_context(nc.psum_tensor("ps%d" % i, [128, 512], F32)) for i in range(8)]
        PB = [Tile(banks[i], 0, (512,), F32, space="P", bank=i) for i in range(8)]
        PB4 = [Tile(banks[i], 0, (4, 128), F32, space="P", bank=i) for i in range(8)]
        S = Sched(nc)

        def T(off, shape, dt, parts=128):
            return Tile(arena, off, shape, dt, parts=parts)

        XT = T(0, (8, SEQ), F32)
        HT = T(65536, (8, SEQ), BF16)
        WOFF = [98304 + i * 16384 for i in range(4)]
        SCR0 = 163840
        coff = ARENA_BYTES
        def calloc(nbytes):
            nonlocal coff
            coff -= (nbytes + 255) // 256 * 256
            return coff
        IDENT = T(calloc(512), (128,), F32)
        MASKLE = T(calloc(512), (128,), F32)
        CP = T(calloc(1568), (784,), BF16)
        SSEL = T(calloc(4096), (16, 128), BF16)
        GCOLS = T(calloc(256), (64,), F32)
        BSROW = T(calloc(2048), (1024,), BF16, parts=1)
        SMALL = T(calloc(256), (64,), F32)
        WP = T(calloc(4096), (2, 1024), BF16)
        SCR_END = coff
        ONES_S = lambda: CP[:, 0:128]
        ONES1 = lambda ps=slice(0, 128): CP[ps, 128:256]
        BDIAG = lambda: CP[:, 256:384]
        TRINEG = lambda: CP[:, 384:512]
        MASK01 = lambda: CP[:, 512:640]
        ESEL = lambda kb: CP[:, 640 + 15 - kb: 640 + 15 - kb + 128]

        scr = [SCR0]

        def salloc(shape, dt, parts=128):
            n = 1
            for s in shape:
                n *= s
            nb = n * (4 if dt == F32 else 2)
            nb = (nb + 255) // 256 * 256
            off = scr[0]
            scr[0] += nb
            assert scr[0] <= SCR_END, ("scratch overflow", scr[0], SCR_END)
            return T(off, shape, dt, parts=parts)

        def sreset():
            scr[0] = SCR0

        def act(out, in_, func, bias=None, scale=None, accum=None, extra=()):
            kw = {}
            if bias is not None:
                kw["bias"] = bias.ap if isinstance(bias, Ref) else bias
            if scale is not None:
                kw["scale"] = scale.ap if isinstance(scale, Ref) else scale
            if accum is not None:
                kw["accum_out"] = accum.ap
            rds = [in_] + [r for r in (bias, scale) if isinstance(r, Ref)] + list(extra)
            wr = [out] + ([accum] if accum is not None else [])
            return S.op("act", lambda e: e.activation(out.ap, in_.ap, func, **kw), reads=rds, writes=wr)

        def tt(eng, out, a, b, op):
            return S.op(eng, lambda e: e.tensor_tensor(out.ap, a.ap, b.ap, op), reads=[a, b], writes=[out])

        def stt(out, a, sc, b, op0, op1):
            scv = sc.ap if isinstance(sc, Ref) else sc
            rds = [a, b] + ([sc] if isinstance(sc, Ref) else [])
            return S.op("dve", lambda e: e.scalar_tensor_tensor(out.ap, a.ap, scv, b.ap, op0, op1), reads=rds, writes=[out])

        def ts(eng, out, a, s1, s2, op0, op1=None):
            s1v = s1.ap if isinstance(s1, Ref) else s1
            s2v = s2.ap if isinstance(s2, Ref) else s2
            rds = [a] + [r for r in (s1, s2) if isinstance(r, Ref)]
            if op1 is None:
                return S.op(eng, lambda e: e.tensor_scalar(out.ap, a.ap, s1v, None, op0), reads=rds, writes=[out])
            return S.op(eng, lambda e: e.tensor_scalar(out.ap, a.ap, s1v, s2v, op0, op1), reads=rds, writes=[out])

        def cp(eng, out, in_):
            if eng == "act":
                return act(out, in_, AF.Copy)
            return S.op(eng, lambda e: e.tensor_copy(out.ap, in_.ap), reads=[in_], writes=[out])

        def recip(out, in_):
            return S.op("dve", lambda e: e.reciprocal(out.ap, in_.ap), reads=[in_], writes=[out])

        def mm(out, pairs, start=True, stop=True):
            n = len(pairs)

            def fn(e):
                ins = None
                for i, (l, r) in enumerate(pairs):
                    ins = e.matmul(out.ap, l.ap, r.ap, start=(start and i == 0), stop=(stop and i == n - 1))
                return ins
            rds = []
            for l, r in pairs:
                rds.append(l)
                rds.append(r)
            return S.op("pe", fn, reads=rds, writes=[out])

        def tr(out, in_, ident):
            return S.op("pe", lambda e: e.transpose(out.ap, in_.ap, ident.ap), reads=[in_, ident], writes=[out])

        def dma(eng, out, in_ap, stream, reads=()):
            return S.op(eng, lambda e: e.dma_start(out=out.ap, in_=in_ap), reads=list(reads), writes=[out], dma=stream)

        class Rot:
            def __init__(self, ids):
                self.ids = list(ids)
                self.i = 0

            def next(self):
                b = self.ids[self.i % len(self.ids)]
                self.i += 1
                return b

        rot = Rot(range(8))
        tog = [0]

        def evac_eng():
            tog[0] ^= 1
            return "act" if tog[0] else "dve"

        def wslot(n, shape):
            return T(WOFF[n % 4], shape, BF16)

        def kp(ap2d):
            return ap2d.rearrange("(k p) f -> p k f", p=128)

        loaders = []

        def L_full(src2d, c0, c1):
            def f(n):
                w = wslot(n, (8, c1 - c0))
                dma("pool", w[:, :, :], kp(src2d)[:, :, c0:c1], "W%da" % (n % 4))
            return f

        def L_mlp(l, gi):
            def f(n):
                wu = Tile(arena, WOFF[n % 4], (8, 512), BF16)
                wd = Tile(arena, WOFF[n % 4] + 8192, (4, 1024), BF16)
                dma("pool", wu[:, :, :], kp(w_up[l])[:, :, gi * 512:(gi + 1) * 512], "W%da" % (n % 4))
                dma("pool", wd[:, :, :], kp(w_down[l][gi * 512:(gi + 1) * 512, :]), "W%db" % (n % 4))
            return f

        loaders.append(L_full(w_in_a[0], 0, 1024))
        loaders.append(L_full(w_in_a[0], 1024, 2048))
        loaders.append(L_full(w_out_a[0], 0, 1024))
        for gi in range(8):
            loaders.append(L_mlp(0, gi))
        loaders.append(L_full(w_ple_gate[0], 0, 1024))
        loaders += [None, None, None]
        for gi in range(8):
            loaders.append(L_mlp(1, gi))
        loaders.append(L_full(w_ple_gate[1], 0, 1024))
        issued = [0]

        def w_issue_upto(n):
            while issued[0] <= n and issued[0] < len(loaders):
                f = loaders[issued[0]]
                if f is not None:
                    f(issued[0])
                issued[0] += 1

        def w_done(n):
            w_issue_upto(n + 4)

        sreset()
        dma("sp", IDENT[:, :], c_ident, "c0")
        dma("sp", MASKLE[:, :], c_maskle, "c1")
        dma("pool", CP[:, :], c_pack, "c2")
        dma("pool", SSEL[:, :, :], c_ssel.rearrange("p (a b) -> p a b", a=16), "c3")
        dma("pool", BSROW[0:1, :], b_spatial[0].rearrange("(o g) t -> o (g t)", o=1), "c4")
        dma("pool", WP[:, :, :], kp(w_ple_proj[0]), "wp")
        w_issue_upto(2)
        gst = salloc((128,), F32, parts=64)
        gi_ = 0
        for src in (ln_mix_a[0], ln_mlp[0], ln_ple[0], ln_kv, ln_mix_b[0], ln_mlp[1], ln_ple[1]):
            dma("sp", gst[gi_ * 8:(gi_ + 1) * 8, :], src.rearrange("(k f) -> k f", f=128), "g%d" % gi_)
            gi_ += 1
        gk2 = g_k.rearrange("(o f) -> o f", o=1)
        dma("sp", gst[56:57, 0:64], gk2, "g7")
        dma("sp", gst[56:57, 64:128], gk2, "g8")
        dma("sp", gst[57:58, 0:64], g_q, "g9")
        dma("sp", gst[57:58, 64:128], g_q, "g10")
        b = rot.next()
        tr(PB[b][:, 0:58], gst[0:58, :], IDENT[0:58, 0:58])
        cp("dve", GCOLS[:, 0:58], PB[b][:, 0:58])
        ts("dve", GCOLS[:, 58:59], GCOLS[:, 57:58], 0.125, None, ALU.mult)

        xs = [T(WOFF[3] + i * 4096, (1024,), F32) for i in range(4)]

        def xload(t):
            for blk in range(t * 4, t * 4 + 4):
                st = xs[blk % 4]
                dma("sp", st[:, :], x[blk * 128:(blk + 1) * 128, :], "x%d" % (blk % 4))
                for half in range(2):
                    b = rot.next()
                    for cc in range(4):
                        c = half * 4 + cc
                        tr(PB4[b][:, cc, :], st[:, c * 128:(c + 1) * 128], IDENT[:, :])
                    cp(evac_eng(), XT[:, half * 4:(half + 1) * 4, blk * 128:(blk + 1) * 128], PB4[b][:, :, :])

        if stop_after < 1:
            for t in range(NT):
                xload(t)

        def norm_A(t, sq):
            tl = slice(t * TT, (t + 1) * TT)
            for c in range(8):
                act(sq[:, c, :], XT[:, c, tl], AF.Square)

        def norm_B(t, gcol0, sq, sd, rstd, dst=HT):
            tl = slice(t * TT, (t + 1) * TT)
            b = rot.next()
            mm(PB[b][:, :], [(ONES_S(), sq[:, c, :]) for c in range(8)])
            act(sd[:, :], PB[b][:, :], AF.Ln, bias=EPS)
            act(rstd[:, :], sd[:, :], AF.Exp, scale=-0.5)
            for c in range(8):
                if gcol0 is None:
                    tt("dve", dst[:, c, tl], XT[:, c, tl], rstd[:, :], ALU.mult)
                else:
                    stt(dst[:, c, tl], XT[:, c, tl], GCOLS[:, gcol0 + c:gcol0 + c + 1], rstd[:, :], ALU.mult, ALU.mult)

        def norm_tile(t, gcol0, sq, sd, rstd, dst=HT):
            norm_A(t, sq)
            norm_B(t, gcol0, sq, sd, rstd, dst)

        if stop_after >= 1:
            sreset()
            uT = salloc((8, TT), BF16)
            sq = uT
            yT = salloc((8, TT), BF16)
            v16 = [salloc((1024,), BF16) for _ in range(2)]
            vn = [salloc((1024,), BF16) for _ in range(2)]
            sd = salloc((TT,), F32)
            rstd = salloc((TT,), F32)
            wsT = salloc((8, 128), BF16)
            gvt = salloc((1024,), F32)
            wsl = T(yT.off, (8, 128), F32)
            dma("sp", gvt[:, :], g_v_a.partition_broadcast(128), "gv")
            dma("sp", wsl[:, :, :], w_spatial[0].rearrange("g t s -> t g s"), "ws")
            xload(0)
            for hb in range(2):
                b = rot.next()
                for gg in range(4):
                    tr(PB4[b][:, gg, :], wsl[:, hb * 4 + gg, :], IDENT[:, :])
                tt("dve", wsT[:, hb * 4:(hb + 1) * 4, :], PB4[b][:, :, :],
                   Ref(MASKLE.ap.unsqueeze(1).to_broadcast([128, 4, 128]), MASKLE[:, :].keys), ALU.mult)
            W0 = wslot(0, (8, 1024))
            W1 = wslot(1, (8, 1024))
            W2 = wslot(2, (8, 1024))
            norm_A(0, sq)
            norm_B(0, G_MIXA, sq, sd, rstd)
            for t in range(NT):
                tl = slice(t * TT, (t + 1) * TT)
                if t + 1 < NT:
                    xload(t + 1)
                    if t + 1 == NT - 1:
                        w_issue_upto(3)
                for f in range(8):
                    b = rot.next()
                    mm(PB[b][:, :], [(W0[:, k, f * 128:(f + 1) * 128], HT[:, k, tl]) for k in range(8)])
                    act(uT[:, f, :], PB[b][:, :], AF.Gelu_apprx_tanh)

                def VST(blk, t=t):
                    tb = t * 4 + blk
                    tbl = slice(tb * 128, (tb + 1) * 128)
                    for half in range(2):
                        b = rot.next()
                        mm(PB[b][:, :], [(HT[:, k, tbl], W1[:, k, half * 512:(half + 1) * 512]) for k in range(8)])
                        act(v16[blk % 2][:, half * 512:(half + 1) * 512], PB[b][:, :], AF.Gelu_apprx_tanh)

                def NST(blk):
                    vb = vn[blk % 2]
                    vv = v16[blk % 2]
                    c4 = (blk % 2) * 4
                    act(vb[:, :], vv[:, :], AF.Square, accum=SMALL[:, c4:c4 + 1])
                    ts("dve", SMALL[:, c4 + 1:c4 + 2], SMALL[:, c4:c4 + 1], 1.0 / 1024, EPS, ALU.mult, ALU.add)
                    act(SMALL[:, c4 + 2:c4 + 3], SMALL[:, c4 + 1:c4 + 2], AF.Sqrt)
                    recip(SMALL[:, c4 + 3:c4 + 4], SMALL[:, c4 + 2:c4 + 3])
                    stt(vb[:, :], vv[:, :], SMALL[:, c4 + 3:c4 + 4], gvt[:, :], ALU.mult, ALU.mult)

                def SPST(blk):
                    vb = vn[blk % 2]
                    for hb in range(2):
                        b = rot.next()

                        def fn(e, b=b, hb=hb, vb=vb):
                            ins = None
                            for gg in range(4):
                                g = hb * 4 + gg
                                e.matmul(PB4[b][:, gg, :].ap, vb[:, g * 128:(g + 1) * 128].ap, wsT[:, g, :].ap,
                                         start=True, stop=False)
                                ins = e.matmul(PB4[b][:, gg, :].ap, ONES1(slice(0, 1)).ap,
                                               BSROW[0:1, g * 128:(g + 1) * 128].ap, start=False, stop=True)
                            return ins
                        S.op("pe", fn, reads=[vb[:, hb * 512:(hb + 1) * 512], wsT[:, hb * 4:(hb + 1) * 4, :],
                                              ONES1(slice(0, 1)), BSROW[0:1, :]], writes=[PB4[b][:, :, :]])
                        tt("dve", yT[:, hb * 4:(hb + 1) * 4, blk * 128:(blk + 1) * 128], PB4[b][:, :, :],
                           uT[:, hb * 4:(hb + 1) * 4, blk * 128:(blk + 1) * 128], ALU.mult)

                VST(0)
                VST(1)
                NST(0)
                for blk in range(4):
                    if blk + 2 < 4:
                        VST(blk + 2)
                    if blk + 1 < 4:
                        NST(blk + 1)
                    SPST(blk)
                if t + 1 < NT:
                    norm_A(t + 1, sq)
                for dch in range(8):
                    b = rot.next()
                    mm(PB[b][:, :], [(W2[:, k, dch * 128:(dch + 1) * 128], yT[:, k, :]) for k in range(8)])
                    tt("dve", XT[:, dch, tl], PB[b][:, :], XT[:, dch, tl], ALU.add)
                if t + 1 < NT:
                    norm_B(t + 1, G_MIXA, sq, sd, rstd)
            w_done(0)
            w_done(1)
            w_done(2)

        def mlp_phase(l, blk0, gcol0):
            sreset()
            sq = salloc((8, TT), BF16)
            sd = salloc((TT,), F32)
            rstd = salloc((TT,), F32)
            aT = [salloc((8, TT), BF16) for _ in range(2)]
            rt = [salloc((TT,), BF16) for _ in range(2)]
            norm_tile(0, gcol0, sq, sd, rstd)
            it = 0
            for sg in range(4):
                nA, nB = blk0 + 2 * sg, blk0 + 2 * sg + 1
                wu = [Tile(arena, WOFF[n % 4], (8, 512), BF16) for n in (nA, nB)]
                wd = [Tile(arena, WOFF[n % 4] + 8192, (4, 1024), BF16) for n in (nA, nB)]
                for t in range(NT):
                    tl = slice(t * TT, (t + 1) * TT)
                    a = aT[it % 2]
                    it += 1
                    for f in range(8):
                        b = rot.next()
                        mm(PB[b][:, :], [(wu[f // 4][:, k, (f % 4) * 128:(f % 4 + 1) * 128], HT[:, k, tl]) for k in range(8)])
                        r = rt[f % 2]
                        act(r[:, :], PB[b][:, :], AF.Relu)
                        tt("dve", a[:, f, :], r[:, :], r[:, :], ALU.mult)
                    if sg == 0 and t + 1 < NT:
                        norm_A(t + 1, sq)
                    for dch in range(8):
                        b = rot.next()
                        mm(PB[b][:, :], [(wd[f // 4][:, f % 4, dch * 128:(dch + 1) * 128], a[:, f, :]) for f in range(8)])
                        tt("dve", XT[:, dch, tl], PB[b][:, :], XT[:, dch, tl], ALU.add)
                    if sg == 0 and t + 1 < NT:
                        norm_B(t + 1, gcol0, sq, sd, rstd)
                w_done(nA)
                w_done(nB)

        out_fin = []

        def emit_out(t, osb):
            for blk in range(t * 4, t * 4 + 4):
                ot = osb[blk % 2]
                for half in range(2):
                    b = rot.next()
                    for cc in range(4):
                        c = half * 4 + cc
                        tr(PB4[b][:, cc, :], XT[:, c, blk * 128:(blk + 1) * 128], IDENT[:, :])
                    cp(evac_eng(), ot[:, half * 512:(half + 1) * 512], PB[b][:, :])
                out_fin.append(S.op("sp", lambda e, ot=ot, blk=blk: e.dma_start(out=y[blk * 128:(blk + 1) * 128, :], in_=ot[:, :].ap),
                                    reads=[ot[:, :]], dma="o%d" % (blk % 2)))

        def ple_phase(l, blkn, gcol0, with_out=False):
            sreset()
            sq = salloc((8, TT), BF16)
            sd = salloc((TT,), F32)
            rstd = salloc((TT,), F32)
            pst = salloc((4, 256), F32)
            pT = salloc((2, TT), BF16)
            gate = [salloc((TT,), F32) for _ in range(2)]
            tmp2 = [salloc((TT,), F32) for _ in range(2)]
            osb = [salloc((1024,), F32) for _ in range(2)] if with_out else None
            Wg = wslot(blkn, (8, 1024))
            for t in range(NT):
                tl = slice(t * TT, (t + 1) * TT)
                dma("sp", pst[:, :, :], p[l][t * TT:(t + 1) * TT, :].rearrange("(b q) f -> q b f", q=128), "pl")
                if t == 0:
                    norm_tile(0, gcol0, sq, sd, rstd)
                for kk in range(2):
                    b = rot.next()
                    for blk in range(4):
                        tr(PB4[b][:, blk, :], pst[:, blk, kk * 128:(kk + 1) * 128], IDENT[:, :])
                    cp(evac_eng(), pT[:, kk, :], PB[b][:, :])
                if t + 1 < NT:
                    norm_A(t + 1, sq)
                for dch in range(8):
                    if dch == 4 and t + 1 < NT:
                        norm_B(t + 1, gcol0, sq, sd, rstd)
                    b1 = rot.next()
                    mm(PB[b1][:, :], [(Wg[:, k, dch * 128:(dch + 1) * 128], HT[:, k, tl]) for k in range(8)])
                    g_ = gate[dch % 2]
                    act(g_[:, :], PB[b1][:, :], AF.Sigmoid)
                    b2 = rot.next()
                    mm(PB[b2][:, :], [(WP[:, kk, dch * 128:(dch + 1) * 128], pT[:, kk, :]) for kk in range(2)])
                    t2 = tmp2[dch % 2]
                    tt("dve", t2[:, :], PB[b2][:, :], g_[:, :], ALU.mult)
                    tt("dve", XT[:, dch, tl], XT[:, dch, tl], t2[:, :], ALU.add)
                if with_out:
                    emit_out(t, osb)
            w_done(blkn)

        if stop_after >= 2:
            mlp_phase(0, 3, G_MLP0)
        if stop_after >= 3:
            ple_phase(0, 11, G_PLE0)
            dma("pool", WP[:, :, :], kp(w_ple_proj[1]), "wp")

        if stop_after >= 4:
            sreset()
            sq = salloc((8, TT), BF16)
            sd = salloc((TT,), F32)
            rstd = salloc((TT,), F32)
            for t in range(NT):
                norm_tile(t, None, sq, sd, rstd)
            sreset()
            KT = salloc((SEQ,), BF16)
            QTh = [salloc((SEQ,), BF16) for _ in range(2)]
            VJ = salloc((16, 128), BF16)
            OT = salloc((SEQ,), BF16)
            e1 = [salloc((TT,), F32) for _ in range(2)]
            AT = [salloc((TT,), BF16) for _ in range(3)]
            csb = [salloc((TT,), BF16) for _ in range(2)]
            PW = [Tile(arena, WOFF[0] + i * 8192, (4096,), BF16) for i in range(2)]
            LS = [Tile(arena, WOFF[1], (16, TT), BF16), Tile(arena, WOFF[2], (16, TT), BF16)]
            sqK = Tile(arena, WOFF[2], (4, TT), BF16)
            sqQ = Tile(arena, WOFF[2] + 4096, (4, TT), BF16)
            sd4 = Tile(arena, WOFF[2] + 8192, (4, TT), F32)
            rawK = Tile(arena, WOFF[1], (4, TT), F32)
            rawQ = Tile(arena, WOFF[1] + 8192, (4, TT), F32)
            srot = Rot(range(5))
            zrot = Rot([0, 1])
            grot = Rot([2, 3])
            frot = Rot([6, 7])
            JUNK = 7

            def dummy(k):
                def fn(e):
                    ins = None
                    for _ in range(k):
                        ins = e.matmul(PB[JUNK][:, :].ap, ONES_S().ap, CP[:, 0:512].ap, start=True, stop=True)
                    return ins
                S.op("pe", fn, reads=[ONES_S(), CP[:, 0:512]], writes=[PB[JUNK][:, :]])
            CSB = 4
            OB = [5, 5]
            for bi in range(2):
                S.op("dve", lambda e, bi=bi: e.memset(csb[bi][:, :].ap, 0.0), writes=[csb[bi][:, :]])
                S.op("dve", lambda e, bi=bi: e.memset(QTh[bi][:, :].ap, 0.0), writes=[QTh[bi][:, :]])

            def pw_views(i):
                t_ = PW[i]
                off = t_.off
                wk = Tile(arena, off, (8, 128), BF16)
                wv = Tile(arena, off + 2048, (8, 128), BF16)
                wq = Tile(arena, off + 4096, (8, 128), BF16)
                wo = Tile(arena, off + 6144, (1024,), BF16)
                return wk, wv, wq, wo

            def load_pair(j):
                wk, wv, wq, wo = pw_views(j % 2)
                sl = slice(j * 128, (j + 1) * 128)
                dma("pool", wk[:, :, :], kp(w_kv)[:, :, j * 128:(j + 1) * 128], "pw%da" % (j % 2))
                dma("pool", wv[:, :, :], kp(w_kv)[:, :, 1024 + j * 128:1024 + (j + 1) * 128], "pw%db" % (j % 2))
                dma("pool", wq[:, :, :], kp(w_q[0])[:, :, j * 128:(j + 1) * 128], "pw%dc" % (j % 2))
                dma("pool", wo[:, :], w_out_b[0][sl, :], "pw%dd" % (j % 2))
                for w_, g0 in ((wk, G_KV), (wv, G_KV), (wq, G_MIXB)):
                    gb = Ref(GCOLS.ap[:, g0:g0 + 8].unsqueeze(2).to_broadcast([128, 8, 128]), GCOLS[:, g0:g0 + 8].keys)
                    tt("dve", w_[:, :, :], w_[:, :, :], gb, ALU.mult)

            load_pair(0)
            for j in range(8):
                wk, wv, wq, wo = pw_views(j % 2)
                if j + 1 < 8:
                    load_pair(j + 1)
                KB = [0, 1, 2, 3]
                QB = [4, 5, 6, 7]
                for (w_, raw, sqx, mb) in ((wk, rawK, sqK, KB), (wq, rawQ, sqQ, QB)):
                    for t in range(NT):
                        tl = slice(t * TT, (t + 1) * TT)
                        mm(PB[mb[t]][:, :], [(w_[:, k, :], HT[:, k, tl]) for k in range(8)])
                        act(raw[:, t, :], PB[mb[t]][:, :], AF.Copy)
                        act(sqx[:, t, :], PB[mb[t]][:, :], AF.Square)
                for (sqx, mb) in ((sqK, KB), (sqQ, QB)):
                    for t in range(NT):
                        mm(PB[mb[t]][:, :], [(BDIAG(), sqx[:, t, :])])
                for t in range(NT):
                    act(sd4[:, t, :], PB[KB[t]][:, :], AF.Ln, bias=EPS)
                for t in range(NT):
                    act(sd4[:, t, :], sd4[:, t, :], AF.Exp, scale=-0.5)
                for t in range(NT):
                    tl = slice(t * TT, (t + 1) * TT)
                    stt(KT[:, tl], rawK[:, t, :], GCOLS[:, G_K:G_K + 1], sd4[:, t, :], ALU.mult, ALU.mult)
                for q4 in range(4):
                    b_ = KB[q4]
                    for bb in range(4):
                        blk = q4 * 4 + bb
                        mm(PB4[b_][:, bb, :], [(HT[:, k, blk * 128:(blk + 1) * 128], wv[:, k, :]) for k in range(8)])
                    cp("dve", VJ[:, q4 * 4:(q4 + 1) * 4, :], PB4[b_][:, :, :])
                for t in range(NT):
                    act(sd4[:, t, :], PB[QB[t]][:, :], AF.Ln, bias=EPS)
                for t in range(NT):
                    act(sd4[:, t, :], sd4[:, t, :], AF.Exp, scale=-0.5)
                for t in range(NT):
                    tl = slice(t * TT, (t + 1) * TT)
                    for h_ in range(2):
                        hp = slice(h_ * 64, (h_ + 1) * 64)
                        stt(QTh[h_][hp, tl], rawQ[hp, t, :], GCOLS[hp, G_QS:G_QS + 1], sd4[hp, t, :], ALU.mult, ALU.mult)
                heads = (slice(0, 64), slice(64, 128))

                def cols(qt, kb):
                    c0 = max(0, kb - 4 * qt) * 128
                    return c0, slice(c0, TT), slice(qt * TT + c0, (qt + 1) * TT)

                def p1_steps(h, qt):
                    nkb = 4 * qt + 4
                    L = LS[h]
                    zb = {}

                    def Z(kb):
                        c0, cs, qs = cols(qt, kb)
                        zb[kb] = zrot.next()
                        mm(PB[zb[kb]][:, cs], [(KT[:, kb * 128:(kb + 1) * 128], QTh[h][:, qs])])

                    def EXP(kb):
                        c0, cs, qs = cols(qt, kb)
                        act(e1[kb % 2][:, cs], PB[zb[kb]][:, cs], AF.Exp)

                    def LN(kb):
                        c0, cs, qs = cols(qt, kb)
                        act(L[:, kb, cs], e1[kb % 2][:, cs], AF.Ln, bias=1.0)
                        if kb >= 4 * qt:
                            tt("pool", L[:, kb, c0:c0 + 128], L[:, kb, c0:c0 + 128], MASK01(), ALU.mult)

                    def CS(kb):
                        c0, cs, qs = cols(qt, kb)
                        mm(PB[CSB][:, cs], [(ESEL(kb), L[:, kb, cs])], start=(kb == 0), stop=(kb == nkb - 1))

                    steps = []
                    for i in range(nkb + 3):
                        def st(i=i):
                            if i < nkb:
                                Z(i)
                            if 0 <= i - 1 < nkb:
                                EXP(i - 1)
                            if 0 <= i - 2 < nkb:
                                LN(i - 2)
                            if 0 <= i - 3 < nkb:
                                CS(i - 3)
                        steps.append(st)

                    def fin():
                        cp("dve", csb[h][0:16, :], PB[CSB][0:16, :])
                    steps.append(fin)
                    return steps

                def p2_steps(h, qt):
                    nkb = 4 * qt + 4
                    L = LS[h]
                    pr = heads[h]
                    gb = {}

                    def G(kb):
                        c0, cs, qs = cols(qt, kb)
                        b = gb[kb] = grot.next()

                        def fn(e):
                            e.matmul(PB[b][:, cs].ap, SSEL[:, kb, :].ap, csb[h][:, cs].ap, start=True, stop=False)
                            e.matmul(PB[b][:, cs].ap, TRINEG().ap, L[:, kb, cs].ap, start=False, stop=False)
                            return e.matmul(PB[b][:, cs].ap, KT[:, kb * 128:(kb + 1) * 128].ap, QTh[h][:, qs].ap,
                                            start=False, stop=True)
                        S.op("pe", fn, reads=[TRINEG(), L[:, kb, cs], SSEL[:, kb, :], csb[h][:, cs],
                                              KT[:, kb * 128:(kb + 1) * 128], QTh[h][:, qs]], writes=[PB[b][:, cs]])

                    def EXPA(kb):
                        c0, cs, qs = cols(qt, kb)
                        a_ = AT[kb % 3]
                        act(a_[:, cs], PB[gb[kb]][:, cs], AF.Exp)
                        if kb >= 4 * qt:
                            tt("pool", a_[:, c0:c0 + 128], a_[:, c0:c0 + 128], MASK01(), ALU.mult)

                    def AV(kb):
                        c0, cs, qs = cols(qt, kb)
                        mm(PB[OB[h]][:, cs], [(VJ[:, kb, :], AT[kb % 3][:, cs])], start=(kb == 0), stop=(kb == nkb - 1))

                    steps = []
                    for i in range(nkb + 1):
                        def st(i=i):
                            if i < nkb:
                                G(i)
                            if 0 <= i - 1 < nkb:
                                EXPA(i - 1)
                                AV(i - 1)
                        steps.append(st)

                    def fin():
                        cp("dve", OT[pr, qt * TT:(qt + 1) * TT], PB[OB[h]][pr, :])
                    steps.append(fin)
                    return steps

                pending = []

                def filler(k):
                    for _ in range(k):
                        if not pending:
                            return
                        qt_, dch = pending.pop(0)
                        tl = slice(qt_ * TT, (qt_ + 1) * TT)
                        b = frot.next()
                        mm(PB[b][:, :], [(wo[:, dch * 128:(dch + 1) * 128], OT[:, tl])])
                        tt("dve", XT[:, dch, tl], PB[b][:, :], XT[:, dch, tl], ALU.add)

                units = [(h, qt) for qt in range(NT) for h in range(2)]
                prev = None
                for u in units + [None]:
                    s1 = p1_steps(*u) if u is not None else []
                    s2 = p2_steps(*prev) if prev is not None else []
                    n = max(len(s1), len(s2))
                    for i in range(n):
                        if i < len(s2):
                            s2[i]()
                        if i < len(s1):
                            s1[i]()
                        if i == n // 2 or i == n - 1:
                            filler(2)
                    if prev is not None and prev[0] == 1:
                        pending.extend((prev[1], dch) for dch in range(8))
                    prev = u
                filler(len(pending))
            w_done(12)
            w_done(13)
            w_done(14)

        if stop_after >= 5:
            w_issue_upto(18)
            mlp_phase(1, 15, G_MLP1)
        if stop_after >= 6:
            ple_phase(1, 23, G_PLE1, with_out=True)

        if not out_fin:
            sreset()
            osb = [salloc((1024,), F32) for _ in range(2)]
            for t in range(NT):
                emit_out(t, osb)
        S.emit(final_waits=out_fin[-2:])
    return nc


def make_consts():
    ident = np.eye(128, dtype=np.float32)
    s = np.arange(128)[:, None]
    t = np.arange(128)[None, :]
    maskle = (s <= t).astype(np.float32)
    pack = np.zeros((128, 784), np.float32)
    pack[:, 0:128] = 1.0 / 1024
    pack[:, 128:256] = 1.0
    bd = np.zeros((128, 128), np.float32)
    bd[0:64, 0:64] = 1.0 / 64
    bd[64:128, 64:128] = 1.0 / 64
    pack[:, 256:384] = bd
    pack[:, 384:512] = -(s >= t).astype(np.float32)
    pack[:, 512:640] = (s < t).astype(np.float32)
    pack[:, 640 + 15] = -1.0
    pack[:, 640 + 47] = -1.0
    ssel = np.zeros((128, 16, 128), np.float32)
    for kb in range(16):
        for r in range(16):
            if r > kb:
                ssel[r, kb, :] = 1.0
                ssel[32 + r, kb, :] = 1.0
    return {"c_ident": ident, "c_maskle": maskle, "c_pack": pack, "c_ssel": ssel.reshape(128, 2048)}


_CACHE = {}


def kernel(**inputs):
    n = 8
    if "nc" not in _CACHE:
        nc = bass.Bass("TRN2", target_bir_lowering=False)
        build_program(nc)
        _CACHE["nc"] = nc
    nc = _CACHE["nc"]
    consts = make_consts()
    shared = {k: np.ascontiguousarray(np.asarray(v, dtype=np.float32)) for k, v in inputs.items() if k not in ("x", "p")}
    shared.update(consts)
    x = np.asarray(inputs["x"], dtype=np.float32)
    p = np.asarray(inputs["p"], dtype=np.float32)
    in_maps = []
    for i in range(n):
        m = dict(shared)
        m["x"] = np.ascontiguousarray(x[i])
        m["p"] = np.ascontiguousarray(p[:, i])
        in_maps.append(m)
    res = run_bass_kernel_spmd(nc, in_maps, core_ids=list(range(n)))
    return np.stack([np.asarray(r["y"], dtype=np.float32) for r in res.results], axis=0)
```

```python
import contextlib
import numpy as np
import concourse.bass as bass
import concourse.mybir as mybir
from concourse.bass_utils import run_bass_kernel_spmd

F32 = mybir.dt.float32
BF16 = mybir.dt.bfloat16
AF = mybir.ActivationFunctionType
ALU = mybir.AluOpType

SEQ = 2048
D = 1024
NT = 4
TT = 512
NC8 = 8
EPS = 1e-6
GR = 256
ENGS = ("pe", "act", "dve", "pool", "sp")

G_MIXA, G_MLP0, G_PLE0, G_KV, G_MIXB, G_MLP1, G_PLE1, G_K, G_Q, G_QS = 0, 8, 16, 24, 32, 40, 48, 56, 57, 58


class Op:
    __slots__ = ("eng", "fn", "deps", "signal", "idx", "dma", "dsem", "dval", "count", "waits")

    def __init__(self, eng, fn, dma):
        self.eng = eng
        self.fn = fn
        self.dma = dma
        self.signal = False
        self.count = 0
        self.dsem = 0
        self.dval = 0
        self.deps = None
        self.waits = None


class Ref:
    __slots__ = ("ap", "keys")

    def __init__(self, ap, keys):
        self.ap = ap
        self.keys = keys


class Sched:
    def __init__(self, nc):
        self.nc = nc
        self.ops = []
        self.lastw = {}
        self.rd = {}
        self.streams = {}

    def op(self, eng, fn, reads=(), writes=(), dma=None):
        o = Op(eng, fn, dma)
        o.idx = len(self.ops)
        deps = {}

        def add(d):
            k = ("d", d.dma) if d.dma is not None else d.eng
            p = deps.get(k)
            if p is None or p.idx < d.idx:
                deps[k] = d

        lastw = self.lastw
        rd = self.rd
        for r in reads:
            for k in r.keys:
                w = lastw.get(k)
                if w is not None:
                    add(w)
        for r in writes:
            for k in r.keys:
                w = lastw.get(k)
                if w is not None:
                    add(w)
                rr = rd.get(k)
                if rr:
                    for x in rr.values():
                        add(x)
        if dma is not None:
            p = self.streams.get(dma)
            if p is not None:
                add(p)
            self.streams[dma] = o
        mykey = ("d", dma) if dma is not None else eng
        if eng == "pe" and dma is None:
            deps.pop("pe", None)
        o.deps = deps
        for r in reads:
            for k in r.keys:
                d = rd.get(k)
                if d is None:
                    rd[k] = {mykey: o}
                else:
                    d[mykey] = o
        for r in writes:
            for k in r.keys:
                lastw[k] = o
                rd[k] = {}
        self.ops.append(o)
        return o

    def emit(self, final_waits=()):
        nc = self.nc
        by_eng = {e: [o for o in self.ops if o.eng == e] for e in ENGS}
        for e in ENGS:
            waited = {}
            for o in by_eng[e]:
                w = []
                for k, d in o.deps.items():
                    if waited.get(k, -1) >= d.idx:
                        continue
                    waited[k] = d.idx
                    d.signal = True
                    w.append(d)
                o.waits = w
        for o in final_waits:
            o.signal = True
        cnt = {e: 0 for e in ENGS}
        streams = {}
        for o in self.ops:
            if o.dma is not None:
                st = streams.get(o.dma)
                if st is None:
                    st = streams[o.dma] = [len(streams), 0]
                st[1] += 16
                o.dsem = st[0]
                o.dval = st[1]
            elif o.signal:
                cnt[o.eng] += 1
                o.count = cnt[o.eng]
        with contextlib.ExitStack() as es:
            esem = {e: es.enter_context(nc.semaphore("s_" + e)) for e in ENGS}
            dsem = [es.enter_context(nc.semaphore("d%d" % i)) for i in range(len(streams))]
            block = es.enter_context(nc.Block())

            def run(ename, eng):
                for o in by_eng[ename]:
                    for d in o.waits:
                        if d.dma is not None:
                            eng.wait_ge(dsem[d.dsem], d.dval)
                        else:
                            eng.wait_ge(esem[d.eng], d.count)
                    ins = o.fn(eng)
                    if o.dma is not None:
                        ins.then_inc(dsem[o.dsem], 16)
                    elif o.signal:
                        ins.then_inc(esem[ename], 1)
                if ename == "sp":
                    for o in final_waits:
                        if o.dma is not None:
                            eng.wait_ge(dsem[o.dsem], o.dval)
                        else:
                            eng.wait_ge(esem[o.eng], o.count)

            @block.tensor
            def _(e):
                run("pe", e)

            @block.scalar
            def _(e):
                run("act", e)

            @block.vector
            def _(e):
                run("dve", e)

            @block.gpsimd
            def _(e):
                run("pool", e)

            @block.sync
            def _(e):
                run("sp", e)


class Tile:
    def __init__(self, base_ap, off, shape, dt, parts=128, space="S", bank=0):
        self.off = off
        self.shape = tuple(shape)
        self.es = 4 if dt == F32 else 2
        self.parts = parts
        self.space = space
        self.bank = bank
        n = 1
        for s in shape:
            n *= s
        self.n = n
        if space == "S":
            v = base_ap[0:parts, off // 2: off // 2 + n * self.es // 2]
            if dt == F32:
                v = v.bitcast(F32)
        else:
            v = base_ap[0:parts, 0:n]
        if len(shape) == 2:
            v = v.rearrange("p (a b) -> p a b", a=shape[0])
        elif len(shape) == 3:
            v = v.rearrange("p (a b c) -> p a b c", a=shape[0], b=shape[1])
        self.ap = v

    def __getitem__(self, idx):
        if not isinstance(idx, tuple):
            idx = (idx,)
        ps = idx[0]
        fidx = list(idx[1:])
        while len(fidx) < len(self.shape):
            fidx.append(slice(None))
        ap = self.ap[(ps,) + tuple(fidx)]
        rng = []
        for i, s in zip(fidx, self.shape):
            if isinstance(i, int):
                rng.append((i, i + 1))
            else:
                a = 0 if i.start is None else i.start
                b = s if i.stop is None else i.stop
                rng.append((a, b))
        strides = []
        acc = 1
        for s in reversed(self.shape):
            strides.append(acc)
            acc *= s
        strides = strides[::-1]
        keys = set()
        outer = [()]
        for (a, b) in rng[:-1]:
            outer = [o + (i,) for o in outer for i in range(a, b)]
        la, lb = rng[-1]
        gr = GR if self.space == "S" else 512
        for o in outer:
            base = sum(i * st for i, st in zip(o, strides[:-1]))
            b0 = self.off + (base + la) * self.es
            b1 = self.off + (base + lb) * self.es
            for g in range(b0 // gr, (b1 - 1) // gr + 1):
                keys.add((self.space, self.bank, g))
        return Ref(ap, keys)


def build_program(nc, stop_after=99):
    dt_in = {}

    def din(name, shape):
        t = nc.dram_tensor(name, list(shape), F32, kind="ExternalInput").ap()
        dt_in[name] = t
        return t

    x = din("x", [SEQ, D])
    p = din("p", [2, SEQ, 256])
    ln_mix_a = din("ln_mix_a", [1, D])
    w_in_a = din("w_in_a", [1, D, 2 * D])
    g_v_a = din("g_v_a", [1, D])
    w_spatial = din("w_spatial", [1, 8, 128, 128])
    b_spatial = din("b_spatial", [1, 8, 128])
    w_out_a = din("w_out_a", [1, D, D])
    ln_kv = din("ln_kv", [D])
    w_kv = din("w_kv", [D, 2 * D])
    g_k = din("g_k", [64])
    ln_mix_b = din("ln_mix_b", [1, D])
    w_q = din("w_q", [1, D, D])
    g_q = din("g_q", [1, 64])
    w_out_b = din("w_out_b", [1, D, D])
    ln_mlp = din("ln_mlp", [2, D])
    w_up = din("w_up", [2, D, 4 * D])
    w_down = din("w_down", [2, 4 * D, D])
    ln_ple = din("ln_ple", [2, D])
    w_ple_gate = din("w_ple_gate", [2, D, D])
    w_ple_proj = din("w_ple_proj", [2, 256, D])
    c_ident = din("c_ident", [128, 128])
    c_maskle = din("c_maskle", [128, 128])
    c_pack = din("c_pack", [128, 784])
    c_ssel = din("c_ssel", [128, 2048])
    y = nc.dram_tensor("y", [SEQ, D], F32, kind="ExternalOutput").ap()

    es = contextlib.ExitStack()
    with es:
        ARENA_BYTES = 212736
        arena = es.enter_context(nc.sbuf_tensor("arena", [128, ARENA_BYTES // 2], BF16))
        banks = [es.enter_context(nc.psum_tensor("ps%d" % i, [128, 512], F32)) for i in range(8)]
        PB = [Tile(banks[i], 0, (512,), F32, space="P", bank=i) for i in range(8)]
        PB4 = [Tile(banks[i], 0, (4, 128), F32, space="P", bank=i) for i in range(8)]
        S = Sched(nc)

        def T(off, shape, dt, parts=128):
            return Tile(arena, off, shape, dt, parts=parts)

        XT = T(0, (8, SEQ), F32)
        HT = T(65536, (8, SEQ), BF16)
        WOFF = [98304 + i * 16384 for i in range(4)]
        SCR0 = 163840
        coff = ARENA_BYTES
        def calloc(nbytes):
            nonlocal coff
            coff -= (nbytes + 255) // 256 * 256
            return coff
        IDENT = T(calloc(512), (128,), F32)
        MASKLE = T(calloc(512), (128,), F32)
        CP = T(calloc(1568), (784,), BF16)
        SSEL = T(calloc(4096), (16, 128), BF16)
        GCOLS = T(calloc(256), (64,), F32)
        BSROW = T(calloc(2048), (1024,), BF16, parts=1)
        SMALL = T(calloc(256), (64,), F32)
        WP = T(calloc(4096), (2, 1024), BF16)
        SCR_END = coff
        ONES_S = lambda: CP[:, 0:128]
        ONES1 = lambda ps=slice(0, 128): CP[ps, 128:256]
        BDIAG = lambda: CP[:, 256:384]
        TRINEG = lambda: CP[:, 384:512]
        MASK01 = lambda: CP[:, 512:640]
        ESEL = lambda kb: CP[:, 640 + 15 - kb: 640 + 15 - kb + 128]

        scr = [SCR0]

        def salloc(shape, dt, parts=128):
            n = 1
            for s in shape:
                n *= s
            nb = n * (4 if dt == F32 else 2)
            nb = (nb + 255) // 256 * 256
            off = scr[0]
            scr[0] += nb
            assert scr[0] <= SCR_END, ("scratch overflow", scr[0], SCR_END)
            return T(off, shape, dt, parts=parts)

        def sreset():
            scr[0] = SCR0

        def act(out, in_, func, bias=None, scale=None, accum=None, extra=()):
            kw = {}
            if bias is not None:
                kw["bias"] = bias.ap if isinstance(bias, Ref) else bias
            if scale is not None:
                kw["scale"] = scale.ap if isinstance(scale, Ref) else scale
            if accum is not None:
                kw["accum_out"] = accum.ap
            rds = [in_] + [r for r in (bias, scale) if isinstance(r, Ref)] + list(extra)
            wr = [out] + ([accum] if accum is not None else [])
            return S.op("act", lambda e: e.activation(out.ap, in_.ap, func, **kw), reads=rds, writes=wr)

        def tt(eng, out, a, b, op):
            return S.op(eng, lambda e: e.tensor_tensor(out.ap, a.ap, b.ap, op), reads=[a, b], writes=[out])

        def stt(out, a, sc, b, op0, op1):
            scv = sc.ap if isinstance(sc, Ref) else sc
            rds = [a, b] + ([sc] if isinstance(sc, Ref) else [])
            return S.op("dve", lambda e: e.scalar_tensor_tensor(out.ap, a.ap, scv, b.ap, op0, op1), reads=rds, writes=[out])

        def ts(eng, out, a, s1, s2, op0, op1=None):
            s1v = s1.ap if isinstance(s1, Ref) else s1
            s2v = s2.ap if isinstance(s2, Ref) else s2
            rds = [a] + [r for r in (s1, s2) if isinstance(r, Ref)]
            if op1 is None:
                return S.op(eng, lambda e: e.tensor_scalar(out.ap, a.ap, s1v, None, op0), reads=rds, writes=[out])
            return S.op(eng, lambda e: e.tensor_scalar(out.ap, a.ap, s1v, s2v, op0, op1), reads=rds, writes=[out])

        def cp(eng, out, in_):
            if eng == "act":
                return act(out, in_, AF.Copy)
            return S.op(eng, lambda e: e.tensor_copy(out.ap, in_.ap), reads=[in_], writes=[out])

        def recip(out, in_):
            return S.op("dve", lambda e: e.reciprocal(out.ap, in_.ap), reads=[in_], writes=[out])

        def mm(out, pairs, start=True, stop=True):
            n = len(pairs)

            def fn(e):
                ins = None
                for i, (l, r) in enumerate(pairs):
                    ins = e.matmul(out.ap, l.ap, r.ap, start=(start and i == 0), stop=(stop and i == n - 1))
                return ins
            rds = []
            for l, r in pairs:
                rds.append(l)
                rds.append(r)
            return S.op("pe", fn, reads=rds, writes=[out])

        def tr(out, in_, ident):
            return S.op("pe", lambda e: e.transpose(out.ap, in_.ap, ident.ap), reads=[in_, ident], writes=[out])

        def dma(eng, out, in_ap, stream, reads=()):
            return S.op(eng, lambda e: e.dma_start(out=out.ap, in_=in_ap), reads=list(reads), writes=[out], dma=stream)

        class Rot:
            def __init__(self, ids):
                self.ids = list(ids)
                self.i = 0

            def next(self):
                b = self.ids[self.i % len(self.ids)]
                self.i += 1
                return b

        rot = Rot(range(8))
        tog = [0]

        def evac_eng():
            tog[0] ^= 1
            return "act" if tog[0] else "dve"

        def wslot(n, shape):
            return T(WOFF[n % 4], shape, BF16)

        def kp(ap2d):
            return ap2d.rearrange("(k p) f -> p k f", p=128)

        loaders = []

        def L_full(src2d, c0, c1):
            def f(n):
                w = wslot(n, (8, c1 - c0))
                dma("pool", w[:, :, :], kp(src2d)[:, :, c0:c1], "W%da" % (n % 4))
            return f

        def L_mlp(l, gi):
            def f(n):
                wu = Tile(arena, WOFF[n % 4], (8, 512), BF16)
                wd = Tile(arena, WOFF[n % 4] + 8192, (4, 1024), BF16)
                dma("pool", wu[:, :, :], kp(w_up[l])[:, :, gi * 512:(gi + 1) * 512], "W%da" % (n % 4))
                dma("pool", wd[:, :, :], kp(w_down[l][gi * 512:(gi + 1) * 512, :]), "W%db" % (n % 4))
            return f

        def L_first(n):
            w = wslot(n, (8, 1024))
            for i, sfx in enumerate("abcd"):
                dma("pool", w[:, :, i * 256:(i + 1) * 256], kp(w_in_a[0])[:, :, i * 256:(i + 1) * 256], "W0" + sfx)
        loaders.append(L_first)
        loaders.append(L_full(w_in_a[0], 1024, 2048))
        loaders.append(L_full(w_out_a[0], 0, 1024))
        for gi in range(8):
            loaders.append(L_mlp(0, gi))
        loaders.append(L_full(w_ple_gate[0], 0, 1024))
        loaders += [None, None, None]
        for gi in range(8):
            loaders.append(L_mlp(1, gi))
        loaders.append(L_full(w_ple_gate[1], 0, 1024))
        issued = [0]

        def w_issue_upto(n):
            while issued[0] <= n and issued[0] < len(loaders):
                f = loaders[issued[0]]
                if f is not None:
                    f(issued[0])
                issued[0] += 1

        def w_done(n):
            w_issue_upto(n + 4)

        sreset()
        dma("sp", IDENT[:, :], c_ident, "c0")
        GST_A = T(97280, (128,), F32, parts=8)
        GST_B = T(97792, (128,), F32, parts=64)
        dma("sp", GST_A[0:8, :], ln_mix_a[0].rearrange("(k f) -> k f", f=128), "g0")
        dma("pool", CP[:, :], c_pack, "c2")
        dma("pool", BSROW[0:1, :], b_spatial[0].rearrange("(o g) t -> o (g t)", o=1), "c4")
        w_issue_upto(2)
        dma("pool", SSEL[:, :, :], c_ssel.rearrange("p (a b) -> p a b", a=16), "c3")
        dma("pool", WP[:, :, :], kp(w_ple_proj[0]), "wp")
        gi_ = 0
        for src in (ln_mlp[0], ln_ple[0], ln_kv, ln_mix_b[0], ln_mlp[1], ln_ple[1]):
            dma("act", GST_B[gi_ * 8:(gi_ + 1) * 8, :], src.rearrange("(k f) -> k f", f=128), "g%d" % (gi_ + 1))
            gi_ += 1
        gk2 = g_k.rearrange("(o f) -> o f", o=1)
        dma("act", GST_B[48:49, 0:64], gk2, "g7")
        dma("act", GST_B[48:49, 64:128], gk2, "g8")
        dma("act", GST_B[49:50, 0:64], g_q, "g9")
        dma("act", GST_B[49:50, 64:128], g_q, "g10")
        b = rot.next()
        tr(PB[b][:, 0:8], GST_A[0:8, :], IDENT[0:8, 0:8])
        cp("dve", GCOLS[:, 0:8], PB[b][:, 0:8])

        def gcols_rest():
            b = rot.next()
            tr(PB[b][:, 0:50], GST_B[0:50, :], IDENT[0:50, 0:50])
            cp("dve", GCOLS[:, 8:58], PB[b][:, 0:50])
            ts("dve", GCOLS[:, 58:59], GCOLS[:, 57:58], 0.125, None, ALU.mult)

        xs = [T(WOFF[3] + i * 4096, (1024,), F32) for i in range(4)]

        def xload(t):
            for blk in range(t * 4, t * 4 + 4):
                st = xs[blk % 4]
                dma("sp", st[:, :], x[blk * 128:(blk + 1) * 128, :], "x%d" % (blk % 4))
                for half in range(2):
                    b = rot.next()
                    for cc in range(4):
                        c = half * 4 + cc
                        tr(PB4[b][:, cc, :], st[:, c * 128:(c + 1) * 128], IDENT[:, :])
                    cp(evac_eng(), XT[:, half * 4:(half + 1) * 4, blk * 128:(blk + 1) * 128], PB4[b][:, :, :])

        xload(0)
        dma("sp", MASKLE[:, :], c_maskle, "c1")
        if stop_after < 1:
            for t in range(1, NT):
                xload(t)
            gcols_rest()

        def norm_A(t, sq):
            tl = slice(t * TT, (t + 1) * TT)
            for c in range(8):
                act(sq[:, c, :], XT[:, c, tl], AF.Square)

        def norm_B(t, gcol0, sq, sd, rstd, dst=HT):
            tl = slice(t * TT, (t + 1) * TT)
            b = rot.next()
            mm(PB[b][:, :], [(ONES_S(), sq[:, c, :]) for c in range(8)])
            act(sd[:, :], PB[b][:, :], AF.Ln, bias=EPS)
            act(rstd[:, :], sd[:, :], AF.Exp, scale=-0.5)
            for c in range(8):
                if gcol0 is None:
                    tt("dve", dst[:, c, tl], XT[:, c, tl], rstd[:, :], ALU.mult)
                else:
                    stt(dst[:, c, tl], XT[:, c, tl], GCOLS[:, gcol0 + c:gcol0 + c + 1], rstd[:, :], ALU.mult, ALU.mult)

        def norm_tile(t, gcol0, sq, sd, rstd, dst=HT):
            norm_A(t, sq)
            norm_B(t, gcol0, sq, sd, rstd, dst)

        if stop_after >= 1:
            sreset()
            uT = salloc((8, TT), BF16)
            sq = uT
            yT = salloc((8, TT), BF16)
            v16 = [salloc((1024,), BF16) for _ in range(2)]
            vn = [salloc((1024,), BF16) for _ in range(2)]
            sd = salloc((TT,), F32)
            rstd = salloc((TT,), F32)
            wsT = salloc((8, 128), BF16)
            gvt = salloc((1024,), F32)
            wsl = T(yT.off, (8, 128), F32)
            dma("sp", gvt[:, :], g_v_a.partition_broadcast(128), "gv")
            dma("sp", wsl[:, :, :], w_spatial[0].rearrange("g t s -> t g s"), "ws")
            def ws_prep():
                for hb in range(2):
                    b = rot.next()
                    for gg in range(4):
                        tr(PB4[b][:, gg, :], wsl[:, hb * 4 + gg, :], IDENT[:, :])
                    tt("dve", wsT[:, hb * 4:(hb + 1) * 4, :], PB4[b][:, :, :],
                       Ref(MASKLE.ap.unsqueeze(1).to_broadcast([128, 4, 128]), MASKLE[:, :].keys), ALU.mult)
            W0 = wslot(0, (8, 1024))
            W1 = wslot(1, (8, 1024))
            W2 = wslot(2, (8, 1024))
            norm_A(0, sq)
            norm_B(0, G_MIXA, sq, sd, rstd)
            for t in range(NT):
                tl = slice(t * TT, (t + 1) * TT)
                if t + 1 < NT:
                    xload(t + 1)
                    if t + 1 == NT - 1:
                        w_issue_upto(3)
                for f in range(8):
                    b = rot.next()
                    mm(PB[b][:, :], [(W0[:, k, f * 128:(f + 1) * 128], HT[:, k, tl]) for k in range(8)])
                    act(uT[:, f, :], PB[b][:, :], AF.Gelu_apprx_tanh)
                if t == 0:
                    gcols_rest()
                    ws_prep()

                def VST(blk, t=t):
                    tb = t * 4 + blk
                    tbl = slice(tb * 128, (tb + 1) * 128)
                    for half in range(2):
                        b = rot.next()
                        mm(PB[b][:, :], [(HT[:, k, tbl], W1[:, k, half * 512:(half + 1) * 512]) for k in range(8)])
                        act(v16[blk % 2][:, half * 512:(half + 1) * 512], PB[b][:, :], AF.Gelu_apprx_tanh)

                def NST(blk):
                    vb = vn[blk % 2]
                    vv = v16[blk % 2]
                    c4 = (blk % 2) * 4
                    act(vb[:, :], vv[:, :], AF.Square, accum=SMALL[:, c4:c4 + 1])
                    ts("dve", SMALL[:, c4 + 1:c4 + 2], SMALL[:, c4:c4 + 1], 1.0 / 1024, EPS, ALU.mult, ALU.add)
                    act(SMALL[:, c4 + 2:c4 + 3], SMALL[:, c4 + 1:c4 + 2], AF.Sqrt)
                    recip(SMALL[:, c4 + 3:c4 + 4], SMALL[:, c4 + 2:c4 + 3])
                    stt(vb[:, :], vv[:, :], SMALL[:, c4 + 3:c4 + 4], gvt[:, :], ALU.mult, ALU.mult)

                def SPST(blk):
                    vb = vn[blk % 2]
                    for hb in range(2):
                        b = rot.next()

                        def fn(e, b=b, hb=hb, vb=vb):
                            ins = None
                            for gg in range(4):
                                g = hb * 4 + gg
                                e.matmul(PB4[b][:, gg, :].ap, vb[:, g * 128:(g + 1) * 128].ap, wsT[:, g, :].ap,
                                         start=True, stop=False)
                                ins = e.matmul(PB4[b][:, gg, :].ap, ONES1(slice(0, 1)).ap,
                                               BSROW[0:1, g * 128:(g + 1) * 128].ap, start=False, stop=True)
                            return ins
                        S.op("pe", fn, reads=[vb[:, hb * 512:(hb + 1) * 512], wsT[:, hb * 4:(hb + 1) * 4, :],
                                              ONES1(slice(0, 1)), BSROW[0:1, :]], writes=[PB4[b][:, :, :]])
                        tt("dve", yT[:, hb * 4:(hb + 1) * 4, blk * 128:(blk + 1) * 128], PB4[b][:, :, :],
                           uT[:, hb * 4:(hb + 1) * 4, blk * 128:(blk + 1) * 128], ALU.mult)

                VST(0)
                VST(1)
                NST(0)
                for blk in range(4):
                    if blk + 2 < 4:
                        VST(blk + 2)
                    if blk + 1 < 4:
                        NST(blk + 1)
                    SPST(blk)
                if t + 1 < NT:
                    norm_A(t + 1, sq)
                for dch in range(8):
                    b = rot.next()
                    mm(PB[b][:, :], [(W2[:, k, dch * 128:(dch + 1) * 128], yT[:, k, :]) for k in range(8)])
                    tt("dve", XT[:, dch, tl], PB[b][:, :], XT[:, dch, tl], ALU.add)
                if t + 1 < NT:
                    norm_B(t + 1, G_MIXA, sq, sd, rstd)
            w_done(0)
            w_done(1)
            w_done(2)

        def mlp_phase(l, blk0, gcol0):
            sreset()
            sq = salloc((8, TT), BF16)
            sd = salloc((TT,), F32)
            rstd = salloc((TT,), F32)
            aT = [salloc((8, TT), BF16) for _ in range(2)]
            rt = [salloc((TT,), BF16) for _ in range(2)]
            norm_tile(0, gcol0, sq, sd, rstd)
            it = 0
            for sg in range(4):
                nA, nB = blk0 + 2 * sg, blk0 + 2 * sg + 1
                wu = [Tile(arena, WOFF[n % 4], (8, 512), BF16) for n in (nA, nB)]
                wd = [Tile(arena, WOFF[n % 4] + 8192, (4, 1024), BF16) for n in (nA, nB)]
                for t in range(NT):
                    tl = slice(t * TT, (t + 1) * TT)
                    a = aT[it % 2]
                    it += 1
                    for f in range(8):
                        b = rot.next()
                        mm(PB[b][:, :], [(wu[f // 4][:, k, (f % 4) * 128:(f % 4 + 1) * 128], HT[:, k, tl]) for k in range(8)])
                        r = rt[f % 2]
                        act(r[:, :], PB[b][:, :], AF.Relu)
                        tt("dve", a[:, f, :], r[:, :], r[:, :], ALU.mult)
                    if sg == 0 and t + 1 < NT:
                        norm_A(t + 1, sq)
                    for dch in range(8):
                        b = rot.next()
                        mm(PB[b][:, :], [(wd[f // 4][:, f % 4, dch * 128:(dch + 1) * 128], a[:, f, :]) for f in range(8)])
                        tt("dve", XT[:, dch, tl], PB[b][:, :], XT[:, dch, tl], ALU.add)
                    if sg == 0 and t + 1 < NT:
                        norm_B(t + 1, gcol0, sq, sd, rstd)
                w_done(nA)
                w_done(nB)

        out_fin = []

        def emit_out(t, osb):
            for blk in range(t * 4, t * 4 + 4):
                ot = osb[blk % 2]
                for half in range(2):
                    b = rot.next()
                    for cc in range(4):
                        c = half * 4 + cc
                        tr(PB4[b][:, cc, :], XT[:, c, blk * 128:(blk + 1) * 128], IDENT[:, :])
                    cp(evac_eng(), ot[:, half * 512:(half + 1) * 512], PB[b][:, :])
                out_fin.append(S.op("sp", lambda e, ot=ot, blk=blk: e.dma_start(out=y[blk * 128:(blk + 1) * 128, :], in_=ot[:, :].ap),
                                    reads=[ot[:, :]], dma="o%d" % (blk % 2)))

        def ple_phase(l, blkn, gcol0, with_out=False):
            sreset()
            sq = salloc((8, TT), BF16)
            sd = salloc((TT,), F32)
            rstd = salloc((TT,), F32)
            pst = salloc((4, 256), F32)
            pT = salloc((2, TT), BF16)
            gate = [salloc((TT,), F32) for _ in range(2)]
            tmp2 = [salloc((TT,), F32) for _ in range(2)]
            osb = [salloc((1024,), F32) for _ in range(2)] if with_out else None
            Wg = wslot(blkn, (8, 1024))
            def pload(t):
                dma("sp", pst[:, :, :], p[l][t * TT:(t + 1) * TT, :].rearrange("(b q) f -> q b f", q=128), "pl")
            pload(0)
            for t in range(NT):
                tl = slice(t * TT, (t + 1) * TT)
                if t == 0:
                    norm_tile(0, gcol0, sq, sd, rstd)
                for kk in range(2):
                    b = rot.next()
                    for blk in range(4):
                        tr(PB4[b][:, blk, :], pst[:, blk, kk * 128:(kk + 1) * 128], IDENT[:, :])
                    cp(evac_eng(), pT[:, kk, :], PB[b][:, :])
                if t + 1 < NT:
                    pload(t + 1)
                if t + 1 < NT:
                    norm_A(t + 1, sq)
                for dch in range(8):
                    if dch == 4 and t + 1 < NT:
                        norm_B(t + 1, gcol0, sq, sd, rstd)
                    b1 = rot.next()
                    mm(PB[b1][:, :], [(Wg[:, k, dch * 128:(dch + 1) * 128], HT[:, k, tl]) for k in range(8)])
                    g_ = gate[dch % 2]
                    act(g_[:, :], PB[b1][:, :], AF.Sigmoid)
                    b2 = rot.next()
                    mm(PB[b2][:, :], [(WP[:, kk, dch * 128:(dch + 1) * 128], pT[:, kk, :]) for kk in range(2)])
                    t2 = tmp2[dch % 2]
                    tt("dve", t2[:, :], PB[b2][:, :], g_[:, :], ALU.mult)
                    tt("dve", XT[:, dch, tl], XT[:, dch, tl], t2[:, :], ALU.add)
                if with_out:
                    emit_out(t, osb)
            w_done(blkn)

        if stop_after >= 2:
            mlp_phase(0, 3, G_MLP0)
        if stop_after >= 3:
            ple_phase(0, 11, G_PLE0)
            dma("pool", WP[:, :, :], kp(w_ple_proj[1]), "wp")

        if stop_after >= 4:
            sreset()
            sq = salloc((8, TT), BF16)
            sd = salloc((TT,), F32)
            rstd = salloc((TT,), F32)
            for t in range(NT):
                norm_tile(t, None, sq, sd, rstd)
            sreset()
            KT = salloc((SEQ,), BF16)
            QTh = [salloc((SEQ,), BF16) for _ in range(2)]
            VJ = salloc((16, 128), BF16)
            OT = salloc((SEQ,), BF16)
            e1 = [salloc((TT,), F32) for _ in range(2)]
            AT = [salloc((TT,), BF16) for _ in range(3)]
            csb = [salloc((TT,), BF16) for _ in range(2)]
            PW = [Tile(arena, WOFF[0] + i * 8192, (4096,), BF16) for i in range(2)]
            LS = [Tile(arena, WOFF[1], (16, TT), BF16), Tile(arena, WOFF[2], (16, TT), BF16)]
            sqK = Tile(arena, WOFF[2], (4, TT), BF16)
            sqQ = Tile(arena, WOFF[2] + 4096, (4, TT), BF16)
            sd4 = Tile(arena, WOFF[2] + 8192, (4, TT), F32)
            rawK = Tile(arena, WOFF[1], (4, TT), F32)
            rawQ = Tile(arena, WOFF[1] + 8192, (4, TT), F32)
            srot = Rot(range(5))
            zrot = Rot([0, 1])
            grot = Rot([2, 3])
            frot = Rot([6, 7])
            JUNK = 7

            def dummy(k):
                def fn(e):
                    ins = None
                    for _ in range(k):
                        ins = e.matmul(PB[JUNK][:, :].ap, ONES_S().ap, CP[:, 0:512].ap, start=True, stop=True)
                    return ins
                S.op("pe", fn, reads=[ONES_S(), CP[:, 0:512]], writes=[PB[JUNK][:, :]])
            CSB = 4
            OB = [5, 5]
            for bi in range(2):
                S.op("dve", lambda e, bi=bi: e.memset(csb[bi][:, :].ap, 0.0), writes=[csb[bi][:, :]])
                S.op("dve", lambda e, bi=bi: e.memset(QTh[bi][:, :].ap, 0.0), writes=[QTh[bi][:, :]])

            def pw_views(i):
                t_ = PW[i]
                off = t_.off
                wk = Tile(arena, off, (8, 128), BF16)
                wv = Tile(arena, off + 2048, (8, 128), BF16)
                wq = Tile(arena, off + 4096, (8, 128), BF16)
                wo = Tile(arena, off + 6144, (1024,), BF16)
                return wk, wv, wq, wo

            def load_pair(j):
                wk, wv, wq, wo = pw_views(j % 2)
                sl = slice(j * 128, (j + 1) * 128)
                dma("pool", wk[:, :, :], kp(w_kv)[:, :, j * 128:(j + 1) * 128], "pw%da" % (j % 2))
                dma("pool", wv[:, :, :], kp(w_kv)[:, :, 1024 + j * 128:1024 + (j + 1) * 128], "pw%db" % (j % 2))
                dma("pool", wq[:, :, :], kp(w_q[0])[:, :, j * 128:(j + 1) * 128], "pw%dc" % (j % 2))
                dma("pool", wo[:, :], w_out_b[0][sl, :], "pw%dd" % (j % 2))
                for w_, g0 in ((wk, G_KV), (wv, G_KV), (wq, G_MIXB)):
                    gb = Ref(GCOLS.ap[:, g0:g0 + 8].unsqueeze(2).to_broadcast([128, 8, 128]), GCOLS[:, g0:g0 + 8].keys)
                    tt("dve", w_[:, :, :], w_[:, :, :], gb, ALU.mult)

            load_pair(0)
            for j in range(8):
                wk, wv, wq, wo = pw_views(j % 2)
                if j + 1 < 8:
                    load_pair(j + 1)
                KB = [0, 1, 2, 3]
                QB = [4, 5, 6, 7]
                for (w_, raw, sqx, mb) in ((wk, rawK, sqK, KB), (wq, rawQ, sqQ, QB)):
                    for t in range(NT):
                        tl = slice(t * TT, (t + 1) * TT)
                        mm(PB[mb[t]][:, :], [(w_[:, k, :], HT[:, k, tl]) for k in range(8)])
                        act(raw[:, t, :], PB[mb[t]][:, :], AF.Copy)
                        act(sqx[:, t, :], PB[mb[t]][:, :], AF.Square)
                for (sqx, mb) in ((sqK, KB), (sqQ, QB)):
                    for t in range(NT):
                        mm(PB[mb[t]][:, :], [(BDIAG(), sqx[:, t, :])])
                for t in range(NT):
                    act(sd4[:, t, :], PB[KB[t]][:, :], AF.Ln, bias=EPS)
                for t in range(NT):
                    act(sd4[:, t, :], sd4[:, t, :], AF.Exp, scale=-0.5)
                for t in range(NT):
                    tl = slice(t * TT, (t + 1) * TT)
                    stt(KT[:, tl], rawK[:, t, :], GCOLS[:, G_K:G_K + 1], sd4[:, t, :], ALU.mult, ALU.mult)
                for q4 in range(4):
                    b_ = KB[q4]
                    for bb in range(4):
                        blk = q4 * 4 + bb
                        mm(PB4[b_][:, bb, :], [(HT[:, k, blk * 128:(blk + 1) * 128], wv[:, k, :]) for k in range(8)])
                    cp("dve", VJ[:, q4 * 4:(q4 + 1) * 4, :], PB4[b_][:, :, :])
                for t in range(NT):
                    act(sd4[:, t, :], PB[QB[t]][:, :], AF.Ln, bias=EPS)
                for t in range(NT):
                    act(sd4[:, t, :], sd4[:, t, :], AF.Exp, scale=-0.5)
                for t in range(NT):
                    tl = slice(t * TT, (t + 1) * TT)
                    for h_ in range(2):
                        hp = slice(h_ * 64, (h_ + 1) * 64)
                        stt(QTh[h_][hp, tl], rawQ[hp, t, :], GCOLS[hp, G_QS:G_QS + 1], sd4[hp, t, :], ALU.mult, ALU.mult)
                heads = (slice(0, 64), slice(64, 128))

                def cols(qt, kb):
                    c0 = max(0, kb - 4 * qt) * 128
                    return c0, slice(c0, TT), slice(qt * TT + c0, (qt + 1) * TT)

                def p1_steps(h, qt):
                    nkb = 4 * qt + 4
                    L = LS[h]
                    zb = {}

                    def Z(kb):
                        c0, cs, qs = cols(qt, kb)
                        zb[kb] = zrot.next()
                        mm(PB[zb[kb]][:, cs], [(KT[:, kb * 128:(kb + 1) * 128], QTh[h][:, qs])])

                    def EXP(kb):
                        c0, cs, qs = cols(qt, kb)
                        act(e1[kb % 2][:, cs], PB[zb[kb]][:, cs], AF.Exp)

                    def LN(kb):
                        c0, cs, qs = cols(qt, kb)
                        act(L[:, kb, cs], e1[kb % 2][:, cs], AF.Ln, bias=1.0)
                        if kb >= 4 * qt:
                            tt("pool", L[:, kb, c0:c0 + 128], L[:, kb, c0:c0 + 128], MASK01(), ALU.mult)

                    def CS(kb):
                        c0, cs, qs = cols(qt, kb)
                        mm(PB[CSB][:, cs], [(ESEL(kb), L[:, kb, cs])], start=(kb == 0), stop=(kb == nkb - 1))

                    steps = []
                    for i in range(nkb + 3):
                        def st(i=i):
                            if i < nkb:
                                Z(i)
                            if 0 <= i - 1 < nkb:
                                EXP(i - 1)
                            if 0 <= i - 2 < nkb:
                                LN(i - 2)
                            if 0 <= i - 3 < nkb:
                                CS(i - 3)
                        steps.append(st)

                    def fin():
                        cp("dve", csb[h][0:16, :], PB[CSB][0:16, :])
                    steps.append(fin)
                    return steps

                def p2_steps(h, qt):
                    nkb = 4 * qt + 4
                    L = LS[h]
                    pr = heads[h]
                    gb = {}

                    def G(kb):
                        c0, cs, qs = cols(qt, kb)
                        b = gb[kb] = grot.next()

                        def fn(e):
                            e.matmul(PB[b][:, cs].ap, SSEL[:, kb, :].ap, csb[h][:, cs].ap, start=True, stop=False)
                            e.matmul(PB[b][:, cs].ap, TRINEG().ap, L[:, kb, cs].ap, start=False, stop=False)
                            return e.matmul(PB[b][:, cs].ap, KT[:, kb * 128:(kb + 1) * 128].ap, QTh[h][:, qs].ap,
                                            start=False, stop=True)
                        S.op("pe", fn, reads=[TRINEG(), L[:, kb, cs], SSEL[:, kb, :], csb[h][:, cs],
                                              KT[:, kb * 128:(kb + 1) * 128], QTh[h][:, qs]], writes=[PB[b][:, cs]])

                    def EXPA(kb):
                        c0, cs, qs = cols(qt, kb)
                        a_ = AT[kb % 3]
                        act(a_[:, cs], PB[gb[kb]][:, cs], AF.Exp)
                        if kb >= 4 * qt:
                            tt("pool", a_[:, c0:c0 + 128], a_[:, c0:c0 + 128], MASK01(), ALU.mult)

                    def AV(kb):
                        c0, cs, qs = cols(qt, kb)
                        mm(PB[OB[h]][:, cs], [(VJ[:, kb, :], AT[kb % 3][:, cs])], start=(kb == 0), stop=(kb == nkb - 1))

                    steps = []
                    for i in range(nkb + 1):
                        def st(i=i):
                            if i < nkb:
                                G(i)
                            if 0 <= i - 1 < nkb:
                                EXPA(i - 1)
                                AV(i - 1)
                        steps.append(st)

                    def fin():
                        cp("dve", OT[pr, qt * TT:(qt + 1) * TT], PB[OB[h]][pr, :])
                    steps.append(fin)
                    return steps

                pending = []

                def filler(k):
                    for _ in range(k):
                        if not pending:
                            return
                        qt_, dch = pending.pop(0)
                        tl = slice(qt_ * TT, (qt_ + 1) * TT)
                        b = frot.next()
                        mm(PB[b][:, :], [(wo[:, dch * 128:(dch + 1) * 128], OT[:, tl])])
                        tt("dve", XT[:, dch, tl], PB[b][:, :], XT[:, dch, tl], ALU.add)

                units = [(h, qt) for qt in range(NT) for h in range(2)]
                prev = None
                for u in units + [None]:
                    s1 = p1_steps(*u) if u is not None else []
                    s2 = p2_steps(*prev) if prev is not None else []
                    n = max(len(s1), len(s2))
                    for i in range(n):
                        if i < len(s2):
                            s2[i]()
                        if i < len(s1):
                            s1[i]()
                        if i == n // 2 or i == n - 1:
                            filler(2)
                    if prev is not None and prev[0] == 1:
                        pending.extend((prev[1], dch) for dch in range(8))
                    prev = u
                filler(len(pending))
            w_done(12)
            w_done(13)
            w_done(14)

        if stop_after >= 5:
            w_issue_upto(18)
            mlp_phase(1, 15, G_MLP1)
        if stop_after >= 6:
            ple_phase(1, 23, G_PLE1, with_out=True)

        if not out_fin:
            sreset()
            osb = [salloc((1024,), F32) for _ in range(2)]
            for t in range(NT):
                emit_out(t, osb)
        S.emit(final_waits=out_fin[-2:])
    return nc


def make_consts():
    ident = np.eye(128, dtype=np.float32)
    s = np.arange(128)[:, None]
    t = np.arange(128)[None, :]
    maskle = (s <= t).astype(np.float32)
    pack = np.zeros((128, 784), np.float32)
    pack[:, 0:128] = 1.0 / 1024
    pack[:, 128:256] = 1.0
    bd = np.zeros((128, 128), np.float32)
    bd[0:64, 0:64] = 1.0 / 64
    bd[64:128, 64:128] = 1.0 / 64
    pack[:, 256:384] = bd
    pack[:, 384:512] = -(s >= t).astype(np.float32)
    pack[:, 512:640] = (s < t).astype(np.float32)
    pack[:, 640 + 15] = -1.0
    pack[:, 640 + 47] = -1.0
    ssel = np.zeros((128, 16, 128), np.float32)
    for kb in range(16):
        for r in range(16):
            if r > kb:
                ssel[r, kb, :] = 1.0
                ssel[32 + r, kb, :] = 1.0
    return {"c_ident": ident, "c_maskle": maskle, "c_pack": pack, "c_ssel": ssel.reshape(128, 2048)}


_CACHE = {}


def kernel(**inputs):
    n = 8
    if "nc" not in _CACHE:
        nc = bass.Bass("TRN2", target_bir_lowering=False)
        build_program(nc)
        _CACHE["nc"] = nc
    nc = _CACHE["nc"]
    consts = make_consts()
    shared = {k: np.ascontiguousarray(np.asarray(v, dtype=np.float32)) for k, v in inputs.items() if k not in ("x", "p")}
    shared.update(consts)
    x = np.asarray(inputs["x"], dtype=np.float32)
    p = np.asarray(inputs["p"], dtype=np.float32)
    in_maps = []
    for i in range(n):
        m = dict(shared)
        m["x"] = np.ascontiguousarray(x[i])
        m["p"] = np.ascontiguousarray(p[:, i])
        in_maps.append(m)
    res = run_bass_kernel_spmd(nc, in_maps, core_ids=list(range(n)))
    return np.stack([np.asarray(r["y"], dtype=np.float32) for r in res.results], axis=0)
```

```python
import contextlib
import numpy as np
import concourse.bass as bass
import concourse.mybir as mybir
from concourse.bass_utils import run_bass_kernel_spmd

F32 = mybir.dt.float32
BF16 = mybir.dt.bfloat16
AF = mybir.ActivationFunctionType
ALU = mybir.AluOpType

SEQ = 2048
D = 1024
NT = 4
TT = 512
NC8 = 8
EPS = 1e-6
GR = 256
ENGS = ("pe", "act", "dve", "pool", "sp")

G_MIXA, G_MLP0, G_PLE0, G_KV, G_MIXB, G_MLP1, G_PLE1, G_K, G_Q, G_QS = 0, 8, 16, 24, 32, 40, 48, 56, 57, 58


class Op:
    __slots__ = ("eng", "fn", "deps", "signal", "idx", "dma", "dsem", "dval", "count", "waits")

    def __init__(self, eng, fn, dma):
        self.eng = eng
        self.fn = fn
        self.dma = dma
        self.signal = False
        self.count = 0
        self.dsem = 0
        self.dval = 0
        self.deps = None
        self.waits = None


class Ref:
    __slots__ = ("ap", "keys")

    def __init__(self, ap, keys):
        self.ap = ap
        self.keys = keys


class Sched:
    def __init__(self, nc):
        self.nc = nc
        self.ops = []
        self.lastw = {}
        self.rd = {}
        self.streams = {}

    def op(self, eng, fn, reads=(), writes=(), dma=None):
        o = Op(eng, fn, dma)
        o.idx = len(self.ops)
        deps = {}

        def add(d):
            k = ("d", d.dma) if d.dma is not None else d.eng
            p = deps.get(k)
            if p is None or p.idx < d.idx:
                deps[k] = d

        lastw = self.lastw
        rd = self.rd
        for r in reads:
            for k in r.keys:
                w = lastw.get(k)
                if w is not None:
                    add(w)
        for r in writes:
            for k in r.keys:
                w = lastw.get(k)
                if w is not None:
                    add(w)
                rr = rd.get(k)
                if rr:
                    for x in rr.values():
                        add(x)
        if dma is not None:
            p = self.streams.get(dma)
            if p is not None:
                add(p)
            self.streams[dma] = o
        mykey = ("d", dma) if dma is not None else eng
        if eng == "pe" and dma is None:
            deps.pop("pe", None)
        o.deps = deps
        for r in reads:
            for k in r.keys:
                d = rd.get(k)
                if d is None:
                    rd[k] = {mykey: o}
                else:
                    d[mykey] = o
        for r in writes:
            for k in r.keys:
                lastw[k] = o
                rd[k] = {}
        self.ops.append(o)
        return o

    def emit(self, final_waits=()):
        nc = self.nc
        by_eng = {e: [o for o in self.ops if o.eng == e] for e in ENGS}
        for e in ENGS:
            waited = {}
            for o in by_eng[e]:
                w = []
                for k, d in o.deps.items():
                    if waited.get(k, -1) >= d.idx:
                        continue
                    waited[k] = d.idx
                    d.signal = True
                    w.append(d)
                o.waits = w
        for o in final_waits:
            o.signal = True
        cnt = {e: 0 for e in ENGS}
        streams = {}
        for o in self.ops:
            if o.dma is not None:
                st = streams.get(o.dma)
                if st is None:
                    st = streams[o.dma] = [len(streams), 0]
                st[1] += 16
                o.dsem = st[0]
                o.dval = st[1]
            elif o.signal:
                cnt[o.eng] += 1
                o.count = cnt[o.eng]
        with contextlib.ExitStack() as es:
            esem = {e: es.enter_context(nc.semaphore("s_" + e)) for e in ENGS}
            dsem = [es.enter_context(nc.semaphore("d%d" % i)) for i in range(len(streams))]
            block = es.enter_context(nc.Block())

            def run(ename, eng):
                for o in by_eng[ename]:
                    for d in o.waits:
                        if d.dma is not None:
                            eng.wait_ge(dsem[d.dsem], d.dval)
                        else:
                            eng.wait_ge(esem[d.eng], d.count)
                    ins = o.fn(eng)
                    if o.dma is not None:
                        ins.then_inc(dsem[o.dsem], 16)
                    elif o.signal:
                        ins.then_inc(esem[ename], 1)
                if ename == "sp":
                    for o in final_waits:
                        if o.dma is not None:
                            eng.wait_ge(dsem[o.dsem], o.dval)
                        else:
                            eng.wait_ge(esem[o.eng], o.count)

            @block.tensor
            def _(e):
                run("pe", e)

            @block.scalar
            def _(e):
                run("act", e)

            @block.vector
            def _(e):
                run("dve", e)

            @block.gpsimd
            def _(e):
                run("pool", e)

            @block.sync
            def _(e):
                run("sp", e)


class Tile:
    def __init__(self, base_ap, off, shape, dt, parts=128, space="S", bank=0):
        self.off = off
        self.shape = tuple(shape)
        self.es = 4 if dt == F32 else 2
        self.parts = parts
        self.space = space
        self.bank = bank
        n = 1
        for s in shape:
            n *= s
        self.n = n
        if space == "S":
            v = base_ap[0:parts, off // 2: off // 2 + n * self.es // 2]
            if dt == F32:
                v = v.bitcast(F32)
        else:
            v = base_ap[0:parts, 0:n]
        if len(shape) == 2:
            v = v.rearrange("p (a b) -> p a b", a=shape[0])
        elif len(shape) == 3:
            v = v.rearrange("p (a b c) -> p a b c", a=shape[0], b=shape[1])
        self.ap = v

    def __getitem__(self, idx):
        if not isinstance(idx, tuple):
            idx = (idx,)
        ps = idx[0]
        fidx = list(idx[1:])
        while len(fidx) < len(self.shape):
            fidx.append(slice(None))
        ap = self.ap[(ps,) + tuple(fidx)]
        rng = []
        for i, s in zip(fidx, self.shape):
            if isinstance(i, int):
                rng.append((i, i + 1))
            else:
                a = 0 if i.start is None else i.start
                b = s if i.stop is None else i.stop
                rng.append((a, b))
        strides = []
        acc = 1
        for s in reversed(self.shape):
            strides.append(acc)
            acc *= s
        strides = strides[::-1]
        keys = set()
        outer = [()]
        for (a, b) in rng[:-1]:
            outer = [o + (i,) for o in outer for i in range(a, b)]
        la, lb = rng[-1]
        gr = GR if self.space == "S" else 512
        for o in outer:
            base = sum(i * st for i, st in zip(o, strides[:-1]))
            b0 = self.off + (base + la) * self.es
            b1 = self.off + (base + lb) * self.es
            for g in range(b0 // gr, (b1 - 1) // gr + 1):
                keys.add((self.space, self.bank, g))
        return Ref(ap, keys)


def build_program(nc, stop_after=99):
    dt_in = {}

    def din(name, shape):
        t = nc.dram_tensor(name, list(shape), F32, kind="ExternalInput").ap()
        dt_in[name] = t
        return t

    x = din("x", [SEQ, D])
    p = din("p", [2, SEQ, 256])
    ln_mix_a = din("ln_mix_a", [1, D])
    w_in_a = din("w_in_a", [1, D, 2 * D])
    g_v_a = din("g_v_a", [1, D])
    w_spatial = din("w_spatial", [1, 8, 128, 128])
    b_spatial = din("b_spatial", [1, 8, 128])
    w_out_a = din("w_out_a", [1, D, D])
    ln_kv = din("ln_kv", [D])
    w_kv = din("w_kv", [D, 2 * D])
    g_k = din("g_k", [64])
    ln_mix_b = din("ln_mix_b", [1, D])
    w_q = din("w_q", [1, D, D])
    g_q = din("g_q", [1, 64])
    w_out_b = din("w_out_b", [1, D, D])
    ln_mlp = din("ln_mlp", [2, D])
    w_up = din("w_up", [2, D, 4 * D])
    w_down = din("w_down", [2, 4 * D, D])
    ln_ple = din("ln_ple", [2, D])
    w_ple_gate = din("w_ple_gate", [2, D, D])
    w_ple_proj = din("w_ple_proj", [2, 256, D])
    c_ident = din("c_ident", [128, 128])
    c_maskle = din("c_maskle", [128, 128])
    c_pack = din("c_pack", [128, 784])
    c_ssel = din("c_ssel", [128, 2048])
    y = nc.dram_tensor("y", [SEQ, D], F32, kind="ExternalOutput").ap()

    es = contextlib.ExitStack()
    with es:
        ARENA_BYTES = 212736
        arena = es.enter_context(nc.sbuf_tensor("arena", [128, ARENA_BYTES // 2], BF16))
        banks = [es.enter_context(nc.psum_tensor("ps%d" % i, [128, 512], F32)) for i in range(8)]
        PB = [Tile(banks[i], 0, (512,), F32, space="P", bank=i) for i in range(8)]
        PB4 = [Tile(banks[i], 0, (4, 128), F32, space="P", bank=i) for i in range(8)]
        S = Sched(nc)

        def T(off, shape, dt, parts=128):
            return Tile(arena, off, shape, dt, parts=parts)

        XT = T(0, (8, SEQ), F32)
        HT = T(65536, (8, SEQ), BF16)
        WOFF = [98304 + i * 16384 for i in range(4)]
        SCR0 = 163840
        coff = ARENA_BYTES
        def calloc(nbytes):
            nonlocal coff
            coff -= (nbytes + 255) // 256 * 256
            return coff
        IDENT = T(calloc(512), (128,), F32)
        MASKLE = T(calloc(512), (128,), F32)
        CP = T(calloc(1568), (784,), BF16)
        SSEL = T(calloc(4096), (16, 128), BF16)
        GCOLS = T(calloc(256), (64,), F32)
        BSROW = T(calloc(2048), (1024,), BF16, parts=1)
        SMALL = T(calloc(256), (64,), F32)
        WP = T(calloc(4096), (2, 1024), BF16)
        SCR_END = coff
        ONES_S = lambda: CP[:, 0:128]
        ONES1 = lambda ps=slice(0, 128): CP[ps, 128:256]
        BDIAG = lambda: CP[:, 256:384]
        TRINEG = lambda: CP[:, 384:512]
        MASK01 = lambda: CP[:, 512:640]
        ESEL = lambda kb: CP[:, 640 + 15 - kb: 640 + 15 - kb + 128]

        scr = [SCR0]

        def salloc(shape, dt, parts=128):
            n = 1
            for s in shape:
                n *= s
            nb = n * (4 if dt == F32 else 2)
            nb = (nb + 255) // 256 * 256
            off = scr[0]
            scr[0] += nb
            assert scr[0] <= SCR_END, ("scratch overflow", scr[0], SCR_END)
            return T(off, shape, dt, parts=parts)

        def sreset():
            scr[0] = SCR0

        def act(out, in_, func, bias=None, scale=None, accum=None, extra=()):
            kw = {}
            if bias is not None:
                kw["bias"] = bias.ap if isinstance(bias, Ref) else bias
            if scale is not None:
                kw["scale"] = scale.ap if isinstance(scale, Ref) else scale
            if accum is not None:
                kw["accum_out"] = accum.ap
            rds = [in_] + [r for r in (bias, scale) if isinstance(r, Ref)] + list(extra)
            wr = [out] + ([accum] if accum is not None else [])
            return S.op("act", lambda e: e.activation(out.ap, in_.ap, func, **kw), reads=rds, writes=wr)

        def tt(eng, out, a, b, op):
            return S.op(eng, lambda e: e.tensor_tensor(out.ap, a.ap, b.ap, op), reads=[a, b], writes=[out])

        def stt(out, a, sc, b, op0, op1):
            scv = sc.ap if isinstance(sc, Ref) else sc
            rds = [a, b] + ([sc] if isinstance(sc, Ref) else [])
            return S.op("dve", lambda e: e.scalar_tensor_tensor(out.ap, a.ap, scv, b.ap, op0, op1), reads=rds, writes=[out])

        def ts(eng, out, a, s1, s2, op0, op1=None):
            s1v = s1.ap if isinstance(s1, Ref) else s1
            s2v = s2.ap if isinstance(s2, Ref) else s2
            rds = [a] + [r for r in (s1, s2) if isinstance(r, Ref)]
            if op1 is None:
                return S.op(eng, lambda e: e.tensor_scalar(out.ap, a.ap, s1v, None, op0), reads=rds, writes=[out])
            return S.op(eng, lambda e: e.tensor_scalar(out.ap, a.ap, s1v, s2v, op0, op1), reads=rds, writes=[out])

        def cp(eng, out, in_):
            if eng == "act":
                return act(out, in_, AF.Copy)
            return S.op(eng, lambda e: e.tensor_copy(out.ap, in_.ap), reads=[in_], writes=[out])

        def recip(out, in_):
            return S.op("dve", lambda e: e.reciprocal(out.ap, in_.ap), reads=[in_], writes=[out])

        def mm(out, pairs, start=True, stop=True):
            n = len(pairs)

            def fn(e):
                ins = None
                for i, (l, r) in enumerate(pairs):
                    ins = e.matmul(out.ap, l.ap, r.ap, start=(start and i == 0), stop=(stop and i == n - 1))
                return ins
            rds = []
            for l, r in pairs:
                rds.append(l)
                rds.append(r)
            return S.op("pe", fn, reads=rds, writes=[out])

        def tr(out, in_, ident):
            return S.op("pe", lambda e: e.transpose(out.ap, in_.ap, ident.ap), reads=[in_, ident], writes=[out])

        def dma(eng, out, in_ap, stream, reads=()):
            return S.op(eng, lambda e: e.dma_start(out=out.ap, in_=in_ap), reads=list(reads), writes=[out], dma=stream)

        class Rot:
            def __init__(self, ids):
                self.ids = list(ids)
                self.i = 0

            def next(self):
                b = self.ids[self.i % len(self.ids)]
                self.i += 1
                return b

        rot = Rot(range(8))
        tog = [0]

        def evac_eng():
            tog[0] ^= 1
            return "act" if tog[0] else "dve"

        def wslot(n, shape):
            return T(WOFF[n % 4], shape, BF16)

        def kp(ap2d):
            return ap2d.rearrange("(k p) f -> p k f", p=128)

        loaders = []

        def L_full(src2d, c0, c1):
            def f(n):
                w = wslot(n, (8, c1 - c0))
                dma("pool", w[:, :, :], kp(src2d)[:, :, c0:c1], "W%da" % (n % 4))
            return f

        def L_mlp(l, gi):
            def f(n):
                wu = Tile(arena, WOFF[n % 4], (8, 512), BF16)
                wd = Tile(arena, WOFF[n % 4] + 8192, (4, 1024), BF16)
                dma("pool", wu[:, :, :], kp(w_up[l])[:, :, gi * 512:(gi + 1) * 512], "W%da" % (n % 4))
                dma("pool", wd[:, :, :], kp(w_down[l][gi * 512:(gi + 1) * 512, :]), "W%db" % (n % 4))
            return f

        def L_first(n):
            w = wslot(n, (8, 1024))
            for i, sfx in enumerate("abcd"):
                dma("pool", w[:, :, i * 256:(i + 1) * 256], kp(w_in_a[0])[:, :, i * 256:(i + 1) * 256], "W0" + sfx)
        loaders.append(L_first)
        loaders.append(L_full(w_in_a[0], 1024, 2048))
        loaders.append(L_full(w_out_a[0], 0, 1024))
        for gi in range(8):
            loaders.append(L_mlp(0, gi))
        loaders.append(L_full(w_ple_gate[0], 0, 1024))
        loaders += [None, None, None]
        for gi in range(8):
            loaders.append(L_mlp(1, gi))
        loaders.append(L_full(w_ple_gate[1], 0, 1024))
        issued = [0]

        def w_issue_upto(n):
            while issued[0] <= n and issued[0] < len(loaders):
                f = loaders[issued[0]]
                if f is not None:
                    f(issued[0])
                issued[0] += 1

        def w_done(n):
            w_issue_upto(n + 4)

        sreset()
        dma("sp", IDENT[:, :], c_ident, "c0")
        GST_A = T(97280, (128,), F32, parts=8)
        GST_B = T(97792, (128,), F32, parts=64)
        dma("sp", GST_A[0:8, :], ln_mix_a[0].rearrange("(k f) -> k f", f=128), "g0")
        dma("pool", CP[:, :], c_pack, "c2")
        dma("pool", BSROW[0:1, :], b_spatial[0].rearrange("(o g) t -> o (g t)", o=1), "c4")
        w_issue_upto(2)
        dma("pool", SSEL[:, :, :], c_ssel.rearrange("p (a b) -> p a b", a=16), "c3")
        dma("pool", WP[:, :, :], kp(w_ple_proj[0]), "wp")
        gi_ = 0
        for src in (ln_mlp[0], ln_ple[0], ln_kv, ln_mix_b[0], ln_mlp[1], ln_ple[1]):
            dma("act", GST_B[gi_ * 8:(gi_ + 1) * 8, :], src.rearrange("(k f) -> k f", f=128), "g%d" % (gi_ + 1))
            gi_ += 1
        gk2 = g_k.rearrange("(o f) -> o f", o=1)
        dma("act", GST_B[48:49, 0:64], gk2, "g7")
        dma("act", GST_B[48:49, 64:128], gk2, "g8")
        dma("act", GST_B[49:50, 0:64], g_q, "g9")
        dma("act", GST_B[49:50, 64:128], g_q, "g10")
        b = rot.next()
        tr(PB[b][:, 0:8], GST_A[0:8, :], IDENT[0:8, 0:8])
        cp("dve", GCOLS[:, 0:8], PB[b][:, 0:8])

        def gcols_rest():
            b = rot.next()
            tr(PB[b][:, 0:50], GST_B[0:50, :], IDENT[0:50, 0:50])
            cp("dve", GCOLS[:, 8:58], PB[b][:, 0:50])
            ts("dve", GCOLS[:, 58:59], GCOLS[:, 57:58], 0.125, None, ALU.mult)

        xs = [T(WOFF[3] + i * 4096, (1024,), F32) for i in range(4)]

        def xload(t):
            for blk in range(t * 4, t * 4 + 4):
                st = xs[blk % 4]
                dma("sp", st[:, :], x[blk * 128:(blk + 1) * 128, :], "x%d" % (blk % 4))
                for half in range(2):
                    b = rot.next()
                    for cc in range(4):
                        c = half * 4 + cc
                        tr(PB4[b][:, cc, :], st[:, c * 128:(c + 1) * 128], IDENT[:, :])
                    cp(evac_eng(), XT[:, half * 4:(half + 1) * 4, blk * 128:(blk + 1) * 128], PB4[b][:, :, :])

        xload(0)
        dma("sp", MASKLE[:, :], c_maskle, "c1")
        if stop_after < 1:
            for t in range(1, NT):
                xload(t)
            gcols_rest()

        def norm_A(t, sq):
            tl = slice(t * TT, (t + 1) * TT)
            for c in range(8):
                act(sq[:, c, :], XT[:, c, tl], AF.Square)

        def norm_B(t, gcol0, sq, sd, rstd, dst=HT):
            tl = slice(t * TT, (t + 1) * TT)
            b = rot.next()
            mm(PB[b][:, :], [(ONES_S(), sq[:, c, :]) for c in range(8)])
            act(sd[:, :], PB[b][:, :], AF.Ln, bias=EPS)
            act(rstd[:, :], sd[:, :], AF.Exp, scale=-0.5)
            for c in range(8):
                if gcol0 is None:
                    tt("dve", dst[:, c, tl], XT[:, c, tl], rstd[:, :], ALU.mult)
                else:
                    stt(dst[:, c, tl], XT[:, c, tl], GCOLS[:, gcol0 + c:gcol0 + c + 1], rstd[:, :], ALU.mult, ALU.mult)

        def norm_tile(t, gcol0, sq, sd, rstd, dst=HT):
            norm_A(t, sq)
            norm_B(t, gcol0, sq, sd, rstd, dst)

        if stop_after >= 1:
            sreset()
            uT = salloc((8, TT), BF16)
            sq = uT
            yT = salloc((8, TT), BF16)
            v16 = [salloc((1024,), BF16) for _ in range(2)]
            vn = [salloc((1024,), BF16) for _ in range(2)]
            sd = salloc((TT,), F32)
            rstd = salloc((TT,), F32)
            wsT = salloc((8, 128), BF16)
            gvt = salloc((1024,), F32)
            wsl = T(yT.off, (8, 128), F32)
            dma("sp", gvt[:, :], g_v_a.partition_broadcast(128), "gv")
            dma("sp", wsl[:, :, :], w_spatial[0].rearrange("g t s -> t g s"), "ws")
            def ws_prep():
                for hb in range(2):
                    b = rot.next()
                    for gg in range(4):
                        tr(PB4[b][:, gg, :], wsl[:, hb * 4 + gg, :], IDENT[:, :])
                    tt("dve", wsT[:, hb * 4:(hb + 1) * 4, :], PB4[b][:, :, :],
                       Ref(MASKLE.ap.unsqueeze(1).to_broadcast([128, 4, 128]), MASKLE[:, :].keys), ALU.mult)
            W0 = wslot(0, (8, 1024))
            W1 = wslot(1, (8, 1024))
            W2 = wslot(2, (8, 1024))
            norm_A(0, sq)
            norm_B(0, G_MIXA, sq, sd, rstd)
            for t in range(NT):
                tl = slice(t * TT, (t + 1) * TT)
                if t + 1 < NT:
                    xload(t + 1)
                    if t + 1 == NT - 1:
                        w_issue_upto(3)
                for f in range(8):
                    b = rot.next()
                    mm(PB[b][:, :], [(W0[:, k, f * 128:(f + 1) * 128], HT[:, k, tl]) for k in range(8)])
                    act(uT[:, f, :], PB[b][:, :], AF.Gelu_apprx_tanh)
                if t == 0:
                    gcols_rest()
                    ws_prep()

                def VST(blk, t=t):
                    tb = t * 4 + blk
                    tbl = slice(tb * 128, (tb + 1) * 128)
                    for half in range(2):
                        b = rot.next()
                        mm(PB[b][:, :], [(HT[:, k, tbl], W1[:, k, half * 512:(half + 1) * 512]) for k in range(8)])
                        act(v16[blk % 2][:, half * 512:(half + 1) * 512], PB[b][:, :], AF.Gelu_apprx_tanh)

                def NST(blk):
                    vb = vn[blk % 2]
                    vv = v16[blk % 2]
                    c4 = (blk % 2) * 4
                    act(vb[:, :], vv[:, :], AF.Square, accum=SMALL[:, c4:c4 + 1])
                    ts("dve", SMALL[:, c4 + 1:c4 + 2], SMALL[:, c4:c4 + 1], 1.0 / 1024, EPS, ALU.mult, ALU.add)
                    act(SMALL[:, c4 + 2:c4 + 3], SMALL[:, c4 + 1:c4 + 2], AF.Sqrt)
                    recip(SMALL[:, c4 + 3:c4 + 4], SMALL[:, c4 + 2:c4 + 3])
                    stt(vb[:, :], vv[:, :], SMALL[:, c4 + 3:c4 + 4], gvt[:, :], ALU.mult, ALU.mult)

                def SPST(blk):
                    vb = vn[blk % 2]
                    for hb in range(2):
                        b = rot.next()

                        def fn(e, b=b, hb=hb, vb=vb):
                            ins = None
                            for gg in range(4):
                                g = hb * 4 + gg
                                e.matmul(PB4[b][:, gg, :].ap, vb[:, g * 128:(g + 1) * 128].ap, wsT[:, g, :].ap,
                                         start=True, stop=False)
                                ins = e.matmul(PB4[b][:, gg, :].ap, ONES1(slice(0, 1)).ap,
                                               BSROW[0:1, g * 128:(g + 1) * 128].ap, start=False, stop=True)
                            return ins
                        S.op("pe", fn, reads=[vb[:, hb * 512:(hb + 1) * 512], wsT[:, hb * 4:(hb + 1) * 4, :],
                                              ONES1(slice(0, 1)), BSROW[0:1, :]], writes=[PB4[b][:, :, :]])
                        tt("dve", yT[:, hb * 4:(hb + 1) * 4, blk * 128:(blk + 1) * 128], PB4[b][:, :, :],
                           uT[:, hb * 4:(hb + 1) * 4, blk * 128:(blk + 1) * 128], ALU.mult)

                VST(0)
                VST(1)
                NST(0)
                for blk in range(4):
                    if blk + 2 < 4:
                        VST(blk + 2)
                    if blk + 1 < 4:
                        NST(blk + 1)
                    SPST(blk)
                if t + 1 < NT:
                    norm_A(t + 1, sq)
                for dch in range(8):
                    if dch == 4 and t + 1 < NT:
                        norm_B(t + 1, G_MIXA, sq, sd, rstd)
                    b = rot.next()
                    mm(PB[b][:, :], [(W2[:, k, dch * 128:(dch + 1) * 128], yT[:, k, :]) for k in range(8)])
                    tt("dve", XT[:, dch, tl], PB[b][:, :], XT[:, dch, tl], ALU.add)
            w_done(0)
            w_done(1)
            w_done(2)

        def mlp_phase(l, blk0, gcol0):
            sreset()
            sq = salloc((8, TT), BF16)
            sd = salloc((TT,), F32)
            rstd = salloc((TT,), F32)
            aT = [salloc((8, TT), BF16) for _ in range(2)]
            rt = [salloc((TT,), BF16) for _ in range(2)]
            norm_tile(0, gcol0, sq, sd, rstd)
            it = 0
            for sg in range(4):
                nA, nB = blk0 + 2 * sg, blk0 + 2 * sg + 1
                wu = [Tile(arena, WOFF[n % 4], (8, 512), BF16) for n in (nA, nB)]
                wd = [Tile(arena, WOFF[n % 4] + 8192, (4, 1024), BF16) for n in (nA, nB)]
                for t in range(NT):
                    tl = slice(t * TT, (t + 1) * TT)
                    a = aT[it % 2]
                    it += 1
                    for f in range(8):
                        b = rot.next()
                        mm(PB[b][:, :], [(wu[f // 4][:, k, (f % 4) * 128:(f % 4 + 1) * 128], HT[:, k, tl]) for k in range(8)])
                        r = rt[f % 2]
                        act(r[:, :], PB[b][:, :], AF.Relu)
                        tt("dve", a[:, f, :], r[:, :], r[:, :], ALU.mult)
                    if sg == 0 and t + 1 < NT:
                        norm_A(t + 1, sq)
                    for dch in range(8):
                        if dch == 4 and sg == 0 and t + 1 < NT:
                            norm_B(t + 1, gcol0, sq, sd, rstd)
                        b = rot.next()
                        mm(PB[b][:, :], [(wd[f // 4][:, f % 4, dch * 128:(dch + 1) * 128], a[:, f, :]) for f in range(8)])
                        tt("dve", XT[:, dch, tl], PB[b][:, :], XT[:, dch, tl], ALU.add)
                w_done(nA)
                w_done(nB)

        out_fin = []

        def emit_out(t, osb):
            for blk in range(t * 4, t * 4 + 4):
                ot = osb[blk % 2]
                for half in range(2):
                    b = rot.next()
                    for cc in range(4):
                        c = half * 4 + cc
                        tr(PB4[b][:, cc, :], XT[:, c, blk * 128:(blk + 1) * 128], IDENT[:, :])
                    cp(evac_eng(), ot[:, half * 512:(half + 1) * 512], PB[b][:, :])
                out_fin.append(S.op("sp", lambda e, ot=ot, blk=blk: e.dma_start(out=y[blk * 128:(blk + 1) * 128, :], in_=ot[:, :].ap),
                                    reads=[ot[:, :]], dma="o%d" % (blk % 2)))

        def ple_phase(l, blkn, gcol0, with_out=False):
            sreset()
            sq = salloc((8, TT), BF16)
            sd = salloc((TT,), F32)
            rstd = salloc((TT,), F32)
            pst = salloc((4, 256), F32)
            pT = salloc((2, TT), BF16)
            gate = [salloc((TT,), F32) for _ in range(2)]
            tmp2 = [salloc((TT,), F32) for _ in range(2)]
            osb = [salloc((1024,), F32) for _ in range(2)] if with_out else None
            Wg = wslot(blkn, (8, 1024))
            def pload(t):
                dma("sp", pst[:, :, :], p[l][t * TT:(t + 1) * TT, :].rearrange("(b q) f -> q b f", q=128), "pl")
            pload(0)
            for t in range(NT):
                tl = slice(t * TT, (t + 1) * TT)
                if t == 0:
                    norm_tile(0, gcol0, sq, sd, rstd)
                for kk in range(2):
                    b = rot.next()
                    for blk in range(4):
                        tr(PB4[b][:, blk, :], pst[:, blk, kk * 128:(kk + 1) * 128], IDENT[:, :])
                    cp(evac_eng(), pT[:, kk, :], PB[b][:, :])
                if t + 1 < NT:
                    pload(t + 1)
                if t + 1 < NT:
                    norm_A(t + 1, sq)
                for dch in range(8):
                    if dch == 4 and t + 1 < NT:
                        norm_B(t + 1, gcol0, sq, sd, rstd)
                    b1 = rot.next()
                    mm(PB[b1][:, :], [(Wg[:, k, dch * 128:(dch + 1) * 128], HT[:, k, tl]) for k in range(8)])
                    g_ = gate[dch % 2]
                    act(g_[:, :], PB[b1][:, :], AF.Sigmoid)
                    b2 = rot.next()
                    mm(PB[b2][:, :], [(WP[:, kk, dch * 128:(dch + 1) * 128], pT[:, kk, :]) for kk in range(2)])
                    t2 = tmp2[dch % 2]
                    tt("dve", t2[:, :], PB[b2][:, :], g_[:, :], ALU.mult)
                    tt("dve", XT[:, dch, tl], XT[:, dch, tl], t2[:, :], ALU.add)
                if with_out:
                    emit_out(t, osb)
            w_done(blkn)

        if stop_after >= 2:
            mlp_phase(0, 3, G_MLP0)
        if stop_after >= 3:
            ple_phase(0, 11, G_PLE0)
            dma("pool", WP[:, :, :], kp(w_ple_proj[1]), "wp")

        if stop_after >= 4:
            sreset()
            sq = salloc((8, TT), BF16)
            sd = salloc((TT,), F32)
            rstd = salloc((TT,), F32)
            for t in range(NT):
                norm_tile(t, None, sq, sd, rstd)
            sreset()
            KT = salloc((SEQ,), BF16)
            QTh = [salloc((SEQ,), BF16) for _ in range(2)]
            VJ = salloc((16, 128), BF16)
            OT = salloc((SEQ,), BF16)
            e1 = [salloc((TT,), F32) for _ in range(2)]
            AT = [salloc((TT,), BF16) for _ in range(3)]
            csb = [salloc((TT,), BF16) for _ in range(2)]
            PW = [Tile(arena, WOFF[0] + i * 8192, (4096,), BF16) for i in range(2)]
            LS = [Tile(arena, WOFF[1], (16, TT), BF16), Tile(arena, WOFF[2], (16, TT), BF16)]
            sqK = Tile(arena, WOFF[2], (4, TT), BF16)
            sqQ = Tile(arena, WOFF[2] + 4096, (4, TT), BF16)
            sd4 = Tile(arena, WOFF[2] + 8192, (4, TT), F32)
            rawK = Tile(arena, WOFF[1], (4, TT), F32)
            rawQ = Tile(arena, WOFF[1] + 8192, (4, TT), F32)
            srot = Rot(range(5))
            zrot = Rot([0, 1])
            grot = Rot([2, 3])
            frot = Rot([6, 7])
            JUNK = 7

            def dummy(k):
                def fn(e):
                    ins = None
                    for _ in range(k):
                        ins = e.matmul(PB[JUNK][:, :].ap, ONES_S().ap, CP[:, 0:512].ap, start=True, stop=True)
                    return ins
                S.op("pe", fn, reads=[ONES_S(), CP[:, 0:512]], writes=[PB[JUNK][:, :]])
            CSB = 4
            OB = [5, 5]
            for bi in range(2):
                S.op("dve", lambda e, bi=bi: e.memset(csb[bi][:, :].ap, 0.0), writes=[csb[bi][:, :]])
                S.op("dve", lambda e, bi=bi: e.memset(QTh[bi][:, :].ap, 0.0), writes=[QTh[bi][:, :]])

            def pw_views(i):
                t_ = PW[i]
                off = t_.off
                wk = Tile(arena, off, (8, 128), BF16)
                wv = Tile(arena, off + 2048, (8, 128), BF16)
                wq = Tile(arena, off + 4096, (8, 128), BF16)
                wo = Tile(arena, off + 6144, (1024,), BF16)
                return wk, wv, wq, wo

            def load_pair(j):
                wk, wv, wq, wo = pw_views(j % 2)
                sl = slice(j * 128, (j + 1) * 128)
                dma("pool", wk[:, :, :], kp(w_kv)[:, :, j * 128:(j + 1) * 128], "pw%da" % (j % 2))
                dma("pool", wv[:, :, :], kp(w_kv)[:, :, 1024 + j * 128:1024 + (j + 1) * 128], "pw%db" % (j % 2))
                dma("pool", wq[:, :, :], kp(w_q[0])[:, :, j * 128:(j + 1) * 128], "pw%dc" % (j % 2))
                dma("pool", wo[:, :], w_out_b[0][sl, :], "pw%dd" % (j % 2))
                for w_, g0 in ((wk, G_KV), (wv, G_KV), (wq, G_MIXB)):
                    gb = Ref(GCOLS.ap[:, g0:g0 + 8].unsqueeze(2).to_broadcast([128, 8, 128]), GCOLS[:, g0:g0 + 8].keys)
                    tt("dve", w_[:, :, :], w_[:, :, :], gb, ALU.mult)

            load_pair(0)
            for j in range(8):
                wk, wv, wq, wo = pw_views(j % 2)
                if j + 1 < 8:
                    load_pair(j + 1)
                KB = [0, 1, 2, 3]
                QB = [4, 5, 6, 7]
                for (w_, raw, sqx, mb) in ((wk, rawK, sqK, KB), (wq, rawQ, sqQ, QB)):
                    for t in range(NT):
                        tl = slice(t * TT, (t + 1) * TT)
                        mm(PB[mb[t]][:, :], [(w_[:, k, :], HT[:, k, tl]) for k in range(8)])
                        act(raw[:, t, :], PB[mb[t]][:, :], AF.Copy)
                        act(sqx[:, t, :], PB[mb[t]][:, :], AF.Square)
                for (sqx, mb) in ((sqK, KB), (sqQ, QB)):
                    for t in range(NT):
                        mm(PB[mb[t]][:, :], [(BDIAG(), sqx[:, t, :])])
                for t in range(NT):
                    act(sd4[:, t, :], PB[KB[t]][:, :], AF.Ln, bias=EPS)
                for t in range(NT):
                    act(sd4[:, t, :], sd4[:, t, :], AF.Exp, scale=-0.5)
                for t in range(NT):
                    tl = slice(t * TT, (t + 1) * TT)
                    stt(KT[:, tl], rawK[:, t, :], GCOLS[:, G_K:G_K + 1], sd4[:, t, :], ALU.mult, ALU.mult)
                for q4 in range(4):
                    b_ = KB[q4]
                    for bb in range(4):
                        blk = q4 * 4 + bb
                        mm(PB4[b_][:, bb, :], [(HT[:, k, blk * 128:(blk + 1) * 128], wv[:, k, :]) for k in range(8)])
                    cp("dve", VJ[:, q4 * 4:(q4 + 1) * 4, :], PB4[b_][:, :, :])
                for t in range(NT):
                    act(sd4[:, t, :], PB[QB[t]][:, :], AF.Ln, bias=EPS)
                for t in range(NT):
                    act(sd4[:, t, :], sd4[:, t, :], AF.Exp, scale=-0.5)
                for t in range(NT):
                    tl = slice(t * TT, (t + 1) * TT)
                    for h_ in range(2):
                        hp = slice(h_ * 64, (h_ + 1) * 64)
                        stt(QTh[h_][hp, tl], rawQ[hp, t, :], GCOLS[hp, G_QS:G_QS + 1], sd4[hp, t, :], ALU.mult, ALU.mult)
                heads = (slice(0, 64), slice(64, 128))

                def cols(qt, kb):
                    c0 = max(0, kb - 4 * qt) * 128
                    return c0, slice(c0, TT), slice(qt * TT + c0, (qt + 1) * TT)

                def p1_steps(h, qt):
                    nkb = 4 * qt + 4
                    L = LS[h]
                    zb = {}

                    def Z(kb):
                        c0, cs, qs = cols(qt, kb)
                        zb[kb] = zrot.next()
                        mm(PB[zb[kb]][:, cs], [(KT[:, kb * 128:(kb + 1) * 128], QTh[h][:, qs])])

                    def EXP(kb):
                        c0, cs, qs = cols(qt, kb)
                        act(e1[kb % 2][:, cs], PB[zb[kb]][:, cs], AF.Exp)

                    def LN(kb):
                        c0, cs, qs = cols(qt, kb)
                        act(L[:, kb, cs], e1[kb % 2][:, cs], AF.Ln, bias=1.0)
                        if kb >= 4 * qt:
                            tt("pool", L[:, kb, c0:c0 + 128], L[:, kb, c0:c0 + 128], MASK01(), ALU.mult)

                    def CS(kb):
                        c0, cs, qs = cols(qt, kb)
                        mm(PB[CSB][:, cs], [(ESEL(kb), L[:, kb, cs])], start=(kb == 0), stop=(kb == nkb - 1))

                    steps = []
                    for i in range(nkb + 3):
                        def st(i=i):
                            if i < nkb:
                                Z(i)
                            if 0 <= i - 1 < nkb:
                                EXP(i - 1)
                            if 0 <= i - 2 < nkb:
                                LN(i - 2)
                            if 0 <= i - 3 < nkb:
                                CS(i - 3)
                        steps.append(st)

                    def fin():
                        cp("dve", csb[h][0:16, :], PB[CSB][0:16, :])
                    steps.append(fin)
                    return steps

                def p2_steps(h, qt):
                    nkb = 4 * qt + 4
                    L = LS[h]
                    pr = heads[h]
                    gb = {}

                    def G(kb):
                        c0, cs, qs = cols(qt, kb)
                        b = gb[kb] = grot.next()

                        def fn(e):
                            e.matmul(PB[b][:, cs].ap, SSEL[:, kb, :].ap, csb[h][:, cs].ap, start=True, stop=False)
                            e.matmul(PB[b][:, cs].ap, TRINEG().ap, L[:, kb, cs].ap, start=False, stop=False)
                            return e.matmul(PB[b][:, cs].ap, KT[:, kb * 128:(kb + 1) * 128].ap, QTh[h][:, qs].ap,
                                            start=False, stop=True)
                        S.op("pe", fn, reads=[TRINEG(), L[:, kb, cs], SSEL[:, kb, :], csb[h][:, cs],
                                              KT[:, kb * 128:(kb + 1) * 128], QTh[h][:, qs]], writes=[PB[b][:, cs]])

                    def EXPA(kb):
                        c0, cs, qs = cols(qt, kb)
                        a_ = AT[kb % 3]
                        act(a_[:, cs], PB[gb[kb]][:, cs], AF.Exp)
                        if kb >= 4 * qt:
                            tt("pool", a_[:, c0:c0 + 128], a_[:, c0:c0 + 128], MASK01(), ALU.mult)

                    def AV(kb):
                        c0, cs, qs = cols(qt, kb)
                        mm(PB[OB[h]][:, cs], [(VJ[:, kb, :], AT[kb % 3][:, cs])], start=(kb == 0), stop=(kb == nkb - 1))

                    steps = []
                    for i in range(nkb + 1):
                        def st(i=i):
                            if i < nkb:
                                G(i)
                            if 0 <= i - 1 < nkb:
                                EXPA(i - 1)
                                AV(i - 1)
                        steps.append(st)

                    def fin():
                        cp("dve", OT[pr, qt * TT:(qt + 1) * TT], PB[OB[h]][pr, :])
                    steps.append(fin)
                    return steps

                pending = []

                def filler(k):
                    for _ in range(k):
                        if not pending:
                            return
                        qt_, dch = pending.pop(0)
                        tl = slice(qt_ * TT, (qt_ + 1) * TT)
                        b = frot.next()
                        mm(PB[b][:, :], [(wo[:, dch * 128:(dch + 1) * 128], OT[:, tl])])
                        tt("dve", XT[:, dch, tl], PB[b][:, :], XT[:, dch, tl], ALU.add)

                units = [(h, qt) for qt in range(NT) for h in range(2)]
                prev = None
                for u in units + [None]:
                    s1 = p1_steps(*u) if u is not None else []
                    s2 = p2_steps(*prev) if prev is not None else []
                    n = max(len(s1), len(s2))
                    for i in range(n):
                        if i < len(s2):
                            s2[i]()
                        if i < len(s1):
                            s1[i]()
                        if i == n // 2 or i == n - 1:
                            filler(2)
                    if prev is not None and prev[0] == 1:
                        pending.extend((prev[1], dch) for dch in range(8))
                    prev = u
                filler(len(pending))
            w_done(12)
            w_done(13)
            w_done(14)

        if stop_after >= 5:
            w_issue_upto(18)
            mlp_phase(1, 15, G_MLP1)
        if stop_after >= 6:
            ple_phase(1, 23, G_PLE1, with_out=True)

        if not out_fin:
            sreset()
            osb = [salloc((1024,), F32) for _ in range(2)]
            for t in range(NT):
                emit_out(t, osb)
        S.emit(final_waits=out_fin[-2:])
    return nc


def make_consts():
    ident = np.eye(128, dtype=np.float32)
    s = np.arange(128)[:, None]
    t = np.arange(128)[None, :]
    maskle = (s <= t).astype(np.float32)
    pack = np.zeros((128, 784), np.float32)
    pack[:, 0:128] = 1.0 / 1024
    pack[:, 128:256] = 1.0
    bd = np.zeros((128, 128), np.float32)
    bd[0:64, 0:64] = 1.0 / 64
    bd[64:128, 64:128] = 1.0 / 64
    pack[:, 256:384] = bd
    pack[:, 384:512] = -(s >= t).astype(np.float32)
    pack[:, 512:640] = (s < t).astype(np.float32)
    pack[:, 640 + 15] = -1.0
    pack[:, 640 + 47] = -1.0
    ssel = np.zeros((128, 16, 128), np.float32)
    for kb in range(16):
        for r in range(16):
            if r > kb:
                ssel[r, kb, :] = 1.0
                ssel[32 + r, kb, :] = 1.0
    return {"c_ident": ident, "c_maskle": maskle, "c_pack": pack, "c_ssel": ssel.reshape(128, 2048)}


_CACHE = {}


def kernel(**inputs):
    n = 8
    if "nc" not in _CACHE:
        nc = bass.Bass("TRN2", target_bir_lowering=False)
        build_program(nc)
        _CACHE["nc"] = nc
    nc = _CACHE["nc"]
    consts = make_consts()
    shared = {k: np.ascontiguousarray(np.asarray(v, dtype=np.float32)) for k, v in inputs.items() if k not in ("x", "p")}
    shared.update(consts)
    x = np.asarray(inputs["x"], dtype=np.float32)
    p = np.asarray(inputs["p"], dtype=np.float32)
    in_maps = []
    for i in range(n):
        m = dict(shared)
        m["x"] = np.ascontiguousarray(x[i])
        m["p"] = np.ascontiguousarray(p[:, i])
        in_maps.append(m)
    res = run_bass_kernel_spmd(nc, in_maps, core_ids=list(range(n)))
    return np.stack([np.asarray(r["y"], dtype=np.float32) for r in res.results], axis=0)
```

```python
import contextlib
import numpy as np
import concourse.bass as bass
import concourse.mybir as mybir
from concourse.bass_utils import run_bass_kernel_spmd

F32 = mybir.dt.float32
BF16 = mybir.dt.bfloat16
AF = mybir.ActivationFunctionType
ALU = mybir.AluOpType

SEQ = 2048
D = 1024
NT = 4
TT = 512
NC8 = 8
EPS = 1e-6
GR = 256
ENGS = ("pe", "act", "dve", "pool", "sp")

G_MIXA, G_MLP0, G_PLE0, G_KV, G_MIXB, G_MLP1, G_PLE1, G_K, G_Q, G_QS = 0, 8, 16, 24, 32, 40, 48, 56, 57, 58


class Op:
    __slots__ = ("eng", "fn", "deps", "signal", "idx", "dma", "dsem", "dval", "count", "waits")

    def __init__(self, eng, fn, dma):
        self.eng = eng
        self.fn = fn
        self.dma = dma
        self.signal = False
        self.count = 0
        self.dsem = 0
        self.dval = 0
        self.deps = None
        self.waits = None


class Ref:
    __slots__ = ("ap", "keys")

    def __init__(self, ap, keys):
        self.ap = ap
        self.keys = keys


class Sched:
    def __init__(self, nc):
        self.nc = nc
        self.ops = []
        self.lastw = {}
        self.rd = {}
        self.streams = {}

    def op(self, eng, fn, reads=(), writes=(), dma=None):
        o = Op(eng, fn, dma)
        o.idx = len(self.ops)
        deps = {}

        def add(d):
            k = ("d", d.dma) if d.dma is not None else d.eng
            p = deps.get(k)
            if p is None or p.idx < d.idx:
                deps[k] = d

        lastw = self.lastw
        rd = self.rd
        for r in reads:
            for k in r.keys:
                w = lastw.get(k)
                if w is not None:
                    add(w)
        for r in writes:
            for k in r.keys:
                w = lastw.get(k)
                if w is not None:
                    add(w)
                rr = rd.get(k)
                if rr:
                    for x in rr.values():
                        add(x)
        if dma is not None:
            p = self.streams.get(dma)
            if p is not None:
                add(p)
            self.streams[dma] = o
        mykey = ("d", dma) if dma is not None else eng
        if eng == "pe" and dma is None:
            deps.pop("pe", None)
        o.deps = deps
        for r in reads:
            for k in r.keys:
                d = rd.get(k)
                if d is None:
                    rd[k] = {mykey: o}
                else:
                    d[mykey] = o
        for r in writes:
            for k in r.keys:
                lastw[k] = o
                rd[k] = {}
        self.ops.append(o)
        return o

    def emit(self, final_waits=()):
        nc = self.nc
        by_eng = {e: [o for o in self.ops if o.eng == e] for e in ENGS}
        for e in ENGS:
            waited = {}
            for o in by_eng[e]:
                w = []
                for k, d in o.deps.items():
                    if waited.get(k, -1) >= d.idx:
                        continue
                    waited[k] = d.idx
                    d.signal = True
                    w.append(d)
                o.waits = w
        for o in final_waits:
            o.signal = True
        cnt = {e: 0 for e in ENGS}
        streams = {}
        for o in self.ops:
            if o.dma is not None:
                st = streams.get(o.dma)
                if st is None:
                    st = streams[o.dma] = [len(streams), 0]
                st[1] += 16
                o.dsem = st[0]
                o.dval = st[1]
            elif o.signal:
                cnt[o.eng] += 1
                o.count = cnt[o.eng]
        with contextlib.ExitStack() as es:
            esem = {e: es.enter_context(nc.semaphore("s_" + e)) for e in ENGS}
            dsem = [es.enter_context(nc.semaphore("d%d" % i)) for i in range(len(streams))]
            block = es.enter_context(nc.Block())

            def run(ename, eng):
                for o in by_eng[ename]:
                    for d in o.waits:
                        if d.dma is not None:
                            eng.wait_ge(dsem[d.dsem], d.dval)
                        else:
                            eng.wait_ge(esem[d.eng], d.count)
                    ins = o.fn(eng)
                    if o.dma is not None:
                        ins.then_inc(dsem[o.dsem], 16)
                    elif o.signal:
                        ins.then_inc(esem[ename], 1)
                if ename == "sp":
                    for o in final_waits:
                        if o.dma is not None:
                            eng.wait_ge(dsem[o.dsem], o.dval)
                        else:
                            eng.wait_ge(esem[o.eng], o.count)

            @block.tensor
            def _(e):
                run("pe", e)

            @block.scalar
            def _(e):
                run("act", e)

            @block.vector
            def _(e):
                run("dve", e)

            @block.gpsimd
            def _(e):
                run("pool", e)

            @block.sync
            def _(e):
                run("sp", e)


class Tile:
    def __init__(self, base_ap, off, shape, dt, parts=128, space="S", bank=0):
        self.off = off
        self.shape = tuple(shape)
        self.es = 4 if dt == F32 else 2
        self.parts = parts
        self.space = space
        self.bank = bank
        n = 1
        for s in shape:
            n *= s
        self.n = n
        if space == "S":
            v = base_ap[0:parts, off // 2: off // 2 + n * self.es // 2]
            if dt == F32:
                v = v.bitcast(F32)
        else:
            v = base_ap[0:parts, 0:n]
        if len(shape) == 2:
            v = v.rearrange("p (a b) -> p a b", a=shape[0])
        elif len(shape) == 3:
            v = v.rearrange("p (a b c) -> p a b c", a=shape[0], b=shape[1])
        self.ap = v

    def __getitem__(self, idx):
        if not isinstance(idx, tuple):
            idx = (idx,)
        ps = idx[0]
        fidx = list(idx[1:])
        while len(fidx) < len(self.shape):
            fidx.append(slice(None))
        ap = self.ap[(ps,) + tuple(fidx)]
        rng = []
        for i, s in zip(fidx, self.shape):
            if isinstance(i, int):
                rng.append((i, i + 1))
            else:
                a = 0 if i.start is None else i.start
                b = s if i.stop is None else i.stop
                rng.append((a, b))
        strides = []
        acc = 1
        for s in reversed(self.shape):
            strides.append(acc)
            acc *= s
        strides = strides[::-1]
        keys = set()
        outer = [()]
        for (a, b) in rng[:-1]:
            outer = [o + (i,) for o in outer for i in range(a, b)]
        la, lb = rng[-1]
        gr = GR if self.space == "S" else 512
        for o in outer:
            base = sum(i * st for i, st in zip(o, strides[:-1]))
            b0 = self.off + (base + la) * self.es
            b1 = self.off + (base + lb) * self.es
            for g in range(b0 // gr, (b1 - 1) // gr + 1):
                keys.add((self.space, self.bank, g))
        return Ref(ap, keys)


def build_program(nc, stop_after=99):
    dt_in = {}

    def din(name, shape):
        t = nc.dram_tensor(name, list(shape), F32, kind="ExternalInput").ap()
        dt_in[name] = t
        return t

    x = din("x", [SEQ, D])
    p = din("p", [2, SEQ, 256])
    ln_mix_a = din("ln_mix_a", [1, D])
    w_in_a = din("w_in_a", [1, D, 2 * D])
    g_v_a = din("g_v_a", [1, D])
    w_spatial = din("w_spatial", [1, 8, 128, 128])
    b_spatial = din("b_spatial", [1, 8, 128])
    w_out_a = din("w_out_a", [1, D, D])
    ln_kv = din("ln_kv", [D])
    w_kv = din("w_kv", [D, 2 * D])
    g_k = din("g_k", [64])
    ln_mix_b = din("ln_mix_b", [1, D])
    w_q = din("w_q", [1, D, D])
    g_q = din("g_q", [1, 64])
    w_out_b = din("w_out_b", [1, D, D])
    ln_mlp = din("ln_mlp", [2, D])
    w_up = din("w_up", [2, D, 4 * D])
    w_down = din("w_down", [2, 4 * D, D])
    ln_ple = din("ln_ple", [2, D])
    w_ple_gate = din("w_ple_gate", [2, D, D])
    w_ple_proj = din("w_ple_proj", [2, 256, D])
    c_ident = din("c_ident", [128, 128])
    c_maskle = din("c_maskle", [128, 128])
    c_pack = din("c_pack", [128, 784])
    c_ssel = din("c_ssel", [128, 2048])
    y = nc.dram_tensor("y", [SEQ, D], F32, kind="ExternalOutput").ap()

    es = contextlib.ExitStack()
    with es:
        ARENA_BYTES = 212736
        arena = es.enter_context(nc.sbuf_tensor("arena", [128, ARENA_BYTES // 2], BF16))
        banks = [es.enter_context(nc.psum_tensor("ps%d" % i, [128, 512], F32)) for i in range(8)]
        PB = [Tile(banks[i], 0, (512,), F32, space="P", bank=i) for i in range(8)]
        PB4 = [Tile(banks[i], 0, (4, 128), F32, space="P", bank=i) for i in range(8)]
        S = Sched(nc)

        def T(off, shape, dt, parts=128):
            return Tile(arena, off, shape, dt, parts=parts)

        XT = T(0, (8, SEQ), F32)
        HT = T(65536, (8, SEQ), BF16)
        WOFF = [98304 + i * 16384 for i in range(4)]
        SCR0 = 163840
        coff = ARENA_BYTES
        def calloc(nbytes):
            nonlocal coff
            coff -= (nbytes + 255) // 256 * 256
            return coff
        IDENT = T(calloc(512), (128,), F32)
        MASKLE = T(calloc(512), (128,), F32)
        CP = T(calloc(1568), (784,), BF16)
        SSEL = T(calloc(4096), (16, 128), BF16)
        GCOLS = T(calloc(256), (64,), F32)
        BSROW = T(calloc(2048), (1024,), BF16, parts=1)
        SMALL = T(calloc(256), (64,), F32)
        WP = T(calloc(4096), (2, 1024), BF16)
        SCR_END = coff
        ONES_S = lambda: CP[:, 0:128]
        ONES1 = lambda ps=slice(0, 128): CP[ps, 128:256]
        BDIAG = lambda: CP[:, 256:384]
        TRINEG = lambda: CP[:, 384:512]
        MASK01 = lambda: CP[:, 512:640]
        ESEL = lambda kb: CP[:, 640 + 15 - kb: 640 + 15 - kb + 128]

        scr = [SCR0]

        def salloc(shape, dt, parts=128):
            n = 1
            for s in shape:
                n *= s
            nb = n * (4 if dt == F32 else 2)
            nb = (nb + 255) // 256 * 256
            off = scr[0]
            scr[0] += nb
            assert scr[0] <= SCR_END, ("scratch overflow", scr[0], SCR_END)
            return T(off, shape, dt, parts=parts)

        def sreset():
            scr[0] = SCR0

        def act(out, in_, func, bias=None, scale=None, accum=None, extra=()):
            kw = {}
            if bias is not None:
                kw["bias"] = bias.ap if isinstance(bias, Ref) else bias
            if scale is not None:
                kw["scale"] = scale.ap if isinstance(scale, Ref) else scale
            if accum is not None:
                kw["accum_out"] = accum.ap
            rds = [in_] + [r for r in (bias, scale) if isinstance(r, Ref)] + list(extra)
            wr = [out] + ([accum] if accum is not None else [])
            return S.op("act", lambda e: e.activation(out.ap, in_.ap, func, **kw), reads=rds, writes=wr)

        def tt(eng, out, a, b, op):
            return S.op(eng, lambda e: e.tensor_tensor(out.ap, a.ap, b.ap, op), reads=[a, b], writes=[out])

        def stt(out, a, sc, b, op0, op1):
            scv = sc.ap if isinstance(sc, Ref) else sc
            rds = [a, b] + ([sc] if isinstance(sc, Ref) else [])
            return S.op("dve", lambda e: e.scalar_tensor_tensor(out.ap, a.ap, scv, b.ap, op0, op1), reads=rds, writes=[out])

        def ts(eng, out, a, s1, s2, op0, op1=None):
            s1v = s1.ap if isinstance(s1, Ref) else s1
            s2v = s2.ap if isinstance(s2, Ref) else s2
            rds = [a] + [r for r in (s1, s2) if isinstance(r, Ref)]
            if op1 is None:
                return S.op(eng, lambda e: e.tensor_scalar(out.ap, a.ap, s1v, None, op0), reads=rds, writes=[out])
            return S.op(eng, lambda e: e.tensor_scalar(out.ap, a.ap, s1v, s2v, op0, op1), reads=rds, writes=[out])

        def cp(eng, out, in_):
            if eng == "act":
                return act(out, in_, AF.Copy)
            return S.op(eng, lambda e: e.tensor_copy(out.ap, in_.ap), reads=[in_], writes=[out])

        def recip(out, in_):
            return S.op("dve", lambda e: e.reciprocal(out.ap, in_.ap), reads=[in_], writes=[out])

        def mm(out, pairs, start=True, stop=True):
            n = len(pairs)

            def fn(e):
                ins = None
                for i, (l, r) in enumerate(pairs):
                    ins = e.matmul(out.ap, l.ap, r.ap, start=(start and i == 0), stop=(stop and i == n - 1))
                return ins
            rds = []
            for l, r in pairs:
                rds.append(l)
                rds.append(r)
            return S.op("pe", fn, reads=rds, writes=[out])

        def tr(out, in_, ident):
            return S.op("pe", lambda e: e.transpose(out.ap, in_.ap, ident.ap), reads=[in_, ident], writes=[out])

        def dma(eng, out, in_ap, stream, reads=()):
            return S.op(eng, lambda e: e.dma_start(out=out.ap, in_=in_ap), reads=list(reads), writes=[out], dma=stream)

        class Rot:
            def __init__(self, ids):
                self.ids = list(ids)
                self.i = 0

            def next(self):
                b = self.ids[self.i % len(self.ids)]
                self.i += 1
                return b

        rot = Rot(range(8))
        tog = [0]

        def evac_eng():
            tog[0] ^= 1
            return "act" if tog[0] else "dve"

        def wslot(n, shape):
            return T(WOFF[n % 4], shape, BF16)

        def kp(ap2d):
            return ap2d.rearrange("(k p) f -> p k f", p=128)

        loaders = []

        def L_full(src2d, c0, c1):
            def f(n):
                w = wslot(n, (8, c1 - c0))
                dma("pool", w[:, :, :], kp(src2d)[:, :, c0:c1], "W%da" % (n % 4))
            return f

        def L_mlp(l, gi):
            def f(n):
                wu = Tile(arena, WOFF[n % 4], (8, 512), BF16)
                wd = Tile(arena, WOFF[n % 4] + 8192, (4, 1024), BF16)
                dma("pool", wu[:, :, :], kp(w_up[l])[:, :, gi * 512:(gi + 1) * 512], "W%da" % (n % 4))
                dma("pool", wd[:, :, :], kp(w_down[l][gi * 512:(gi + 1) * 512, :]), "W%db" % (n % 4))
            return f

        def L_first(n):
            w = wslot(n, (8, 1024))
            for i, sfx in enumerate("abcd"):
                dma("pool", w[:, :, i * 256:(i + 1) * 256], kp(w_in_a[0])[:, :, i * 256:(i + 1) * 256], "W0" + sfx)
        loaders.append(L_first)
        loaders.append(L_full(w_in_a[0], 1024, 2048))
        loaders.append(L_full(w_out_a[0], 0, 1024))
        for gi in range(8):
            loaders.append(L_mlp(0, gi))
        loaders.append(L_full(w_ple_gate[0], 0, 1024))
        loaders += [None, None, None]
        for gi in range(8):
            loaders.append(L_mlp(1, gi))
        loaders.append(L_full(w_ple_gate[1], 0, 1024))
        issued = [0]

        def w_issue_upto(n):
            while issued[0] <= n and issued[0] < len(loaders):
                f = loaders[issued[0]]
                if f is not None:
                    f(issued[0])
                issued[0] += 1

        def w_done(n):
            w_issue_upto(n + 4)

        sreset()
        dma("sp", IDENT[:, :], c_ident, "c0")
        GST_A = T(97280, (128,), F32, parts=8)
        GST_B = T(97792, (128,), F32, parts=64)
        dma("sp", GST_A[0:8, :], ln_mix_a[0].rearrange("(k f) -> k f", f=128), "g0")
        dma("pool", CP[:, :], c_pack, "c2")
        dma("pool", BSROW[0:1, :], b_spatial[0].rearrange("(o g) t -> o (g t)", o=1), "c4")
        w_issue_upto(2)
        dma("pool", SSEL[:, :, :], c_ssel.rearrange("p (a b) -> p a b", a=16), "c3")
        dma("pool", WP[:, :, :], kp(w_ple_proj[0]), "wp")
        gi_ = 0
        for src in (ln_mlp[0], ln_ple[0], ln_kv, ln_mix_b[0], ln_mlp[1], ln_ple[1]):
            dma("act", GST_B[gi_ * 8:(gi_ + 1) * 8, :], src.rearrange("(k f) -> k f", f=128), "g%d" % (gi_ + 1))
            gi_ += 1
        gk2 = g_k.rearrange("(o f) -> o f", o=1)
        dma("act", GST_B[48:49, 0:64], gk2, "g7")
        dma("act", GST_B[48:49, 64:128], gk2, "g8")
        dma("act", GST_B[49:50, 0:64], g_q, "g9")
        dma("act", GST_B[49:50, 64:128], g_q, "g10")
        b = rot.next()
        tr(PB[b][:, 0:8], GST_A[0:8, :], IDENT[0:8, 0:8])
        cp("dve", GCOLS[:, 0:8], PB[b][:, 0:8])

        def gcols_rest():
            b = rot.next()
            tr(PB[b][:, 0:50], GST_B[0:50, :], IDENT[0:50, 0:50])
            cp("dve", GCOLS[:, 8:58], PB[b][:, 0:50])
            ts("dve", GCOLS[:, 58:59], GCOLS[:, 57:58], 0.125, None, ALU.mult)

        xs = [T(WOFF[3] + i * 4096, (1024,), F32) for i in range(4)]

        def xload(t):
            for blk in range(t * 4, t * 4 + 4):
                st = xs[blk % 4]
                dma("sp", st[:, :], x[blk * 128:(blk + 1) * 128, :], "x%d" % (blk % 4))
                for half in range(2):
                    b = rot.next()
                    for cc in range(4):
                        c = half * 4 + cc
                        tr(PB4[b][:, cc, :], st[:, c * 128:(c + 1) * 128], IDENT[:, :])
                    cp(evac_eng(), XT[:, half * 4:(half + 1) * 4, blk * 128:(blk + 1) * 128], PB4[b][:, :, :])

        xload(0)
        dma("sp", MASKLE[:, :], c_maskle, "c1")
        if stop_after < 1:
            for t in range(1, NT):
                xload(t)
            gcols_rest()

        def norm_A(t, sq):
            tl = slice(t * TT, (t + 1) * TT)
            for c in range(8):
                act(sq[:, c, :], XT[:, c, tl], AF.Square)

        def norm_B(t, gcol0, sq, sd, rstd, dst=HT):
            tl = slice(t * TT, (t + 1) * TT)
            b = rot.next()
            mm(PB[b][:, :], [(ONES_S(), sq[:, c, :]) for c in range(8)])
            act(sd[:, :], PB[b][:, :], AF.Ln, bias=EPS)
            act(rstd[:, :], sd[:, :], AF.Exp, scale=-0.5)
            for c in range(8):
                if gcol0 is None:
                    tt("dve", dst[:, c, tl], XT[:, c, tl], rstd[:, :], ALU.mult)
                else:
                    stt(dst[:, c, tl], XT[:, c, tl], GCOLS[:, gcol0 + c:gcol0 + c + 1], rstd[:, :], ALU.mult, ALU.mult)

        def norm_tile(t, gcol0, sq, sd, rstd, dst=HT):
            norm_A(t, sq)
            norm_B(t, gcol0, sq, sd, rstd, dst)

        if stop_after >= 1:
            sreset()
            uT = salloc((8, TT), BF16)
            sq = uT
            yT = salloc((8, TT), BF16)
            v16 = [salloc((1024,), BF16) for _ in range(2)]
            vn = [salloc((1024,), BF16) for _ in range(2)]
            sd = salloc((TT,), F32)
            rstd = salloc((TT,), F32)
            wsT = salloc((8, 128), BF16)
            gvt = salloc((1024,), F32)
            wsl = T(yT.off, (8, 128), F32)
            dma("sp", gvt[:, :], g_v_a.partition_broadcast(128), "gv")
            dma("sp", wsl[:, :, :], w_spatial[0].rearrange("g t s -> t g s"), "ws")
            def ws_prep():
                for hb in range(2):
                    b = rot.next()
                    for gg in range(4):
                        tr(PB4[b][:, gg, :], wsl[:, hb * 4 + gg, :], IDENT[:, :])
                    tt("dve", wsT[:, hb * 4:(hb + 1) * 4, :], PB4[b][:, :, :],
                       Ref(MASKLE.ap.unsqueeze(1).to_broadcast([128, 4, 128]), MASKLE[:, :].keys), ALU.mult)
            W0 = wslot(0, (8, 1024))
            W1 = wslot(1, (8, 1024))
            W2 = wslot(2, (8, 1024))
            norm_A(0, sq)
            norm_B(0, G_MIXA, sq, sd, rstd)
            for t in range(NT):
                tl = slice(t * TT, (t + 1) * TT)
                if t + 1 < NT:
                    xload(t + 1)
                    if t + 1 == NT - 1:
                        w_issue_upto(3)
                for f in range(8):
                    b = rot.next()
                    mm(PB[b][:, :], [(W0[:, k, f * 128:(f + 1) * 128], HT[:, k, tl]) for k in range(8)])
                    act(uT[:, f, :], PB[b][:, :], AF.Gelu_apprx_tanh)
                if t == 0:
                    gcols_rest()
                    ws_prep()

                def VST(blk, t=t):
                    tb = t * 4 + blk
                    tbl = slice(tb * 128, (tb + 1) * 128)
                    for half in range(2):
                        b = rot.next()
                        mm(PB[b][:, :], [(HT[:, k, tbl], W1[:, k, half * 512:(half + 1) * 512]) for k in range(8)])
                        act(v16[blk % 2][:, half * 512:(half + 1) * 512], PB[b][:, :], AF.Gelu_apprx_tanh)

                def NST(blk):
                    vb = vn[blk % 2]
                    vv = v16[blk % 2]
                    c4 = (blk % 2) * 4
                    act(vb[:, :], vv[:, :], AF.Square, accum=SMALL[:, c4:c4 + 1])
                    ts("dve", SMALL[:, c4 + 1:c4 + 2], SMALL[:, c4:c4 + 1], 1.0 / 1024, EPS, ALU.mult, ALU.add)
                    act(SMALL[:, c4 + 2:c4 + 3], SMALL[:, c4 + 1:c4 + 2], AF.Sqrt)
                    recip(SMALL[:, c4 + 3:c4 + 4], SMALL[:, c4 + 2:c4 + 3])
                    stt(vb[:, :], vv[:, :], SMALL[:, c4 + 3:c4 + 4], gvt[:, :], ALU.mult, ALU.mult)

                def SPST(blk):
                    vb = vn[blk % 2]
                    for hb in range(2):
                        b = rot.next()

                        def fn(e, b=b, hb=hb, vb=vb):
                            ins = None
                            for gg in range(4):
                                g = hb * 4 + gg
                                e.matmul(PB4[b][:, gg, :].ap, vb[:, g * 128:(g + 1) * 128].ap, wsT[:, g, :].ap,
                                         start=True, stop=False)
                                ins = e.matmul(PB4[b][:, gg, :].ap, ONES1(slice(0, 1)).ap,
                                               BSROW[0:1, g * 128:(g + 1) * 128].ap, start=False, stop=True)
                            return ins
                        S.op("pe", fn, reads=[vb[:, hb * 512:(hb + 1) * 512], wsT[:, hb * 4:(hb + 1) * 4, :],
                                              ONES1(slice(0, 1)), BSROW[0:1, :]], writes=[PB4[b][:, :, :]])
                        tt("dve", yT[:, hb * 4:(hb + 1) * 4, blk * 128:(blk + 1) * 128], PB4[b][:, :, :],
                           uT[:, hb * 4:(hb + 1) * 4, blk * 128:(blk + 1) * 128], ALU.mult)

                VST(0)
                VST(1)
                NST(0)
                for blk in range(4):
                    if blk + 2 < 4:
                        VST(blk + 2)
                    if blk + 1 < 4:
                        NST(blk + 1)
                    SPST(blk)
                if t + 1 < NT:
                    norm_A(t + 1, sq)
                for dch in range(8):
                    if dch == 4 and t + 1 < NT:
                        norm_B(t + 1, G_MIXA, sq, sd, rstd)
                    b = rot.next()
                    mm(PB[b][:, :], [(W2[:, k, dch * 128:(dch + 1) * 128], yT[:, k, :]) for k in range(8)])
                    tt("dve", XT[:, dch, tl], PB[b][:, :], XT[:, dch, tl], ALU.add)
            w_done(0)
            w_done(1)
            w_done(2)

        def mlp_phase(l, blk0, gcol0):
            sreset()
            sq = salloc((8, TT), BF16)
            sd = salloc((TT,), F32)
            rstd = salloc((TT,), F32)
            aT = [salloc((8, TT), BF16) for _ in range(2)]
            rt = [salloc((TT,), BF16) for _ in range(2)]
            norm_tile(0, gcol0, sq, sd, rstd)

            def wts(sg):
                nA, nB = blk0 + 2 * sg, blk0 + 2 * sg + 1
                wu = [Tile(arena, WOFF[n % 4], (8, 512), BF16) for n in (nA, nB)]
                wd = [Tile(arena, WOFF[n % 4] + 8192, (4, 1024), BF16) for n in (nA, nB)]
                return wu, wd

            def up(k):
                sg, t = divmod(k, NT)
                wu, wd = wts(sg)
                tl = slice(t * TT, (t + 1) * TT)
                a = aT[k % 2]
                for f in range(8):
                    b = rot.next()
                    mm(PB[b][:, :], [(wu[f // 4][:, kk, (f % 4) * 128:(f % 4 + 1) * 128], HT[:, kk, tl]) for kk in range(8)])
                    r = rt[f % 2]
                    act(r[:, :], PB[b][:, :], AF.Relu)
                    tt("dve", a[:, f, :], r[:, :], r[:, :], ALU.mult)
                if sg == 0 and t + 1 < NT:
                    norm_A(t + 1, sq)

            def down(k):
                sg, t = divmod(k, NT)
                wu, wd = wts(sg)
                tl = slice(t * TT, (t + 1) * TT)
                a = aT[k % 2]
                for dch in range(8):
                    if dch == 4 and sg == 0 and t + 1 < NT:
                        norm_B(t + 1, gcol0, sq, sd, rstd)
                    b = rot.next()
                    mm(PB[b][:, :], [(wd[f // 4][:, f % 4, dch * 128:(dch + 1) * 128], a[:, f, :]) for f in range(8)])
                    tt("dve", XT[:, dch, tl], PB[b][:, :], XT[:, dch, tl], ALU.add)
                if t == NT - 1:
                    w_done(blk0 + 2 * sg)
                    w_done(blk0 + 2 * sg + 1)

            NI = 4 * NT
            for k in range(NT):
                up(k)
                down(k)
            for k in range(NT, NI):
                up(k)
                if k > NT:
                    down(k - 1)
            down(NI - 1)

        out_fin = []

        def emit_out(t, osb):
            for blk in range(t * 4, t * 4 + 4):
                ot = osb[blk % 2]
                for half in range(2):
                    b = rot.next()
                    for cc in range(4):
                        c = half * 4 + cc
                        tr(PB4[b][:, cc, :], XT[:, c, blk * 128:(blk + 1) * 128], IDENT[:, :])
                    cp(evac_eng(), ot[:, half * 512:(half + 1) * 512], PB[b][:, :])
                out_fin.append(S.op("sp", lambda e, ot=ot, blk=blk: e.dma_start(out=y[blk * 128:(blk + 1) * 128, :], in_=ot[:, :].ap),
                                    reads=[ot[:, :]], dma="o%d" % (blk % 2)))

        def ple_phase(l, blkn, gcol0, with_out=False):
            sreset()
            sq = salloc((8, TT), BF16)
            sd = salloc((TT,), F32)
            rstd = salloc((TT,), F32)
            pst = salloc((4, 256), F32)
            pT = salloc((2, TT), BF16)
            gate = [salloc((TT,), F32) for _ in range(2)]
            tmp2 = [salloc((TT,), F32) for _ in range(2)]
            osb = [salloc((1024,), F32) for _ in range(2)] if with_out else None
            Wg = wslot(blkn, (8, 1024))
            def pload(t):
                dma("sp", pst[:, :, :], p[l][t * TT:(t + 1) * TT, :].rearrange("(b q) f -> q b f", q=128), "pl")
            pload(0)
            for t in range(NT):
                tl = slice(t * TT, (t + 1) * TT)
                if t == 0:
                    norm_tile(0, gcol0, sq, sd, rstd)
                for kk in range(2):
                    b = rot.next()
                    for blk in range(4):
                        tr(PB4[b][:, blk, :], pst[:, blk, kk * 128:(kk + 1) * 128], IDENT[:, :])
                    cp(evac_eng(), pT[:, kk, :], PB[b][:, :])
                if t + 1 < NT:
                    pload(t + 1)
                if t + 1 < NT:
                    norm_A(t + 1, sq)
                for dch in range(8):
                    if dch == 4 and t + 1 < NT:
                        norm_B(t + 1, gcol0, sq, sd, rstd)
                    b1 = rot.next()
                    mm(PB[b1][:, :], [(Wg[:, k, dch * 128:(dch + 1) * 128], HT[:, k, tl]) for k in range(8)])
                    g_ = gate[dch % 2]
                    act(g_[:, :], PB[b1][:, :], AF.Sigmoid)
                    b2 = rot.next()
                    mm(PB[b2][:, :], [(WP[:, kk, dch * 128:(dch + 1) * 128], pT[:, kk, :]) for kk in range(2)])
                    t2 = tmp2[dch % 2]
                    tt("dve", t2[:, :], PB[b2][:, :], g_[:, :], ALU.mult)
                    tt("dve", XT[:, dch, tl], XT[:, dch, tl], t2[:, :], ALU.add)
                if with_out:
                    emit_out(t, osb)
            w_done(blkn)

        if stop_after >= 2:
            mlp_phase(0, 3, G_MLP0)
        if stop_after >= 3:
            ple_phase(0, 11, G_PLE0)
            dma("pool", WP[:, :, :], kp(w_ple_proj[1]), "wp")

        if stop_after >= 4:
            sreset()
            sq = salloc((8, TT), BF16)
            sd = salloc((TT,), F32)
            rstd = salloc((TT,), F32)
            for t in range(NT):
                norm_tile(t, None, sq, sd, rstd)
            sreset()
            KT = salloc((SEQ,), BF16)
            QTh = [salloc((SEQ,), BF16) for _ in range(2)]
            VJ = salloc((16, 128), BF16)
            OT = salloc((SEQ,), BF16)
            e1 = [salloc((TT,), F32) for _ in range(2)]
            AT = [salloc((TT,), BF16) for _ in range(3)]
            csb = [salloc((TT,), BF16) for _ in range(2)]
            PW = [Tile(arena, WOFF[0] + i * 8192, (4096,), BF16) for i in range(2)]
            LS = [Tile(arena, WOFF[1], (16, TT), BF16), Tile(arena, WOFF[2], (16, TT), BF16)]
            sqK = Tile(arena, WOFF[2], (4, TT), BF16)
            sqQ = Tile(arena, WOFF[2] + 4096, (4, TT), BF16)
            sd4 = Tile(arena, WOFF[2] + 8192, (4, TT), F32)
            rawK = Tile(arena, WOFF[1], (4, TT), F32)
            rawQ = Tile(arena, WOFF[1] + 8192, (4, TT), F32)
            srot = Rot(range(5))
            zrot = Rot([0, 1])
            grot = Rot([2, 3])
            frot = Rot([6, 7])
            JUNK = 7

            def dummy(k):
                def fn(e):
                    ins = None
                    for _ in range(k):
                        ins = e.matmul(PB[JUNK][:, :].ap, ONES_S().ap, CP[:, 0:512].ap, start=True, stop=True)
                    return ins
                S.op("pe", fn, reads=[ONES_S(), CP[:, 0:512]], writes=[PB[JUNK][:, :]])
            CSB = 4
            OB = [5, 5]
            for bi in range(2):
                S.op("dve", lambda e, bi=bi: e.memset(csb[bi][:, :].ap, 0.0), writes=[csb[bi][:, :]])
                S.op("dve", lambda e, bi=bi: e.memset(QTh[bi][:, :].ap, 0.0), writes=[QTh[bi][:, :]])

            def pw_views(i):
                t_ = PW[i]
                off = t_.off
                wk = Tile(arena, off, (8, 128), BF16)
                wv = Tile(arena, off + 2048, (8, 128), BF16)
                wq = Tile(arena, off + 4096, (8, 128), BF16)
                wo = Tile(arena, off + 6144, (1024,), BF16)
                return wk, wv, wq, wo

            def load_pair(j):
                wk, wv, wq, wo = pw_views(j % 2)
                sl = slice(j * 128, (j + 1) * 128)
                dma("pool", wk[:, :, :], kp(w_kv)[:, :, j * 128:(j + 1) * 128], "pw%da" % (j % 2))
                dma("pool", wv[:, :, :], kp(w_kv)[:, :, 1024 + j * 128:1024 + (j + 1) * 128], "pw%db" % (j % 2))
                dma("pool", wq[:, :, :], kp(w_q[0])[:, :, j * 128:(j + 1) * 128], "pw%dc" % (j % 2))
                dma("pool", wo[:, :], w_out_b[0][sl, :], "pw%dd" % (j % 2))
                for w_, g0 in ((wk, G_KV), (wv, G_KV), (wq, G_MIXB)):
                    gb = Ref(GCOLS.ap[:, g0:g0 + 8].unsqueeze(2).to_broadcast([128, 8, 128]), GCOLS[:, g0:g0 + 8].keys)
                    tt("dve", w_[:, :, :], w_[:, :, :], gb, ALU.mult)

            load_pair(0)
            for j in range(8):
                wk, wv, wq, wo = pw_views(j % 2)
                if j + 1 < 8:
                    load_pair(j + 1)
                KB = [0, 1, 2, 3]
                QB = [4, 5, 6, 7]
                for (w_, raw, sqx, mb) in ((wk, rawK, sqK, KB), (wq, rawQ, sqQ, QB)):
                    for t in range(NT):
                        tl = slice(t * TT, (t + 1) * TT)
                        mm(PB[mb[t]][:, :], [(w_[:, k, :], HT[:, k, tl]) for k in range(8)])
                        act(raw[:, t, :], PB[mb[t]][:, :], AF.Copy)
                        act(sqx[:, t, :], PB[mb[t]][:, :], AF.Square)
                for (sqx, mb) in ((sqK, KB), (sqQ, QB)):
                    for t in range(NT):
                        mm(PB[mb[t]][:, :], [(BDIAG(), sqx[:, t, :])])
                for t in range(NT):
                    act(sd4[:, t, :], PB[KB[t]][:, :], AF.Ln, bias=EPS)
                for t in range(NT):
                    act(sd4[:, t, :], sd4[:, t, :], AF.Exp, scale=-0.5)
                for t in range(NT):
                    tl = slice(t * TT, (t + 1) * TT)
                    stt(KT[:, tl], rawK[:, t, :], GCOLS[:, G_K:G_K + 1], sd4[:, t, :], ALU.mult, ALU.mult)
                for q4 in range(4):
                    b_ = KB[q4]
                    for bb in range(4):
                        blk = q4 * 4 + bb
                        mm(PB4[b_][:, bb, :], [(HT[:, k, blk * 128:(blk + 1) * 128], wv[:, k, :]) for k in range(8)])
                    cp("dve", VJ[:, q4 * 4:(q4 + 1) * 4, :], PB4[b_][:, :, :])
                for t in range(NT):
                    act(sd4[:, t, :], PB[QB[t]][:, :], AF.Ln, bias=EPS)
                for t in range(NT):
                    act(sd4[:, t, :], sd4[:, t, :], AF.Exp, scale=-0.5)
                for t in range(NT):
                    tl = slice(t * TT, (t + 1) * TT)
                    for h_ in range(2):
                        hp = slice(h_ * 64, (h_ + 1) * 64)
                        stt(QTh[h_][hp, tl], rawQ[hp, t, :], GCOLS[hp, G_QS:G_QS + 1], sd4[hp, t, :], ALU.mult, ALU.mult)
                heads = (slice(0, 64), slice(64, 128))

                def cols(qt, kb):
                    c0 = max(0, kb - 4 * qt) * 128
                    return c0, slice(c0, TT), slice(qt * TT + c0, (qt + 1) * TT)

                def p1_steps(h, qt):
                    nkb = 4 * qt + 4
                    L = LS[h]
                    zb = {}

                    def Z(kb):
                        c0, cs, qs = cols(qt, kb)
                        zb[kb] = zrot.next()
                        mm(PB[zb[kb]][:, cs], [(KT[:, kb * 128:(kb + 1) * 128], QTh[h][:, qs])])

                    def EXP(kb):
                        c0, cs, qs = cols(qt, kb)
                        act(e1[kb % 2][:, cs], PB[zb[kb]][:, cs], AF.Exp)

                    def LN(kb):
                        c0, cs, qs = cols(qt, kb)
                        act(L[:, kb, cs], e1[kb % 2][:, cs], AF.Ln, bias=1.0)
                        if kb >= 4 * qt:
                            tt("pool", L[:, kb, c0:c0 + 128], L[:, kb, c0:c0 + 128], MASK01(), ALU.mult)

                    def CS(kb):
                        c0, cs, qs = cols(qt, kb)
                        mm(PB[CSB][:, cs], [(ESEL(kb), L[:, kb, cs])], start=(kb == 0), stop=(kb == nkb - 1))

                    steps = []
                    for i in range(nkb + 3):
                        def st(i=i):
                            if i < nkb:
                                Z(i)
                            if 0 <= i - 1 < nkb:
                                EXP(i - 1)
                            if 0 <= i - 2 < nkb:
                                LN(i - 2)
                            if 0 <= i - 3 < nkb:
                                CS(i - 3)
                        steps.append(st)

                    def fin():
                        cp("dve", csb[h][0:16, :], PB[CSB][0:16, :])
                    steps.append(fin)
                    return steps

                def p2_steps(h, qt):
                    nkb = 4 * qt + 4
                    L = LS[h]
                    pr = heads[h]
                    gb = {}

                    def G(kb):
                        c0, cs, qs = cols(qt, kb)
                        b = gb[kb] = grot.next()

                        def fn(e):
                            e.matmul(PB[b][:, cs].ap, SSEL[:, kb, :].ap, csb[h][:, cs].ap, start=True, stop=False)
                            e.matmul(PB[b][:, cs].ap, TRINEG().ap, L[:, kb, cs].ap, start=False, stop=False)
                            return e.matmul(PB[b][:, cs].ap, KT[:, kb * 128:(kb + 1) * 128].ap, QTh[h][:, qs].ap,
                                            start=False, stop=True)
                        S.op("pe", fn, reads=[TRINEG(), L[:, kb, cs], SSEL[:, kb, :], csb[h][:, cs],
                                              KT[:, kb * 128:(kb + 1) * 128], QTh[h][:, qs]], writes=[PB[b][:, cs]])

                    def EXPA(kb):
                        c0, cs, qs = cols(qt, kb)
                        a_ = AT[kb % 3]
                        act(a_[:, cs], PB[gb[kb]][:, cs], AF.Exp)
                        if kb >= 4 * qt:
                            tt("pool", a_[:, c0:c0 + 128], a_[:, c0:c0 + 128], MASK01(), ALU.mult)

                    def AV(kb):
                        c0, cs, qs = cols(qt, kb)
                        mm(PB[OB[h]][:, cs], [(VJ[:, kb, :], AT[kb % 3][:, cs])], start=(kb == 0), stop=(kb == nkb - 1))

                    steps = []
                    for i in range(nkb + 1):
                        def st(i=i):
                            if i < nkb:
                                G(i)
                            if 0 <= i - 1 < nkb:
                                EXPA(i - 1)
                                AV(i - 1)
                        steps.append(st)

                    def fin():
                        cp("dve", OT[pr, qt * TT:(qt + 1) * TT], PB[OB[h]][pr, :])
                    steps.append(fin)
                    return steps

                pending = []

                def filler(k):
                    for _ in range(k):
                        if not pending:
                            return
                        qt_, dch = pending.pop(0)
                        tl = slice(qt_ * TT, (qt_ + 1) * TT)
                        b = frot.next()
                        mm(PB[b][:, :], [(wo[:, dch * 128:(dch + 1) * 128], OT[:, tl])])
                        tt("dve", XT[:, dch, tl], PB[b][:, :], XT[:, dch, tl], ALU.add)

                units = [(h, qt) for qt in range(NT) for h in range(2)]
                prev = None
                for u in units + [None]:
                    s1 = p1_steps(*u) if u is not None else []
                    s2 = p2_steps(*prev) if prev is not None else []
                    n = max(len(s1), len(s2))
                    for i in range(n):
                        if i < len(s2):
                            s2[i]()
                        if i < len(s1):
                            s1[i]()
                        if i == n // 2 or i == n - 1:
                            filler(2)
                    if prev is not None and prev[0] == 1:
                        pending.extend((prev[1], dch) for dch in range(8))
                    prev = u
                filler(len(pending))
            w_done(12)
            w_done(13)
            w_done(14)

        if stop_after >= 5:
            w_issue_upto(18)
            mlp_phase(1, 15, G_MLP1)
        if stop_after >= 6:
            ple_phase(1, 23, G_PLE1, with_out=True)

        if not out_fin:
            sreset()
            osb = [salloc((1024,), F32) for _ in range(2)]
            for t in range(NT):
                emit_out(t, osb)
        S.emit(final_waits=out_fin[-2:])
    return nc


def make_consts():
    ident = np.eye(128, dtype=np.float32)
    s = np.arange(128)[:, None]
    t = np.arange(128)[None, :]
    maskle = (s <= t).astype(np.float32)
    pack = np.zeros((128, 784), np.float32)
    pack[:, 0:128] = 1.0 / 1024
    pack[:, 128:256] = 1.0
    bd = np.zeros((128, 128), np.float32)
    bd[0:64, 0:64] = 1.0 / 64
    bd[64:128, 64:128] = 1.0 / 64
    pack[:, 256:384] = bd
    pack[:, 384:512] = -(s >= t).astype(np.float32)
    pack[:, 512:640] = (s < t).astype(np.float32)
    pack[:, 640 + 15] = -1.0
    pack[:, 640 + 47] = -1.0
    ssel = np.zeros((128, 16, 128), np.float32)
    for kb in range(16):
        for r in range(16):
            if r > kb:
                ssel[r, kb, :] = 1.0
                ssel[32 + r, kb, :] = 1.0
    return {"c_ident": ident, "c_maskle": maskle, "c_pack": pack, "c_ssel": ssel.reshape(128, 2048)}


_CACHE = {}


def kernel(**inputs):
    n = 8
    if "nc" not in _CACHE:
        nc = bass.Bass("TRN2", target_bir_lowering=False)
        build_program(nc)
        _CACHE["nc"] = nc
    nc = _CACHE["nc"]
    consts = make_consts()
    shared = {k: np.ascontiguousarray(np.asarray(v, dtype=np.float32)) for k, v in inputs.items() if k not in ("x", "p")}
    shared.update(consts)
    x = np.asarray(inputs["x"], dtype=np.float32)
    p = np.asarray(inputs["p"], dtype=np.float32)
    in_maps = []
    for i in range(n):
        m = dict(shared)
        m["x"] = np.ascontiguousarray(x[i])
        m["p"] = np.ascontiguousarray(p[:, i])
        in_maps.append(m)
    res = run_bass_kernel_spmd(nc, in_maps, core_ids=list(range(n)))
    return np.stack([np.asarray(r["y"], dtype=np.float32) for r in res.results], axis=0)
```

```python
import contextlib
import numpy as np
import concourse.bass as bass
import concourse.mybir as mybir
from concourse.bass_utils import run_bass_kernel_spmd

F32 = mybir.dt.float32
BF16 = mybir.dt.bfloat16
AF = mybir.ActivationFunctionType
ALU = mybir.AluOpType

SEQ = 2048
D = 1024
NT = 4
TT = 512
NC8 = 8
EPS = 1e-6
GR = 256
ENGS = ("pe", "act", "dve", "pool", "sp")

G_MIXA, G_MLP0, G_PLE0, G_KV, G_MIXB, G_MLP1, G_PLE1, G_K, G_Q, G_QS = 0, 8, 16, 24, 32, 40, 48, 56, 57, 58


class Op:
    __slots__ = ("eng", "fn", "deps", "signal", "idx", "dma", "dsem", "dval", "count", "waits")

    def __init__(self, eng, fn, dma):
        self.eng = eng
        self.fn = fn
        self.dma = dma
        self.signal = False
        self.count = 0
        self.dsem = 0
        self.dval = 0
        self.deps = None
        self.waits = None


class Ref:
    __slots__ = ("ap", "keys")

    def __init__(self, ap, keys):
        self.ap = ap
        self.keys = keys


class Sched:
    def __init__(self, nc):
        self.nc = nc
        self.ops = []
        self.lastw = {}
        self.rd = {}
        self.streams = {}

    def op(self, eng, fn, reads=(), writes=(), dma=None):
        o = Op(eng, fn, dma)
        o.idx = len(self.ops)
        deps = {}

        def add(d):
            k = ("d", d.dma) if d.dma is not None else d.eng
            p = deps.get(k)
            if p is None or p.idx < d.idx:
                deps[k] = d

        lastw = self.lastw
        rd = self.rd
        for r in reads:
            for k in r.keys:
                w = lastw.get(k)
                if w is not None:
                    add(w)
        for r in writes:
            for k in r.keys:
                w = lastw.get(k)
                if w is not None:
                    add(w)
                rr = rd.get(k)
                if rr:
                    for x in rr.values():
                        add(x)
        if dma is not None:
            p = self.streams.get(dma)
            if p is not None:
                add(p)
            self.streams[dma] = o
        mykey = ("d", dma) if dma is not None else eng
        if eng == "pe" and dma is None:
            deps.pop("pe", None)
        o.deps = deps
        for r in reads:
            for k in r.keys:
                d = rd.get(k)
                if d is None:
                    rd[k] = {mykey: o}
                else:
                    d[mykey] = o
        for r in writes:
            for k in r.keys:
                lastw[k] = o
                rd[k] = {}
        self.ops.append(o)
        return o

    def emit(self, final_waits=()):
        nc = self.nc
        by_eng = {e: [o for o in self.ops if o.eng == e] for e in ENGS}
        for e in ENGS:
            waited = {}
            for o in by_eng[e]:
                w = []
                for k, d in o.deps.items():
                    if waited.get(k, -1) >= d.idx:
                        continue
                    waited[k] = d.idx
                    d.signal = True
                    w.append(d)
                o.waits = w
        for o in final_waits:
            o.signal = True
        cnt = {e: 0 for e in ENGS}
        streams = {}
        for o in self.ops:
            if o.dma is not None:
                st = streams.get(o.dma)
                if st is None:
                    st = streams[o.dma] = [len(streams), 0]
                st[1] += 16
                o.dsem = st[0]
                o.dval = st[1]
            elif o.signal:
                cnt[o.eng] += 1
                o.count = cnt[o.eng]
        with contextlib.ExitStack() as es:
            esem = {e: es.enter_context(nc.semaphore("s_" + e)) for e in ENGS}
            dsem = [es.enter_context(nc.semaphore("d%d" % i)) for i in range(len(streams))]
            block = es.enter_context(nc.Block())

            def run(ename, eng):
                for o in by_eng[ename]:
                    for d in o.waits:
                        if d.dma is not None:
                            eng.wait_ge(dsem[d.dsem], d.dval)
                        else:
                            eng.wait_ge(esem[d.eng], d.count)
                    ins = o.fn(eng)
                    if o.dma is not None:
                        ins.then_inc(dsem[o.dsem], 16)
                    elif o.signal:
                        ins.then_inc(esem[ename], 1)
                if ename == "sp":
                    for o in final_waits:
                        if o.dma is not None:
                            eng.wait_ge(dsem[o.dsem], o.dval)
                        else:
                            eng.wait_ge(esem[o.eng], o.count)

            @block.tensor
            def _(e):
                run("pe", e)

            @block.scalar
            def _(e):
                run("act", e)

            @block.vector
            def _(e):
                run("dve", e)

            @block.gpsimd
            def _(e):
                run("pool", e)

            @block.sync
            def _(e):
                run("sp", e)


class Tile:
    def __init__(self, base_ap, off, shape, dt, parts=128, space="S", bank=0):
        self.off = off
        self.shape = tuple(shape)
        self.es = 4 if dt == F32 else 2
        self.parts = parts
        self.space = space
        self.bank = bank
        n = 1
        for s in shape:
            n *= s
        self.n = n
        if space == "S":
            v = base_ap[0:parts, off // 2: off // 2 + n * self.es // 2]
            if dt == F32:
                v = v.bitcast(F32)
        else:
            v = base_ap[0:parts, 0:n]
        if len(shape) == 2:
            v = v.rearrange("p (a b) -> p a b", a=shape[0])
        elif len(shape) == 3:
            v = v.rearrange("p (a b c) -> p a b c", a=shape[0], b=shape[1])
        self.ap = v

    def __getitem__(self, idx):
        if not isinstance(idx, tuple):
            idx = (idx,)
        ps = idx[0]
        fidx = list(idx[1:])
        while len(fidx) < len(self.shape):
            fidx.append(slice(None))
        ap = self.ap[(ps,) + tuple(fidx)]
        rng = []
        for i, s in zip(fidx, self.shape):
            if isinstance(i, int):
                rng.append((i, i + 1))
            else:
                a = 0 if i.start is None else i.start
                b = s if i.stop is None else i.stop
                rng.append((a, b))
        strides = []
        acc = 1
        for s in reversed(self.shape):
            strides.append(acc)
            acc *= s
        strides = strides[::-1]
        keys = set()
        outer = [()]
        for (a, b) in rng[:-1]:
            outer = [o + (i,) for o in outer for i in range(a, b)]
        la, lb = rng[-1]
        gr = GR if self.space == "S" else 512
        for o in outer:
            base = sum(i * st for i, st in zip(o, strides[:-1]))
            b0 = self.off + (base + la) * self.es
            b1 = self.off + (base + lb) * self.es
            for g in range(b0 // gr, (b1 - 1) // gr + 1):
                keys.add((self.space, self.bank, g))
        return Ref(ap, keys)


def build_program(nc, stop_after=99):
    dt_in = {}

    def din(name, shape):
        t = nc.dram_tensor(name, list(shape), F32, kind="ExternalInput").ap()
        dt_in[name] = t
        return t

    x = din("x", [SEQ, D])
    p = din("p", [2, SEQ, 256])
    ln_mix_a = din("ln_mix_a", [1, D])
    w_in_a = din("w_in_a", [1, D, 2 * D])
    g_v_a = din("g_v_a", [1, D])
    w_spatial = din("w_spatial", [1, 8, 128, 128])
    b_spatial = din("b_spatial", [1, 8, 128])
    w_out_a = din("w_out_a", [1, D, D])
    ln_kv = din("ln_kv", [D])
    w_kv = din("w_kv", [D, 2 * D])
    g_k = din("g_k", [64])
    ln_mix_b = din("ln_mix_b", [1, D])
    w_q = din("w_q", [1, D, D])
    g_q = din("g_q", [1, 64])
    w_out_b = din("w_out_b", [1, D, D])
    ln_mlp = din("ln_mlp", [2, D])
    w_up = din("w_up", [2, D, 4 * D])
    w_down = din("w_down", [2, 4 * D, D])
    ln_ple = din("ln_ple", [2, D])
    w_ple_gate = din("w_ple_gate", [2, D, D])
    w_ple_proj = din("w_ple_proj", [2, 256, D])
    c_ident = din("c_ident", [128, 128])
    c_maskle = din("c_maskle", [128, 128])
    c_pack = din("c_pack", [128, 784])
    c_ssel = din("c_ssel", [128, 2048])
    y = nc.dram_tensor("y", [SEQ, D], F32, kind="ExternalOutput").ap()

    es = contextlib.ExitStack()
    with es:
        ARENA_BYTES = 212736
        arena = es.enter_context(nc.sbuf_tensor("arena", [128, ARENA_BYTES // 2], BF16))
        banks = [es.enter_context(nc.psum_tensor("ps%d" % i, [128, 512], F32)) for i in range(8)]
        PB = [Tile(banks[i], 0, (512,), F32, space="P", bank=i) for i in range(8)]
        PB4 = [Tile(banks[i], 0, (4, 128), F32, space="P", bank=i) for i in range(8)]
        S = Sched(nc)

        def T(off, shape, dt, parts=128):
            return Tile(arena, off, shape, dt, parts=parts)

        XT = T(0, (8, SEQ), F32)
        HT = T(65536, (8, SEQ), BF16)
        WOFF = [98304 + i * 16384 for i in range(4)]
        SCR0 = 163840
        coff = ARENA_BYTES
        def calloc(nbytes):
            nonlocal coff
            coff -= (nbytes + 255) // 256 * 256
            return coff
        IDENT = T(calloc(512), (128,), F32)
        MASKLE = T(calloc(512), (128,), F32)
        CP = T(calloc(1568), (784,), BF16)
        SSEL = T(calloc(4096), (16, 128), BF16)
        GCOLS = T(calloc(256), (64,), F32)
        BSROW = T(calloc(2048), (1024,), BF16, parts=1)
        SMALL = T(calloc(256), (64,), F32)
        WP = T(calloc(4096), (2, 1024), BF16)
        SCR_END = coff
        ONES_S = lambda: CP[:, 0:128]
        ONES1 = lambda ps=slice(0, 128): CP[ps, 128:256]
        BDIAG = lambda: CP[:, 256:384]
        TRINEG = lambda: CP[:, 384:512]
        MASK01 = lambda: CP[:, 512:640]
        ESEL = lambda kb: CP[:, 640 + 15 - kb: 640 + 15 - kb + 128]

        scr = [SCR0]

        def salloc(shape, dt, parts=128):
            n = 1
            for s in shape:
                n *= s
            nb = n * (4 if dt == F32 else 2)
            nb = (nb + 255) // 256 * 256
            off = scr[0]
            scr[0] += nb
            assert scr[0] <= SCR_END, ("scratch overflow", scr[0], SCR_END)
            return T(off, shape, dt, parts=parts)

        def sreset():
            scr[0] = SCR0

        def act(out, in_, func, bias=None, scale=None, accum=None, extra=()):
            kw = {}
            if bias is not None:
                kw["bias"] = bias.ap if isinstance(bias, Ref) else bias
            if scale is not None:
                kw["scale"] = scale.ap if isinstance(scale, Ref) else scale
            if accum is not None:
                kw["accum_out"] = accum.ap
            rds = [in_] + [r for r in (bias, scale) if isinstance(r, Ref)] + list(extra)
            wr = [out] + ([accum] if accum is not None else [])
            return S.op("act", lambda e: e.activation(out.ap, in_.ap, func, **kw), reads=rds, writes=wr)

        def tt(eng, out, a, b, op):
            return S.op(eng, lambda e: e.tensor_tensor(out.ap, a.ap, b.ap, op), reads=[a, b], writes=[out])

        def stt(out, a, sc, b, op0, op1):
            scv = sc.ap if isinstance(sc, Ref) else sc
            rds = [a, b] + ([sc] if isinstance(sc, Ref) else [])
            return S.op("dve", lambda e: e.scalar_tensor_tensor(out.ap, a.ap, scv, b.ap, op0, op1), reads=rds, writes=[out])

        def ts(eng, out, a, s1, s2, op0, op1=None):
            s1v = s1.ap if isinstance(s1, Ref) else s1
            s2v = s2.ap if isinstance(s2, Ref) else s2
            rds = [a] + [r for r in (s1, s2) if isinstance(r, Ref)]
            if op1 is None:
                return S.op(eng, lambda e: e.tensor_scalar(out.ap, a.ap, s1v, None, op0), reads=rds, writes=[out])
            return S.op(eng, lambda e: e.tensor_scalar(out.ap, a.ap, s1v, s2v, op0, op1), reads=rds, writes=[out])

        def cp(eng, out, in_):
            if eng == "act":
                return act(out, in_, AF.Copy)
            return S.op(eng, lambda e: e.tensor_copy(out.ap, in_.ap), reads=[in_], writes=[out])

        def recip(out, in_):
            return S.op("dve", lambda e: e.reciprocal(out.ap, in_.ap), reads=[in_], writes=[out])

        def mm(out, pairs, start=True, stop=True):
            n = len(pairs)

            def fn(e):
                ins = None
                for i, (l, r) in enumerate(pairs):
                    ins = e.matmul(out.ap, l.ap, r.ap, start=(start and i == 0), stop=(stop and i == n - 1))
                return ins
            rds = []
            for l, r in pairs:
                rds.append(l)
                rds.append(r)
            return S.op("pe", fn, reads=rds, writes=[out])

        def tr(out, in_, ident):
            return S.op("pe", lambda e: e.transpose(out.ap, in_.ap, ident.ap), reads=[in_, ident], writes=[out])

        def dma(eng, out, in_ap, stream, reads=()):
            return S.op(eng, lambda e: e.dma_start(out=out.ap, in_=in_ap), reads=list(reads), writes=[out], dma=stream)

        class Rot:
            def __init__(self, ids):
                self.ids = list(ids)
                self.i = 0

            def next(self):
                b = self.ids[self.i % len(self.ids)]
                self.i += 1
                return b

        rot = Rot(range(8))
        tog = [0]

        def evac_eng():
            tog[0] ^= 1
            return "act" if tog[0] else "dve"

        def wslot(n, shape):
            return T(WOFF[n % 4], shape, BF16)

        def kp(ap2d):
            return ap2d.rearrange("(k p) f -> p k f", p=128)

        loaders = []

        def L_full(src2d, c0, c1):
            def f(n):
                w = wslot(n, (8, c1 - c0))
                dma("pool", w[:, :, :], kp(src2d)[:, :, c0:c1], "W%da" % (n % 4))
            return f

        def L_mlp(l, gi):
            def f(n):
                wu = Tile(arena, WOFF[n % 4], (8, 512), BF16)
                wd = Tile(arena, WOFF[n % 4] + 8192, (4, 1024), BF16)
                dma("pool", wu[:, :, :], kp(w_up[l])[:, :, gi * 512:(gi + 1) * 512], "W%da" % (n % 4))
                dma("pool", wd[:, :, :], kp(w_down[l][gi * 512:(gi + 1) * 512, :]), "W%db" % (n % 4))
            return f

        def L_first(n):
            w = wslot(n, (8, 1024))
            for i, sfx in enumerate("abcd"):
                dma("pool", w[:, :, i * 256:(i + 1) * 256], kp(w_in_a[0])[:, :, i * 256:(i + 1) * 256], "W0" + sfx)
        loaders.append(L_first)
        loaders.append(L_full(w_in_a[0], 1024, 2048))
        loaders.append(L_full(w_out_a[0], 0, 1024))
        for gi in range(8):
            loaders.append(L_mlp(0, gi))
        loaders.append(L_full(w_ple_gate[0], 0, 1024))
        loaders += [None, None, None]
        for gi in range(8):
            loaders.append(L_mlp(1, gi))
        loaders.append(L_full(w_ple_gate[1], 0, 1024))
        issued = [0]

        def w_issue_upto(n):
            while issued[0] <= n and issued[0] < len(loaders):
                f = loaders[issued[0]]
                if f is not None:
                    f(issued[0])
                issued[0] += 1

        def w_done(n):
            w_issue_upto(n + 4)

        sreset()
        dma("sp", IDENT[:, :], c_ident, "c0")
        GST_A = T(97280, (128,), F32, parts=8)
        GST_B = T(97792, (128,), F32, parts=64)
        dma("sp", GST_A[0:8, :], ln_mix_a[0].rearrange("(k f) -> k f", f=128), "g0")
        dma("pool", CP[:, :], c_pack, "c2")
        dma("pool", BSROW[0:1, :], b_spatial[0].rearrange("(o g) t -> o (g t)", o=1), "c4")
        w_issue_upto(2)
        dma("pool", SSEL[:, :, :], c_ssel.rearrange("p (a b) -> p a b", a=16), "c3")
        dma("pool", WP[:, :, :], kp(w_ple_proj[0]), "wp")
        gi_ = 0
        for src in (ln_mlp[0], ln_ple[0], ln_kv, ln_mix_b[0], ln_mlp[1], ln_ple[1]):
            dma("act", GST_B[gi_ * 8:(gi_ + 1) * 8, :], src.rearrange("(k f) -> k f", f=128), "g%d" % (gi_ + 1))
            gi_ += 1
        gk2 = g_k.rearrange("(o f) -> o f", o=1)
        dma("act", GST_B[48:49, 0:64], gk2, "g7")
        dma("act", GST_B[48:49, 64:128], gk2, "g8")
        dma("act", GST_B[49:50, 0:64], g_q, "g9")
        dma("act", GST_B[49:50, 64:128], g_q, "g10")
        b = rot.next()
        tr(PB[b][:, 0:8], GST_A[0:8, :], IDENT[0:8, 0:8])
        cp("dve", GCOLS[:, 0:8], PB[b][:, 0:8])

        def gcols_rest():
            b = rot.next()
            tr(PB[b][:, 0:50], GST_B[0:50, :], IDENT[0:50, 0:50])
            cp("dve", GCOLS[:, 8:58], PB[b][:, 0:50])
            ts("dve", GCOLS[:, 58:59], GCOLS[:, 57:58], 0.125, None, ALU.mult)

        xs = [T(WOFF[3] + i * 4096, (1024,), F32) for i in range(4)]

        def xload(t):
            for blk in range(t * 4, t * 4 + 4):
                st = xs[blk % 4]
                dma("sp", st[:, :], x[blk * 128:(blk + 1) * 128, :], "x%d" % (blk % 4))
                for half in range(2):
                    b = rot.next()
                    for cc in range(4):
                        c = half * 4 + cc
                        tr(PB4[b][:, cc, :], st[:, c * 128:(c + 1) * 128], IDENT[:, :])
                    cp(evac_eng(), XT[:, half * 4:(half + 1) * 4, blk * 128:(blk + 1) * 128], PB4[b][:, :, :])

        xload(0)
        dma("sp", MASKLE[:, :], c_maskle, "c1")
        if stop_after < 1:
            for t in range(1, NT):
                xload(t)
            gcols_rest()

        def norm_A(t, sq):
            tl = slice(t * TT, (t + 1) * TT)
            for c in range(8):
                act(sq[:, c, :], XT[:, c, tl], AF.Square)

        def norm_B(t, gcol0, sq, sd, rstd, dst=HT):
            tl = slice(t * TT, (t + 1) * TT)
            b = rot.next()
            mm(PB[b][:, :], [(ONES_S(), sq[:, c, :]) for c in range(8)])
            act(sd[:, :], PB[b][:, :], AF.Ln, bias=EPS)
            act(rstd[:, :], sd[:, :], AF.Exp, scale=-0.5)
            for c in range(8):
                if gcol0 is None:
                    tt("dve", dst[:, c, tl], XT[:, c, tl], rstd[:, :], ALU.mult)
                else:
                    stt(dst[:, c, tl], XT[:, c, tl], GCOLS[:, gcol0 + c:gcol0 + c + 1], rstd[:, :], ALU.mult, ALU.mult)

        def norm_tile(t, gcol0, sq, sd, rstd, dst=HT):
            norm_A(t, sq)
            norm_B(t, gcol0, sq, sd, rstd, dst)

        if stop_after >= 1:
            sreset()
            uT = salloc((8, TT), BF16)
            sq = uT
            yT = salloc((8, TT), BF16)
            v16 = [salloc((1024,), BF16) for _ in range(2)]
            vn = [salloc((1024,), BF16) for _ in range(2)]
            sd = salloc((TT,), F32)
            rstd = salloc((TT,), F32)
            wsT = salloc((8, 128), BF16)
            gvt = salloc((1024,), F32)
            wsl = T(yT.off, (8, 128), F32)
            dma("sp", gvt[:, :], g_v_a.partition_broadcast(128), "gv")
            dma("sp", wsl[:, :, :], w_spatial[0].rearrange("g t s -> t g s"), "ws")
            def ws_prep():
                for hb in range(2):
                    b = rot.next()
                    for gg in range(4):
                        tr(PB4[b][:, gg, :], wsl[:, hb * 4 + gg, :], IDENT[:, :])
                    tt("dve", wsT[:, hb * 4:(hb + 1) * 4, :], PB4[b][:, :, :],
                       Ref(MASKLE.ap.unsqueeze(1).to_broadcast([128, 4, 128]), MASKLE[:, :].keys), ALU.mult)
            W0 = wslot(0, (8, 1024))
            W1 = wslot(1, (8, 1024))
            W2 = wslot(2, (8, 1024))
            norm_A(0, sq)
            norm_B(0, G_MIXA, sq, sd, rstd)
            for t in range(NT):
                tl = slice(t * TT, (t + 1) * TT)
                if t + 1 < NT:
                    xload(t + 1)
                    if t + 1 == NT - 1:
                        w_issue_upto(3)
                for f in range(8):
                    b = rot.next()
                    mm(PB[b][:, :], [(W0[:, k, f * 128:(f + 1) * 128], HT[:, k, tl]) for k in range(8)])
                    act(uT[:, f, :], PB[b][:, :], AF.Gelu_apprx_tanh)
                if t == 0:
                    gcols_rest()
                    ws_prep()

                def VST(blk, t=t):
                    tb = t * 4 + blk
                    tbl = slice(tb * 128, (tb + 1) * 128)
                    for half in range(2):
                        b = rot.next()
                        mm(PB[b][:, :], [(HT[:, k, tbl], W1[:, k, half * 512:(half + 1) * 512]) for k in range(8)])
                        act(v16[blk % 2][:, half * 512:(half + 1) * 512], PB[b][:, :], AF.Gelu_apprx_tanh)

                def NST(blk):
                    vb = vn[blk % 2]
                    vv = v16[blk % 2]
                    c4 = (blk % 2) * 4
                    act(vb[:, :], vv[:, :], AF.Square, accum=SMALL[:, c4:c4 + 1])
                    ts("dve", SMALL[:, c4 + 1:c4 + 2], SMALL[:, c4:c4 + 1], 1.0 / 1024, EPS, ALU.mult, ALU.add)
                    act(SMALL[:, c4 + 2:c4 + 3], SMALL[:, c4 + 1:c4 + 2], AF.Sqrt)
                    recip(SMALL[:, c4 + 3:c4 + 4], SMALL[:, c4 + 2:c4 + 3])
                    stt(vb[:, :], vv[:, :], SMALL[:, c4 + 3:c4 + 4], gvt[:, :], ALU.mult, ALU.mult)

                def SPST(blk):
                    vb = vn[blk % 2]
                    for hb in range(2):
                        b = rot.next()

                        def fn(e, b=b, hb=hb, vb=vb):
                            ins = None
                            for gg in range(4):
                                g = hb * 4 + gg
                                e.matmul(PB4[b][:, gg, :].ap, vb[:, g * 128:(g + 1) * 128].ap, wsT[:, g, :].ap,
                                         start=True, stop=False)
                                ins = e.matmul(PB4[b][:, gg, :].ap, ONES1(slice(0, 1)).ap,
                                               BSROW[0:1, g * 128:(g + 1) * 128].ap, start=False, stop=True)
                            return ins
                        S.op("pe", fn, reads=[vb[:, hb * 512:(hb + 1) * 512], wsT[:, hb * 4:(hb + 1) * 4, :],
                                              ONES1(slice(0, 1)), BSROW[0:1, :]], writes=[PB4[b][:, :, :]])
                        tt("dve", yT[:, hb * 4:(hb + 1) * 4, blk * 128:(blk + 1) * 128], PB4[b][:, :, :],
                           uT[:, hb * 4:(hb + 1) * 4, blk * 128:(blk + 1) * 128], ALU.mult)

                VST(0)
                VST(1)
                NST(0)
                for blk in range(4):
                    if blk + 2 < 4:
                        VST(blk + 2)
                    if blk + 1 < 4:
                        NST(blk + 1)
                    SPST(blk)
                if t + 1 < NT:
                    norm_A(t + 1, sq)
                for dch in range(8):
                    if dch == 4 and t + 1 < NT:
                        norm_B(t + 1, G_MIXA, sq, sd, rstd)
                    b = rot.next()
                    mm(PB[b][:, :], [(W2[:, k, dch * 128:(dch + 1) * 128], yT[:, k, :]) for k in range(8)])
                    tt("dve", XT[:, dch, tl], PB[b][:, :], XT[:, dch, tl], ALU.add)
            w_done(0)
            w_done(1)
            w_done(2)

        def mlp_phase(l, blk0, gcol0):
            sreset()
            sq = salloc((8, TT), BF16)
            sd = salloc((TT,), F32)
            rstd = salloc((TT,), F32)
            aT = [salloc((8, TT), BF16) for _ in range(2)]
            rt = [salloc((TT,), BF16) for _ in range(2)]
            norm_tile(0, gcol0, sq, sd, rstd)

            def wts(sg):
                nA, nB = blk0 + 2 * sg, blk0 + 2 * sg + 1
                wu = [Tile(arena, WOFF[n % 4], (8, 512), BF16) for n in (nA, nB)]
                wd = [Tile(arena, WOFF[n % 4] + 8192, (4, 1024), BF16) for n in (nA, nB)]
                return wu, wd

            def up(k):
                sg, t = divmod(k, NT)
                wu, wd = wts(sg)
                tl = slice(t * TT, (t + 1) * TT)
                a = aT[k % 2]
                for f in range(8):
                    b = rot.next()
                    mm(PB[b][:, :], [(wu[f // 4][:, kk, (f % 4) * 128:(f % 4 + 1) * 128], HT[:, kk, tl]) for kk in range(8)])
                    r = rt[f % 2]
                    act(r[:, :], PB[b][:, :], AF.Relu)
                    tt("dve", a[:, f, :], r[:, :], r[:, :], ALU.mult)
                if sg == 0 and t + 1 < NT:
                    norm_A(t + 1, sq)

            def down(k):
                sg, t = divmod(k, NT)
                wu, wd = wts(sg)
                tl = slice(t * TT, (t + 1) * TT)
                a = aT[k % 2]
                for dch in range(8):
                    if dch == 4 and sg == 0 and t + 1 < NT:
                        norm_B(t + 1, gcol0, sq, sd, rstd)
                    b = rot.next()
                    mm(PB[b][:, :], [(wd[f // 4][:, f % 4, dch * 128:(dch + 1) * 128], a[:, f, :]) for f in range(8)])
                    tt("dve", XT[:, dch, tl], PB[b][:, :], XT[:, dch, tl], ALU.add)
                if t == NT - 1:
                    w_done(blk0 + 2 * sg)
                    w_done(blk0 + 2 * sg + 1)

            NI = 4 * NT
            for k in range(NT):
                up(k)
                down(k)
            for k in range(NT, NI):
                up(k)
                if k > NT:
                    down(k - 1)
            down(NI - 1)

        out_fin = []

        def emit_out(t, osb):
            for blk in range(t * 4, t * 4 + 4):
                ot = osb[blk % 2]
                for half in range(2):
                    b = rot.next()
                    for cc in range(4):
                        c = half * 4 + cc
                        tr(PB4[b][:, cc, :], XT[:, c, blk * 128:(blk + 1) * 128], IDENT[:, :])
                    cp(evac_eng(), ot[:, half * 512:(half + 1) * 512], PB[b][:, :])
                out_fin.append(S.op("sp", lambda e, ot=ot, blk=blk: e.dma_start(out=y[blk * 128:(blk + 1) * 128, :], in_=ot[:, :].ap),
                                    reads=[ot[:, :]], dma="o%d" % (blk % 2)))

        def ple_phase(l, blkn, gcol0, with_out=False):
            sreset()
            sq = salloc((8, TT), BF16)
            sd = salloc((TT,), F32)
            rstd = salloc((TT,), F32)
            pst = salloc((4, 256), F32)
            pT = salloc((2, TT), BF16)
            gate = [salloc((TT,), F32) for _ in range(2)]
            tmp2 = [salloc((TT,), F32) for _ in range(2)]
            osb = [salloc((1024,), F32) for _ in range(2)] if with_out else None
            Wg = wslot(blkn, (8, 1024))
            def pload(t):
                dma("sp", pst[:, :, :], p[l][t * TT:(t + 1) * TT, :].rearrange("(b q) f -> q b f", q=128), "pl")
            pload(0)
            for t in range(NT):
                tl = slice(t * TT, (t + 1) * TT)
                if t == 0:
                    norm_tile(0, gcol0, sq, sd, rstd)
                for kk in range(2):
                    b = rot.next()
                    for blk in range(4):
                        tr(PB4[b][:, blk, :], pst[:, blk, kk * 128:(kk + 1) * 128], IDENT[:, :])
                    cp(evac_eng(), pT[:, kk, :], PB[b][:, :])
                if t + 1 < NT:
                    pload(t + 1)
                if t + 1 < NT:
                    norm_A(t + 1, sq)
                for dch in range(8):
                    if dch == 4 and t + 1 < NT:
                        norm_B(t + 1, gcol0, sq, sd, rstd)
                    b1 = rot.next()
                    mm(PB[b1][:, :], [(Wg[:, k, dch * 128:(dch + 1) * 128], HT[:, k, tl]) for k in range(8)])
                    g_ = gate[dch % 2]
                    act(g_[:, :], PB[b1][:, :], AF.Sigmoid)
                    b2 = rot.next()
                    mm(PB[b2][:, :], [(WP[:, kk, dch * 128:(dch + 1) * 128], pT[:, kk, :]) for kk in range(2)])
                    t2 = tmp2[dch % 2]
                    tt("dve", t2[:, :], PB[b2][:, :], g_[:, :], ALU.mult)
                    tt("dve", XT[:, dch, tl], XT[:, dch, tl], t2[:, :], ALU.add)
                if with_out:
                    emit_out(t, osb)
            w_done(blkn)

        if stop_after >= 2:
            mlp_phase(0, 3, G_MLP0)
        if stop_after >= 3:
            ple_phase(0, 11, G_PLE0)
            dma("pool", WP[:, :, :], kp(w_ple_proj[1]), "wp")

        if stop_after >= 4:
            sreset()
            sq = salloc((8, TT), BF16)
            sd = salloc((TT,), F32)
            rstd = salloc((TT,), F32)
            for t in range(NT):
                norm_tile(t, None, sq, sd, rstd)
            sreset()
            KT = salloc((SEQ,), BF16)
            QTh = [salloc((SEQ,), BF16) for _ in range(2)]
            VJ = salloc((16, 128), BF16)
            OT = salloc((SEQ,), BF16)
            e1 = [salloc((TT,), F32) for _ in range(2)]
            AT = [salloc((TT,), BF16) for _ in range(3)]
            csb = [salloc((TT,), BF16) for _ in range(2)]
            PW = [Tile(arena, WOFF[0] + i * 8192, (4096,), BF16) for i in range(2)]
            LS = [Tile(arena, WOFF[1], (16, TT), BF16), Tile(arena, WOFF[2], (16, TT), BF16)]
            sqK = Tile(arena, WOFF[2], (4, TT), BF16)
            sqQ = Tile(arena, WOFF[2] + 4096, (4, TT), BF16)
            sd4 = Tile(arena, WOFF[2] + 8192, (4, TT), F32)
            rawK = Tile(arena, WOFF[1], (4, TT), F32)
            rawQ = Tile(arena, WOFF[1] + 8192, (4, TT), F32)
            srot = Rot(range(5))
            zrot = Rot([0, 1])
            grot = Rot([2, 3])
            frot = Rot([6, 7])
            JUNK = 7

            def dummy(k):
                def fn(e):
                    ins = None
                    for _ in range(k):
                        ins = e.matmul(PB[JUNK][:, :].ap, ONES_S().ap, CP[:, 0:512].ap, start=True, stop=True)
                    return ins
                S.op("pe", fn, reads=[ONES_S(), CP[:, 0:512]], writes=[PB[JUNK][:, :]])
            CSB = 4
            OB = [5, 5]
            for bi in range(2):
                S.op("dve", lambda e, bi=bi: e.memset(csb[bi][:, :].ap, 0.0), writes=[csb[bi][:, :]])
                S.op("dve", lambda e, bi=bi: e.memset(QTh[bi][:, :].ap, 0.0), writes=[QTh[bi][:, :]])

            def pw_views(i):
                t_ = PW[i]
                off = t_.off
                wk = Tile(arena, off, (8, 128), BF16)
                wv = Tile(arena, off + 2048, (8, 128), BF16)
                wq = Tile(arena, off + 4096, (8, 128), BF16)
                wo = Tile(arena, off + 6144, (1024,), BF16)
                return wk, wv, wq, wo

            def load_pair(j):
                wk, wv, wq, wo = pw_views(j % 2)
                sl = slice(j * 128, (j + 1) * 128)
                dma("pool", wk[:, :, :], kp(w_kv)[:, :, j * 128:(j + 1) * 128], "pw%da" % (j % 2))
                dma("pool", wv[:, :, :], kp(w_kv)[:, :, 1024 + j * 128:1024 + (j + 1) * 128], "pw%db" % (j % 2))
                dma("pool", wq[:, :, :], kp(w_q[0])[:, :, j * 128:(j + 1) * 128], "pw%dc" % (j % 2))
                dma("pool", wo[:, :], w_out_b[0][sl, :], "pw%dd" % (j % 2))
                for w_, g0 in ((wk, G_KV), (wv, G_KV), (wq, G_MIXB)):
                    gb = Ref(GCOLS.ap[:, g0:g0 + 8].unsqueeze(2).to_broadcast([128, 8, 128]), GCOLS[:, g0:g0 + 8].keys)
                    tt("dve", w_[:, :, :], w_[:, :, :], gb, ALU.mult)

            load_pair(0)
            pending = []
            for j in range(8):
                wk, wv, wq, wo = pw_views(j % 2)
                KB = [0, 1, 2, 3]
                QB = [4, 5, 6, 7]
                for (w_, raw, sqx, mb) in ((wk, rawK, sqK, KB), (wq, rawQ, sqQ, QB)):
                    for t in range(NT):
                        tl = slice(t * TT, (t + 1) * TT)
                        mm(PB[mb[t]][:, :], [(w_[:, k, :], HT[:, k, tl]) for k in range(8)])
                        act(raw[:, t, :], PB[mb[t]][:, :], AF.Copy)
                        act(sqx[:, t, :], PB[mb[t]][:, :], AF.Square)
                for (sqx, mb) in ((sqK, KB), (sqQ, QB)):
                    for t in range(NT):
                        mm(PB[mb[t]][:, :], [(BDIAG(), sqx[:, t, :])])
                for t in range(NT):
                    act(sd4[:, t, :], PB[KB[t]][:, :], AF.Ln, bias=EPS)
                for t in range(NT):
                    act(sd4[:, t, :], sd4[:, t, :], AF.Exp, scale=-0.5)
                for t in range(NT):
                    tl = slice(t * TT, (t + 1) * TT)
                    stt(KT[:, tl], rawK[:, t, :], GCOLS[:, G_K:G_K + 1], sd4[:, t, :], ALU.mult, ALU.mult)
                for q4 in range(4):
                    b_ = KB[q4]
                    for bb in range(4):
                        blk = q4 * 4 + bb
                        mm(PB4[b_][:, bb, :], [(HT[:, k, blk * 128:(blk + 1) * 128], wv[:, k, :]) for k in range(8)])
                    cp("dve", VJ[:, q4 * 4:(q4 + 1) * 4, :], PB4[b_][:, :, :])
                for t in range(NT):
                    act(sd4[:, t, :], PB[QB[t]][:, :], AF.Ln, bias=EPS)
                for t in range(NT):
                    act(sd4[:, t, :], sd4[:, t, :], AF.Exp, scale=-0.5)
                for t in range(NT):
                    tl = slice(t * TT, (t + 1) * TT)
                    for h_ in range(2):
                        hp = slice(h_ * 64, (h_ + 1) * 64)
                        stt(QTh[h_][hp, tl], rawQ[hp, t, :], GCOLS[hp, G_QS:G_QS + 1], sd4[hp, t, :], ALU.mult, ALU.mult)
                heads = (slice(0, 64), slice(64, 128))

                def cols(qt, kb):
                    c0 = max(0, kb - 4 * qt) * 128
                    return c0, slice(c0, TT), slice(qt * TT + c0, (qt + 1) * TT)

                def p1_steps(h, qt):
                    nkb = 4 * qt + 4
                    L = LS[h]
                    zb = {}

                    def Z(kb):
                        c0, cs, qs = cols(qt, kb)
                        zb[kb] = zrot.next()
                        mm(PB[zb[kb]][:, cs], [(KT[:, kb * 128:(kb + 1) * 128], QTh[h][:, qs])])

                    def EXP(kb):
                        c0, cs, qs = cols(qt, kb)
                        act(e1[kb % 2][:, cs], PB[zb[kb]][:, cs], AF.Exp)

                    def LN(kb):
                        c0, cs, qs = cols(qt, kb)
                        act(L[:, kb, cs], e1[kb % 2][:, cs], AF.Ln, bias=1.0)
                        if kb >= 4 * qt:
                            tt("pool", L[:, kb, c0:c0 + 128], L[:, kb, c0:c0 + 128], MASK01(), ALU.mult)

                    def CS(kb):
                        c0, cs, qs = cols(qt, kb)
                        mm(PB[CSB][:, cs], [(ESEL(kb), L[:, kb, cs])], start=(kb == 0), stop=(kb == nkb - 1))

                    steps = []
                    for i in range(nkb + 3):
                        def st(i=i):
                            if i < nkb:
                                Z(i)
                            if 0 <= i - 1 < nkb:
                                EXP(i - 1)
                            if 0 <= i - 2 < nkb:
                                LN(i - 2)
                            if 0 <= i - 3 < nkb:
                                CS(i - 3)
                        steps.append(st)

                    def fin():
                        cp("dve", csb[h][0:16, :], PB[CSB][0:16, :])
                    steps.append(fin)
                    return steps

                def p2_steps(h, qt):
                    nkb = 4 * qt + 4
                    L = LS[h]
                    pr = heads[h]
                    gb = {}

                    def G(kb):
                        c0, cs, qs = cols(qt, kb)
                        b = gb[kb] = grot.next()

                        def fn(e):
                            e.matmul(PB[b][:, cs].ap, SSEL[:, kb, :].ap, csb[h][:, cs].ap, start=True, stop=False)
                            e.matmul(PB[b][:, cs].ap, TRINEG().ap, L[:, kb, cs].ap, start=False, stop=False)
                            return e.matmul(PB[b][:, cs].ap, KT[:, kb * 128:(kb + 1) * 128].ap, QTh[h][:, qs].ap,
                                            start=False, stop=True)
                        S.op("pe", fn, reads=[TRINEG(), L[:, kb, cs], SSEL[:, kb, :], csb[h][:, cs],
                                              KT[:, kb * 128:(kb + 1) * 128], QTh[h][:, qs]], writes=[PB[b][:, cs]])

                    def EXPA(kb):
                        c0, cs, qs = cols(qt, kb)
                        a_ = AT[kb % 3]
                        act(a_[:, cs], PB[gb[kb]][:, cs], AF.Exp)
                        if kb >= 4 * qt:
                            tt("pool", a_[:, c0:c0 + 128], a_[:, c0:c0 + 128], MASK01(), ALU.mult)

                    def AV(kb):
                        c0, cs, qs = cols(qt, kb)
                        mm(PB[OB[h]][:, cs], [(VJ[:, kb, :], AT[kb % 3][:, cs])], start=(kb == 0), stop=(kb == nkb - 1))

                    steps = []
                    for i in range(nkb + 1):
                        def st(i=i):
                            if i < nkb:
                                G(i)
                            if 0 <= i - 1 < nkb:
                                EXPA(i - 1)
                                AV(i - 1)
                        steps.append(st)

                    def fin():
                        cp("dve", OT[pr, qt * TT:(qt + 1) * TT], PB[OB[h]][pr, :])
                    steps.append(fin)
                    return steps

                def filler(k):
                    for _ in range(k):
                        if not pending:
                            return
                        wo_, qt_, dch = pending.pop(0)
                        tl = slice(qt_ * TT, (qt_ + 1) * TT)
                        b = frot.next()
                        mm(PB[b][:, :], [(wo_[:, dch * 128:(dch + 1) * 128], OT[:, tl])])
                        tt("dve", XT[:, dch, tl], PB[b][:, :], XT[:, dch, tl], ALU.add)

                units = [(h, qt) for qt in range(NT) for h in range(2)]
                prev = None
                for slot_i, u in enumerate(units + [None]):
                    if slot_i == 4 and j + 1 < 8:
                        while pending and pending[0][0] is not wo:
                            filler(1)
                        load_pair(j + 1)
                    s1 = p1_steps(*u) if u is not None else []
                    s2 = p2_steps(*prev) if prev is not None else []
                    n = max(len(s1), len(s2))
                    for i in range(n):
                        if i < len(s2):
                            s2[i]()
                        if i < len(s1):
                            s1[i]()
                        if i == n // 2 or i == n - 1:
                            filler(2)
                    if prev is not None and prev[0] == 1:
                        pending.extend((wo, prev[1], dch) for dch in range(8))
                    prev = u
                if j == 7:
                    filler(len(pending))
            w_done(12)
            w_done(13)
            w_done(14)

        if stop_after >= 5:
            w_issue_upto(18)
            mlp_phase(1, 15, G_MLP1)
        if stop_after >= 6:
            ple_phase(1, 23, G_PLE1, with_out=True)

        if not out_fin:
            sreset()
            osb = [salloc((1024,), F32) for _ in range(2)]
            for t in range(NT):
                emit_out(t, osb)
        S.emit(final_waits=out_fin[-2:])
    return nc


def make_consts():
    ident = np.eye(128, dtype=np.float32)
    s = np.arange(128)[:, None]
    t = np.arange(128)[None, :]
    maskle = (s <= t).astype(np.float32)
    pack = np.zeros((128, 784), np.float32)
    pack[:, 0:128] = 1.0 / 1024
    pack[:, 128:256] = 1.0
    bd = np.zeros((128, 128), np.float32)
    bd[0:64, 0:64] = 1.0 / 64
    bd[64:128, 64:128] = 1.0 / 64
    pack[:, 256:384] = bd
    pack[:, 384:512] = -(s >= t).astype(np.float32)
    pack[:, 512:640] = (s < t).astype(np.float32)
    pack[:, 640 + 15] = -1.0
    pack[:, 640 + 47] = -1.0
    ssel = np.zeros((128, 16, 128), np.float32)
    for kb in range(16):
        for r in range(16):
            if r > kb:
                ssel[r, kb, :] = 1.0
                ssel[32 + r, kb, :] = 1.0
    return {"c_ident": ident, "c_maskle": maskle, "c_pack": pack, "c_ssel": ssel.reshape(128, 2048)}


_CACHE = {}


def kernel(**inputs):
    n = 8
    if "nc" not in _CACHE:
        nc = bass.Bass("TRN2", target_bir_lowering=False)
        build_program(nc)
        _CACHE["nc"] = nc
    nc = _CACHE["nc"]
    consts = make_consts()
    shared = {k: np.ascontiguousarray(np.asarray(v, dtype=np.float32)) for k, v in inputs.items() if k not in ("x", "p")}
    shared.update(consts)
    x = np.asarray(inputs["x"], dtype=np.float32)
    p = np.asarray(inputs["p"], dtype=np.float32)
    in_maps = []
    for i in range(n):
        m = dict(shared)
        m["x"] = np.ascontiguousarray(x[i])
        m["p"] = np.ascontiguousarray(p[:, i])
        in_maps.append(m)
    res = run_bass_kernel_spmd(nc, in_maps, core_ids=list(range(n)))
    return np.stack([np.asarray(r["y"], dtype=np.float32) for r in res.results], axis=0)
```

```python
import contextlib
import numpy as np
import concourse.bass as bass
import concourse.mybir as mybir
from concourse.bass_utils import run_bass_kernel_spmd

F32 = mybir.dt.float32
BF16 = mybir.dt.bfloat16
AF = mybir.ActivationFunctionType
ALU = mybir.AluOpType

SEQ = 2048
D = 1024
NT = 4
TT = 512
NC8 = 8
EPS = 1e-6
GR = 256
ENGS = ("pe", "act", "dve", "pool", "sp")

G_MIXA, G_MLP0, G_PLE0, G_KV, G_MIXB, G_MLP1, G_PLE1, G_K, G_Q, G_QS = 0, 8, 16, 24, 32, 40, 48, 56, 57, 58


class Op:
    __slots__ = ("eng", "fn", "deps", "signal", "idx", "dma", "dsem", "dval", "count", "waits")

    def __init__(self, eng, fn, dma):
        self.eng = eng
        self.fn = fn
        self.dma = dma
        self.signal = False
        self.count = 0
        self.dsem = 0
        self.dval = 0
        self.deps = None
        self.waits = None


class Ref:
    __slots__ = ("ap", "keys")

    def __init__(self, ap, keys):
        self.ap = ap
        self.keys = keys


class Sched:
    def __init__(self, nc):
        self.nc = nc
        self.ops = []
        self.lastw = {}
        self.rd = {}
        self.streams = {}

    def op(self, eng, fn, reads=(), writes=(), dma=None):
        o = Op(eng, fn, dma)
        o.idx = len(self.ops)
        deps = {}

        def add(d):
            k = ("d", d.dma) if d.dma is not None else d.eng
            p = deps.get(k)
            if p is None or p.idx < d.idx:
                deps[k] = d

        lastw = self.lastw
        rd = self.rd
        for r in reads:
            for k in r.keys:
                w = lastw.get(k)
                if w is not None:
                    add(w)
        for r in writes:
            for k in r.keys:
                w = lastw.get(k)
                if w is not None:
                    add(w)
                rr = rd.get(k)
                if rr:
                    for x in rr.values():
                        add(x)
        if dma is not None:
            p = self.streams.get(dma)
            if p is not None:
                add(p)
            self.streams[dma] = o
        mykey = ("d", dma) if dma is not None else eng
        if eng == "pe" and dma is None:
            deps.pop("pe", None)
        o.deps = deps
        for r in reads:
            for k in r.keys:
                d = rd.get(k)
                if d is None:
                    rd[k] = {mykey: o}
                else:
                    d[mykey] = o
        for r in writes:
            for k in r.keys:
                lastw[k] = o
                rd[k] = {}
        self.ops.append(o)
        return o

    def emit(self, final_waits=()):
        nc = self.nc
        by_eng = {e: [o for o in self.ops if o.eng == e] for e in ENGS}
        for e in ENGS:
            waited = {}
            for o in by_eng[e]:
                w = []
                for k, d in o.deps.items():
                    if waited.get(k, -1) >= d.idx:
                        continue
                    waited[k] = d.idx
                    d.signal = True
                    w.append(d)
                o.waits = w
        for o in final_waits:
            o.signal = True
        cnt = {e: 0 for e in ENGS}
        streams = {}
        for o in self.ops:
            if o.dma is not None:
                st = streams.get(o.dma)
                if st is None:
                    st = streams[o.dma] = [len(streams), 0]
                st[1] += 16
                o.dsem = st[0]
                o.dval = st[1]
            elif o.signal:
                cnt[o.eng] += 1
                o.count = cnt[o.eng]
        with contextlib.ExitStack() as es:
            esem = {e: es.enter_context(nc.semaphore("s_" + e)) for e in ENGS}
            dsem = [es.enter_context(nc.semaphore("d%d" % i)) for i in range(len(streams))]
            block = es.enter_context(nc.Block())

            def run(ename, eng):
                for o in by_eng[ename]:
                    for d in o.waits:
                        if d.dma is not None:
                            eng.wait_ge(dsem[d.dsem], d.dval)
                        else:
                            eng.wait_ge(esem[d.eng], d.count)
                    ins = o.fn(eng)
                    if o.dma is not None:
                        ins.then_inc(dsem[o.dsem], 16)
                    elif o.signal:
                        ins.then_inc(esem[ename], 1)
                if ename == "sp":
                    for o in final_waits:
                        if o.dma is not None:
                            eng.wait_ge(dsem[o.dsem], o.dval)
                        else:
                            eng.wait_ge(esem[o.eng], o.count)

            @block.tensor
            def _(e):
                run("pe", e)

            @block.scalar
            def _(e):
                run("act", e)

            @block.vector
            def _(e):
                run("dve", e)

            @block.gpsimd
            def _(e):
                run("pool", e)

            @block.sync
            def _(e):
                run("sp", e)


class Tile:
    def __init__(self, base_ap, off, shape, dt, parts=128, space="S", bank=0):
        self.off = off
        self.shape = tuple(shape)
        self.es = 4 if dt == F32 else 2
        self.parts = parts
        self.space = space
        self.bank = bank
        n = 1
        for s in shape:
            n *= s
        self.n = n
        if space == "S":
            v = base_ap[0:parts, off // 2: off // 2 + n * self.es // 2]
            if dt == F32:
                v = v.bitcast(F32)
        else:
            v = base_ap[0:parts, 0:n]
        if len(shape) == 2:
            v = v.rearrange("p (a b) -> p a b", a=shape[0])
        elif len(shape) == 3:
            v = v.rearrange("p (a b c) -> p a b c", a=shape[0], b=shape[1])
        self.ap = v

    def __getitem__(self, idx):
        if not isinstance(idx, tuple):
            idx = (idx,)
        ps = idx[0]
        fidx = list(idx[1:])
        while len(fidx) < len(self.shape):
            fidx.append(slice(None))
        ap = self.ap[(ps,) + tuple(fidx)]
        rng = []
        for i, s in zip(fidx, self.shape):
            if isinstance(i, int):
                rng.append((i, i + 1))
            else:
                a = 0 if i.start is None else i.start
                b = s if i.stop is None else i.stop
                rng.append((a, b))
        strides = []
        acc = 1
        for s in reversed(self.shape):
            strides.append(acc)
            acc *= s
        strides = strides[::-1]
        keys = set()
        outer = [()]
        for (a, b) in rng[:-1]:
            outer = [o + (i,) for o in outer for i in range(a, b)]
        la, lb = rng[-1]
        gr = GR if self.space == "S" else 512
        for o in outer:
            base = sum(i * st for i, st in zip(o, strides[:-1]))
            b0 = self.off + (base + la) * self.es
            b1 = self.off + (base + lb) * self.es
            for g in range(b0 // gr, (b1 - 1) // gr + 1):
                keys.add((self.space, self.bank, g))
        return Ref(ap, keys)


def build_program(nc, stop_after=99):
    dt_in = {}

    def din(name, shape):
        t = nc.dram_tensor(name, list(shape), F32, kind="ExternalInput").ap()
        dt_in[name] = t
        return t

    x = din("x", [SEQ, D])
    p = din("p", [2, SEQ, 256])
    ln_mix_a = din("ln_mix_a", [1, D])
    w_in_a = din("w_in_a", [1, D, 2 * D])
    g_v_a = din("g_v_a", [1, D])
    w_spatial = din("w_spatial", [1, 8, 128, 128])
    b_spatial = din("b_spatial", [1, 8, 128])
    w_out_a = din("w_out_a", [1, D, D])
    ln_kv = din("ln_kv", [D])
    w_kv = din("w_kv", [D, 2 * D])
    g_k = din("g_k", [64])
    ln_mix_b = din("ln_mix_b", [1, D])
    w_q = din("w_q", [1, D, D])
    g_q = din("g_q", [1, 64])
    w_out_b = din("w_out_b", [1, D, D])
    ln_mlp = din("ln_mlp", [2, D])
    w_up = din("w_up", [2, D, 4 * D])
    w_down = din("w_down", [2, 4 * D, D])
    ln_ple = din("ln_ple", [2, D])
    w_ple_gate = din("w_ple_gate", [2, D, D])
    w_ple_proj = din("w_ple_proj", [2, 256, D])
    c_ident = din("c_ident", [128, 128])
    c_maskle = din("c_maskle", [128, 128])
    c_pack = din("c_pack", [128, 784])
    c_ssel = din("c_ssel", [128, 2048])
    y = nc.dram_tensor("y", [SEQ, D], F32, kind="ExternalOutput").ap()

    es = contextlib.ExitStack()
    with es:
        ARENA_BYTES = 212736
        arena = es.enter_context(nc.sbuf_tensor("arena", [128, ARENA_BYTES // 2], BF16))
        banks = [es.enter_context(nc.psum_tensor("ps%d" % i, [128, 512], F32)) for i in range(8)]
        PB = [Tile(banks[i], 0, (512,), F32, space="P", bank=i) for i in range(8)]
        PB4 = [Tile(banks[i], 0, (4, 128), F32, space="P", bank=i) for i in range(8)]
        S = Sched(nc)

        def T(off, shape, dt, parts=128):
            return Tile(arena, off, shape, dt, parts=parts)

        XT = T(0, (8, SEQ), F32)
        HT = T(65536, (8, SEQ), BF16)
        WOFF = [98304 + i * 16384 for i in range(4)]
        SCR0 = 163840
        coff = ARENA_BYTES
        def calloc(nbytes):
            nonlocal coff
            coff -= (nbytes + 255) // 256 * 256
            return coff
        IDENT = T(calloc(512), (128,), F32)
        MASKLE = T(calloc(512), (128,), F32)
        CP = T(calloc(1568), (784,), BF16)
        SSEL = T(calloc(4096), (16, 128), BF16)
        GCOLS = T(calloc(256), (64,), F32)
        BSROW = T(calloc(2048), (1024,), BF16, parts=1)
        SMALL = T(calloc(256), (64,), F32)
        WP = T(calloc(4096), (2, 1024), BF16)
        SCR_END = coff
        ONES_S = lambda: CP[:, 0:128]
        ONES1 = lambda ps=slice(0, 128): CP[ps, 128:256]
        BDIAG = lambda: CP[:, 256:384]
        TRINEG = lambda: CP[:, 384:512]
        MASK01 = lambda: CP[:, 512:640]
        ESEL = lambda kb: CP[:, 640 + 15 - kb: 640 + 15 - kb + 128]

        scr = [SCR0]

        def salloc(shape, dt, parts=128):
            n = 1
            for s in shape:
                n *= s
            nb = n * (4 if dt == F32 else 2)
            nb = (nb + 255) // 256 * 256
            off = scr[0]
            scr[0] += nb
            assert scr[0] <= SCR_END, ("scratch overflow", scr[0], SCR_END)
            return T(off, shape, dt, parts=parts)

        def sreset():
            scr[0] = SCR0

        def act(out, in_, func, bias=None, scale=None, accum=None, extra=()):
            kw = {}
            if bias is not None:
                kw["bias"] = bias.ap if isinstance(bias, Ref) else bias
            if scale is not None:
                kw["scale"] = scale.ap if isinstance(scale, Ref) else scale
            if accum is not None:
                kw["accum_out"] = accum.ap
            rds = [in_] + [r for r in (bias, scale) if isinstance(r, Ref)] + list(extra)
            wr = [out] + ([accum] if accum is not None else [])
            return S.op("act", lambda e: e.activation(out.ap, in_.ap, func, **kw), reads=rds, writes=wr)

        def tt(eng, out, a, b, op):
            return S.op(eng, lambda e: e.tensor_tensor(out.ap, a.ap, b.ap, op), reads=[a, b], writes=[out])

        def stt(out, a, sc, b, op0, op1):
            scv = sc.ap if isinstance(sc, Ref) else sc
            rds = [a, b] + ([sc] if isinstance(sc, Ref) else [])
            return S.op("dve", lambda e: e.scalar_tensor_tensor(out.ap, a.ap, scv, b.ap, op0, op1), reads=rds, writes=[out])

        def ts(eng, out, a, s1, s2, op0, op1=None):
            s1v = s1.ap if isinstance(s1, Ref) else s1
            s2v = s2.ap if isinstance(s2, Ref) else s2
            rds = [a] + [r for r in (s1, s2) if isinstance(r, Ref)]
            if op1 is None:
                return S.op(eng, lambda e: e.tensor_scalar(out.ap, a.ap, s1v, None, op0), reads=rds, writes=[out])
            return S.op(eng, lambda e: e.tensor_scalar(out.ap, a.ap, s1v, s2v, op0, op1), reads=rds, writes=[out])

        def cp(eng, out, in_):
            if eng == "act":
                return act(out, in_, AF.Copy)
            return S.op(eng, lambda e: e.tensor_copy(out.ap, in_.ap), reads=[in_], writes=[out])

        def recip(out, in_):
            return S.op("dve", lambda e: e.reciprocal(out.ap, in_.ap), reads=[in_], writes=[out])

        def mm(out, pairs, start=True, stop=True):
            n = len(pairs)

            def fn(e):
                ins = None
                for i, (l, r) in enumerate(pairs):
                    ins = e.matmul(out.ap, l.ap, r.ap, start=(start and i == 0), stop=(stop and i == n - 1))
                return ins
            rds = []
            for l, r in pairs:
                rds.append(l)
                rds.append(r)
            return S.op("pe", fn, reads=rds, writes=[out])

        def tr(out, in_, ident):
            return S.op("pe", lambda e: e.transpose(out.ap, in_.ap, ident.ap), reads=[in_, ident], writes=[out])

        def dma(eng, out, in_ap, stream, reads=()):
            return S.op(eng, lambda e: e.dma_start(out=out.ap, in_=in_ap), reads=list(reads), writes=[out], dma=stream)

        class Rot:
            def __init__(self, ids):
                self.ids = list(ids)
                self.i = 0

            def next(self):
                b = self.ids[self.i % len(self.ids)]
                self.i += 1
                return b

        rot = Rot(range(8))
        tog = [0]

        def evac_eng():
            tog[0] ^= 1
            return "act" if tog[0] else "dve"

        def wslot(n, shape):
            return T(WOFF[n % 4], shape, BF16)

        def kp(ap2d):
            return ap2d.rearrange("(k p) f -> p k f", p=128)

        loaders = []

        def L_full(src2d, c0, c1):
            def f(n):
                w = wslot(n, (8, c1 - c0))
                dma("pool", w[:, :, :], kp(src2d)[:, :, c0:c1], "W%da" % (n % 4))
            return f

        def L_mlp(l, gi):
            def f(n):
                wu = Tile(arena, WOFF[n % 4], (8, 512), BF16)
                wd = Tile(arena, WOFF[n % 4] + 8192, (4, 1024), BF16)
                dma("pool", wu[:, :, :], kp(w_up[l])[:, :, gi * 512:(gi + 1) * 512], "W%da" % (n % 4))
                dma("pool", wd[:, :, :], kp(w_down[l][gi * 512:(gi + 1) * 512, :]), "W%db" % (n % 4))
            return f

        def L_first(n):
            w = wslot(n, (8, 1024))
            for i, sfx in enumerate("abcd"):
                dma("pool", w[:, :, i * 256:(i + 1) * 256], kp(w_in_a[0])[:, :, i * 256:(i + 1) * 256], "W0" + sfx)
        loaders.append(L_first)
        loaders.append(L_full(w_in_a[0], 1024, 2048))
        loaders.append(L_full(w_out_a[0], 0, 1024))
        for gi in range(8):
            loaders.append(L_mlp(0, gi))
        loaders.append(L_full(w_ple_gate[0], 0, 1024))
        loaders += [None, None, None]
        for gi in range(8):
            loaders.append(L_mlp(1, gi))
        loaders.append(L_full(w_ple_gate[1], 0, 1024))
        issued = [0]

        def w_issue_upto(n):
            while issued[0] <= n and issued[0] < len(loaders):
                f = loaders[issued[0]]
                if f is not None:
                    f(issued[0])
                issued[0] += 1

        def w_done(n):
            w_issue_upto(n + 4)

        sreset()
        dma("sp", IDENT[:, :], c_ident, "c0")
        GST_A = T(97280, (128,), F32, parts=8)
        GST_B = T(97792, (128,), F32, parts=64)
        dma("sp", GST_A[0:8, :], ln_mix_a[0].rearrange("(k f) -> k f", f=128), "g0")
        dma("pool", CP[:, :], c_pack, "c2")
        dma("pool", BSROW[0:1, :], b_spatial[0].rearrange("(o g) t -> o (g t)", o=1), "c4")
        w_issue_upto(2)
        dma("pool", SSEL[:, :, :], c_ssel.rearrange("p (a b) -> p a b", a=16), "c3")
        dma("pool", WP[:, :, :], kp(w_ple_proj[0]), "wp")
        gi_ = 0
        for src in (ln_mlp[0], ln_ple[0], ln_kv, ln_mix_b[0], ln_mlp[1], ln_ple[1]):
            dma("act", GST_B[gi_ * 8:(gi_ + 1) * 8, :], src.rearrange("(k f) -> k f", f=128), "g%d" % (gi_ + 1))
            gi_ += 1
        gk2 = g_k.rearrange("(o f) -> o f", o=1)
        dma("act", GST_B[48:49, 0:64], gk2, "g7")
        dma("act", GST_B[48:49, 64:128], gk2, "g8")
        dma("act", GST_B[49:50, 0:64], g_q, "g9")
        dma("act", GST_B[49:50, 64:128], g_q, "g10")
        b = rot.next()
        tr(PB[b][:, 0:8], GST_A[0:8, :], IDENT[0:8, 0:8])
        cp("dve", GCOLS[:, 0:8], PB[b][:, 0:8])

        def gcols_rest():
            b = rot.next()
            tr(PB[b][:, 0:50], GST_B[0:50, :], IDENT[0:50, 0:50])
            cp("dve", GCOLS[:, 8:58], PB[b][:, 0:50])
            ts("dve", GCOLS[:, 58:59], GCOLS[:, 57:58], 0.125, None, ALU.mult)

        xs = [T(WOFF[3] + i * 4096, (1024,), F32) for i in range(4)]

        def xload(t):
            for blk in range(t * 4, t * 4 + 4):
                st = xs[blk % 4]
                dma("sp", st[:, :], x[blk * 128:(blk + 1) * 128, :], "x%d" % (blk % 4))
                for half in range(2):
                    b = rot.next()
                    for cc in range(4):
                        c = half * 4 + cc
                        tr(PB4[b][:, cc, :], st[:, c * 128:(c + 1) * 128], IDENT[:, :])
                    cp(evac_eng(), XT[:, half * 4:(half + 1) * 4, blk * 128:(blk + 1) * 128], PB4[b][:, :, :])

        xload(0)
        dma("sp", MASKLE[:, :], c_maskle, "c1")
        if stop_after < 1:
            for t in range(1, NT):
                xload(t)
            gcols_rest()

        def norm_A(t, sq):
            tl = slice(t * TT, (t + 1) * TT)
            for c in range(8):
                act(sq[:, c, :], XT[:, c, tl], AF.Square)

        def norm_B(t, gcol0, sq, sd, rstd, dst=HT):
            tl = slice(t * TT, (t + 1) * TT)
            b = rot.next()
            mm(PB[b][:, :], [(ONES_S(), sq[:, c, :]) for c in range(8)])
            act(sd[:, :], PB[b][:, :], AF.Ln, bias=EPS)
            act(rstd[:, :], sd[:, :], AF.Exp, scale=-0.5)
            for c in range(8):
                if gcol0 is None:
                    tt("dve", dst[:, c, tl], XT[:, c, tl], rstd[:, :], ALU.mult)
                else:
                    stt(dst[:, c, tl], XT[:, c, tl], GCOLS[:, gcol0 + c:gcol0 + c + 1], rstd[:, :], ALU.mult, ALU.mult)

        def norm_tile(t, gcol0, sq, sd, rstd, dst=HT):
            norm_A(t, sq)
            norm_B(t, gcol0, sq, sd, rstd, dst)

        if stop_after >= 1:
            sreset()
            uT = salloc((8, TT), BF16)
            sq = uT
            yT = salloc((8, TT), BF16)
            v16 = [salloc((1024,), BF16) for _ in range(2)]
            vn = [salloc((1024,), BF16) for _ in range(2)]
            sd = salloc((TT,), F32)
            rstd = salloc((TT,), F32)
            wsT = salloc((8, 128), BF16)
            gvt = salloc((1024,), F32)
            wsl = T(yT.off, (8, 128), F32)
            dma("sp", gvt[:, :], g_v_a.partition_broadcast(128), "gv")
            dma("sp", wsl[:, :, :], w_spatial[0].rearrange("g t s -> t g s"), "ws")
            def ws_prep():
                for hb in range(2):
                    b = rot.next()
                    for gg in range(4):
                        tr(PB4[b][:, gg, :], wsl[:, hb * 4 + gg, :], IDENT[:, :])
                    tt("dve", wsT[:, hb * 4:(hb + 1) * 4, :], PB4[b][:, :, :],
                       Ref(MASKLE.ap.unsqueeze(1).to_broadcast([128, 4, 128]), MASKLE[:, :].keys), ALU.mult)
            W0 = wslot(0, (8, 1024))
            W1 = wslot(1, (8, 1024))
            W2 = wslot(2, (8, 1024))
            norm_A(0, sq)
            norm_B(0, G_MIXA, sq, sd, rstd)
            for t in range(NT):
                tl = slice(t * TT, (t + 1) * TT)
                if t + 1 < NT:
                    xload(t + 1)
                    if t + 1 == NT - 1:
                        w_issue_upto(3)
                for f in range(8):
                    b = rot.next()
                    mm(PB[b][:, :], [(W0[:, k, f * 128:(f + 1) * 128], HT[:, k, tl]) for k in range(8)])
                    act(uT[:, f, :], PB[b][:, :], AF.Gelu_apprx_tanh)
                if t == 0:
                    gcols_rest()
                    ws_prep()

                def VST(blk, t=t):
                    tb = t * 4 + blk
                    tbl = slice(tb * 128, (tb + 1) * 128)
                    for half in range(2):
                        b = rot.next()
                        mm(PB[b][:, :], [(HT[:, k, tbl], W1[:, k, half * 512:(half + 1) * 512]) for k in range(8)])
                        act(v16[blk % 2][:, half * 512:(half + 1) * 512], PB[b][:, :], AF.Gelu_apprx_tanh)

                def NST(blk):
                    vb = vn[blk % 2]
                    vv = v16[blk % 2]
                    c4 = (blk % 2) * 4
                    act(vb[:, :], vv[:, :], AF.Square, accum=SMALL[:, c4:c4 + 1])
                    ts("dve", SMALL[:, c4 + 1:c4 + 2], SMALL[:, c4:c4 + 1], 1.0 / 1024, EPS, ALU.mult, ALU.add)
                    act(SMALL[:, c4 + 2:c4 + 3], SMALL[:, c4 + 1:c4 + 2], AF.Sqrt)
                    recip(SMALL[:, c4 + 3:c4 + 4], SMALL[:, c4 + 2:c4 + 3])
                    stt(vb[:, :], vv[:, :], SMALL[:, c4 + 3:c4 + 4], gvt[:, :], ALU.mult, ALU.mult)

                def SPST(blk):
                    vb = vn[blk % 2]
                    for hb in range(2):
                        b = rot.next()

                        def fn(e, b=b, hb=hb, vb=vb):
                            ins = None
                            for gg in range(4):
                                g = hb * 4 + gg
                                e.matmul(PB4[b][:, gg, :].ap, vb[:, g * 128:(g + 1) * 128].ap, wsT[:, g, :].ap,
                                         start=True, stop=False)
                                ins = e.matmul(PB4[b][:, gg, :].ap, ONES1(slice(0, 1)).ap,
                                               BSROW[0:1, g * 128:(g + 1) * 128].ap, start=False, stop=True)
                            return ins
                        S.op("pe", fn, reads=[vb[:, hb * 512:(hb + 1) * 512], wsT[:, hb * 4:(hb + 1) * 4, :],
                                              ONES1(slice(0, 1)), BSROW[0:1, :]], writes=[PB4[b][:, :, :]])
                        tt("dve", yT[:, hb * 4:(hb + 1) * 4, blk * 128:(blk + 1) * 128], PB4[b][:, :, :],
                           uT[:, hb * 4:(hb + 1) * 4, blk * 128:(blk + 1) * 128], ALU.mult)

                VST(0)
                VST(1)
                NST(0)
                for blk in range(4):
                    if blk + 2 < 4:
                        VST(blk + 2)
                    if blk + 1 < 4:
                        NST(blk + 1)
                    SPST(blk)
                if t + 1 < NT:
                    norm_A(t + 1, sq)
                for dch in range(8):
                    if dch == 4 and t + 1 < NT:
                        norm_B(t + 1, G_MIXA, sq, sd, rstd)
                    b = rot.next()
                    mm(PB[b][:, :], [(W2[:, k, dch * 128:(dch + 1) * 128], yT[:, k, :]) for k in range(8)])
                    tt("dve", XT[:, dch, tl], PB[b][:, :], XT[:, dch, tl], ALU.add)
            w_done(0)
            w_done(1)
            w_done(2)

        def mlp_phase(l, blk0, gcol0):
            sreset()
            sq = salloc((8, TT), BF16)
            sd = salloc((TT,), F32)
            rstd = salloc((TT,), F32)
            aT = [salloc((8, TT), BF16) for _ in range(2)]
            rt = [salloc((TT,), BF16) for _ in range(2)]
            norm_tile(0, gcol0, sq, sd, rstd)

            def wts(sg):
                nA, nB = blk0 + 2 * sg, blk0 + 2 * sg + 1
                wu = [Tile(arena, WOFF[n % 4], (8, 512), BF16) for n in (nA, nB)]
                wd = [Tile(arena, WOFF[n % 4] + 8192, (4, 1024), BF16) for n in (nA, nB)]
                return wu, wd

            def up(k):
                sg, t = divmod(k, NT)
                wu, wd = wts(sg)
                tl = slice(t * TT, (t + 1) * TT)
                a = aT[k % 2]
                for f in range(8):
                    b = rot.next()
                    mm(PB[b][:, :], [(wu[f // 4][:, kk, (f % 4) * 128:(f % 4 + 1) * 128], HT[:, kk, tl]) for kk in range(8)])
                    r = rt[f % 2]
                    act(r[:, :], PB[b][:, :], AF.Relu)
                    tt("dve", a[:, f, :], r[:, :], r[:, :], ALU.mult)
                if sg == 0 and t + 1 < NT:
                    norm_A(t + 1, sq)

            def down(k):
                sg, t = divmod(k, NT)
                wu, wd = wts(sg)
                tl = slice(t * TT, (t + 1) * TT)
                a = aT[k % 2]
                for dch in range(8):
                    if dch == 4 and sg == 0 and t + 1 < NT:
                        norm_B(t + 1, gcol0, sq, sd, rstd)
                    b = rot.next()
                    mm(PB[b][:, :], [(wd[f // 4][:, f % 4, dch * 128:(dch + 1) * 128], a[:, f, :]) for f in range(8)])
                    tt("dve", XT[:, dch, tl], PB[b][:, :], XT[:, dch, tl], ALU.add)
                if t == NT - 1:
                    w_done(blk0 + 2 * sg)
                    w_done(blk0 + 2 * sg + 1)

            NI = 4 * NT
            for k in range(NT):
                up(k)
                down(k)
            for k in range(NT, NI):
                up(k)
                if k > NT:
                    down(k - 1)
            down(NI - 1)

        out_fin = []

        def emit_out(t, osb):
            for blk in range(t * 4, t * 4 + 4):
                ot = osb[blk % 2]
                for half in range(2):
                    b = rot.next()
                    for cc in range(4):
                        c = half * 4 + cc
                        tr(PB4[b][:, cc, :], XT[:, c, blk * 128:(blk + 1) * 128], IDENT[:, :])
                    cp(evac_eng(), ot[:, half * 512:(half + 1) * 512], PB[b][:, :])
                out_fin.append(S.op("sp", lambda e, ot=ot, blk=blk: e.dma_start(out=y[blk * 128:(blk + 1) * 128, :], in_=ot[:, :].ap),
                                    reads=[ot[:, :]], dma="o%d" % (blk % 2)))

        def ple_phase(l, blkn, gcol0, with_out=False):
            sreset()
            sq = salloc((8, TT), BF16)
            sd = salloc((TT,), F32)
            rstd = salloc((TT,), F32)
            pst = salloc((4, 256), F32)
            pT = salloc((2, TT), BF16)
            gate = [salloc((TT,), F32) for _ in range(2)]
            tmp2 = [salloc((TT,), F32) for _ in range(2)]
            osb = [salloc((1024,), F32) for _ in range(2)] if with_out else None
            Wg = wslot(blkn, (8, 1024))
            def pload(t):
                dma("sp", pst[:, :, :], p[l][t * TT:(t + 1) * TT, :].rearrange("(b q) f -> q b f", q=128), "pl")
            pload(0)
            for t in range(NT):
                tl = slice(t * TT, (t + 1) * TT)
                if t == 0:
                    norm_tile(0, gcol0, sq, sd, rstd)
                for kk in range(2):
                    b = rot.next()
                    for blk in range(4):
                        tr(PB4[b][:, blk, :], pst[:, blk, kk * 128:(kk + 1) * 128], IDENT[:, :])
                    cp("act", pT[:, kk, :], PB[b][:, :])
                if t + 1 < NT:
                    pload(t + 1)
                for dch in range(8):
                    if dch == 1 and t + 1 < NT:
                        norm_A(t + 1, sq)
                    if dch == 5 and t + 1 < NT:
                        norm_B(t + 1, gcol0, sq, sd, rstd)
                    b1 = rot.next()
                    mm(PB[b1][:, :], [(Wg[:, k, dch * 128:(dch + 1) * 128], HT[:, k, tl]) for k in range(8)])
                    g_ = gate[dch % 2]
                    act(g_[:, :], PB[b1][:, :], AF.Sigmoid)
                    b2 = rot.next()
                    mm(PB[b2][:, :], [(WP[:, kk, dch * 128:(dch + 1) * 128], pT[:, kk, :]) for kk in range(2)])
                    t2 = tmp2[dch % 2]
                    tt("dve", t2[:, :], PB[b2][:, :], g_[:, :], ALU.mult)
                    tt("dve", XT[:, dch, tl], XT[:, dch, tl], t2[:, :], ALU.add)
                if with_out:
                    emit_out(t, osb)
            w_done(blkn)

        PW = [Tile(arena, WOFF[0] + i * 8192, (4096,), BF16) for i in range(2)]

        def pw_views(i):
            t_ = PW[i]
            off = t_.off
            wk = Tile(arena, off, (8, 128), BF16)
            wv = Tile(arena, off + 2048, (8, 128), BF16)
            wq = Tile(arena, off + 4096, (8, 128), BF16)
            wo = Tile(arena, off + 6144, (1024,), BF16)
            return wk, wv, wq, wo

        def load_pair(j):
            wk, wv, wq, wo = pw_views(j % 2)
            sl = slice(j * 128, (j + 1) * 128)
            dma("pool", wk[:, :, :], kp(w_kv)[:, :, j * 128:(j + 1) * 128], "pw%da" % (j % 2))
            dma("pool", wv[:, :, :], kp(w_kv)[:, :, 1024 + j * 128:1024 + (j + 1) * 128], "pw%db" % (j % 2))
            dma("pool", wq[:, :, :], kp(w_q[0])[:, :, j * 128:(j + 1) * 128], "pw%dc" % (j % 2))
            dma("pool", wo[:, :], w_out_b[0][sl, :], "pw%dd" % (j % 2))
            for w_, g0 in ((wk, G_KV), (wv, G_KV), (wq, G_MIXB)):
                gb = Ref(GCOLS.ap[:, g0:g0 + 8].unsqueeze(2).to_broadcast([128, 8, 128]), GCOLS[:, g0:g0 + 8].keys)
                tt("dve", w_[:, :, :], w_[:, :, :], gb, ALU.mult)

        if stop_after >= 2:
            mlp_phase(0, 3, G_MLP0)
            if stop_after >= 4:
                load_pair(0)
        if stop_after >= 3:
            ple_phase(0, 11, G_PLE0)
            dma("pool", WP[:, :, :], kp(w_ple_proj[1]), "wp")

        if stop_after >= 4:
            sreset()
            sq = salloc((8, TT), BF16)
            sd = salloc((TT,), F32)
            rstd = salloc((TT,), F32)
            psq, psd, prstd = sq, sd, rstd
            sreset()
            KT = salloc((SEQ,), BF16)
            QTh = [salloc((SEQ,), BF16) for _ in range(2)]
            VJ = salloc((16, 128), BF16)
            OT = salloc((SEQ,), BF16)
            e1 = [salloc((TT,), F32) for _ in range(2)]
            AT = [salloc((TT,), BF16) for _ in range(3)]
            csb = [salloc((TT,), BF16) for _ in range(2)]
            LS = [Tile(arena, WOFF[1], (16, TT), BF16), Tile(arena, WOFF[2], (16, TT), BF16)]
            sqK = Tile(arena, WOFF[2], (4, TT), BF16)
            sqQ = Tile(arena, WOFF[2] + 4096, (4, TT), BF16)
            sd4 = Tile(arena, WOFF[2] + 8192, (4, TT), F32)
            rawK = Tile(arena, WOFF[1], (4, TT), F32)
            rawQ = Tile(arena, WOFF[1] + 8192, (4, TT), F32)
            srot = Rot(range(5))
            zrot = Rot([0, 1])
            grot = Rot([2, 3])
            frot = Rot([6, 7])
            JUNK = 7

            def dummy(k):
                def fn(e):
                    ins = None
                    for _ in range(k):
                        ins = e.matmul(PB[JUNK][:, :].ap, ONES_S().ap, CP[:, 0:512].ap, start=True, stop=True)
                    return ins
                S.op("pe", fn, reads=[ONES_S(), CP[:, 0:512]], writes=[PB[JUNK][:, :]])
            CSB = 4
            OB = [5, 5]

            pending = []
            for j in range(8):
                wk, wv, wq, wo = pw_views(j % 2)
                KB = [0, 1, 2, 3]
                QB = [4, 5, 6, 7]
                def kq_main_t(w_, raw, sqx, mb, t):
                    tl = slice(t * TT, (t + 1) * TT)
                    mm(PB[mb[t]][:, :], [(w_[:, k, :], HT[:, k, tl]) for k in range(8)])
                    act(raw[:, t, :], PB[mb[t]][:, :], AF.Copy)
                    act(sqx[:, t, :], PB[mb[t]][:, :], AF.Square)

                if j == 0:
                    norm_A(0, psq)
                    norm_B(0, None, psq, psd, prstd)
                    for t in range(NT):
                        if t + 1 < NT:
                            norm_A(t + 1, psq)
                        kq_main_t(wk, rawK, sqK, KB, t)
                        kq_main_t(wq, rawQ, sqQ, QB, t)
                        if t + 1 < NT:
                            norm_B(t + 1, None, psq, psd, prstd)
                    for bi in range(2):
                        S.op("dve", lambda e, bi=bi: e.memset(csb[bi][:, :].ap, 0.0), writes=[csb[bi][:, :]])
                        S.op("dve", lambda e, bi=bi: e.memset(QTh[bi][:, :].ap, 0.0), writes=[QTh[bi][:, :]])
                else:
                    for (w_, raw, sqx, mb) in ((wk, rawK, sqK, KB), (wq, rawQ, sqQ, QB)):
                        for t in range(NT):
                            kq_main_t(w_, raw, sqx, mb, t)
                for (sqx, mb) in ((sqK, KB), (sqQ, QB)):
                    for t in range(NT):
                        mm(PB[mb[t]][:, :], [(BDIAG(), sqx[:, t, :])])
                for t in range(NT):
                    act(sd4[:, t, :], PB[KB[t]][:, :], AF.Ln, bias=EPS)
                for t in range(NT):
                    act(sd4[:, t, :], sd4[:, t, :], AF.Exp, scale=-0.5)
                for t in range(NT):
                    tl = slice(t * TT, (t + 1) * TT)
                    stt(KT[:, tl], rawK[:, t, :], GCOLS[:, G_K:G_K + 1], sd4[:, t, :], ALU.mult, ALU.mult)
                for q4 in range(4):
                    b_ = KB[q4]
                    for bb in range(4):
                        blk = q4 * 4 + bb
                        mm(PB4[b_][:, bb, :], [(HT[:, k, blk * 128:(blk + 1) * 128], wv[:, k, :]) for k in range(8)])
                    cp("dve", VJ[:, q4 * 4:(q4 + 1) * 4, :], PB4[b_][:, :, :])
                for t in range(NT):
                    act(sd4[:, t, :], PB[QB[t]][:, :], AF.Ln, bias=EPS)
                for t in range(NT):
                    act(sd4[:, t, :], sd4[:, t, :], AF.Exp, scale=-0.5)
                for t in range(NT):
                    tl = slice(t * TT, (t + 1) * TT)
                    for h_ in range(2):
                        hp = slice(h_ * 64, (h_ + 1) * 64)
                        stt(QTh[h_][hp, tl], rawQ[hp, t, :], GCOLS[hp, G_QS:G_QS + 1], sd4[hp, t, :], ALU.mult, ALU.mult)
                heads = (slice(0, 64), slice(64, 128))

                def cols(qt, kb):
                    c0 = max(0, kb - 4 * qt) * 128
                    return c0, slice(c0, TT), slice(qt * TT + c0, (qt + 1) * TT)

                def p1_steps(h, qt):
                    nkb = 4 * qt + 4
                    L = LS[h]
                    zb = {}

                    def Z(kb):
                        c0, cs, qs = cols(qt, kb)
                        zb[kb] = zrot.next()
                        mm(PB[zb[kb]][:, cs], [(KT[:, kb * 128:(kb + 1) * 128], QTh[h][:, qs])])

                    def EXP(kb):
                        c0, cs, qs = cols(qt, kb)
                        act(e1[kb % 2][:, cs], PB[zb[kb]][:, cs], AF.Exp)

                    def LN(kb):
                        c0, cs, qs = cols(qt, kb)
                        act(L[:, kb, cs], e1[kb % 2][:, cs], AF.Ln, bias=1.0)
                        if kb >= 4 * qt:
                            tt("pool", L[:, kb, c0:c0 + 128], L[:, kb, c0:c0 + 128], MASK01(), ALU.mult)

                    def CS(kb):
                        c0, cs, qs = cols(qt, kb)
                        mm(PB[CSB][:, cs], [(ESEL(kb), L[:, kb, cs])], start=(kb == 0), stop=(kb == nkb - 1))

                    steps = []
                    for i in range(nkb + 3):
                        def st(i=i):
                            if i < nkb:
                                Z(i)
                            if 0 <= i - 1 < nkb:
                                EXP(i - 1)
                            if 0 <= i - 2 < nkb:
                                LN(i - 2)
                            if 0 <= i - 3 < nkb:
                                CS(i - 3)
                        steps.append(st)

                    def fin():
                        cp("dve", csb[h][0:16, :], PB[CSB][0:16, :])
                    steps.append(fin)
                    return steps

                def p2_steps(h, qt):
                    nkb = 4 * qt + 4
                    L = LS[h]
                    pr = heads[h]
                    gb = {}

                    def G(kb):
                        c0, cs, qs = cols(qt, kb)
                        b = gb[kb] = grot.next()

                        def fn(e):
                            e.matmul(PB[b][:, cs].ap, SSEL[:, kb, :].ap, csb[h][:, cs].ap, start=True, stop=False)
                            e.matmul(PB[b][:, cs].ap, TRINEG().ap, L[:, kb, cs].ap, start=False, stop=False)
                            return e.matmul(PB[b][:, cs].ap, KT[:, kb * 128:(kb + 1) * 128].ap, QTh[h][:, qs].ap,
                                            start=False, stop=True)
                        S.op("pe", fn, reads=[TRINEG(), L[:, kb, cs], SSEL[:, kb, :], csb[h][:, cs],
                                              KT[:, kb * 128:(kb + 1) * 128], QTh[h][:, qs]], writes=[PB[b][:, cs]])

                    def EXPA(kb):
                        c0, cs, qs = cols(qt, kb)
                        a_ = AT[kb % 3]
                        act(a_[:, cs], PB[gb[kb]][:, cs], AF.Exp)
                        if kb >= 4 * qt:
                            tt("pool", a_[:, c0:c0 + 128], a_[:, c0:c0 + 128], MASK01(), ALU.mult)

                    def AV(kb):
                        c0, cs, qs = cols(qt, kb)
                        mm(PB[OB[h]][:, cs], [(VJ[:, kb, :], AT[kb % 3][:, cs])], start=(kb == 0), stop=(kb == nkb - 1))

                    steps = []
                    for i in range(nkb + 1):
                        def st(i=i):
                            if i < nkb:
                                G(i)
                            if 0 <= i - 1 < nkb:
                                EXPA(i - 1)
                                AV(i - 1)
                        steps.append(st)

                    def fin():
                        cp("dve", OT[pr, qt * TT:(qt + 1) * TT], PB[OB[h]][pr, :])
                    steps.append(fin)
                    return steps

                def filler(k):
                    for _ in range(k):
                        if not pending:
                            return
                        wo_, qt_, dch = pending.pop(0)
                        tl = slice(qt_ * TT, (qt_ + 1) * TT)
                        b = frot.next()
                        mm(PB[b][:, :], [(wo_[:, dch * 128:(dch + 1) * 128], OT[:, tl])])
                        tt("dve", XT[:, dch, tl], PB[b][:, :], XT[:, dch, tl], ALU.add)

                units = [(h, qt) for qt in range(NT) for h in range(2)]
                prev = None
                for slot_i, u in enumerate(units + [None]):
                    if slot_i == 4 and j + 1 < 8:
                        while pending and pending[0][0] is not wo:
                            filler(1)
                        load_pair(j + 1)
                    s1 = p1_steps(*u) if u is not None else []
                    s2 = p2_steps(*prev) if prev is not None else []
                    n = max(len(s1), len(s2))
                    for i in range(n):
                        if i < len(s2):
                            s2[i]()
                        if i < len(s1):
                            s1[i]()
                        if i == n // 2 or i == n - 1:
                            filler(2)
                    if prev is not None and prev[0] == 1:
                        pending.extend((wo, prev[1], dch) for dch in range(8))
                    prev = u
                if j == 7:
                    filler(len(pending))
            w_done(12)
            w_done(13)
            w_done(14)

        if stop_after >= 5:
            w_issue_upto(18)
            mlp_phase(1, 15, G_MLP1)
        if stop_after >= 6:
            ple_phase(1, 23, G_PLE1, with_out=True)

        if not out_fin:
            sreset()
            osb = [salloc((1024,), F32) for _ in range(2)]
            for t in range(NT):
                emit_out(t, osb)
        S.emit(final_waits=out_fin[-2:])
    return nc


def make_consts():
    ident = np.eye(128, dtype=np.float32)
    s = np.arange(128)[:, None]
    t = np.arange(128)[None, :]
    maskle = (s <= t).astype(np.float32)
    pack = np.zeros((128, 784), np.float32)
    pack[:, 0:128] = 1.0 / 1024
    pack[:, 128:256] = 1.0
    bd = np.zeros((128, 128), np.float32)
    bd[0:64, 0:64] = 1.0 / 64
    bd[64:128, 64:128] = 1.0 / 64
    pack[:, 256:384] = bd
    pack[:, 384:512] = -(s >= t).astype(np.float32)
    pack[:, 512:640] = (s < t).astype(np.float32)
    pack[:, 640 + 15] = -1.0
    pack[:, 640 + 47] = -1.0
    ssel = np.zeros((128, 16, 128), np.float32)
    for kb in range(16):
        for r in range(16):
            if r > kb:
                ssel[r, kb, :] = 1.0
                ssel[32 + r, kb, :] = 1.0
    return {"c_ident": ident, "c_maskle": maskle, "c_pack": pack, "c_ssel": ssel.reshape(128, 2048)}


_CACHE = {}


def kernel(**inputs):
    n = 8
    if "nc" not in _CACHE:
        nc = bass.Bass("TRN2", target_bir_lowering=False)
        build_program(nc)
        _CACHE["nc"] = nc
    nc = _CACHE["nc"]
    consts = make_consts()
    shared = {k: np.ascontiguousarray(np.asarray(v, dtype=np.float32)) for k, v in inputs.items() if k not in ("x", "p")}
    shared.update(consts)
    x = np.asarray(inputs["x"], dtype=np.float32)
    p = np.asarray(inputs["p"], dtype=np.float32)
    in_maps = []
    for i in range(n):
        m = dict(shared)
        m["x"] = np.ascontiguousarray(x[i])
        m["p"] = np.ascontiguousarray(p[:, i])
        in_maps.append(m)
    res = run_bass_kernel_spmd(nc, in_maps, core_ids=list(range(n)))
    return np.stack([np.asarray(r["y"], dtype=np.float32) for r in res.results], axis=0)
```

```python
import contextlib
import numpy as np
import concourse.bass as bass
import concourse.mybir as mybir
from concourse.bass_utils import run_bass_kernel_spmd

F32 = mybir.dt.float32
BF16 = mybir.dt.bfloat16
AF = mybir.ActivationFunctionType
ALU = mybir.AluOpType

SEQ = 2048
D = 1024
NT = 4
TT = 512
NC8 = 8
EPS = 1e-6
GR = 256
ENGS = ("pe", "act", "dve", "pool", "sp")

G_MIXA, G_MLP0, G_PLE0, G_KV, G_MIXB, G_MLP1, G_PLE1, G_K, G_Q, G_QS = 0, 8, 16, 24, 32, 40, 48, 56, 57, 58


class Op:
    __slots__ = ("eng", "fn", "deps", "signal", "idx", "dma", "dsem", "dval", "count", "waits")

    def __init__(self, eng, fn, dma):
        self.eng = eng
        self.fn = fn
        self.dma = dma
        self.signal = False
        self.count = 0
        self.dsem = 0
        self.dval = 0
        self.deps = None
        self.waits = None


class Ref:
    __slots__ = ("ap", "keys")

    def __init__(self, ap, keys):
        self.ap = ap
        self.keys = keys


class Sched:
    def __init__(self, nc):
        self.nc = nc
        self.ops = []
        self.lastw = {}
        self.rd = {}
        self.streams = {}

    def op(self, eng, fn, reads=(), writes=(), dma=None):
        o = Op(eng, fn, dma)
        o.idx = len(self.ops)
        deps = {}

        def add(d):
            k = ("d", d.dma) if d.dma is not None else d.eng
            p = deps.get(k)
            if p is None or p.idx < d.idx:
                deps[k] = d

        lastw = self.lastw
        rd = self.rd
        for r in reads:
            for k in r.keys:
                w = lastw.get(k)
                if w is not None:
                    add(w)
        for r in writes:
            for k in r.keys:
                w = lastw.get(k)
                if w is not None:
                    add(w)
                rr = rd.get(k)
                if rr:
                    for x in rr.values():
                        add(x)
        if dma is not None:
            p = self.streams.get(dma)
            if p is not None:
                add(p)
            self.streams[dma] = o
        mykey = ("d", dma) if dma is not None else eng
        if eng == "pe" and dma is None:
            deps.pop("pe", None)
        o.deps = deps
        for r in reads:
            for k in r.keys:
                d = rd.get(k)
                if d is None:
                    rd[k] = {mykey: o}
                else:
                    d[mykey] = o
        for r in writes:
            for k in r.keys:
                lastw[k] = o
                rd[k] = {}
        self.ops.append(o)
        return o

    def emit(self, final_waits=()):
        nc = self.nc
        by_eng = {e: [o for o in self.ops if o.eng == e] for e in ENGS}
        prev_front = {e: {} for e in ENGS}
        fstart = {}
        for o in self.ops:
            f = dict(prev_front[o.eng])
            w = []
            for k, d in sorted(o.deps.items(), key=lambda kv: -kv[1].idx):
                if f.get(k, -1) >= d.idx:
                    continue
                d.signal = True
                w.append(d)
                for k2, v2 in fstart[d.idx].items():
                    if f.get(k2, -1) < v2:
                        f[k2] = v2
                f[k] = d.idx
            o.waits = w
            fstart[o.idx] = f
            prev_front[o.eng] = f
        for o in final_waits:
            o.signal = True
        cnt = {e: 0 for e in ENGS}
        streams = {}
        for o in self.ops:
            if o.dma is not None:
                st = streams.get(o.dma)
                if st is None:
                    st = streams[o.dma] = [len(streams), 0]
                st[1] += 16
                o.dsem = st[0]
                o.dval = st[1]
            elif o.signal:
                cnt[o.eng] += 1
                o.count = cnt[o.eng]
        with contextlib.ExitStack() as es:
            esem = {e: es.enter_context(nc.semaphore("s_" + e)) for e in ENGS}
            dsem = [es.enter_context(nc.semaphore("d%d" % i)) for i in range(len(streams))]
            block = es.enter_context(nc.Block())

            def run(ename, eng):
                for o in by_eng[ename]:
                    for d in o.waits:
                        if d.dma is not None:
                            eng.wait_ge(dsem[d.dsem], d.dval)
                        else:
                            eng.wait_ge(esem[d.eng], d.count)
                    ins = o.fn(eng)
                    if o.dma is not None:
                        ins.then_inc(dsem[o.dsem], 16)
                    elif o.signal:
                        ins.then_inc(esem[ename], 1)
                if ename == "sp":
                    for o in final_waits:
                        if o.dma is not None:
                            eng.wait_ge(dsem[o.dsem], o.dval)
                        else:
                            eng.wait_ge(esem[o.eng], o.count)

            @block.tensor
            def _(e):
                run("pe", e)

            @block.scalar
            def _(e):
                run("act", e)

            @block.vector
            def _(e):
                run("dve", e)

            @block.gpsimd
            def _(e):
                run("pool", e)

            @block.sync
            def _(e):
                run("sp", e)


class Tile:
    def __init__(self, base_ap, off, shape, dt, parts=128, space="S", bank=0):
        self.off = off
        self.shape = tuple(shape)
        self.es = 4 if dt == F32 else 2
        self.parts = parts
        self.space = space
        self.bank = bank
        n = 1
        for s in shape:
            n *= s
        self.n = n
        if space == "S":
            v = base_ap[0:parts, off // 2: off // 2 + n * self.es // 2]
            if dt == F32:
                v = v.bitcast(F32)
        else:
            v = base_ap[0:parts, 0:n]
        if len(shape) == 2:
            v = v.rearrange("p (a b) -> p a b", a=shape[0])
        elif len(shape) == 3:
            v = v.rearrange("p (a b c) -> p a b c", a=shape[0], b=shape[1])
        self.ap = v

    def __getitem__(self, idx):
        if not isinstance(idx, tuple):
            idx = (idx,)
        ps = idx[0]
        fidx = list(idx[1:])
        while len(fidx) < len(self.shape):
            fidx.append(slice(None))
        ap = self.ap[(ps,) + tuple(fidx)]
        rng = []
        for i, s in zip(fidx, self.shape):
            if isinstance(i, int):
                rng.append((i, i + 1))
            else:
                a = 0 if i.start is None else i.start
                b = s if i.stop is None else i.stop
                rng.append((a, b))
        strides = []
        acc = 1
        for s in reversed(self.shape):
            strides.append(acc)
            acc *= s
        strides = strides[::-1]
        keys = set()
        outer = [()]
        for (a, b) in rng[:-1]:
            outer = [o + (i,) for o in outer for i in range(a, b)]
        la, lb = rng[-1]
        gr = GR if self.space == "S" else 512
        for o in outer:
            base = sum(i * st for i, st in zip(o, strides[:-1]))
            b0 = self.off + (base + la) * self.es
            b1 = self.off + (base + lb) * self.es
            for g in range(b0 // gr, (b1 - 1) // gr + 1):
                keys.add((self.space, self.bank, g))
        return Ref(ap, keys)


def build_program(nc, stop_after=99):
    dt_in = {}

    def din(name, shape):
        t = nc.dram_tensor(name, list(shape), F32, kind="ExternalInput").ap()
        dt_in[name] = t
        return t

    x = din("x", [SEQ, D])
    p = din("p", [2, SEQ, 256])
    ln_mix_a = din("ln_mix_a", [1, D])
    w_in_a = din("w_in_a", [1, D, 2 * D])
    g_v_a = din("g_v_a", [1, D])
    w_spatial = din("w_spatial", [1, 8, 128, 128])
    b_spatial = din("b_spatial", [1, 8, 128])
    w_out_a = din("w_out_a", [1, D, D])
    ln_kv = din("ln_kv", [D])
    w_kv = din("w_kv", [D, 2 * D])
    g_k = din("g_k", [64])
    ln_mix_b = din("ln_mix_b", [1, D])
    w_q = din("w_q", [1, D, D])
    g_q = din("g_q", [1, 64])
    w_out_b = din("w_out_b", [1, D, D])
    ln_mlp = din("ln_mlp", [2, D])
    w_up = din("w_up", [2, D, 4 * D])
    w_down = din("w_down", [2, 4 * D, D])
    ln_ple = din("ln_ple", [2, D])
    w_ple_gate = din("w_ple_gate", [2, D, D])
    w_ple_proj = din("w_ple_proj", [2, 256, D])
    c_ident = din("c_ident", [128, 128])
    c_maskle = din("c_maskle", [128, 128])
    c_pack = din("c_pack", [128, 784])
    c_ssel = din("c_ssel", [128, 2048])
    y = nc.dram_tensor("y", [SEQ, D], F32, kind="ExternalOutput").ap()

    es = contextlib.ExitStack()
    with es:
        ARENA_BYTES = 212736
        arena = es.enter_context(nc.sbuf_tensor("arena", [128, ARENA_BYTES // 2], BF16))
        banks = [es.enter_context(nc.psum_tensor("ps%d" % i, [128, 512], F32)) for i in range(8)]
        PB = [Tile(banks[i], 0, (512,), F32, space="P", bank=i) for i in range(8)]
        PB4 = [Tile(banks[i], 0, (4, 128), F32, space="P", bank=i) for i in range(8)]
        S = Sched(nc)

        def T(off, shape, dt, parts=128):
            return Tile(arena, off, shape, dt, parts=parts)

        XT = T(0, (8, SEQ), F32)
        HT = T(65536, (8, SEQ), BF16)
        WOFF = [98304 + i * 16384 for i in range(4)]
        SCR0 = 163840
        coff = ARENA_BYTES
        def calloc(nbytes):
            nonlocal coff
            coff -= (nbytes + 255) // 256 * 256
            return coff
        IDENT = T(calloc(512), (128,), F32)
        MASKLE = T(calloc(512), (128,), F32)
        CP = T(calloc(1568), (784,), BF16)
        SSEL = T(calloc(4096), (16, 128), BF16)
        GCOLS = T(calloc(256), (64,), F32)
        BSROW = T(calloc(2048), (1024,), BF16, parts=1)
        SMALL = T(calloc(256), (64,), F32)
        WP = T(calloc(4096), (2, 1024), BF16)
        SCR_END = coff
        ONES_S = lambda: CP[:, 0:128]
        ONES1 = lambda ps=slice(0, 128): CP[ps, 128:256]
        BDIAG = lambda: CP[:, 256:384]
        TRINEG = lambda: CP[:, 384:512]
        MASK01 = lambda: CP[:, 512:640]
        ESEL = lambda kb: CP[:, 640 + 15 - kb: 640 + 15 - kb + 128]

        scr = [SCR0]

        def salloc(shape, dt, parts=128):
            n = 1
            for s in shape:
                n *= s
            nb = n * (4 if dt == F32 else 2)
            nb = (nb + 255) // 256 * 256
            off = scr[0]
            scr[0] += nb
            assert scr[0] <= SCR_END, ("scratch overflow", scr[0], SCR_END)
            return T(off, shape, dt, parts=parts)

        def sreset():
            scr[0] = SCR0

        def act(out, in_, func, bias=None, scale=None, accum=None, extra=()):
            kw = {}
            if bias is not None:
                kw["bias"] = bias.ap if isinstance(bias, Ref) else bias
            if scale is not None:
                kw["scale"] = scale.ap if isinstance(scale, Ref) else scale
            if accum is not None:
                kw["accum_out"] = accum.ap
            rds = [in_] + [r for r in (bias, scale) if isinstance(r, Ref)] + list(extra)
            wr = [out] + ([accum] if accum is not None else [])
            return S.op("act", lambda e: e.activation(out.ap, in_.ap, func, **kw), reads=rds, writes=wr)

        def tt(eng, out, a, b, op):
            return S.op(eng, lambda e: e.tensor_tensor(out.ap, a.ap, b.ap, op), reads=[a, b], writes=[out])

        def stt(out, a, sc, b, op0, op1):
            scv = sc.ap if isinstance(sc, Ref) else sc
            rds = [a, b] + ([sc] if isinstance(sc, Ref) else [])
            return S.op("dve", lambda e: e.scalar_tensor_tensor(out.ap, a.ap, scv, b.ap, op0, op1), reads=rds, writes=[out])

        def ts(eng, out, a, s1, s2, op0, op1=None):
            s1v = s1.ap if isinstance(s1, Ref) else s1
            s2v = s2.ap if isinstance(s2, Ref) else s2
            rds = [a] + [r for r in (s1, s2) if isinstance(r, Ref)]
            if op1 is None:
                return S.op(eng, lambda e: e.tensor_scalar(out.ap, a.ap, s1v, None, op0), reads=rds, writes=[out])
            return S.op(eng, lambda e: e.tensor_scalar(out.ap, a.ap, s1v, s2v, op0, op1), reads=rds, writes=[out])

        def cp(eng, out, in_):
            if eng == "act":
                return act(out, in_, AF.Copy)
            return S.op(eng, lambda e: e.tensor_copy(out.ap, in_.ap), reads=[in_], writes=[out])

        def recip(out, in_):
            return S.op("dve", lambda e: e.reciprocal(out.ap, in_.ap), reads=[in_], writes=[out])

        def mm(out, pairs, start=True, stop=True):
            n = len(pairs)

            def fn(e):
                ins = None
                for i, (l, r) in enumerate(pairs):
                    ins = e.matmul(out.ap, l.ap, r.ap, start=(start and i == 0), stop=(stop and i == n - 1))
                return ins
            rds = []
            for l, r in pairs:
                rds.append(l)
                rds.append(r)
            return S.op("pe", fn, reads=rds, writes=[out])

        def tr(out, in_, ident):
            return S.op("pe", lambda e: e.transpose(out.ap, in_.ap, ident.ap), reads=[in_, ident], writes=[out])

        def dma(eng, out, in_ap, stream, reads=()):
            return S.op(eng, lambda e: e.dma_start(out=out.ap, in_=in_ap), reads=list(reads), writes=[out], dma=stream)

        class Rot:
            def __init__(self, ids):
                self.ids = list(ids)
                self.i = 0

            def next(self):
                b = self.ids[self.i % len(self.ids)]
                self.i += 1
                return b

        rot = Rot(range(8))
        tog = [0]

        def evac_eng():
            tog[0] ^= 1
            return "act" if tog[0] else "dve"

        def wslot(n, shape):
            return T(WOFF[n % 4], shape, BF16)

        def kp(ap2d):
            return ap2d.rearrange("(k p) f -> p k f", p=128)

        loaders = []

        def L_full(src2d, c0, c1):
            def f(n):
                w = wslot(n, (8, c1 - c0))
                dma("pool", w[:, :, :], kp(src2d)[:, :, c0:c1], "W%da" % (n % 4))
            return f

        def L_mlp(l, gi):
            def f(n):
                wu = Tile(arena, WOFF[n % 4], (8, 512), BF16)
                wd = Tile(arena, WOFF[n % 4] + 8192, (4, 1024), BF16)
                dma("pool", wu[:, :, :], kp(w_up[l])[:, :, gi * 512:(gi + 1) * 512], "W%da" % (n % 4))
                dma("pool", wd[:, :, :], kp(w_down[l][gi * 512:(gi + 1) * 512, :]), "W%db" % (n % 4))
            return f

        def L_first(n):
            w = wslot(n, (8, 1024))
            for i, sfx in enumerate("abcd"):
                dma("pool", w[:, :, i * 256:(i + 1) * 256], kp(w_in_a[0])[:, :, i * 256:(i + 1) * 256], "W0" + sfx)
        loaders.append(L_first)
        loaders.append(L_full(w_in_a[0], 1024, 2048))
        loaders.append(L_full(w_out_a[0], 0, 1024))
        for gi in range(8):
            loaders.append(L_mlp(0, gi))
        loaders.append(L_full(w_ple_gate[0], 0, 1024))
        loaders += [None, None, None]
        for gi in range(8):
            loaders.append(L_mlp(1, gi))
        loaders.append(L_full(w_ple_gate[1], 0, 1024))
        issued = [0]

        def w_issue_upto(n):
            while issued[0] <= n and issued[0] < len(loaders):
                f = loaders[issued[0]]
                if f is not None:
                    f(issued[0])
                issued[0] += 1

        def w_done(n):
            w_issue_upto(n + 4)

        sreset()
        dma("sp", IDENT[:, :], c_ident, "c0")
        GST_A = T(97280, (128,), F32, parts=8)
        GST_B = T(97792, (128,), F32, parts=64)
        dma("sp", GST_A[0:8, :], ln_mix_a[0].rearrange("(k f) -> k f", f=128), "g0")
        dma("pool", CP[:, :], c_pack, "c2")
        dma("pool", BSROW[0:1, :], b_spatial[0].rearrange("(o g) t -> o (g t)", o=1), "c4")
        w_issue_upto(2)
        dma("pool", SSEL[:, :, :], c_ssel.rearrange("p (a b) -> p a b", a=16), "c3")
        dma("pool", WP[:, :, :], kp(w_ple_proj[0]), "wp")
        gi_ = 0
        for src in (ln_mlp[0], ln_ple[0], ln_kv, ln_mix_b[0], ln_mlp[1], ln_ple[1]):
            dma("act", GST_B[gi_ * 8:(gi_ + 1) * 8, :], src.rearrange("(k f) -> k f", f=128), "g%d" % (gi_ + 1))
            gi_ += 1
        gk2 = g_k.rearrange("(o f) -> o f", o=1)
        dma("act", GST_B[48:49, 0:64], gk2, "g7")
        dma("act", GST_B[48:49, 64:128], gk2, "g8")
        dma("act", GST_B[49:50, 0:64], g_q, "g9")
        dma("act", GST_B[49:50, 64:128], g_q, "g10")
        b = rot.next()
        tr(PB[b][:, 0:8], GST_A[0:8, :], IDENT[0:8, 0:8])
        cp("dve", GCOLS[:, 0:8], PB[b][:, 0:8])

        def gcols_rest():
            b = rot.next()
            tr(PB[b][:, 0:50], GST_B[0:50, :], IDENT[0:50, 0:50])
            cp("dve", GCOLS[:, 8:58], PB[b][:, 0:50])
            ts("dve", GCOLS[:, 58:59], GCOLS[:, 57:58], 0.125, None, ALU.mult)

        xs = [T(WOFF[3] + i * 4096, (1024,), F32) for i in range(4)]

        def xload(t):
            for blk in range(t * 4, t * 4 + 4):
                st = xs[blk % 4]
                dma("sp", st[:, :], x[blk * 128:(blk + 1) * 128, :], "x%d" % (blk % 4))
                for half in range(2):
                    b = rot.next()
                    for cc in range(4):
                        c = half * 4 + cc
                        tr(PB4[b][:, cc, :], st[:, c * 128:(c + 1) * 128], IDENT[:, :])
                    cp(evac_eng(), XT[:, half * 4:(half + 1) * 4, blk * 128:(blk + 1) * 128], PB4[b][:, :, :])

        xload(0)
        dma("sp", MASKLE[:, :], c_maskle, "c1")
        if stop_after < 1:
            for t in range(1, NT):
                xload(t)
            gcols_rest()

        def norm_A(t, sq):
            tl = slice(t * TT, (t + 1) * TT)
            for c in range(8):
                act(sq[:, c, :], XT[:, c, tl], AF.Square)

        def norm_B(t, gcol0, sq, sd, rstd, dst=HT):
            tl = slice(t * TT, (t + 1) * TT)
            b = rot.next()
            mm(PB[b][:, :], [(ONES_S(), sq[:, c, :]) for c in range(8)])
            act(sd[:, :], PB[b][:, :], AF.Ln, bias=EPS)
            act(rstd[:, :], sd[:, :], AF.Exp, scale=-0.5)
            for c in range(8):
                if gcol0 is None:
                    tt("dve", dst[:, c, tl], XT[:, c, tl], rstd[:, :], ALU.mult)
                else:
                    stt(dst[:, c, tl], XT[:, c, tl], GCOLS[:, gcol0 + c:gcol0 + c + 1], rstd[:, :], ALU.mult, ALU.mult)

        def norm_tile(t, gcol0, sq, sd, rstd, dst=HT):
            norm_A(t, sq)
            norm_B(t, gcol0, sq, sd, rstd, dst)

        if stop_after >= 1:
            sreset()
            uT = salloc((8, TT), BF16)
            sq = uT
            yT = salloc((8, TT), BF16)
            v16 = [salloc((1024,), BF16) for _ in range(2)]
            vn = [salloc((1024,), BF16) for _ in range(2)]
            sd = salloc((TT,), F32)
            rstd = salloc((TT,), F32)
            wsT = salloc((8, 128), BF16)
            gvt = salloc((1024,), F32)
            wsl = T(yT.off, (8, 128), F32)
            dma("sp", gvt[:, :], g_v_a.partition_broadcast(128), "gv")
            dma("sp", wsl[:, :, :], w_spatial[0].rearrange("g t s -> t g s"), "ws")
            def ws_prep():
                for hb in range(2):
                    b = rot.next()
                    for gg in range(4):
                        tr(PB4[b][:, gg, :], wsl[:, hb * 4 + gg, :], IDENT[:, :])
                    tt("dve", wsT[:, hb * 4:(hb + 1) * 4, :], PB4[b][:, :, :],
                       Ref(MASKLE.ap.unsqueeze(1).to_broadcast([128, 4, 128]), MASKLE[:, :].keys), ALU.mult)
            W0 = wslot(0, (8, 1024))
            W1 = wslot(1, (8, 1024))
            W2 = wslot(2, (8, 1024))
            norm_A(0, sq)
            norm_B(0, G_MIXA, sq, sd, rstd)
            for t in range(NT):
                tl = slice(t * TT, (t + 1) * TT)
                if t + 1 < NT:
                    xload(t + 1)
                    if t + 1 == NT - 1:
                        w_issue_upto(3)
                for f in range(8):
                    b = rot.next()
                    mm(PB[b][:, :], [(W0[:, k, f * 128:(f + 1) * 128], HT[:, k, tl]) for k in range(8)])
                    act(uT[:, f, :], PB[b][:, :], AF.Gelu_apprx_tanh)
                if t == 0:
                    gcols_rest()
                    ws_prep()

                def VST(blk, t=t):
                    tb = t * 4 + blk
                    tbl = slice(tb * 128, (tb + 1) * 128)
                    for half in range(2):
                        b = rot.next()
                        mm(PB[b][:, :], [(HT[:, k, tbl], W1[:, k, half * 512:(half + 1) * 512]) for k in range(8)])
                        act(v16[blk % 2][:, half * 512:(half + 1) * 512], PB[b][:, :], AF.Gelu_apprx_tanh)

                def NST(blk):
                    vb = vn[blk % 2]
                    vv = v16[blk % 2]
                    c4 = (blk % 2) * 4
                    act(vb[:, :], vv[:, :], AF.Square, accum=SMALL[:, c4:c4 + 1])
                    ts("dve", SMALL[:, c4 + 1:c4 + 2], SMALL[:, c4:c4 + 1], 1.0 / 1024, EPS, ALU.mult, ALU.add)
                    act(SMALL[:, c4 + 2:c4 + 3], SMALL[:, c4 + 1:c4 + 2], AF.Sqrt)
                    recip(SMALL[:, c4 + 3:c4 + 4], SMALL[:, c4 + 2:c4 + 3])
                    stt(vb[:, :], vv[:, :], SMALL[:, c4 + 3:c4 + 4], gvt[:, :], ALU.mult, ALU.mult)

                def SPST(blk):
                    vb = vn[blk % 2]
                    for hb in range(2):
                        b = rot.next()

                        def fn(e, b=b, hb=hb, vb=vb):
                            ins = None
                            for gg in range(4):
                                g = hb * 4 + gg
                                e.matmul(PB4[b][:, gg, :].ap, vb[:, g * 128:(g + 1) * 128].ap, wsT[:, g, :].ap,
                                         start=True, stop=False)
                                ins = e.matmul(PB4[b][:, gg, :].ap, ONES1(slice(0, 1)).ap,
                                               BSROW[0:1, g * 128:(g + 1) * 128].ap, start=False, stop=True)
                            return ins
                        S.op("pe", fn, reads=[vb[:, hb * 512:(hb + 1) * 512], wsT[:, hb * 4:(hb + 1) * 4, :],
                                              ONES1(slice(0, 1)), BSROW[0:1, :]], writes=[PB4[b][:, :, :]])
                        tt("dve", yT[:, hb * 4:(hb + 1) * 4, blk * 128:(blk + 1) * 128], PB4[b][:, :, :],
                           uT[:, hb * 4:(hb + 1) * 4, blk * 128:(blk + 1) * 128], ALU.mult)

                VST(0)
                VST(1)
                NST(0)
                for blk in range(4):
                    if blk + 2 < 4:
                        VST(blk + 2)
                    if blk + 1 < 4:
                        NST(blk + 1)
                    SPST(blk)
                if t + 1 < NT:
                    norm_A(t + 1, sq)
                for dch in range(8):
                    if dch == 4 and t + 1 < NT:
                        norm_B(t + 1, G_MIXA, sq, sd, rstd)
                    b = rot.next()
                    mm(PB[b][:, :], [(W2[:, k, dch * 128:(dch + 1) * 128], yT[:, k, :]) for k in range(8)])
                    tt("dve", XT[:, dch, tl], PB[b][:, :], XT[:, dch, tl], ALU.add)
            w_done(0)
            w_done(1)
            w_done(2)

        def mlp_phase(l, blk0, gcol0):
            sreset()
            sq = salloc((8, TT), BF16)
            sd = salloc((TT,), F32)
            rstd = salloc((TT,), F32)
            aT = [salloc((8, TT), BF16) for _ in range(2)]
            rt = [salloc((TT,), BF16) for _ in range(2)]
            norm_tile(0, gcol0, sq, sd, rstd)

            def wts(sg):
                nA, nB = blk0 + 2 * sg, blk0 + 2 * sg + 1
                wu = [Tile(arena, WOFF[n % 4], (8, 512), BF16) for n in (nA, nB)]
                wd = [Tile(arena, WOFF[n % 4] + 8192, (4, 1024), BF16) for n in (nA, nB)]
                return wu, wd

            def up(k):
                sg, t = divmod(k, NT)
                wu, wd = wts(sg)
                tl = slice(t * TT, (t + 1) * TT)
                a = aT[k % 2]
                for f in range(8):
                    b = rot.next()
                    mm(PB[b][:, :], [(wu[f // 4][:, kk, (f % 4) * 128:(f % 4 + 1) * 128], HT[:, kk, tl]) for kk in range(8)])
                    r = rt[f % 2]
                    act(r[:, :], PB[b][:, :], AF.Relu)
                    tt("dve", a[:, f, :], r[:, :], r[:, :], ALU.mult)
                if sg == 0 and t + 1 < NT:
                    norm_A(t + 1, sq)

            def down(k):
                sg, t = divmod(k, NT)
                wu, wd = wts(sg)
                tl = slice(t * TT, (t + 1) * TT)
                a = aT[k % 2]
                for dch in range(8):
                    if dch == 4 and sg == 0 and t + 1 < NT:
                        norm_B(t + 1, gcol0, sq, sd, rstd)
                    b = rot.next()
                    mm(PB[b][:, :], [(wd[f // 4][:, f % 4, dch * 128:(dch + 1) * 128], a[:, f, :]) for f in range(8)])
                    tt("dve", XT[:, dch, tl], PB[b][:, :], XT[:, dch, tl], ALU.add)
                if t == NT - 1:
                    w_done(blk0 + 2 * sg)
                    w_done(blk0 + 2 * sg + 1)

            NI = 4 * NT
            for k in range(NT):
                up(k)
                down(k)
            for k in range(NT, NI):
                up(k)
                if k > NT:
                    down(k - 1)
            down(NI - 1)

        out_fin = []

        def emit_out(t, osb):
            for blk in range(t * 4, t * 4 + 4):
                ot = osb[blk % 2]
                for half in range(2):
                    b = rot.next()
                    for cc in range(4):
                        c = half * 4 + cc
                        tr(PB4[b][:, cc, :], XT[:, c, blk * 128:(blk + 1) * 128], IDENT[:, :])
                    cp(evac_eng(), ot[:, half * 512:(half + 1) * 512], PB[b][:, :])
                out_fin.append(S.op("sp", lambda e, ot=ot, blk=blk: e.dma_start(out=y[blk * 128:(blk + 1) * 128, :], in_=ot[:, :].ap),
                                    reads=[ot[:, :]], dma="o%d" % (blk % 2)))

        def ple_phase(l, blkn, gcol0, with_out=False):
            sreset()
            sq = salloc((8, TT), BF16)
            sd = salloc((TT,), F32)
            rstd = salloc((TT,), F32)
            pst = salloc((4, 256), F32)
            pT = salloc((2, TT), BF16)
            gate = [salloc((TT,), F32) for _ in range(2)]
            tmp2 = [salloc((TT,), F32) for _ in range(2)]
            osb = [salloc((1024,), F32) for _ in range(2)] if with_out else None
            Wg = wslot(blkn, (8, 1024))
            def pload(t):
                dma("sp", pst[:, :, :], p[l][t * TT:(t + 1) * TT, :].rearrange("(b q) f -> q b f", q=128), "pl")
            pload(0)
            for t in range(NT):
                tl = slice(t * TT, (t + 1) * TT)
                if t == 0:
                    norm_tile(0, gcol0, sq, sd, rstd)
                for kk in range(2):
                    b = rot.next()
                    for blk in range(4):
                        tr(PB4[b][:, blk, :], pst[:, blk, kk * 128:(kk + 1) * 128], IDENT[:, :])
                    cp("act", pT[:, kk, :], PB[b][:, :])
                if t + 1 < NT:
                    pload(t + 1)
                for dch in range(8):
                    if dch == 1 and t + 1 < NT:
                        norm_A(t + 1, sq)
                    if dch == 5 and t + 1 < NT:
                        norm_B(t + 1, gcol0, sq, sd, rstd)
                    b1 = rot.next()
                    mm(PB[b1][:, :], [(Wg[:, k, dch * 128:(dch + 1) * 128], HT[:, k, tl]) for k in range(8)])
                    g_ = gate[dch % 2]
                    act(g_[:, :], PB[b1][:, :], AF.Sigmoid)
                    b2 = rot.next()
                    mm(PB[b2][:, :], [(WP[:, kk, dch * 128:(dch + 1) * 128], pT[:, kk, :]) for kk in range(2)])
                    t2 = tmp2[dch % 2]
                    tt("dve", t2[:, :], PB[b2][:, :], g_[:, :], ALU.mult)
                    tt("dve", XT[:, dch, tl], XT[:, dch, tl], t2[:, :], ALU.add)
                if with_out:
                    emit_out(t, osb)
            w_done(blkn)

        PW = [Tile(arena, WOFF[0] + i * 8192, (4096,), BF16) for i in range(2)]

        def pw_views(i):
            t_ = PW[i]
            off = t_.off
            wk = Tile(arena, off, (8, 128), BF16)
            wv = Tile(arena, off + 2048, (8, 128), BF16)
            wq = Tile(arena, off + 4096, (8, 128), BF16)
            wo = Tile(arena, off + 6144, (1024,), BF16)
            return wk, wv, wq, wo

        def load_pair(j):
            wk, wv, wq, wo = pw_views(j % 2)
            sl = slice(j * 128, (j + 1) * 128)
            dma("pool", wk[:, :, :], kp(w_kv)[:, :, j * 128:(j + 1) * 128], "pw%da" % (j % 2))
            dma("pool", wv[:, :, :], kp(w_kv)[:, :, 1024 + j * 128:1024 + (j + 1) * 128], "pw%db" % (j % 2))
            dma("pool", wq[:, :, :], kp(w_q[0])[:, :, j * 128:(j + 1) * 128], "pw%dc" % (j % 2))
            dma("pool", wo[:, :], w_out_b[0][sl, :], "pw%dd" % (j % 2))
            for w_, g0 in ((wk, G_KV), (wv, G_KV), (wq, G_MIXB)):
                gb = Ref(GCOLS.ap[:, g0:g0 + 8].unsqueeze(2).to_broadcast([128, 8, 128]), GCOLS[:, g0:g0 + 8].keys)
                tt("dve", w_[:, :, :], w_[:, :, :], gb, ALU.mult)

        if stop_after >= 2:
            mlp_phase(0, 3, G_MLP0)
            if stop_after >= 4:
                load_pair(0)
        if stop_after >= 3:
            ple_phase(0, 11, G_PLE0)
            dma("pool", WP[:, :, :], kp(w_ple_proj[1]), "wp")

        if stop_after >= 4:
            sreset()
            sq = salloc((8, TT), BF16)
            sd = salloc((TT,), F32)
            rstd = salloc((TT,), F32)
            psq, psd, prstd = sq, sd, rstd
            sreset()
            KT = salloc((SEQ,), BF16)
            QTh = [salloc((SEQ,), BF16) for _ in range(2)]
            VJ = salloc((16, 128), BF16)
            OT = salloc((SEQ,), BF16)
            e1 = [salloc((TT,), F32) for _ in range(2)]
            AT = [salloc((TT,), BF16) for _ in range(3)]
            csb = [salloc((TT,), BF16) for _ in range(2)]
            LS = [Tile(arena, WOFF[1], (16, TT), BF16), Tile(arena, WOFF[2], (16, TT), BF16)]
            sqK = Tile(arena, WOFF[2], (4, TT), BF16)
            sqQ = Tile(arena, WOFF[2] + 4096, (4, TT), BF16)
            sd4 = Tile(arena, WOFF[2] + 8192, (4, TT), F32)
            rawK = Tile(arena, WOFF[1], (4, TT), F32)
            rawQ = Tile(arena, WOFF[1] + 8192, (4, TT), F32)
            srot = Rot(range(5))
            zrot = Rot([0, 1])
            grot = Rot([2, 3])
            frot = Rot([6, 7])
            JUNK = 7

            def dummy(k):
                def fn(e):
                    ins = None
                    for _ in range(k):
                        ins = e.matmul(PB[JUNK][:, :].ap, ONES_S().ap, CP[:, 0:512].ap, start=True, stop=True)
                    return ins
                S.op("pe", fn, reads=[ONES_S(), CP[:, 0:512]], writes=[PB[JUNK][:, :]])
            CSB = 4
            OB = [5, 5]

            pending = []
            for j in range(8):
                wk, wv, wq, wo = pw_views(j % 2)
                KB = [0, 1, 2, 3]
                QB = [4, 5, 6, 7]
                def kq_main_t(w_, raw, sqx, mb, t):
                    tl = slice(t * TT, (t + 1) * TT)
                    mm(PB[mb[t]][:, :], [(w_[:, k, :], HT[:, k, tl]) for k in range(8)])
                    act(raw[:, t, :], PB[mb[t]][:, :], AF.Copy)
                    act(sqx[:, t, :], PB[mb[t]][:, :], AF.Square)

                if j == 0:
                    norm_A(0, psq)
                    norm_B(0, None, psq, psd, prstd)
                    for t in range(NT):
                        if t + 1 < NT:
                            norm_A(t + 1, psq)
                        kq_main_t(wk, rawK, sqK, KB, t)
                        kq_main_t(wq, rawQ, sqQ, QB, t)
                        if t + 1 < NT:
                            norm_B(t + 1, None, psq, psd, prstd)
                    for bi in range(2):
                        S.op("dve", lambda e, bi=bi: e.memset(csb[bi][:, :].ap, 0.0), writes=[csb[bi][:, :]])
                        S.op("dve", lambda e, bi=bi: e.memset(QTh[bi][:, :].ap, 0.0), writes=[QTh[bi][:, :]])
                else:
                    for (w_, raw, sqx, mb) in ((wk, rawK, sqK, KB), (wq, rawQ, sqQ, QB)):
                        for t in range(NT):
                            kq_main_t(w_, raw, sqx, mb, t)
                for (sqx, mb) in ((sqK, KB), (sqQ, QB)):
                    for t in range(NT):
                        mm(PB[mb[t]][:, :], [(BDIAG(), sqx[:, t, :])])
                for t in range(NT):
                    act(sd4[:, t, :], PB[KB[t]][:, :], AF.Ln, bias=EPS)
                for t in range(NT):
                    act(sd4[:, t, :], sd4[:, t, :], AF.Exp, scale=-0.5)
                for t in range(NT):
                    tl = slice(t * TT, (t + 1) * TT)
                    stt(KT[:, tl], rawK[:, t, :], GCOLS[:, G_K:G_K + 1], sd4[:, t, :], ALU.mult, ALU.mult)
                for q4 in range(4):
                    b_ = KB[q4]
                    for bb in range(4):
                        blk = q4 * 4 + bb
                        mm(PB4[b_][:, bb, :], [(HT[:, k, blk * 128:(blk + 1) * 128], wv[:, k, :]) for k in range(8)])
                    cp("dve", VJ[:, q4 * 4:(q4 + 1) * 4, :], PB4[b_][:, :, :])
                for t in range(NT):
                    act(sd4[:, t, :], PB[QB[t]][:, :], AF.Ln, bias=EPS)
                for t in range(NT):
                    act(sd4[:, t, :], sd4[:, t, :], AF.Exp, scale=-0.5)
                for t in range(NT):
                    tl = slice(t * TT, (t + 1) * TT)
                    for h_ in range(2):
                        hp = slice(h_ * 64, (h_ + 1) * 64)
                        stt(QTh[h_][hp, tl], rawQ[hp, t, :], GCOLS[hp, G_QS:G_QS + 1], sd4[hp, t, :], ALU.mult, ALU.mult)
                heads = (slice(0, 64), slice(64, 128))

                def cols(qt, kb):
                    c0 = max(0, kb - 4 * qt) * 128
                    return c0, slice(c0, TT), slice(qt * TT + c0, (qt + 1) * TT)

                def p1_steps(h, qt):
                    nkb = 4 * qt + 4
                    L = LS[h]
                    zb = {}

                    def Z(kb):
                        c0, cs, qs = cols(qt, kb)
                        zb[kb] = zrot.next()
                        mm(PB[zb[kb]][:, cs], [(KT[:, kb * 128:(kb + 1) * 128], QTh[h][:, qs])])

                    def EXP(kb):
                        c0, cs, qs = cols(qt, kb)
                        act(e1[kb % 2][:, cs], PB[zb[kb]][:, cs], AF.Exp)

                    def LN(kb):
                        c0, cs, qs = cols(qt, kb)
                        act(L[:, kb, cs], e1[kb % 2][:, cs], AF.Ln, bias=1.0)
                        if kb >= 4 * qt:
                            tt("pool", L[:, kb, c0:c0 + 128], L[:, kb, c0:c0 + 128], MASK01(), ALU.mult)

                    def CS(kb):
                        c0, cs, qs = cols(qt, kb)
                        mm(PB[CSB][:, cs], [(ESEL(kb), L[:, kb, cs])], start=(kb == 0), stop=(kb == nkb - 1))

                    steps = []
                    for i in range(nkb + 3):
                        def st(i=i):
                            if i < nkb:
                                Z(i)
                            if 0 <= i - 1 < nkb:
                                EXP(i - 1)
                            if 0 <= i - 2 < nkb:
                                LN(i - 2)
                            if 0 <= i - 3 < nkb:
                                CS(i - 3)
                        steps.append(st)

                    def fin():
                        cp("dve", csb[h][0:16, :], PB[CSB][0:16, :])
                    steps.append(fin)
                    return steps

                def p2_steps(h, qt):
                    nkb = 4 * qt + 4
                    L = LS[h]
                    pr = heads[h]
                    gb = {}

                    def G(kb):
                        c0, cs, qs = cols(qt, kb)
                        b = gb[kb] = grot.next()

                        def fn(e):
                            e.matmul(PB[b][:, cs].ap, SSEL[:, kb, :].ap, csb[h][:, cs].ap, start=True, stop=False)
                            e.matmul(PB[b][:, cs].ap, TRINEG().ap, L[:, kb, cs].ap, start=False, stop=False)
                            return e.matmul(PB[b][:, cs].ap, KT[:, kb * 128:(kb + 1) * 128].ap, QTh[h][:, qs].ap,
                                            start=False, stop=True)
                        S.op("pe", fn, reads=[TRINEG(), L[:, kb, cs], SSEL[:, kb, :], csb[h][:, cs],
                                              KT[:, kb * 128:(kb + 1) * 128], QTh[h][:, qs]], writes=[PB[b][:, cs]])

                    def EXPA(kb):
                        c0, cs, qs = cols(qt, kb)
                        a_ = AT[kb % 3]
                        act(a_[:, cs], PB[gb[kb]][:, cs], AF.Exp)
                        if kb >= 4 * qt:
                            tt("pool", a_[:, c0:c0 + 128], a_[:, c0:c0 + 128], MASK01(), ALU.mult)

                    def AV(kb):
                        c0, cs, qs = cols(qt, kb)
                        mm(PB[OB[h]][:, cs], [(VJ[:, kb, :], AT[kb % 3][:, cs])], start=(kb == 0), stop=(kb == nkb - 1))

                    steps = []
                    for i in range(nkb + 1):
                        def st(i=i):
                            if i < nkb:
                                G(i)
                            if 0 <= i - 1 < nkb:
                                EXPA(i - 1)
                                AV(i - 1)
                        steps.append(st)

                    def fin():
                        cp("dve", OT[pr, qt * TT:(qt + 1) * TT], PB[OB[h]][pr, :])
                    steps.append(fin)
                    return steps

                def filler(k):
                    for _ in range(k):
                        if not pending:
                            return
                        wo_, qt_, dch = pending.pop(0)
                        tl = slice(qt_ * TT, (qt_ + 1) * TT)
                        b = frot.next()
                        mm(PB[b][:, :], [(wo_[:, dch * 128:(dch + 1) * 128], OT[:, tl])])
                        tt("dve", XT[:, dch, tl], PB[b][:, :], XT[:, dch, tl], ALU.add)

                units = [(h, qt) for qt in range(NT) for h in range(2)]
                prev = None
                for slot_i, u in enumerate(units + [None]):
                    if slot_i == 4 and j + 1 < 8:
                        while pending and pending[0][0] is not wo:
                            filler(1)
                        load_pair(j + 1)
                    s1 = p1_steps(*u) if u is not None else []
                    s2 = p2_steps(*prev) if prev is not None else []
                    n = max(len(s1), len(s2))
                    for i in range(n):
                        if i < len(s2):
                            s2[i]()
                        if i < len(s1):
                            s1[i]()
                        if i == n // 2 or i == n - 1:
                            filler(2)
                    if prev is not None and prev[0] == 1:
                        pending.extend((wo, prev[1], dch) for dch in range(8))
                    prev = u
                if j == 7:
                    filler(len(pending))
            w_done(12)
            w_done(13)
            w_done(14)

        if stop_after >= 5:
            w_issue_upto(18)
            mlp_phase(1, 15, G_MLP1)
        if stop_after >= 6:
            ple_phase(1, 23, G_PLE1, with_out=True)

        if not out_fin:
            sreset()
            osb = [salloc((1024,), F32) for _ in range(2)]
            for t in range(NT):
                emit_out(t, osb)
        S.emit(final_waits=out_fin[-2:])
    return nc


def make_consts():
    ident = np.eye(128, dtype=np.float32)
    s = np.arange(128)[:, None]
    t = np.arange(128)[None, :]
    maskle = (s <= t).astype(np.float32)
    pack = np.zeros((128, 784), np.float32)
    pack[:, 0:128] = 1.0 / 1024
    pack[:, 128:256] = 1.0
    bd = np.zeros((128, 128), np.float32)
    bd[0:64, 0:64] = 1.0 / 64
    bd[64:128, 64:128] = 1.0 / 64
    pack[:, 256:384] = bd
    pack[:, 384:512] = -(s >= t).astype(np.float32)
    pack[:, 512:640] = (s < t).astype(np.float32)
    pack[:, 640 + 15] = -1.0
    pack[:, 640 + 47] = -1.0
    ssel = np.zeros((128, 16, 128), np.float32)
    for kb in range(16):
        for r in range(16):
            if r > kb:
                ssel[r, kb, :] = 1.0
                ssel[32 + r, kb, :] = 1.0
    return {"c_ident": ident, "c_maskle": maskle, "c_pack": pack, "c_ssel": ssel.reshape(128, 2048)}


_CACHE = {}


def kernel(**inputs):
    n = 8
    if "nc" not in _CACHE:
        nc = bass.Bass("TRN2", target_bir_lowering=False)
        build_program(nc)
        _CACHE["nc"] = nc
    nc = _CACHE["nc"]
    consts = make_consts()
    shared = {k: np.ascontiguousarray(np.asarray(v, dtype=np.float32)) for k, v in inputs.items() if k not in ("x", "p")}
    shared.update(consts)
    x = np.asarray(inputs["x"], dtype=np.float32)
    p = np.asarray(inputs["p"], dtype=np.float32)
    in_maps = []
    for i in range(n):
        m = dict(shared)
        m["x"] = np.ascontiguousarray(x[i])
        m["p"] = np.ascontiguousarray(p[:, i])
        in_maps.append(m)
    res = run_bass_kernel_spmd(nc, in_maps, core_ids=list(range(n)))
    return np.stack([np.asarray(r["y"], dtype=np.float32) for r in res.results], axis=0)
```

```python
import contextlib
import numpy as np
import concourse.bass as bass
import concourse.mybir as mybir
from concourse.bass_utils import run_bass_kernel_spmd

F32 = mybir.dt.float32
BF16 = mybir.dt.bfloat16
AF = mybir.ActivationFunctionType
ALU = mybir.AluOpType

SEQ = 2048
D = 1024
NT = 4
TT = 512
NC8 = 8
EPS = 1e-6
GR = 256
ENGS = ("pe", "act", "dve", "pool", "sp")

G_MIXA, G_MLP0, G_PLE0, G_KV, G_MIXB, G_MLP1, G_PLE1, G_K, G_Q, G_QS = 0, 8, 16, 24, 32, 40, 48, 56, 57, 58


class Op:
    __slots__ = ("eng", "fn", "deps", "signal", "idx", "dma", "dsem", "dval", "count", "waits")

    def __init__(self, eng, fn, dma):
        self.eng = eng
        self.fn = fn
        self.dma = dma
        self.signal = False
        self.count = 0
        self.dsem = 0
        self.dval = 0
        self.deps = None
        self.waits = None


class Ref:
    __slots__ = ("ap", "keys")

    def __init__(self, ap, keys):
        self.ap = ap
        self.keys = keys


class Sched:
    def __init__(self, nc):
        self.nc = nc
        self.ops = []
        self.lastw = {}
        self.rd = {}
        self.streams = {}

    def op(self, eng, fn, reads=(), writes=(), dma=None):
        o = Op(eng, fn, dma)
        o.idx = len(self.ops)
        deps = {}

        def add(d):
            k = ("d", d.dma) if d.dma is not None else d.eng
            p = deps.get(k)
            if p is None or p.idx < d.idx:
                deps[k] = d

        lastw = self.lastw
        rd = self.rd
        for r in reads:
            for k in r.keys:
                w = lastw.get(k)
                if w is not None:
                    add(w)
        for r in writes:
            for k in r.keys:
                w = lastw.get(k)
                if w is not None:
                    add(w)
                rr = rd.get(k)
                if rr:
                    for x in rr.values():
                        add(x)
        if dma is not None:
            p = self.streams.get(dma)
            if p is not None:
                add(p)
            self.streams[dma] = o
        mykey = ("d", dma) if dma is not None else eng
        if eng == "pe" and dma is None:
            deps.pop("pe", None)
        o.deps = deps
        for r in reads:
            for k in r.keys:
                d = rd.get(k)
                if d is None:
                    rd[k] = {mykey: o}
                else:
                    d[mykey] = o
        for r in writes:
            for k in r.keys:
                lastw[k] = o
                rd[k] = {}
        self.ops.append(o)
        return o

    def emit(self, final_waits=()):
        nc = self.nc
        by_eng = {e: [o for o in self.ops if o.eng == e] for e in ENGS}
        prev_front = {e: {} for e in ENGS}
        fstart = {}
        for o in self.ops:
            f = dict(prev_front[o.eng])
            w = []
            for k, d in sorted(o.deps.items(), key=lambda kv: -kv[1].idx):
                if f.get(k, -1) >= d.idx:
                    continue
                d.signal = True
                w.append(d)
                for k2, v2 in fstart[d.idx].items():
                    if f.get(k2, -1) < v2:
                        f[k2] = v2
                f[k] = d.idx
            o.waits = w
            fstart[o.idx] = f
            prev_front[o.eng] = f
        for o in final_waits:
            o.signal = True
        cnt = {e: 0 for e in ENGS}
        streams = {}
        for o in self.ops:
            if o.dma is not None:
                st = streams.get(o.dma)
                if st is None:
                    st = streams[o.dma] = [len(streams), 0]
                st[1] += 16
                o.dsem = st[0]
                o.dval = st[1]
            elif o.signal:
                cnt[o.eng] += 1
                o.count = cnt[o.eng]
        with contextlib.ExitStack() as es:
            esem = {e: es.enter_context(nc.semaphore("s_" + e)) for e in ENGS}
            dsem = [es.enter_context(nc.semaphore("d%d" % i)) for i in range(len(streams))]
            block = es.enter_context(nc.Block())

            def run(ename, eng):
                for o in by_eng[ename]:
                    for d in o.waits:
                        if d.dma is not None:
                            eng.wait_ge(dsem[d.dsem], d.dval)
                        else:
                            eng.wait_ge(esem[d.eng], d.count)
                    ins = o.fn(eng)
                    if o.dma is not None:
                        ins.then_inc(dsem[o.dsem], 16)
                    elif o.signal:
                        ins.then_inc(esem[ename], 1)
                if ename == "sp":
                    for o in final_waits:
                        if o.dma is not None:
                            eng.wait_ge(dsem[o.dsem], o.dval)
                        else:
                            eng.wait_ge(esem[o.eng], o.count)

            @block.tensor
            def _(e):
                run("pe", e)

            @block.scalar
            def _(e):
                run("act", e)

            @block.vector
            def _(e):
                run("dve", e)

            @block.gpsimd
            def _(e):
                run("pool", e)

            @block.sync
            def _(e):
                run("sp", e)


class Tile:
    def __init__(self, base_ap, off, shape, dt, parts=128, space="S", bank=0):
        self.off = off
        self.shape = tuple(shape)
        self.es = 4 if dt == F32 else 2
        self.parts = parts
        self.space = space
        self.bank = bank
        n = 1
        for s in shape:
            n *= s
        self.n = n
        if space == "S":
            v = base_ap[0:parts, off // 2: off // 2 + n * self.es // 2]
            if dt == F32:
                v = v.bitcast(F32)
        else:
            v = base_ap[0:parts, 0:n]
        if len(shape) == 2:
            v = v.rearrange("p (a b) -> p a b", a=shape[0])
        elif len(shape) == 3:
            v = v.rearrange("p (a b c) -> p a b c", a=shape[0], b=shape[1])
        self.ap = v

    def __getitem__(self, idx):
        if not isinstance(idx, tuple):
            idx = (idx,)
        ps = idx[0]
        fidx = list(idx[1:])
        while len(fidx) < len(self.shape):
            fidx.append(slice(None))
        ap = self.ap[(ps,) + tuple(fidx)]
        rng = []
        for i, s in zip(fidx, self.shape):
            if isinstance(i, int):
                rng.append((i, i + 1))
            else:
                a = 0 if i.start is None else i.start
                b = s if i.stop is None else i.stop
                rng.append((a, b))
        strides = []
        acc = 1
        for s in reversed(self.shape):
            strides.append(acc)
            acc *= s
        strides = strides[::-1]
        keys = set()
        outer = [()]
        for (a, b) in rng[:-1]:
            outer = [o + (i,) for o in outer for i in range(a, b)]
        la, lb = rng[-1]
        gr = GR if self.space == "S" else 512
        for o in outer:
            base = sum(i * st for i, st in zip(o, strides[:-1]))
            b0 = self.off + (base + la) * self.es
            b1 = self.off + (base + lb) * self.es
            for g in range(b0 // gr, (b1 - 1) // gr + 1):
                keys.add((self.space, self.bank, g))
        return Ref(ap, keys)


def build_program(nc, stop_after=99):
    dt_in = {}

    def din(name, shape):
        t = nc.dram_tensor(name, list(shape), F32, kind="ExternalInput").ap()
        dt_in[name] = t
        return t

    x = din("x", [SEQ, D])
    p = din("p", [2, SEQ, 256])
    ln_mix_a = din("ln_mix_a", [1, D])
    w_in_a = din("w_in_a", [1, D, 2 * D])
    g_v_a = din("g_v_a", [1, D])
    w_spatial = din("w_spatial", [1, 8, 128, 128])
    b_spatial = din("b_spatial", [1, 8, 128])
    w_out_a = din("w_out_a", [1, D, D])
    ln_kv = din("ln_kv", [D])
    w_kv = din("w_kv", [D, 2 * D])
    g_k = din("g_k", [64])
    ln_mix_b = din("ln_mix_b", [1, D])
    w_q = din("w_q", [1, D, D])
    g_q = din("g_q", [1, 64])
    w_out_b = din("w_out_b", [1, D, D])
    ln_mlp = din("ln_mlp", [2, D])
    w_up = din("w_up", [2, D, 4 * D])
    w_down = din("w_down", [2, 4 * D, D])
    ln_ple = din("ln_ple", [2, D])
    w_ple_gate = din("w_ple_gate", [2, D, D])
    w_ple_proj = din("w_ple_proj", [2, 256, D])
    c_ident = din("c_ident", [128, 128])
    c_maskle = din("c_maskle", [128, 128])
    c_pack = din("c_pack", [128, 1040])
    c_ssel = din("c_ssel", [128, 2048])
    y = nc.dram_tensor("y", [SEQ, D], F32, kind="ExternalOutput").ap()

    es = contextlib.ExitStack()
    with es:
        ARENA_BYTES = 212736
        arena = es.enter_context(nc.sbuf_tensor("arena", [128, ARENA_BYTES // 2], BF16))
        banks = [es.enter_context(nc.psum_tensor("ps%d" % i, [128, 512], F32)) for i in range(8)]
        PB = [Tile(banks[i], 0, (512,), F32, space="P", bank=i) for i in range(8)]
        PB4 = [Tile(banks[i], 0, (4, 128), F32, space="P", bank=i) for i in range(8)]
        S = Sched(nc)

        def T(off, shape, dt, parts=128):
            return Tile(arena, off, shape, dt, parts=parts)

        XT = T(0, (8, SEQ), F32)
        HT = T(65536, (8, SEQ), BF16)
        WOFF = [98304 + i * 16384 for i in range(4)]
        SCR0 = 163840
        coff = ARENA_BYTES
        def calloc(nbytes):
            nonlocal coff
            coff -= (nbytes + 255) // 256 * 256
            return coff
        IDENT = T(calloc(512), (128,), F32)
        MASKLE = T(calloc(512), (128,), F32)
        CP = T(calloc(2080), (1040,), BF16)
        SSEL = T(calloc(4096), (16, 128), BF16)
        GCOLS = T(calloc(256), (64,), F32)
        BSROW = T(calloc(2048), (1024,), BF16, parts=1)
        SMALL = T(calloc(256), (64,), F32)
        WP = T(calloc(4096), (2, 1024), BF16)
        SCR_END = coff
        ONES_S = lambda: CP[:, 0:128]
        ONES1 = lambda ps=slice(0, 128): CP[ps, 128:256]
        BDIAG = lambda: CP[:, 256:384]
        TRINEG = lambda: CP[:, 384:512]
        MASK01 = lambda: CP[:, 512:640]
        IDB = lambda: CP[:, 784:912]
        MBIAS = lambda: CP[:, 912:1040]
        ESEL = lambda kb: CP[:, 640 + 15 - kb: 640 + 15 - kb + 128]

        scr = [SCR0]

        def salloc(shape, dt, parts=128):
            n = 1
            for s in shape:
                n *= s
            nb = n * (4 if dt == F32 else 2)
            nb = (nb + 255) // 256 * 256
            off = scr[0]
            scr[0] += nb
            assert scr[0] <= SCR_END, ("scratch overflow", scr[0], SCR_END)
            return T(off, shape, dt, parts=parts)

        def sreset():
            scr[0] = SCR0

        def act(out, in_, func, bias=None, scale=None, accum=None, extra=()):
            kw = {}
            if bias is not None:
                kw["bias"] = bias.ap if isinstance(bias, Ref) else bias
            if scale is not None:
                kw["scale"] = scale.ap if isinstance(scale, Ref) else scale
            if accum is not None:
                kw["accum_out"] = accum.ap
            rds = [in_] + [r for r in (bias, scale) if isinstance(r, Ref)] + list(extra)
            wr = [out] + ([accum] if accum is not None else [])
            return S.op("act", lambda e: e.activation(out.ap, in_.ap, func, **kw), reads=rds, writes=wr)

        def tt(eng, out, a, b, op):
            return S.op(eng, lambda e: e.tensor_tensor(out.ap, a.ap, b.ap, op), reads=[a, b], writes=[out])

        def stt(out, a, sc, b, op0, op1):
            scv = sc.ap if isinstance(sc, Ref) else sc
            rds = [a, b] + ([sc] if isinstance(sc, Ref) else [])
            return S.op("dve", lambda e: e.scalar_tensor_tensor(out.ap, a.ap, scv, b.ap, op0, op1), reads=rds, writes=[out])

        def ts(eng, out, a, s1, s2, op0, op1=None):
            s1v = s1.ap if isinstance(s1, Ref) else s1
            s2v = s2.ap if isinstance(s2, Ref) else s2
            rds = [a] + [r for r in (s1, s2) if isinstance(r, Ref)]
            if op1 is None:
                return S.op(eng, lambda e: e.tensor_scalar(out.ap, a.ap, s1v, None, op0), reads=rds, writes=[out])
            return S.op(eng, lambda e: e.tensor_scalar(out.ap, a.ap, s1v, s2v, op0, op1), reads=rds, writes=[out])

        def cp(eng, out, in_):
            if eng == "act":
                return act(out, in_, AF.Copy)
            return S.op(eng, lambda e: e.tensor_copy(out.ap, in_.ap), reads=[in_], writes=[out])

        def recip(out, in_):
            return S.op("dve", lambda e: e.reciprocal(out.ap, in_.ap), reads=[in_], writes=[out])

        def mm(out, pairs, start=True, stop=True):
            n = len(pairs)

            def fn(e):
                ins = None
                for i, (l, r) in enumerate(pairs):
                    ins = e.matmul(out.ap, l.ap, r.ap, start=(start and i == 0), stop=(stop and i == n - 1))
                return ins
            rds = []
            for l, r in pairs:
                rds.append(l)
                rds.append(r)
            return S.op("pe", fn, reads=rds, writes=[out])

        def tr(out, in_, ident):
            return S.op("pe", lambda e: e.transpose(out.ap, in_.ap, ident.ap), reads=[in_, ident], writes=[out])

        def dma(eng, out, in_ap, stream, reads=()):
            return S.op(eng, lambda e: e.dma_start(out=out.ap, in_=in_ap), reads=list(reads), writes=[out], dma=stream)

        class Rot:
            def __init__(self, ids):
                self.ids = list(ids)
                self.i = 0

            def next(self):
                b = self.ids[self.i % len(self.ids)]
                self.i += 1
                return b

        rot = Rot(range(8))
        tog = [0]

        def evac_eng():
            tog[0] ^= 1
            return "act" if tog[0] else "dve"

        def wslot(n, shape):
            return T(WOFF[n % 4], shape, BF16)

        def kp(ap2d):
            return ap2d.rearrange("(k p) f -> p k f", p=128)

        loaders = []

        def L_full(src2d, c0, c1):
            def f(n):
                w = wslot(n, (8, c1 - c0))
                dma("pool", w[:, :, :], kp(src2d)[:, :, c0:c1], "W%da" % (n % 4))
            return f

        def L_mlp(l, gi):
            def f(n):
                wu = Tile(arena, WOFF[n % 4], (8, 512), BF16)
                wd = Tile(arena, WOFF[n % 4] + 8192, (4, 1024), BF16)
                dma("pool", wu[:, :, :], kp(w_up[l])[:, :, gi * 512:(gi + 1) * 512], "W%da" % (n % 4))
                dma("pool", wd[:, :, :], kp(w_down[l][gi * 512:(gi + 1) * 512, :]), "W%db" % (n % 4))
            return f

        def L_first(n):
            w = wslot(n, (8, 1024))
            for i, sfx in enumerate("abcd"):
                dma("pool", w[:, :, i * 256:(i + 1) * 256], kp(w_in_a[0])[:, :, i * 256:(i + 1) * 256], "W0" + sfx)
        loaders.append(L_first)
        loaders.append(L_full(w_in_a[0], 1024, 2048))
        loaders.append(L_full(w_out_a[0], 0, 1024))
        for gi in range(8):
            loaders.append(L_mlp(0, gi))
        loaders.append(L_full(w_ple_gate[0], 0, 1024))
        loaders += [None, None, None]
        for gi in range(8):
            loaders.append(L_mlp(1, gi))
        loaders.append(L_full(w_ple_gate[1], 0, 1024))
        issued = [0]

        def w_issue_upto(n):
            while issued[0] <= n and issued[0] < len(loaders):
                f = loaders[issued[0]]
                if f is not None:
                    f(issued[0])
                issued[0] += 1

        def w_done(n):
            w_issue_upto(n + 4)

        sreset()
        dma("sp", IDENT[:, :], c_ident, "c0")
        GST_A = T(97280, (128,), F32, parts=8)
        GST_B = T(97792, (128,), F32, parts=64)
        dma("sp", GST_A[0:8, :], ln_mix_a[0].rearrange("(k f) -> k f", f=128), "g0")
        dma("pool", CP[:, :], c_pack, "c2")
        dma("pool", BSROW[0:1, :], b_spatial[0].rearrange("(o g) t -> o (g t)", o=1), "c4")
        w_issue_upto(2)
        dma("pool", SSEL[:, :, :], c_ssel.rearrange("p (a b) -> p a b", a=16), "c3")
        dma("pool", WP[:, :, :], kp(w_ple_proj[0]), "wp")
        gi_ = 0
        for src in (ln_mlp[0], ln_ple[0], ln_kv, ln_mix_b[0], ln_mlp[1], ln_ple[1]):
            dma("act", GST_B[gi_ * 8:(gi_ + 1) * 8, :], src.rearrange("(k f) -> k f", f=128), "g%d" % (gi_ + 1))
            gi_ += 1
        gk2 = g_k.rearrange("(o f) -> o f", o=1)
        dma("act", GST_B[48:49, 0:64], gk2, "g7")
        dma("act", GST_B[48:49, 64:128], gk2, "g8")
        dma("act", GST_B[49:50, 0:64], g_q, "g9")
        dma("act", GST_B[49:50, 64:128], g_q, "g10")
        b = rot.next()
        tr(PB[b][:, 0:8], GST_A[0:8, :], IDENT[0:8, 0:8])
        cp("dve", GCOLS[:, 0:8], PB[b][:, 0:8])

        def gcols_rest():
            b = rot.next()
            tr(PB[b][:, 0:50], GST_B[0:50, :], IDENT[0:50, 0:50])
            cp("dve", GCOLS[:, 8:58], PB[b][:, 0:50])
            ts("dve", GCOLS[:, 58:59], GCOLS[:, 57:58], 0.125, None, ALU.mult)

        xs = [T(WOFF[3] + i * 4096, (1024,), F32) for i in range(4)]

        def xload(t):
            for blk in range(t * 4, t * 4 + 4):
                st = xs[blk % 4]
                dma("sp", st[:, :], x[blk * 128:(blk + 1) * 128, :], "x%d" % (blk % 4))
                for half in range(2):
                    b = rot.next()
                    for cc in range(4):
                        c = half * 4 + cc
                        tr(PB4[b][:, cc, :], st[:, c * 128:(c + 1) * 128], IDENT[:, :])
                    cp(evac_eng(), XT[:, half * 4:(half + 1) * 4, blk * 128:(blk + 1) * 128], PB4[b][:, :, :])

        xload(0)
        dma("sp", MASKLE[:, :], c_maskle, "c1")
        if stop_after < 1:
            for t in range(1, NT):
                xload(t)
            gcols_rest()

        def norm_A(t, sq):
            tl = slice(t * TT, (t + 1) * TT)
            for c in range(8):
                act(sq[:, c, :], XT[:, c, tl], AF.Square)

        def norm_B(t, gcol0, sq, sd, rstd, dst=HT):
            tl = slice(t * TT, (t + 1) * TT)
            b = rot.next()
            mm(PB[b][:, :], [(ONES_S(), sq[:, c, :]) for c in range(8)])
            act(sd[:, :], PB[b][:, :], AF.Ln, bias=EPS)
            act(rstd[:, :], sd[:, :], AF.Exp, scale=-0.5)
            for c in range(8):
                if gcol0 is None:
                    tt("dve", dst[:, c, tl], XT[:, c, tl], rstd[:, :], ALU.mult)
                else:
                    stt(dst[:, c, tl], XT[:, c, tl], GCOLS[:, gcol0 + c:gcol0 + c + 1], rstd[:, :], ALU.mult, ALU.mult)

        def norm_tile(t, gcol0, sq, sd, rstd, dst=HT):
            norm_A(t, sq)
            norm_B(t, gcol0, sq, sd, rstd, dst)

        if stop_after >= 1:
            sreset()
            uT = salloc((8, TT), BF16)
            sq = uT
            yT = salloc((8, TT), BF16)
            v16 = [salloc((1024,), BF16) for _ in range(2)]
            vn = [salloc((1024,), BF16) for _ in range(2)]
            sd = salloc((TT,), F32)
            rstd = salloc((TT,), F32)
            wsT = salloc((8, 128), BF16)
            gvt = salloc((1024,), F32)
            wsl = T(yT.off, (8, 128), F32)
            dma("sp", gvt[:, :], g_v_a.partition_broadcast(128), "gv")
            dma("sp", wsl[:, :, :], w_spatial[0].rearrange("g t s -> t g s"), "ws")
            def ws_prep():
                for hb in range(2):
                    b = rot.next()
                    for gg in range(4):
                        tr(PB4[b][:, gg, :], wsl[:, hb * 4 + gg, :], IDENT[:, :])
                    tt("dve", wsT[:, hb * 4:(hb + 1) * 4, :], PB4[b][:, :, :],
                       Ref(MASKLE.ap.unsqueeze(1).to_broadcast([128, 4, 128]), MASKLE[:, :].keys), ALU.mult)
            W0 = wslot(0, (8, 1024))
            W1 = wslot(1, (8, 1024))
            W2 = wslot(2, (8, 1024))
            norm_A(0, sq)
            norm_B(0, G_MIXA, sq, sd, rstd)
            for t in range(NT):
                tl = slice(t * TT, (t + 1) * TT)
                if t + 1 < NT:
                    xload(t + 1)
                    if t + 1 == NT - 1:
                        w_issue_upto(3)
                for f in range(8):
                    b = rot.next()
                    mm(PB[b][:, :], [(W0[:, k, f * 128:(f + 1) * 128], HT[:, k, tl]) for k in range(8)])
                    act(uT[:, f, :], PB[b][:, :], AF.Gelu_apprx_tanh)
                if t == 0:
                    gcols_rest()
                    ws_prep()

                def VST(blk, t=t):
                    tb = t * 4 + blk
                    tbl = slice(tb * 128, (tb + 1) * 128)
                    for half in range(2):
                        b = rot.next()
                        mm(PB[b][:, :], [(HT[:, k, tbl], W1[:, k, half * 512:(half + 1) * 512]) for k in range(8)])
                        act(v16[blk % 2][:, half * 512:(half + 1) * 512], PB[b][:, :], AF.Gelu_apprx_tanh)

                def NST(blk):
                    vb = vn[blk % 2]
                    vv = v16[blk % 2]
                    c4 = (blk % 2) * 4
                    act(vb[:, :], vv[:, :], AF.Square, accum=SMALL[:, c4:c4 + 1])
                    ts("dve", SMALL[:, c4 + 1:c4 + 2], SMALL[:, c4:c4 + 1], 1.0 / 1024, EPS, ALU.mult, ALU.add)
                    act(SMALL[:, c4 + 2:c4 + 3], SMALL[:, c4 + 1:c4 + 2], AF.Sqrt)
                    recip(SMALL[:, c4 + 3:c4 + 4], SMALL[:, c4 + 2:c4 + 3])
                    stt(vb[:, :], vv[:, :], SMALL[:, c4 + 3:c4 + 4], gvt[:, :], ALU.mult, ALU.mult)

                def SPST(blk):
                    vb = vn[blk % 2]
                    for hb in range(2):
                        b = rot.next()

                        def fn(e, b=b, hb=hb, vb=vb):
                            ins = None
                            for gg in range(4):
                                g = hb * 4 + gg
                                e.matmul(PB4[b][:, gg, :].ap, vb[:, g * 128:(g + 1) * 128].ap, wsT[:, g, :].ap,
                                         start=True, stop=False)
                                ins = e.matmul(PB4[b][:, gg, :].ap, ONES1(slice(0, 1)).ap,
                                               BSROW[0:1, g * 128:(g + 1) * 128].ap, start=False, stop=True)
                            return ins
                        S.op("pe", fn, reads=[vb[:, hb * 512:(hb + 1) * 512], wsT[:, hb * 4:(hb + 1) * 4, :],
                                              ONES1(slice(0, 1)), BSROW[0:1, :]], writes=[PB4[b][:, :, :]])
                        tt("dve", yT[:, hb * 4:(hb + 1) * 4, blk * 128:(blk + 1) * 128], PB4[b][:, :, :],
                           uT[:, hb * 4:(hb + 1) * 4, blk * 128:(blk + 1) * 128], ALU.mult)

                VST(0)
                VST(1)
                NST(0)
                for blk in range(4):
                    if blk + 2 < 4:
                        VST(blk + 2)
                    if blk + 1 < 4:
                        NST(blk + 1)
                    SPST(blk)
                if t + 1 < NT:
                    norm_A(t + 1, sq)
                for dch in range(8):
                    if dch == 4 and t + 1 < NT:
                        norm_B(t + 1, G_MIXA, sq, sd, rstd)
                    b = rot.next()
                    mm(PB[b][:, :], [(W2[:, k, dch * 128:(dch + 1) * 128], yT[:, k, :]) for k in range(8)])
                    tt("dve", XT[:, dch, tl], PB[b][:, :], XT[:, dch, tl], ALU.add)
            w_done(0)
            w_done(1)
            w_done(2)

        def mlp_phase(l, blk0, gcol0):
            sreset()
            sq = salloc((8, TT), BF16)
            sd = salloc((TT,), F32)
            rstd = salloc((TT,), F32)
            aT = [salloc((8, TT), BF16) for _ in range(2)]
            rt = [salloc((TT,), BF16) for _ in range(2)]
            norm_tile(0, gcol0, sq, sd, rstd)

            def wts(sg):
                nA, nB = blk0 + 2 * sg, blk0 + 2 * sg + 1
                wu = [Tile(arena, WOFF[n % 4], (8, 512), BF16) for n in (nA, nB)]
                wd = [Tile(arena, WOFF[n % 4] + 8192, (4, 1024), BF16) for n in (nA, nB)]
                return wu, wd

            def up(k):
                sg, t = divmod(k, NT)
                wu, wd = wts(sg)
                tl = slice(t * TT, (t + 1) * TT)
                a = aT[k % 2]
                for f in range(8):
                    b = rot.next()
                    mm(PB[b][:, :], [(wu[f // 4][:, kk, (f % 4) * 128:(f % 4 + 1) * 128], HT[:, kk, tl]) for kk in range(8)])
                    r = rt[f % 2]
                    act(r[:, :], PB[b][:, :], AF.Relu)
                    tt("dve", a[:, f, :], r[:, :], r[:, :], ALU.mult)
                if sg == 0 and t + 1 < NT:
                    norm_A(t + 1, sq)

            def down(k):
                sg, t = divmod(k, NT)
                wu, wd = wts(sg)
                tl = slice(t * TT, (t + 1) * TT)
                a = aT[k % 2]
                for dch in range(8):
                    if dch == 4 and sg == 0 and t + 1 < NT:
                        norm_B(t + 1, gcol0, sq, sd, rstd)
                    b = rot.next()
                    mm(PB[b][:, :], [(wd[f // 4][:, f % 4, dch * 128:(dch + 1) * 128], a[:, f, :]) for f in range(8)])
                    tt("dve", XT[:, dch, tl], PB[b][:, :], XT[:, dch, tl], ALU.add)
                if t == NT - 1:
                    w_done(blk0 + 2 * sg)
                    w_done(blk0 + 2 * sg + 1)

            NI = 4 * NT
            for k in range(NT):
                up(k)
                down(k)
            for k in range(NT, NI):
                up(k)
                if k > NT:
                    down(k - 1)
            down(NI - 1)

        out_fin = []

        def emit_out(t, osb):
            for blk in range(t * 4, t * 4 + 4):
                ot = osb[blk % 2]
                for half in range(2):
                    b = rot.next()
                    for cc in range(4):
                        c = half * 4 + cc
                        tr(PB4[b][:, cc, :], XT[:, c, blk * 128:(blk + 1) * 128], IDENT[:, :])
                    cp(evac_eng(), ot[:, half * 512:(half + 1) * 512], PB[b][:, :])
                out_fin.append(S.op("sp", lambda e, ot=ot, blk=blk: e.dma_start(out=y[blk * 128:(blk + 1) * 128, :], in_=ot[:, :].ap),
                                    reads=[ot[:, :]], dma="o%d" % (blk % 2)))

        def ple_phase(l, blkn, gcol0, with_out=False):
            sreset()
            sq = salloc((8, TT), BF16)
            sd = salloc((TT,), F32)
            rstd = salloc((TT,), F32)
            pst = salloc((4, 256), F32)
            pT = salloc((2, TT), BF16)
            gate = [salloc((TT,), F32) for _ in range(2)]
            tmp2 = [salloc((TT,), F32) for _ in range(2)]
            osb = [salloc((1024,), F32) for _ in range(2)] if with_out else None
            Wg = wslot(blkn, (8, 1024))
            def pload(t):
                dma("sp", pst[:, :, :], p[l][t * TT:(t + 1) * TT, :].rearrange("(b q) f -> q b f", q=128), "pl")
            pload(0)
            for t in range(NT):
                tl = slice(t * TT, (t + 1) * TT)
                if t == 0:
                    norm_tile(0, gcol0, sq, sd, rstd)
                for kk in range(2):
                    b = rot.next()
                    for blk in range(4):
                        tr(PB4[b][:, blk, :], pst[:, blk, kk * 128:(kk + 1) * 128], IDENT[:, :])
                    cp("act", pT[:, kk, :], PB[b][:, :])
                if t + 1 < NT:
                    pload(t + 1)
                for dch in range(8):
                    if dch == 1 and t + 1 < NT:
                        norm_A(t + 1, sq)
                    if dch == 5 and t + 1 < NT:
                        norm_B(t + 1, gcol0, sq, sd, rstd)
                    b1 = rot.next()
                    mm(PB[b1][:, :], [(Wg[:, k, dch * 128:(dch + 1) * 128], HT[:, k, tl]) for k in range(8)])
                    g_ = gate[dch % 2]
                    act(g_[:, :], PB[b1][:, :], AF.Sigmoid)
                    b2 = rot.next()
                    mm(PB[b2][:, :], [(WP[:, kk, dch * 128:(dch + 1) * 128], pT[:, kk, :]) for kk in range(2)])
                    t2 = tmp2[dch % 2]
                    tt("dve", t2[:, :], PB[b2][:, :], g_[:, :], ALU.mult)
                    tt("dve", XT[:, dch, tl], XT[:, dch, tl], t2[:, :], ALU.add)
                if with_out:
                    emit_out(t, osb)
            w_done(blkn)

        PW = [Tile(arena, WOFF[0] + i * 8192, (4096,), BF16) for i in range(2)]

        def pw_views(i):
            t_ = PW[i]
            off = t_.off
            wk = Tile(arena, off, (8, 128), BF16)
            wv = Tile(arena, off + 2048, (8, 128), BF16)
            wq = Tile(arena, off + 4096, (8, 128), BF16)
            wo = Tile(arena, off + 6144, (1024,), BF16)
            return wk, wv, wq, wo

        def load_pair(j):
            wk, wv, wq, wo = pw_views(j % 2)
            sl = slice(j * 128, (j + 1) * 128)
            dma("pool", wk[:, :, :], kp(w_kv)[:, :, j * 128:(j + 1) * 128], "pw%da" % (j % 2))
            dma("pool", wv[:, :, :], kp(w_kv)[:, :, 1024 + j * 128:1024 + (j + 1) * 128], "pw%db" % (j % 2))
            dma("pool", wq[:, :, :], kp(w_q[0])[:, :, j * 128:(j + 1) * 128], "pw%dc" % (j % 2))
            dma("pool", wo[:, :], w_out_b[0][sl, :], "pw%dd" % (j % 2))
            for w_, g0 in ((wk, G_KV), (wv, G_KV), (wq, G_MIXB)):
                gb = Ref(GCOLS.ap[:, g0:g0 + 8].unsqueeze(2).to_broadcast([128, 8, 128]), GCOLS[:, g0:g0 + 8].keys)
                tt("dve", w_[:, :, :], w_[:, :, :], gb, ALU.mult)

        if stop_after >= 2:
            mlp_phase(0, 3, G_MLP0)
            if stop_after >= 4:
                load_pair(0)
        if stop_after >= 3:
            ple_phase(0, 11, G_PLE0)
            dma("pool", WP[:, :, :], kp(w_ple_proj[1]), "wp")

        if stop_after >= 4:
            sreset()
            sq = salloc((8, TT), BF16)
            sd = salloc((TT,), F32)
            rstd = salloc((TT,), F32)
            psq, psd, prstd = sq, sd, rstd
            sreset()
            KT = salloc((SEQ,), BF16)
            QTh = [salloc((SEQ,), BF16) for _ in range(2)]
            VJ = salloc((16, 128), BF16)
            OT = salloc((SEQ,), BF16)
            e1 = [salloc((TT,), F32) for _ in range(2)]
            AT = [salloc((TT,), BF16) for _ in range(3)]
            csb = [salloc((TT,), BF16) for _ in range(2)]
            LS = [Tile(arena, WOFF[1], (16, TT), BF16), Tile(arena, WOFF[2], (16, TT), BF16)]
            sqK = Tile(arena, WOFF[2], (4, TT), BF16)
            sqQ = Tile(arena, WOFF[2] + 4096, (4, TT), BF16)
            sd4 = Tile(arena, WOFF[2] + 8192, (4, TT), F32)
            rawK = Tile(arena, WOFF[1], (4, TT), F32)
            rawQ = Tile(arena, WOFF[1] + 8192, (4, TT), F32)
            srot = Rot(range(5))
            zrot = Rot([0, 1])
            grot = Rot([2, 3])
            frot = Rot([6, 7])
            JUNK = 7

            def dummy(k):
                def fn(e):
                    ins = None
                    for _ in range(k):
                        ins = e.matmul(PB[JUNK][:, :].ap, ONES_S().ap, CP[:, 0:512].ap, start=True, stop=True)
                    return ins
                S.op("pe", fn, reads=[ONES_S(), CP[:, 0:512]], writes=[PB[JUNK][:, :]])
            CSB = 4
            OB = [5, 5]

            pending = []
            for j in range(8):
                wk, wv, wq, wo = pw_views(j % 2)
                KB = [0, 1, 2, 3]
                QB = [4, 5, 6, 7]
                def kq_main_t(w_, raw, sqx, mb, t):
                    tl = slice(t * TT, (t + 1) * TT)
                    mm(PB[mb[t]][:, :], [(w_[:, k, :], HT[:, k, tl]) for k in range(8)])
                    act(raw[:, t, :], PB[mb[t]][:, :], AF.Copy)
                    act(sqx[:, t, :], PB[mb[t]][:, :], AF.Square)

                if j == 0:
                    norm_A(0, psq)
                    norm_B(0, None, psq, psd, prstd)
                    for t in range(NT):
                        if t + 1 < NT:
                            norm_A(t + 1, psq)
                        kq_main_t(wk, rawK, sqK, KB, t)
                        kq_main_t(wq, rawQ, sqQ, QB, t)
                        if t + 1 < NT:
                            norm_B(t + 1, None, psq, psd, prstd)
                    for bi in range(2):
                        S.op("dve", lambda e, bi=bi: e.memset(csb[bi][:, :].ap, 0.0), writes=[csb[bi][:, :]])
                        S.op("dve", lambda e, bi=bi: e.memset(QTh[bi][:, :].ap, 0.0), writes=[QTh[bi][:, :]])
                else:
                    for (w_, raw, sqx, mb) in ((wk, rawK, sqK, KB), (wq, rawQ, sqQ, QB)):
                        for t in range(NT):
                            kq_main_t(w_, raw, sqx, mb, t)
                for (sqx, mb) in ((sqK, KB), (sqQ, QB)):
                    for t in range(NT):
                        mm(PB[mb[t]][:, :], [(BDIAG(), sqx[:, t, :])])
                for t in range(NT):
                    act(sd4[:, t, :], PB[KB[t]][:, :], AF.Ln, bias=EPS)
                for t in range(NT):
                    act(sd4[:, t, :], sd4[:, t, :], AF.Exp, scale=-0.5)
                for t in range(NT):
                    tl = slice(t * TT, (t + 1) * TT)
                    stt(KT[:, tl], rawK[:, t, :], GCOLS[:, G_K:G_K + 1], sd4[:, t, :], ALU.mult, ALU.mult)
                for q4 in range(4):
                    b_ = KB[q4]
                    for bb in range(4):
                        blk = q4 * 4 + bb
                        mm(PB4[b_][:, bb, :], [(HT[:, k, blk * 128:(blk + 1) * 128], wv[:, k, :]) for k in range(8)])
                    cp("dve", VJ[:, q4 * 4:(q4 + 1) * 4, :], PB4[b_][:, :, :])
                for t in range(NT):
                    act(sd4[:, t, :], PB[QB[t]][:, :], AF.Ln, bias=EPS)
                for t in range(NT):
                    act(sd4[:, t, :], sd4[:, t, :], AF.Exp, scale=-0.5)
                for t in range(NT):
                    tl = slice(t * TT, (t + 1) * TT)
                    for h_ in range(2):
                        hp = slice(h_ * 64, (h_ + 1) * 64)
                        stt(QTh[h_][hp, tl], rawQ[hp, t, :], GCOLS[hp, G_QS:G_QS + 1], sd4[hp, t, :], ALU.mult, ALU.mult)
                heads = (slice(0, 64), slice(64, 128))

                def cols(qt, kb):
                    c0 = max(0, kb - 4 * qt) * 128
                    return c0, slice(c0, TT), slice(qt * TT + c0, (qt + 1) * TT)

                def p1_steps(h, qt):
                    nkb = 4 * qt + 4
                    L = LS[h]
                    zb = {}

                    def Z(kb):
                        c0, cs, qs = cols(qt, kb)
                        zb[kb] = zrot.next()
                        if kb >= 4 * qt:
                            b = zb[kb]

                            def fn(e, b=b, kb=kb, c0=c0, cs=cs, qs=qs):
                                e.matmul(PB[b][:, cs].ap, KT[:, kb * 128:(kb + 1) * 128].ap, QTh[h][:, qs].ap, start=True, stop=False)
                                return e.matmul(PB[b][:, c0:c0 + 128].ap, IDB().ap, MBIAS().ap, start=False, stop=True)
                            S.op("pe", fn, reads=[KT[:, kb * 128:(kb + 1) * 128], QTh[h][:, qs], IDB(), MBIAS()], writes=[PB[b][:, cs]])
                        else:
                            mm(PB[zb[kb]][:, cs], [(KT[:, kb * 128:(kb + 1) * 128], QTh[h][:, qs])])

                    def EXP(kb):
                        c0, cs, qs = cols(qt, kb)
                        act(e1[kb % 2][:, cs], PB[zb[kb]][:, cs], AF.Exp)

                    def LN(kb):
                        c0, cs, qs = cols(qt, kb)
                        act(L[:, kb, cs], e1[kb % 2][:, cs], AF.Ln, bias=1.0)

                    def CS(kb):
                        c0, cs, qs = cols(qt, kb)
                        mm(PB[CSB][:, cs], [(ESEL(kb), L[:, kb, cs])], start=(kb == 0), stop=(kb == nkb - 1))

                    steps = []
                    for i in range(nkb + 3):
                        def st(i=i):
                            if i < nkb:
                                Z(i)
                            if 0 <= i - 1 < nkb:
                                EXP(i - 1)
                            if 0 <= i - 2 < nkb:
                                LN(i - 2)
                            if 0 <= i - 3 < nkb:
                                CS(i - 3)
                        steps.append(st)

                    def fin():
                        cp("dve", csb[h][0:16, :], PB[CSB][0:16, :])
                    steps.append(fin)
                    return steps

                def p2_steps(h, qt):
                    nkb = 4 * qt + 4
                    L = LS[h]
                    pr = heads[h]
                    gb = {}

                    def G(kb):
                        c0, cs, qs = cols(qt, kb)
                        b = gb[kb] = grot.next()

                        def fn(e):
                            e.matmul(PB[b][:, cs].ap, SSEL[:, kb, :].ap, csb[h][:, cs].ap, start=True, stop=False)
                            e.matmul(PB[b][:, cs].ap, TRINEG().ap, L[:, kb, cs].ap, start=False, stop=False)
                            if kb >= 4 * qt:
                                e.matmul(PB[b][:, c0:c0 + 128].ap, IDB().ap, MBIAS().ap, start=False, stop=False)
                            return e.matmul(PB[b][:, cs].ap, KT[:, kb * 128:(kb + 1) * 128].ap, QTh[h][:, qs].ap,
                                            start=False, stop=True)
                        S.op("pe", fn, reads=[TRINEG(), L[:, kb, cs], SSEL[:, kb, :], csb[h][:, cs],
                                              KT[:, kb * 128:(kb + 1) * 128], QTh[h][:, qs]], writes=[PB[b][:, cs]])

                    def EXPA(kb):
                        c0, cs, qs = cols(qt, kb)
                        a_ = AT[kb % 3]
                        act(a_[:, cs], PB[gb[kb]][:, cs], AF.Exp)

                    def AV(kb):
                        c0, cs, qs = cols(qt, kb)
                        mm(PB[OB[h]][:, cs], [(VJ[:, kb, :], AT[kb % 3][:, cs])], start=(kb == 0), stop=(kb == nkb - 1))

                    steps = []
                    for i in range(nkb + 1):
                        def st(i=i):
                            if i < nkb:
                                G(i)
                            if 0 <= i - 1 < nkb:
                                EXPA(i - 1)
                                AV(i - 1)
                        steps.append(st)

                    def fin():
                        cp("dve", OT[pr, qt * TT:(qt + 1) * TT], PB[OB[h]][pr, :])
                    steps.append(fin)
                    return steps

                def filler(k):
                    for _ in range(k):
                        if not pending:
                            return
                        wo_, qt_, dch = pending.pop(0)
                        tl = slice(qt_ * TT, (qt_ + 1) * TT)
                        b = frot.next()
                        mm(PB[b][:, :], [(wo_[:, dch * 128:(dch + 1) * 128], OT[:, tl])])
                        tt("dve", XT[:, dch, tl], PB[b][:, :], XT[:, dch, tl], ALU.add)

                units = [(h, qt) for qt in range(NT) for h in range(2)]
                prev = None
                for slot_i, u in enumerate(units + [None]):
                    if slot_i == 4 and j + 1 < 8:
                        while pending and pending[0][0] is not wo:
                            filler(1)
                        load_pair(j + 1)
                    s1 = p1_steps(*u) if u is not None else []
                    s2 = p2_steps(*prev) if prev is not None else []
                    n = max(len(s1), len(s2))
                    for i in range(n):
                        if i < len(s2):
                            s2[i]()
                        if i < len(s1):
                            s1[i]()
                        if i == n // 2 or i == n - 1:
                            filler(2)
                    if prev is not None and prev[0] == 1:
                        pending.extend((wo, prev[1], dch) for dch in range(8))
                    prev = u
                if j == 7:
                    filler(len(pending))
            w_done(12)
            w_done(13)
            w_done(14)

        if stop_after >= 5:
            w_issue_upto(18)
            mlp_phase(1, 15, G_MLP1)
        if stop_after >= 6:
            ple_phase(1, 23, G_PLE1, with_out=True)

        if not out_fin:
            sreset()
            osb = [salloc((1024,), F32) for _ in range(2)]
            for t in range(NT):
                emit_out(t, osb)
        S.emit(final_waits=out_fin[-2:])
    return nc


def make_consts():
    ident = np.eye(128, dtype=np.float32)
    s = np.arange(128)[:, None]
    t = np.arange(128)[None, :]
    maskle = (s <= t).astype(np.float32)
    pack = np.zeros((128, 1040), np.float32)
    pack[:, 784:912] = np.eye(128, dtype=np.float32)
    pack[:, 912:1040] = -30000.0 * (s >= t).astype(np.float32)
    pack[:, 0:128] = 1.0 / 1024
    pack[:, 128:256] = 1.0
    bd = np.zeros((128, 128), np.float32)
    bd[0:64, 0:64] = 1.0 / 64
    bd[64:128, 64:128] = 1.0 / 64
    pack[:, 256:384] = bd
    pack[:, 384:512] = -(s >= t).astype(np.float32)
    pack[:, 512:640] = (s < t).astype(np.float32)
    pack[:, 640 + 15] = -1.0
    pack[:, 640 + 47] = -1.0
    ssel = np.zeros((128, 16, 128), np.float32)
    for kb in range(16):
        for r in range(16):
            if r > kb:
                ssel[r, kb, :] = 1.0
                ssel[32 + r, kb, :] = 1.0
    return {"c_ident": ident, "c_maskle": maskle, "c_pack": pack, "c_ssel": ssel.reshape(128, 2048)}


_CACHE = {}


def kernel(**inputs):
    n = 8
    if "nc" not in _CACHE:
        nc = bass.Bass("TRN2", target_bir_lowering=False)
        build_program(nc)
        _CACHE["nc"] = nc
    nc = _CACHE["nc"]
    consts = make_consts()
    shared = {k: np.ascontiguousarray(np.asarray(v, dtype=np.float32)) for k, v in inputs.items() if k not in ("x", "p")}
    shared.update(consts)
    x = np.asarray(inputs["x"], dtype=np.float32)
    p = np.asarray(inputs["p"], dtype=np.float32)
    in_maps = []
    for i in range(n):
        m = dict(shared)
        m["x"] = np.ascontiguousarray(x[i])
        m["p"] = np.ascontiguousarray(p[:, i])
        in_maps.append(m)
    res = run_bass_kernel_spmd(nc, in_maps, core_ids=list(range(n)))
    return np.stack([np.asarray(r["y"], dtype=np.float32) for r in res.results], axis=0)
```
